# Optimizing a Trainium2 kernel written in Bass

```python
import jax, jax.numpy as jnp
from jax import lax
import numpy as np

D_MODEL = 1024
BATCH = 16
SEQ = 2048
DEPTH = 1
DEC_BATCH = 16
DEC_SEQ = 64
PAST_LEN = 2048

CHUNK = 64
EPS = 1e-6
D_CONV_A = D_MODEL
CONV_A_WIDTH = 3
SSM_EXPAND = 2
D_INNER = SSM_EXPAND * D_MODEL
SSM_HEAD_DIM = 64
SSM_HEADS = D_INNER // SSM_HEAD_DIM
SSM_GROUPS = 8
HEADS_PER_GROUP = SSM_HEADS // SSM_GROUPS
SSM_STATE = 128
SSM_CONV_WIDTH = 4
SSM_CONV_DIM = D_INNER + 2 * SSM_GROUPS * SSM_STATE
D_FF = 4 * D_MODEL
N_BRANCHES = 2
OFF_A_B = 0
OFF_A_C = OFF_A_B + D_CONV_A
OFF_A_H = OFF_A_C + D_CONV_A
OFF_Z = OFF_A_H + D_CONV_A
OFF_XBC = OFF_Z + D_INNER
OFF_DT = OFF_XBC + SSM_CONV_DIM
OFF_GATE = OFF_DT + SSM_HEADS
PROJ_DIM = OFF_GATE + N_BRANCHES * D_MODEL

kernel_name = 'hybrid_shortconv_ssd_stream_step'

PARAM_NAMES = ('norm_mix_pre', 'w_in', 'b_gate', 'conv_a_w', 'w_a_out', 'conv_ssm_w', 'conv_ssm_b', 'dt_bias', 'a_log', 'd_skip', 'ssm_norm', 'w_ssm_out', 'w_o', 'norm_mix_post', 'norm_ffn_pre', 'w_ff1', 'w_ff2', 'norm_ffn_post')


def rms_norm(x, w):
    xf = x.astype(jnp.float32)
    xf = xf * lax.rsqrt(jnp.mean(xf * xf, axis=-1, keepdims=True) + EPS)
    return (xf * w.astype(jnp.float32)).astype(x.dtype)


def causal_dwconv(u, hist, w):
    k = w.shape[0]
    l = u.shape[1]
    full = jnp.concatenate([hist.astype(u.dtype), u], axis=1)
    out = full[:, 0:l] * w[0]
    for i in range(1, k):
        out = out + full[:, i:i + l] * w[i]
    return out, full[:, l:]


def segsum_exp(a):
    t = a.shape[-1]
    cs = jnp.cumsum(a, axis=-1)
    diff = cs[..., :, None] - cs[..., None, :]
    mask = jnp.tril(jnp.ones((t, t), dtype=bool))
    return jnp.where(mask, jnp.exp(jnp.where(mask, diff, 0.0)), 0.0)


def ssd_scan(x, dt, a, bm, cm, h0):
    b, l = x.shape[0], x.shape[1]
    cs = CHUNK if l % CHUNK == 0 else l
    nc = l // cs
    g, j, p, n = SSM_GROUPS, HEADS_PER_GROUP, SSM_HEAD_DIM, SSM_STATE
    xc = (x * dt[..., None]).reshape(b, nc, cs, g, j, p)
    adt = (dt * a).reshape(b, nc, cs, g, j)
    bc = bm.reshape(b, nc, cs, g, n)
    cc = cm.reshape(b, nc, cs, g, n)
    a_cum = jnp.cumsum(adt, axis=2)
    lmat = segsum_exp(jnp.moveaxis(adt, 2, -1))
    cb = jnp.einsum('bclgn,bcsgn->bcgls', cc, bc)
    y_diag = jnp.einsum('bcgjls,bcsgjp->bclgjp', cb[:, :, :, None] * lmat, xc)
    decay_in = jnp.exp(a_cum[:, :, -1:] - a_cum)
    chunk_states = jnp.einsum('bcsgn,bcsgj,bcsgjp->cbgjpn', bc, decay_in, xc)
    chunk_decay = jnp.moveaxis(jnp.exp(a_cum[:, :, -1]), 1, 0)

    def step(h, inp):
        s_c, d_c = inp
        return d_c[..., None, None] * h + s_c, h

    h_last, h_prev = lax.scan(step, h0.reshape(b, g, j, p, n), (chunk_states, chunk_decay))
    y_off = jnp.einsum('bclgn,cbgjpn,bclgj->bclgjp', cc, h_prev, jnp.exp(a_cum))
    y = (y_diag + y_off).reshape(b, l, SSM_HEADS, p)
    return y, h_last.reshape(b, SSM_HEADS, p, n)


def layer(x, conv_a_hist, conv_m_hist, ssm_h, prm):
    f32 = jnp.float32
    b, l = x.shape[0], x.shape[1]
    xn = rms_norm(x, prm['norm_mix_pre'])
    proj = xn @ prm['w_in']
    a_b = proj[..., OFF_A_B:OFF_A_C]
    a_c = proj[..., OFF_A_C:OFF_A_H]
    a_h = proj[..., OFF_A_H:OFF_Z]
    z = proj[..., OFF_Z:OFF_XBC]
    xbc = proj[..., OFF_XBC:OFF_DT]
    dt_raw = proj[..., OFF_DT:OFF_GATE]
    gates = jax.nn.sigmoid(proj[..., OFF_GATE:] + prm['b_gate'])
    conv_a, conv_a_new = causal_dwconv(a_c * a_h, conv_a_hist, prm['conv_a_w'])
    p_a = (a_b * conv_a) @ prm['w_a_out']
    xbc_c, conv_m_new = causal_dwconv(xbc, conv_m_hist, prm['conv_ssm_w'])
    xbc_c = jax.nn.silu(xbc_c + prm['conv_ssm_b'])
    xs = xbc_c[..., :D_INNER].reshape(b, l, SSM_HEADS, SSM_HEAD_DIM).astype(f32)
    gn = SSM_GROUPS * SSM_STATE
    bm = xbc_c[..., D_INNER:D_INNER + gn].reshape(b, l, SSM_GROUPS, SSM_STATE).astype(f32)
    cm = xbc_c[..., D_INNER + gn:].reshape(b, l, SSM_GROUPS, SSM_STATE).astype(f32)
    dt = jax.nn.softplus(dt_raw.astype(f32) + prm['dt_bias'].astype(f32))
    a = -jnp.exp(prm['a_log'].astype(f32))
    y, ssm_new = ssd_scan(xs, dt, a, bm, cm, ssm_h.astype(f32))
    y = y + prm['d_skip'].astype(f32)[:, None] * xs
    y = y.reshape(b, l, D_INNER) * jax.nn.silu(z.astype(f32))
    yg = y.reshape(b, l, SSM_GROUPS, D_INNER // SSM_GROUPS)
    yg = yg * lax.rsqrt(jnp.mean(yg * yg, axis=-1, keepdims=True) + EPS)
    y = (yg.reshape(b, l, D_INNER) * prm['ssm_norm'].astype(f32)).astype(x.dtype)
    p_b = y @ prm['w_ssm_out']
    merged = gates[..., :D_MODEL] * p_a + gates[..., D_MODEL:] * p_b
    h = x + rms_norm(merged @ prm['w_o'], prm['norm_mix_post'])
    f = rms_norm(h, prm['norm_ffn_pre']) @ prm['w_ff1']
    f = jnp.square(jax.nn.relu(f)) @ prm['w_ff2']
    out = h + rms_norm(f, prm['norm_ffn_post'])
    return out, conv_a_new, conv_m_new, ssm_new.astype(ssm_h.dtype)


def setup_inputs(seed: int = 0) -> dict:
    key = jax.random.key(seed)
    ks = jax.random.split(key, 32)
    nrm = jax.random.normal
    f32 = jnp.float32
    dt0 = jnp.exp(jax.random.uniform(ks[10], (DEPTH, SSM_HEADS), f32, np.log(1e-3), np.log(1e-1)))
    return {
        'x_prompt': nrm(ks[0], (BATCH, SEQ, D_MODEL), f32),
        'x_sample': nrm(ks[1], (DEC_BATCH, DEC_SEQ, D_MODEL), f32),
        'state_conv_a': nrm(ks[2], (DEPTH, DEC_BATCH, CONV_A_WIDTH - 1, D_CONV_A), f32),
        'state_conv_ssm': nrm(ks[3], (DEPTH, DEC_BATCH, SSM_CONV_WIDTH - 1, SSM_CONV_DIM), f32),
        'state_ssm': 0.1 * nrm(ks[4], (DEPTH, DEC_BATCH, SSM_HEADS, SSM_HEAD_DIM, SSM_STATE), f32),
        'norm_mix_pre': 1.0 + 0.02 * nrm(ks[5], (DEPTH, D_MODEL), f32),
        'w_in': nrm(ks[6], (DEPTH, D_MODEL, PROJ_DIM), f32) * D_MODEL ** -0.5,
        'b_gate': 0.01 * nrm(ks[7], (DEPTH, N_BRANCHES * D_MODEL), f32),
        'conv_a_w': nrm(ks[8], (DEPTH, CONV_A_WIDTH, D_CONV_A), f32) * CONV_A_WIDTH ** -0.5,
        'w_a_out': nrm(ks[9], (DEPTH, D_CONV_A, D_MODEL), f32) * D_CONV_A ** -0.5,
        'conv_ssm_w': nrm(ks[11], (DEPTH, SSM_CONV_WIDTH, SSM_CONV_DIM), f32) * SSM_CONV_WIDTH ** -0.5,
        'conv_ssm_b': 0.01 * nrm(ks[12], (DEPTH, SSM_CONV_DIM), f32),
        'dt_bias': dt0 + jnp.log(-jnp.expm1(-dt0)),
        'a_log': jnp.log(jax.random.uniform(ks[13], (DEPTH, SSM_HEADS), f32, 1.0, 16.0)),
        'd_skip': 1.0 + 0.02 * nrm(ks[14], (DEPTH, SSM_HEADS), f32),
        'ssm_norm': 1.0 + 0.02 * nrm(ks[15], (DEPTH, D_INNER), f32),
        'w_ssm_out': nrm(ks[16], (DEPTH, D_INNER, D_MODEL), f32) * D_INNER ** -0.5,
        'w_o': nrm(ks[17], (DEPTH, D_MODEL, D_MODEL), f32) * D_MODEL ** -0.5,
        'norm_mix_post': 1.0 + 0.02 * nrm(ks[18], (DEPTH, D_MODEL), f32),
        'norm_ffn_pre': 1.0 + 0.02 * nrm(ks[19], (DEPTH, D_MODEL), f32),
        'w_ff1': nrm(ks[20], (DEPTH, D_MODEL, D_FF), f32) * D_MODEL ** -0.5,
        'w_ff2': nrm(ks[21], (DEPTH, D_FF, D_MODEL), f32) * D_FF ** -0.5,
        'norm_ffn_post': 1.0 + 0.02 * nrm(ks[22], (DEPTH, D_MODEL), f32),
    }


def reference(x_prompt, x_sample, state_conv_a, state_conv_ssm, state_ssm, norm_mix_pre, w_in, b_gate, conv_a_w, w_a_out, conv_ssm_w, conv_ssm_b, dt_bias, a_log, d_skip, ssm_norm, w_ssm_out, w_o, norm_mix_post, norm_ffn_pre, w_ff1, w_ff2, norm_ffn_post):
    weights = (norm_mix_pre, w_in, b_gate, conv_a_w, w_a_out, conv_ssm_w, conv_ssm_b, dt_bias, a_log, d_skip, ssm_norm, w_ssm_out, w_o, norm_mix_post, norm_ffn_pre, w_ff1, w_ff2, norm_ffn_post)
    bp = x_prompt.shape[0]
    dtp = x_prompt.dtype
    yp, ys = x_prompt, x_sample
    pa, pm, ps, sa, sm, ss = [], [], [], [], [], []
    for i in range(DEPTH):
        prm = {name: arr[i] for name, arr in zip(PARAM_NAMES, weights)}
        zero_a = jnp.zeros((bp, CONV_A_WIDTH - 1, D_CONV_A), dtp)
        zero_m = jnp.zeros((bp, SSM_CONV_WIDTH - 1, SSM_CONV_DIM), dtp)
        zero_s = jnp.zeros((bp, SSM_HEADS, SSM_HEAD_DIM, SSM_STATE), state_ssm.dtype)
        yp, ca_p, cm_p, s_p = layer(yp, zero_a, zero_m, zero_s, prm)
        ys, ca_s, cm_s, s_s = layer(ys, state_conv_a[i], state_conv_ssm[i], state_ssm[i], prm)
        pa.append(ca_p); pm.append(cm_p); ps.append(s_p)
        sa.append(ca_s); sm.append(cm_s); ss.append(s_s)
    return (yp, ys, jnp.stack(pa), jnp.stack(pm), jnp.stack(ps), jnp.stack(sa), jnp.stack(sm), jnp.stack(ss))
```

```python
import numpy as np
import ml_dtypes
from contextlib import ExitStack
import concourse.bass as bass
import concourse.mybir as mybir
from concourse.bass_utils import run_bass_kernel_spmd

F32 = mybir.dt.float32
BF16 = mybir.dt.bfloat16
AF = mybir.ActivationFunctionType
ALU = mybir.AluOpType
AX = mybir.AxisListType

NCORES = 8
D = 1024
SEQ = 2048
DSEQ = 64
PROJ = 11296
OFF_AB, OFF_AC, OFF_AH, OFF_Z, OFF_XBC, OFF_DT, OFF_GATE = 0, 1024, 2048, 3072, 5120, 9216, 9248
EPS = 1e-6
NS = 4

PF_CAW = 0
PF_CSW = 24
PF_CSB = 152
PF_BG = 184
PF_NMP = 200
PF_NFP = 208
PF_SSN = 216
PF_DTB = 232
PF_ALOG = 264
PF_DSK = 296
PF_NMPOST = 328
PF_NFPOST = 1352
PF_COLS = 2376
CB_ID = 0
CB_TRI2 = 128
CB_BONES = 192
CB_ONESA = 320
CB_ONESB = 448
CB_TRI128 = 576
CB_NEG = 704
CB_COLS = 1728


class Res:
    __slots__ = ("name", "lw", "rd", "ov", "dsem", "dcnt", "rng", "excl")

    def __init__(self, name):
        self.name = name
        self.lw = None
        self.rd = {}
        self.ov = []
        self.dsem = None
        self.dcnt = 0
        self.rng = None
        self.excl = False


class KB:
    ENG = ("pe", "act", "dve", "pool")

    def __init__(self, nc, es):
        self.nc = nc
        self.es = es
        self.streams = {k: [] for k in ("pe", "act", "dve", "pool", "sp")}
        self.sem = {e: es.enter_context(nc.semaphore("s_" + e)) for e in self.ENG}
        self.tick = {e: 0 for e in self.ENG}
        self.waited = {k: {} for k in self.streams}
        self.dsems = []
        self.all_res = []

    def res(self, name):
        r = Res(name)
        self.all_res.append(r)
        return r

    def _dsem(self, r):
        if r.dsem is None:
            r.dsem = self.es.enter_context(self.nc.semaphore("d%d" % len(self.dsems)))
            self.dsems.append(r)
        return r.dsem

    def _deps(self, eng, reads, writes):
        deps = []
        for r in reads:
            if r.lw is not None:
                deps.append(r.lw)
            if r.excl:
                for q, t in r.rd.items():
                    if q != eng:
                        deps.append((q, t))
        for w in writes:
            for x in [w] + w.ov:
                if x.lw is not None and x.lw[0] != eng:
                    deps.append(x.lw)
                for q, t in x.rd.items():
                    if q != eng:
                        deps.append((q, t))
        wd = self.waited[eng]
        for q, t in deps:
            if wd.get(q, 0) < t:
                wd[q] = t
                self.streams[eng].append(("w", q, t))

    def wait(self, eng, q, t):
        wd = self.waited[eng]
        if wd.get(q, 0) < t:
            wd[q] = t
            self.streams[eng].append(("w", q, t))

    def op(self, eng, fn, reads=(), writes=(), inc=True):
        self._deps(eng, reads, writes)
        t = self.tick[eng] + 1
        if inc:
            self.tick[eng] = t
        self.streams[eng].append(("op", fn, inc))
        for r in reads:
            r.rd[eng] = t
        for w in writes:
            w.lw = (eng, t)
            w.rd = {}
        return t

    def dma(self, eng, out, in_, dres, reads=(), writes=()):
        self._deps(eng, reads, writes)
        self._dsem(dres)
        dres.dcnt += 16
        key = (id(dres), dres.dcnt)
        q = ("d", id(dres))
        self._dq = getattr(self, "_dq", {})
        self._dq[q] = dres
        self.streams[eng].append(("dma", out, in_, dres))
        for r in reads:
            r.rd[q] = dres.dcnt
        for w in writes:
            w.lw = (q, dres.dcnt)
            w.rd = {}

    def semof(self, q):
        if isinstance(q, tuple):
            return self._dq[q].dsem
        return self.sem[q]

    def finish(self):
        for r in self.dsems:
            self.wait("sp", ("d", id(r)), r.dcnt)

    def replay(self, block):
        handles = {"pe": block.tensor, "act": block.scalar, "dve": block.vector, "pool": block.gpsimd,
                   "sp": block.sync}
        for eng, deco in handles.items():
            stream = self.streams[eng]
            own = self.sem.get(eng)

            def body(e, stream=stream, own=own):
                for ent in stream:
                    k = ent[0]
                    if k == "w":
                        e.wait_ge(self.semof(ent[1]), ent[2])
                    elif k == "op":
                        ins = ent[1](e)
                        if ent[2]:
                            ins.then_inc(own, 1)
                    else:
                        e.dma_start(out=ent[1], in_=ent[2]).then_inc(ent[3].dsem, 16)
            deco(body)


class Arena:
    def __init__(self, kb, tensor, nbytes):
        self.kb = kb
        self.t = tensor
        self.nbytes = nbytes
        self.views = []

    def view(self, name, off, shape, dt):
        esz = 2 if dt == BF16 else 4
        n = int(np.prod(shape))
        nb = n * esz
        assert off % 4 == 0 and off + nb <= self.nbytes, (name, off, nb, self.nbytes)
        w0, w1 = off // 4, (off + nb + 3) // 4
        ap = self.t[:, w0:w1]
        if dt == BF16:
            ap = ap.bitcast(BF16)[:, 0:n]
        if len(shape) == 2:
            ap = ap.rearrange("p (a b) -> p a b", a=shape[0])
        elif len(shape) == 3:
            ap = ap.rearrange("p (a b c) -> p a b c", a=shape[0], b=shape[1])
        r = self.kb.res(name)
        r.rng = (off, off + nb)
        for (o, rr) in self.views:
            if o.rng[0] < r.rng[1] and r.rng[0] < o.rng[1]:
                r.ov.append(o)
                o.ov.append(r)
        self.views.append((r, ap))
        return ap, r


import os
VARP = int(os.environ.get('VARP', '128'))


class _Stop(Exception):
    pass


def build_program(tiles_limit=None, dbg=False, stop_after=None):
    nc = bass.Bass("TRN2", target_bir_lowering=False)
    dram_in = lambda name, shape, dt=F32: nc.dram_tensor(name, shape, dt, kind="ExternalInput").ap()
    dram_out = lambda name, shape, dt=F32: nc.dram_tensor(name, shape, dt, kind="ExternalOutput").ap()
    xp = dram_in("xp", [2, SEQ, D])
    xs = dram_in("xs", [2, DSEQ, D])
    sca = dram_in("sca", [2, 128, 16])
    scm = dram_in("scm", [2, 128, 96])
    sst = dram_in("sst", [2, 128, 2048])
    pf_d = dram_in("pf", [128, PF_COLS])
    cb_d = dram_in("cb", [128, CB_COLS], BF16)
    w_in = dram_in("w_in", [D, PROJ])
    w_a = dram_in("w_a_out", [D, D])
    w_s = dram_in("w_ssm_out", [2 * D, D])
    w_o = dram_in("w_o", [D, D])
    w_f1 = dram_in("w_ff1", [D, 4 * D])
    w_f2 = dram_in("w_ff2", [4 * D, D])
    yp = dram_out("yp", [2, SEQ, D])
    ys = dram_out("ys", [2, DSEQ, D])
    oca = dram_out("oca", [4, 128, 16])
    ocm = dram_out("ocm", [4, 128, 96])
    ost = dram_out("ost", [4, 128, 2048])
    wi_b = nc.dram_tensor("wi_b", [D, PROJ], BF16).ap()
    wa_b = nc.dram_tensor("wa_b", [D, D], BF16).ap()
    ws_b = nc.dram_tensor("ws_b", [2 * D, D], BF16).ap()
    wo_b = nc.dram_tensor("wo_b", [D, D], BF16).ap()
    wf1_b = nc.dram_tensor("wf1_b", [D, 4 * D], BF16).ap()
    wf2_b = nc.dram_tensor("wf2_b", [4 * D, D], BF16).ap()

    with ExitStack() as es:
        kb = KB(nc, es)
        sbt = lambda name, shape, dt: es.enter_context(nc.sbuf_tensor(name, shape, dt))
        PF = sbt("PF", [128, PF_COLS], F32); rPF = kb.res("PF")
        CB = sbt("CB", [128, CB_COLS], BF16); rCB = kb.res("CB")
        PAN = [sbt("PAN%d" % i, [128, 8, 512], BF16) for i in range(NS)]
        rPAN = [kb.res("PAN%d" % i) for i in range(NS)]
        XN = sbt("XN", [128, 8, 512], BF16); rXN = kb.res("XN")
        DTW = sbt("DTW", [128, 8, 32], BF16); rDTW = kb.res("DTW")
        HISTA = sbt("HISTA", [128, 8, 2], F32); rHISTA = kb.res("HISTA")
        HISTM = sbt("HISTM", [128, 32, 3], F32); rHISTM = kb.res("HISTM")
        ST = sbt("ST", [128, 2048], F32); rST = kb.res("ST")
        SBF = [sbt("SBF%d" % i, [128, 2048], BF16) for i in range(2)]
        rSBF = [kb.res("SBF%d" % i) for i in range(2)]
        ANEG = sbt("ANEG", [128, 32], F32); rANEG = kb.res("ANEG")
        SM = sbt("SM", [128, 4, 320], F32)
        rSM = [kb.res("SM%d" % b) for b in range(4)]
        STAT = sbt("STAT", [128, 16], F32); rSTAT = kb.res("STAT")
        EPSC = sbt("EPSC", [128, 2], F32); rEPSC = kb.res("EPSC")
        C_DT, C_ADT, C_ACUM, C_EA, C_CD, C_DEC, C_T0, C_RS = 0, 32, 64, 96, 128, 192, 224, 256
        SMH = sbt("SMH", [128, 4, 64], BF16)
        rSMH = [kb.res("SMH%d" % b) for b in range(4)]
        KBY = 1024
        AR_BYTES = 114 * KBY
        ARt = sbt("ARENA", [128, AR_BYTES // 4], F32)
        ar = Arena(kb, ARt, AR_BYTES)
        XT = [ar.view("XT%d" % i, i * 4 * KBY, [1024], F32) for i in range(2)]
        XS16 = [ar.view("XS16_%d" % i, 8 * KBY + i * 2 * KBY, [1024], BF16) for i in range(2)]
        JUNK = ar.view("JUNK", 12 * KBY, [1024], BF16)
        SZ = ar.view("SZ", 0, [4, 2048], BF16)
        XSF = ar.view("XSF", 16 * KBY, [16, 512], BF16)
        BF_ = ar.view("BF", 32 * KBY, [8, 512], BF16)
        CF = ar.view("CF", 40 * KBY, [8, 512], BF16)
        XB4 = ar.view("XB4", 48 * KBY, [4, 516], F32)
        CV4 = ar.view("CV4", 57 * KBY, [4, 512], F32)
        XSt = ar.view("XSt", 48 * KBY, [2048], BF16)
        XC = ar.view("XC", 52 * KBY, [2048], BF16)
        XCD = ar.view("XCD", 56 * KBY, [2048], BF16)
        Bt = ar.view("Bt", 60 * KBY, [1024], BF16)
        RHI = ar.view("RHI", 62 * KBY, [32, 64], BF16)
        RLO = ar.view("RLO", 66 * KBY, [32, 64], BF16)
        Dh = ar.view("Dh", 70 * KBY, [16, 64], F32)
        Eh = ar.view("Eh", 74 * KBY, [16, 64], F32)
        Mh = ar.view("Mh", 78 * KBY, [16, 64], BF16)
        T1 = ar.view("T1", 80 * KBY, [1024], F32)
        Y1 = ar.view("Y1", 84 * KBY, [1024], F32)
        Y3 = ar.view("Y3", 88 * KBY, [1024], F32)
        YG = ar.view("YG", 92 * KBY, [1024], F32)
        YN = ar.view("YN", 96 * KBY, [1024], BF16)
        YNF = ar.view("YNF", 98 * KBY, [16, 512], BF16)
        AC4 = ar.view("AC4", 0, [4, 512], F32)
        U4 = ar.view("U4", 8 * KBY, [4, 516], F32)
        TT4 = ar.view("TT4", 17 * KBY, [4, 512], F32)
        V = ar.view("V", 25 * KBY, [8, 512], BF16)
        G4 = ar.view("G4", 33 * KBY, [4, 512], F32)
        PA = ar.view("PA", 41 * KBY, [8, 512], F32)
        MG = ar.view("MG", 57 * KBY, [8, 512], BF16)
        MO = [ar.view("MO%d" % b, 65 * KBY + b * 4 * KBY, [1024], F32) for b in range(4)]
        XT2 = [ar.view("XT2_%d" % i, 81 * KBY + i * 4 * KBY, [1024], F32) for i in range(2)]
        HS16 = [ar.view("HS16_%d" % i, 89 * KBY + i * 2 * KBY, [1024], BF16) for i in range(2)]
        FF = ar.view("FF", 0, [32, 512], BF16)
        RT = [ar.view("RT%d" % i, 32 * KBY + i * 2 * KBY, [512], F32) for i in range(2)]
        FO = [ar.view("FO%d" % b, 36 * KBY + b * 4 * KBY, [1024], F32) for b in range(4)]
        SGW = 2824
        SGI = [ar.view("SGI%d" % i, i * 12 * KBY, [SGW], F32) for i in range(3)]
        SGO = [ar.view("SGO%d" % i, 36 * KBY + i * 6 * KBY, [SGW], BF16) for i in range(3)]
        PS = es.enter_context(nc.psum_tensor("PS", [128, 8, 512], F32))
        rB = [kb.res("bank%d" % i) for i in range(8)]
        for r_ in rB:
            r_.excl = True
        PSb = PS[:].bitcast(BF16)

        ident = CB[:, CB_ID:CB_ID + 128]
        tri2 = CB[:, CB_TRI2:CB_TRI2 + 64]
        bones = CB[:, CB_BONES:CB_BONES + 128]
        onesA = CB[:, CB_ONESA:CB_ONESA + 128]
        onesB = CB[:, CB_ONESB:CB_ONESB + 128]
        tri128 = CB[:, CB_TRI128:CB_TRI128 + 128]
        negm = CB[:, CB_NEG:CB_NEG + 1024]

        dbg_names = []

        def dump(name, ap, res, shape, dt):
            if not dbg:
                return
            d = nc.dram_tensor("dbg_" + name, shape, dt, kind="ExternalOutput").ap()
            kb.dma("sp", d, ap, res, reads=[res])
            dbg_names.append("dbg_" + name)

        kb.dma("sp", PF[:], pf_d, rPF, writes=[rPF])
        kb.dma("sp", CB[:], cb_d, rCB, writes=[rCB])
        kb.op("pool", lambda e: e.memset(EPSC[:, 0:1], EPS), writes=[rEPSC])
        kb.op("pool", lambda e: e.memset(EPSC[:, 1:2], 1.0), writes=[rEPSC])
        kb.op("act", lambda e: e.activation(out=ANEG[:], in_=PF[:, PF_ALOG:PF_ALOG + 32], func=AF.Exp),
              reads=[rPF], writes=[rANEG])
        kb.op("dve", lambda e: e.tensor_scalar(out=ANEG[:], in0=ANEG[:], scalar1=-1.0, scalar2=None, op0=ALU.mult),
              reads=[rANEG], writes=[rANEG])

        conv_jobs = []
        def add_mat(src, dst, K, N, piece, scol):
            for rc in range(K // 128):
                for c0 in range(0, N, piece):
                    w = min(piece, N - c0)
                    conv_jobs.append((src[rc * 128:(rc + 1) * 128, c0:c0 + w], dst[rc * 128:(rc + 1) * 128, c0:c0 + w],
                                      None if scol is None else scol + rc, w))
        add_mat(w_in, wi_b, D, PROJ, SGW, PF_NMP)
        add_mat(w_a, wa_b, D, D, 1024, None)
        add_mat(w_s, ws_b, 2 * D, D, 1024, PF_SSN)
        add_mat(w_o, wo_b, D, D, 1024, None)
        add_mat(w_f1, wf1_b, D, 4 * D, 2048, PF_NFP)
        add_mat(w_f2, wf2_b, 4 * D, D, 1024, None)
        for i, (src, dst, scol, w) in enumerate(conv_jobs):
            si, ri = SGI[i % 3]
            so, ro = SGO[i % 3]
            kb.dma("sp", si[:, 0:w], src, ri, writes=[ri])
            eng = "dve" if (i % 3) != 2 else "pool"
            if scol is None:
                kb.op(eng, lambda e, so=so, si=si, w=w: e.tensor_copy(out=so[:, 0:w], in_=si[:, 0:w]),
                      reads=[ri], writes=[ro])
            else:
                kb.op(eng, lambda e, so=so, si=si, w=w, scol=scol: e.tensor_scalar(
                    out=so[:, 0:w], in0=si[:, 0:w], scalar1=PF[:, scol:scol + 1], scalar2=None, op0=ALU.mult),
                    reads=[ri, rPF], writes=[ro])
            kb.dma("act", dst, so[:, 0:w], ro, reads=[ro])
        for so, ro in SGO:
            kb.wait("sp", ("d", id(ro)), ro.dcnt)
        kb.dma("sp", DTW[:], wi_b[:, OFF_DT:OFF_DT + 32].rearrange("(kc p) n -> p kc n", p=128), rDTW, writes=[rDTW])

        tiles = []
        for s in range(2):
            for ti in range(4):
                tiles.append((s, xp[s, ti * 512:(ti + 1) * 512, :], yp[s, ti * 512:(ti + 1) * 512, :], 512,
                              ti == 0, ti == 3, None))
        for s in range(2):
            tiles.append((2 + s, xs[s], ys[s], 64, True, True, s))
        if tiles_limit is not None:
            tiles = [tiles[i] for i in tiles_limit]

        def panel_list(T):
            pl = []
            for q in range(4):
                pl.append((wi_b, 0, OFF_Z + q * 512))
            for q in range(8):
                pl.append((wi_b, 0, OFF_XBC + q * 512))
            for hf in range(2):
                pl.append((wi_b, 0, OFF_AC + hf * 512))
                pl.append((wi_b, 0, OFF_AH + hf * 512))
                pl.append((wi_b, 0, OFF_AB + hf * 512))
            for hf in range(2):
                pl.append((wi_b, 0, OFF_GATE + hf * 512))
                pl.append((wa_b, 0, hf * 512))
            for hf in range(2):
                pl.append((wi_b, 0, OFF_GATE + 1024 + hf * 512))
                pl.append((ws_b, 0, hf * 512))
                pl.append((ws_b, 1, hf * 512))
            for hf in range(2):
                pl.append((wo_b, 0, hf * 512))
            for q in range(8):
                pl.append((wf1_b, 0, q * 512))
            for hf in range(2):
                for kg in range(4):
                    pl.append((wf2_b, kg, hf * 512))
            return pl
        all_panels = []
        for tl in tiles:
            all_panels += panel_list(tl[3])
        pstate = {"issued": 0, "used": 0}

        def issue_panels(upto):
            while pstate["issued"] < min(upto, len(all_panels)):
                i = pstate["issued"]
                mat, kg, c0 = all_panels[i]
                slot = i % NS
                src = mat[kg * 1024:(kg + 1) * 1024, c0:c0 + 512].rearrange("(kc p) n -> p kc n", p=128)
                kb.dma("sp", PAN[slot][:], src, rPAN[slot], writes=[rPAN[slot]])
                pstate["issued"] += 1

        def next_panel():
            i = pstate["used"]
            issue_panels(i + NS - 1)
            pstate["used"] += 1
            return PAN[i % NS], rPAN[i % NS]

        bank_rr = {"i": 0}

        def acc_bank():
            i = bank_rr["i"]
            bank_rr["i"] = (i + 1) % 4
            return i

        def mm_run(out, pairs, reads, writes, first=True, last=True):
            n = len(pairs)
            for i, (l, r) in enumerate(pairs):
                kb.op("pe", lambda e, l=l, r=r, st=(first and i == 0), sp_=(last and i == n - 1):
                      e.matmul(out, lhsT=l, rhs=r, start=st, stop=sp_),
                      reads=reads if i == 0 else (), writes=writes if i == 0 else (), inc=(i == n - 1))

        def tr_run(outs_ins, reads, writes, idn):
            n = len(outs_ins)
            for i, (o, a) in enumerate(outs_ins):
                kb.op("pe", lambda e, o=o, a=a: e.transpose(o, a, idn),
                      reads=reads if i == 0 else (), writes=writes if i == 0 else (), inc=(i == n - 1))

        def rstd_from_ss(ss_col, rs_col, n, tb, w=1):
            kb.op("act", lambda e: e.activation(out=STAT[0:tb, rs_col:rs_col + w], in_=STAT[0:tb, ss_col:ss_col + w],
                                                func=AF.Ln, scale=1.0 / n, bias=EPSC[0:tb, 0:1]),
                  reads=[rSTAT, rEPSC], writes=[rSTAT])
            kb.op("act", lambda e: e.activation(out=STAT[0:tb, rs_col:rs_col + w], in_=STAT[0:tb, rs_col:rs_col + w],
                                                func=AF.Exp, scale=-0.5),
                  reads=[rSTAT], writes=[rSTAT])

        def norm_to_fm(src_ap, src_res, b, tb, s16, dst, dst_res):
            (s16ap, s16res) = s16
            kb.op("act", lambda e: e.activation(out=JUNK[0][0:tb, :], in_=src_ap[0:tb, :], func=AF.Square,
                                                accum_out=STAT[0:tb, 0:1]),
                  reads=[src_res], writes=[JUNK[1], rSTAT])
            rstd_from_ss(0, 1, 1024, tb)
            kb.op("dve", lambda e: e.tensor_scalar(out=s16ap[0:tb, :], in0=src_ap[0:tb, :], scalar1=STAT[0:tb, 1:2],
                                                   scalar2=None, op0=ALU.mult),
                  reads=[src_res, rSTAT], writes=[s16res])
            tpb = 4
            tr_run([(PSb[:, tpb, kc * 128:kc * 128 + tb], s16ap[0:tb, kc * 128:(kc + 1) * 128]) for kc in range(8)],
                   reads=[s16res, rCB], writes=[rB[tpb]], idn=ident[0:tb, 0:tb])
            kb.op("act", lambda e: e.copy(out=dst[:, :, b * 128:b * 128 + tb],
                                          in_=PSb[:, tpb, :].rearrange("p (k t) -> p k t", k=8)[:, :, 0:tb]),
                  reads=[rB[tpb]], writes=[dst_res])

        def gate(n):
            if stop_after is not None and n > stop_after:
                raise _Stop()

        def run_tile(sidx, xd, yd, T, first, last, samp):
            tb = min(T, 128)
            nb = T // tb
            nch = tb // 64
            if first:
                if samp is None:
                    kb.op("pool", lambda e: e.memset(HISTA[:], 0.0), writes=[rHISTA])
                    kb.op("pool", lambda e: e.memset(HISTM[:], 0.0), writes=[rHISTM])
                    kb.op("pool", lambda e: e.memset(ST[:], 0.0), writes=[rST])
                else:
                    kb.dma("sp", HISTA[:], sca[samp].rearrange("p (j i) -> p j i", i=2), rHISTA, writes=[rHISTA])
                    kb.dma("sp", HISTM[:], scm[samp].rearrange("p (c i) -> p c i", i=3), rHISTM, writes=[rHISTM])
                    kb.dma("sp", ST[:], sst[samp], rST, writes=[rST])
                kb.op("act", lambda e: e.copy(out=SBF[0][:], in_=ST[:]), reads=[rST], writes=[rSBF[0]])

            for b in range(nb):
                xt, rxt = XT[b % 2]
                kb.dma("sp", xt[0:tb, :], xd[b * 128:b * 128 + tb, :], rxt, writes=[rxt])
                norm_to_fm(xt, rxt, b, tb, XS16[b % 2], XN, rXN)

            gate(1)
            for q in range(4):
                pan, rpan = next_panel()
                for b in range(nb):
                    bk = acc_bank()
                    mm_run(PS[0:tb, bk, :], [(XN[:, kc, b * 128:b * 128 + tb], pan[:, kc, :]) for kc in range(8)],
                           reads=[rXN, rpan], writes=[rB[bk]])
                    kb.op("act", lambda e, bk=bk, b=b, q=q: e.activation(
                        out=SZ[0][0:tb, b, q * 512:(q + 1) * 512], in_=PS[0:tb, bk, :], func=AF.Silu),
                        reads=[rB[bk]], writes=[SZ[1]])
            gate(2)
            for q in range(8):
                pan, rpan = next_panel()
                kb.op("pool", lambda e, q=q: e.tensor_copy(out=XB4[0][:, :, 0:3], in_=HISTM[:, q * 4:(q + 1) * 4, :]),
                      reads=[rHISTM], writes=[XB4[1]])
                for jj in range(4):
                    c = q * 4 + jj
                    bk = acc_bank()
                    mm_run(PS[:, bk, 0:T], [(pan[:, kc, jj * 128:(jj + 1) * 128], XN[:, kc, 0:T]) for kc in range(8)],
                           reads=[rXN, rpan], writes=[rB[bk]])
                    kb.op("act", lambda e, bk=bk, jj=jj: e.copy(out=XB4[0][:, jj, 3:3 + T], in_=PS[:, bk, 0:T]),
                          reads=[rB[bk]], writes=[XB4[1]])
                kb.op("pool", lambda e, q=q: e.tensor_copy(out=HISTM[:, q * 4:(q + 1) * 4, :], in_=XB4[0][:, :, T:T + 3]),
                      reads=[XB4[1]], writes=[rHISTM])
                for jj in range(4):
                    c = q * 4 + jj
                    kb.op("act", lambda e, jj=jj, c=c: e.activation(
                        out=CV4[0][:, jj, 0:T], in_=XB4[0][:, jj, 0:T], func=AF.Identity,
                        scale=PF[:, PF_CSW + c * 4:PF_CSW + c * 4 + 1]),
                        reads=[XB4[1], rPF], writes=[CV4[1]])
                    for i in (1, 2, 3):
                        kb.op("dve", lambda e, jj=jj, c=c, i=i: e.scalar_tensor_tensor(
                            out=CV4[0][:, jj, 0:T], in0=XB4[0][:, jj, i:i + T],
                            scalar=PF[:, PF_CSW + c * 4 + i:PF_CSW + c * 4 + i + 1], in1=CV4[0][:, jj, 0:T],
                            op0=ALU.mult, op1=ALU.add),
                            reads=[XB4[1], CV4[1], rPF], writes=[CV4[1]])
                    if c < 16:
                        dst, dres = XSF[0][:, c, 0:T], XSF[1]
                    elif c < 24:
                        dst, dres = BF_[0][:, c - 16, 0:T], BF_[1]
                    else:
                        dst, dres = CF[0][:, c - 24, 0:T], CF[1]
                    kb.op("act", lambda e, jj=jj, c=c, dst=dst: e.activation(
                        out=dst, in_=CV4[0][:, jj, 0:T], func=AF.Silu, bias=PF[:, PF_CSB + c:PF_CSB + c + 1]),
                        reads=[CV4[1], rPF], writes=[dres])
            if last:
                kb.dma("sp", ocm[sidx].rearrange("p (c i) -> p c i", i=3), HISTM[:], rHISTM, reads=[rHISTM])

            gate(3)
            for b in range(nb):
                sm = SM[:, b, :]
                rsm = rSM[b]
                smh = SMH[:, b, :]
                rsmh = rSMH[b]
                tok = slice(b * 128, b * 128 + tb)
                bk = acc_bank()
                mm_run(PS[0:tb, bk, 0:32], [(XN[:, kc, tok], DTW[:, kc, :]) for kc in range(8)],
                       reads=[rXN, rDTW], writes=[rB[bk]])
                gate(3.05)
                kb.op("dve", lambda e, bk=bk, sm=sm: e.tensor_tensor(out=sm[0:tb, C_T0:C_T0 + 32], in0=PS[0:tb, bk, 0:32],
                                                                    in1=PF[0:tb, PF_DTB:PF_DTB + 32], op=ALU.add),
                      reads=[rB[bk], rPF], writes=[rsm])
                kb.op("act", lambda e, sm=sm: e.activation(out=sm[0:tb, C_T0:C_T0 + 32], in_=sm[0:tb, C_T0:C_T0 + 32],
                                                           func=AF.Exp), reads=[rsm], writes=[rsm])
                kb.op("act", lambda e, sm=sm: e.activation(out=sm[0:tb, C_DT:C_DT + 32], in_=sm[0:tb, C_T0:C_T0 + 32],
                                                           func=AF.Ln, bias=EPSC[0:tb, 1:2]), reads=[rsm, rEPSC], writes=[rsm])
                kb.op("dve", lambda e, sm=sm: e.tensor_tensor(out=sm[0:tb, C_ADT:C_ADT + 32], in0=sm[0:tb, C_DT:C_DT + 32],
                                                              in1=ANEG[0:tb, :], op=ALU.mult),
                      reads=[rsm, rANEG], writes=[rsm])
                kb.op("dve", lambda e, sm=sm, smh=smh: e.tensor_copy(out=smh[0:tb, 0:32], in_=sm[0:tb, C_ADT:C_ADT + 32]),
                      reads=[rsm], writes=[rsmh])
                kb.op("dve", lambda e, sm=sm, smh=smh: e.tensor_tensor(out=smh[0:tb, 32:64], in0=sm[0:tb, C_ADT:C_ADT + 32],
                                                                      in1=smh[0:tb, 0:32], op=ALU.subtract),
                      reads=[rsm, rsmh], writes=[rsmh])
                gate(3.1)
                bk = acc_bank()
                mm_run(PS[0:tb, bk, 0:32], [(tri128[0:tb, 0:tb], smh[0:tb, 0:32]), (tri128[0:tb, 0:tb], smh[0:tb, 32:64])],
                       reads=[rsmh, rCB], writes=[rB[bk]])
                gate(3.11)
                mm_run(PS[:, bk, 32:64], [(onesA[0:tb, :], smh[0:tb, 0:32]), (onesA[0:tb, :], smh[0:tb, 32:64])],
                       reads=[rsmh, rCB], writes=[rB[bk]])
                if nch == 2:
                    mm_run(PS[:, bk, 64:96], [(onesB[0:tb, :], smh[0:tb, 0:32]), (onesB[0:tb, :], smh[0:tb, 32:64])],
                           reads=[rsmh, rCB], writes=[rB[bk]])
                gate(3.12)
                kb.op("dve", lambda e, bk=bk, sm=sm: e.tensor_copy(out=sm[0:tb, C_ACUM:C_ACUM + 32], in_=PS[0:tb, bk, 0:32]),
                      reads=[rB[bk]], writes=[rsm])
                gate(3.13)
                kb.op("act", lambda e, bk=bk, sm=sm: e.activation(out=sm[0:tb, C_EA:C_EA + 32], in_=PS[0:tb, bk, 0:32],
                                                                 func=AF.Exp), reads=[rB[bk]], writes=[rsm])
                gate(3.14)
                kb.op("act", lambda e, bk=bk, sm=sm: e.activation(out=sm[0:VARP, C_CD:C_CD + 32 * nch],
                                                                 in_=PS[0:VARP, bk, 32:32 + 32 * nch], func=AF.Exp),
                      reads=[rB[bk]], writes=[rsm])
                gate(3.2)
                tr_run([(PSb[0:tb, 5 + c // 8, (c % 8) * 128:(c % 8 + 1) * 128], XSF[0][:, c, tok]) for c in range(16)],
                       reads=[XSF[1], rCB], writes=[rB[5], rB[6]], idn=ident)
                kb.op("act", lambda e: e.copy(out=XSt[0][0:tb, :], in_=PSb[0:tb, 5:7, :].rearrange("p a n -> p (a n)")),
                      reads=[rB[5], rB[6]], writes=[XSt[1]])
                kb.op("dve", lambda e, sm=sm: e.tensor_tensor(
                    out=XC[0][0:tb, :].rearrange("p (h d) -> p h d", h=32),
                    in0=PSb[0:tb, 5:7, :].rearrange("p a (h d) -> p (a h) d", d=64),
                    in1=sm[0:tb, C_DT:C_DT + 32].unsqueeze(2).broadcast_to([tb, 32, 64]), op=ALU.mult),
                    reads=[rB[5], rB[6], rsm], writes=[XC[1]])
                tr_run([(PSb[0:tb, 7, g * 128:(g + 1) * 128], BF_[0][:, g, tok]) for g in range(8)],
                       reads=[BF_[1], rCB], writes=[rB[7]], idn=ident)
                kb.op("act", lambda e: e.copy(out=Bt[0][0:tb, :], in_=PSb[0:tb, 7, :]), reads=[rB[7]], writes=[Bt[1]])
                gate(3.3)
                for (rr, lo) in ((RHI, 0), (RLO, 32)):
                    kb.op("pool", lambda e, rr=rr, lo=lo, smh=smh: e.tensor_tensor(
                        out=rr[0][0:tb, :, :], in0=smh[0:tb, lo:lo + 32].unsqueeze(2).broadcast_to([tb, 32, 64]),
                        in1=tri2[0:tb, :].unsqueeze(1).broadcast_to([tb, 32, 64]), op=ALU.mult),
                        reads=[rsmh, rCB], writes=[rr[1]])
                gate(3.4)
                for hf in range(2):
                    hs = slice(hf * 16, hf * 16 + 16)
                    for nn in range(2):
                        hh = slice(hf * 16 + nn * 8, hf * 16 + nn * 8 + 8)
                        mm_run(PS[0:tb, nn, :],
                               [(bones[0:tb, 0:tb], RHI[0][0:tb, hh, :]), (bones[0:tb, 0:tb], RLO[0][0:tb, hh, :]),
                                (ident[0:tb, 0:tb], negm[0:tb, nn * 512:(nn + 1) * 512])],
                               reads=[RHI[1], RLO[1], rCB], writes=[rB[nn]])
                    kb.op("dve", lambda e, hs=hs, sm=sm: e.tensor_tensor(
                        out=Dh[0][0:tb, :, :], in0=PS[0:tb, 0:2, :].rearrange("p a (h l) -> p (a h) l", l=64),
                        in1=sm[0:tb, C_ACUM + hs.start:C_ACUM + hs.stop].unsqueeze(2).broadcast_to([tb, 16, 64]),
                        op=ALU.subtract), reads=[rB[0], rB[1], rsm], writes=[Dh[1]])
                    kb.op("act", lambda e: e.activation(out=Eh[0][0:tb, :, :], in_=Dh[0][0:tb, :, :], func=AF.Exp),
                          reads=[Dh[1]], writes=[Eh[1]])
                    kb.op("pool", lambda e, hs=hs, sm=sm: e.tensor_copy(
                        out=sm[0:tb, C_DEC + hs.start:C_DEC + hs.stop].unsqueeze(2), in_=Eh[0][0:tb, :, 63:64]),
                        reads=[Eh[1]], writes=[rsm])
                    gate(3.5)
                    for g in range(4):
                        gg = hf * 4 + g
                        for ch in range(nch):
                            r0 = ch * 64
                            cs = slice(b * 128 + r0, b * 128 + r0 + 64)
                            mm_run(PS[r0:r0 + 64, 4, g * 64:(g + 1) * 64], [(BF_[0][:, gg, cs], CF[0][:, gg, cs])],
                                   reads=[BF_[1], CF[1]], writes=[rB[4]])
                    kb.op("dve", lambda e: e.tensor_tensor(
                        out=Mh[0][0:tb, :, :].rearrange("p (g j) l -> p g j l", g=4),
                        in0=Eh[0][0:tb, :, :].rearrange("p (g j) l -> p g j l", g=4),
                        in1=PS[0:tb, 4, 0:256].rearrange("p (g l) -> p g l", g=4).unsqueeze(2).broadcast_to([tb, 4, 4, 64]),
                        op=ALU.mult), reads=[Eh[1], rB[4]], writes=[Mh[1]])
                    kb.op("pool", lambda e, hs=hs, sm=sm, hf=hf: e.tensor_tensor(
                        out=XCD[0][0:tb, hf * 1024:(hf + 1) * 1024].rearrange("p (h d) -> p h d", h=16),
                        in0=XC[0][0:tb, hf * 1024:(hf + 1) * 1024].rearrange("p (h d) -> p h d", h=16),
                        in1=sm[0:tb, C_DEC + hs.start:C_DEC + hs.stop].unsqueeze(2).broadcast_to([tb, 16, 64]),
                        op=ALU.mult), reads=[XC[1], rsm], writes=[XCD[1]])
                    gate(3.6)
                    for ch in range(nch):
                        r0 = ch * 64
                        cs = slice(b * 128 + r0, b * 128 + r0 + 64)
                        sb_in = SBF[ch % 2]
                        rsb_in = rSBF[ch % 2]
                        firstmm = True
                        for g in range(4):
                            gg = hf * 4 + g
                            for j in range(4):
                                hl = g * 4 + j
                                h = hf * 16 + hl
                                mm_run(PS[r0:r0 + 64, 2 + hl // 8, (hl % 8) * 64:(hl % 8 + 1) * 64],
                                       [(Mh[0][r0:r0 + 64, hl, :], XC[0][r0:r0 + 64, h * 64:(h + 1) * 64])],
                                       reads=[Mh[1], XC[1]], writes=[rB[2], rB[3]])
                                firstmm = False
                        for g in range(4):
                            gg = hf * 4 + g
                            mm_run(PS[r0:r0 + 64, 5 + g // 2, (g % 2) * 256:(g % 2 + 1) * 256],
                                   [(CF[0][:, gg, cs], sb_in[:, gg * 256:(gg + 1) * 256])],
                                   reads=[CF[1], rsb_in], writes=[rB[5], rB[6]])
                        for g in range(4):
                            gg = hf * 4 + g
                            mm_run(PS[:, g // 2, (g % 2) * 256:(g % 2 + 1) * 256],
                                   [(Bt[0][r0:r0 + 64, gg * 128:(gg + 1) * 128], XCD[0][r0:r0 + 64, gg * 256:(gg + 1) * 256])],
                                   reads=[Bt[1], XCD[1]], writes=[rB[0], rB[1]])
                        gate(3.7)
                        stv = ST[:, hf * 1024:(hf + 1) * 1024]
                        kb.op("pool", lambda e, stv=stv, sm=sm, ch=ch, hs=hs: e.tensor_tensor(
                            out=T1[0][:, :].rearrange("p (h d) -> p h d", h=16), in0=stv.rearrange("p (h d) -> p h d", h=16),
                            in1=sm[:, C_CD + ch * 32 + hs.start:C_CD + ch * 32 + hs.stop].unsqueeze(2).broadcast_to([128, 16, 64]),
                            op=ALU.mult), reads=[rST, rsm], writes=[T1[1]])
                        kb.op("dve", lambda e, stv=stv: e.tensor_tensor(
                            out=stv, in0=PS[:, 0:2, :].rearrange("p a n -> p (a n)"), in1=T1[0][:, :], op=ALU.add),
                            reads=[rB[0], rB[1], T1[1]], writes=[rST])
                        sb_out = SBF[(ch + 1) % 2] if nch == 2 else SBF[0]
                        rsb_out = rSBF[(ch + 1) % 2] if nch == 2 else rSBF[0]
                        kb.op("act", lambda e, stv=stv, sb_out=sb_out, hf=hf: e.copy(
                            out=sb_out[:, hf * 1024:(hf + 1) * 1024], in_=stv), reads=[rST], writes=[rsb_out])
                    gate(3.8)
                    kb.op("dve", lambda e, hs=hs, sm=sm: e.tensor_tensor(
                        out=Y1[0][0:tb, :].rearrange("p (h d) -> p h d", h=16),
                        in0=PS[0:tb, 5:7, :].rearrange("p a (h d) -> p (a h) d", d=64),
                        in1=sm[0:tb, C_EA + hs.start:C_EA + hs.stop].unsqueeze(2).broadcast_to([tb, 16, 64]),
                        op=ALU.mult), reads=[rB[5], rB[6], rsm], writes=[Y1[1]])
                    kb.op("dve", lambda e: e.tensor_tensor(
                        out=Y1[0][0:tb, :], in0=PS[0:tb, 2:4, :].rearrange("p a n -> p (a n)"), in1=Y1[0][0:tb, :],
                        op=ALU.add), reads=[rB[2], rB[3], Y1[1]], writes=[Y1[1]])
                    kb.op("pool", lambda e, hs=hs, hf=hf: e.tensor_tensor(
                        out=Y3[0][0:tb, :].rearrange("p (h d) -> p h d", h=16),
                        in0=XSt[0][0:tb, hf * 1024:(hf + 1) * 1024].rearrange("p (h d) -> p h d", h=16),
                        in1=PF[0:tb, PF_DSK + hs.start:PF_DSK + hs.stop].unsqueeze(2).broadcast_to([tb, 16, 64]),
                        op=ALU.mult), reads=[XSt[1], rPF], writes=[Y3[1]])
                    kb.op("pool", lambda e: e.tensor_tensor(out=Y3[0][0:tb, :], in0=Y3[0][0:tb, :], in1=Y1[0][0:tb, :],
                                                           op=ALU.add), reads=[Y3[1], Y1[1]], writes=[Y3[1]])
                    kb.op("pool", lambda e, hf=hf, b=b: e.tensor_tensor(
                        out=YG[0][0:tb, :], in0=Y3[0][0:tb, :], in1=SZ[0][0:tb, b, hf * 1024:(hf + 1) * 1024], op=ALU.mult),
                        reads=[Y3[1], SZ[1]], writes=[YG[1]])
                    kb.op("pool", lambda e: e.tensor_tensor(out=Y3[0][0:tb, :], in0=YG[0][0:tb, :], in1=YG[0][0:tb, :],
                                                           op=ALU.mult), reads=[YG[1]], writes=[Y3[1]])
                    kb.op("dve", lambda e: e.tensor_reduce(out=STAT[0:tb, 4:8], in_=Y3[0][0:tb, :].rearrange("p (g d) -> p g d", g=4),
                                                          axis=AX.X, op=ALU.add), reads=[Y3[1]], writes=[rSTAT])
                    rstd_from_ss(4, 8, 256, tb, w=4)
                    kb.op("dve", lambda e: e.tensor_tensor(
                        out=YN[0][0:tb, :].rearrange("p (g d) -> p g d", g=4), in0=YG[0][0:tb, :].rearrange("p (g d) -> p g d", g=4),
                        in1=STAT[0:tb, 8:12].unsqueeze(2).broadcast_to([tb, 4, 256]), op=ALU.mult),
                        reads=[YG[1], rSTAT], writes=[YN[1]])
                    tr_run([(PSb[:, 7, c * 128:c * 128 + tb], YN[0][0:tb, c * 128:(c + 1) * 128]) for c in range(8)],
                           reads=[YN[1], rCB], writes=[rB[7]], idn=ident[0:tb, 0:tb])
                    kb.op("act", lambda e, hf=hf, b=b: e.copy(
                        out=YNF[0][:, hf * 8:(hf + 1) * 8, b * 128:b * 128 + tb],
                        in_=PSb[:, 7, :].rearrange("p (k t) -> p k t", k=8)[:, :, 0:tb]), reads=[rB[7]], writes=[YNF[1]])
                if nch == 2:
                    pass
            if last:
                kb.dma("sp", ost[sidx], ST[:], rST, reads=[rST])
            dump("xn", XN[:, :, 0:T], rXN, [128, 8, T], BF16)
            dump("sz", SZ[0][0:tb, 0:nb, :], SZ[1], [tb, nb, 2048], BF16)
            dump("xsf", XSF[0][:, :, 0:T], XSF[1], [128, 16, T], BF16)
            dump("bf", BF_[0][:, :, 0:T], BF_[1], [128, 8, T], BF16)
            dump("cf", CF[0][:, :, 0:T], CF[1], [128, 8, T], BF16)
            dump("ynf", YNF[0][:, :, 0:T], YNF[1], [128, 16, T], BF16)
            dump("sm", SM[0:tb, 0, 0:C_CD], rSM[0], [tb, C_CD], F32)
            dump("y1", Y1[0][0:tb, :], Y1[1], [tb, 1024], F32)
            dump("yg", YG[0][0:tb, :], YG[1], [tb, 1024], F32)

            gate(4)
            for hf in range(2):
                pan, rpan = next_panel()
                for jj in range(4):
                    bk = acc_bank()
                    mm_run(PS[:, bk, 0:T], [(pan[:, kc, jj * 128:(jj + 1) * 128], XN[:, kc, 0:T]) for kc in range(8)],
                           reads=[rXN, rpan], writes=[rB[bk]])
                    kb.op("act", lambda e, bk=bk, jj=jj: e.copy(out=AC4[0][:, jj, 0:T], in_=PS[:, bk, 0:T]),
                          reads=[rB[bk]], writes=[AC4[1]])
                pan, rpan = next_panel()
                kb.op("pool", lambda e, hf=hf: e.tensor_copy(out=U4[0][:, :, 0:2], in_=HISTA[:, hf * 4:(hf + 1) * 4, :]),
                      reads=[rHISTA], writes=[U4[1]])
                for jj in range(4):
                    bk = acc_bank()
                    mm_run(PS[:, bk, 0:T], [(pan[:, kc, jj * 128:(jj + 1) * 128], XN[:, kc, 0:T]) for kc in range(8)],
                           reads=[rXN, rpan], writes=[rB[bk]])
                    kb.op("dve", lambda e, bk=bk, jj=jj: e.tensor_tensor(out=U4[0][:, jj, 2:2 + T], in0=PS[:, bk, 0:T],
                                                                        in1=AC4[0][:, jj, 0:T], op=ALU.mult),
                          reads=[rB[bk], AC4[1]], writes=[U4[1]])
                kb.op("pool", lambda e, hf=hf: e.tensor_copy(out=HISTA[:, hf * 4:(hf + 1) * 4, :], in_=U4[0][:, :, T:T + 2]),
                      reads=[U4[1]], writes=[rHISTA])
                for jj in range(4):
                    j = hf * 4 + jj
                    kb.op("act", lambda e, jj=jj, j=j: e.activation(
                        out=TT4[0][:, jj, 0:T], in_=U4[0][:, jj, 0:T], func=AF.Identity,
                        scale=PF[:, PF_CAW + j * 3:PF_CAW + j * 3 + 1]), reads=[U4[1], rPF], writes=[TT4[1]])
                    for i in (1, 2):
                        kb.op("dve", lambda e, jj=jj, j=j, i=i: e.scalar_tensor_tensor(
                            out=TT4[0][:, jj, 0:T], in0=U4[0][:, jj, i:i + T],
                            scalar=PF[:, PF_CAW + j * 3 + i:PF_CAW + j * 3 + i + 1], in1=TT4[0][:, jj, 0:T],
                            op0=ALU.mult, op1=ALU.add), reads=[U4[1], TT4[1], rPF], writes=[TT4[1]])
                pan, rpan = next_panel()
                for jj in range(4):
                    j = hf * 4 + jj
                    bk = acc_bank()
                    mm_run(PS[:, bk, 0:T], [(pan[:, kc, jj * 128:(jj + 1) * 128], XN[:, kc, 0:T]) for kc in range(8)],
                           reads=[rXN, rpan], writes=[rB[bk]])
                    kb.op("dve", lambda e, bk=bk, jj=jj, j=j: e.tensor_tensor(out=V[0][:, j, 0:T], in0=PS[:, bk, 0:T],
                                                                             in1=TT4[0][:, jj, 0:T], op=ALU.mult),
                          reads=[rB[bk], TT4[1]], writes=[V[1]])
            if last:
                kb.dma("sp", oca[sidx].rearrange("p (j i) -> p j i", i=2), HISTA[:], rHISTA, reads=[rHISTA])
            for hf in range(2):
                pan, rpan = next_panel()
                for jj in range(4):
                    j = hf * 4 + jj
                    bk = acc_bank()
                    mm_run(PS[:, bk, 0:T], [(pan[:, kc, jj * 128:(jj + 1) * 128], XN[:, kc, 0:T]) for kc in range(8)],
                           reads=[rXN, rpan], writes=[rB[bk]])
                    kb.op("act", lambda e, bk=bk, jj=jj, j=j: e.activation(
                        out=G4[0][:, jj, 0:T], in_=PS[:, bk, 0:T], func=AF.Sigmoid, bias=PF[:, PF_BG + j:PF_BG + j + 1]),
                        reads=[rB[bk], rPF], writes=[G4[1]])
                pan, rpan = next_panel()
                for jj in range(4):
                    j = hf * 4 + jj
                    bk = acc_bank()
                    mm_run(PS[:, bk, 0:T], [(pan[:, kc, jj * 128:(jj + 1) * 128], V[0][:, kc, 0:T]) for kc in range(8)],
                           reads=[V[1], rpan], writes=[rB[bk]])
                    kb.op("dve", lambda e, bk=bk, jj=jj, j=j: e.tensor_tensor(out=PA[0][:, j, 0:T], in0=PS[:, bk, 0:T],
                                                                             in1=G4[0][:, jj, 0:T], op=ALU.mult),
                          reads=[rB[bk], G4[1]], writes=[PA[1]])
            gate(5)
            dump("v", V[0][:, :, 0:T], V[1], [128, 8, T], BF16)
            dump("pa", PA[0][:, :, 0:T], PA[1], [128, 8, T], F32)
            for hf in range(2):
                pan, rpan = next_panel()
                for jj in range(4):
                    j = hf * 4 + jj
                    bk = acc_bank()
                    mm_run(PS[:, bk, 0:T], [(pan[:, kc, jj * 128:(jj + 1) * 128], XN[:, kc, 0:T]) for kc in range(8)],
                           reads=[rXN, rpan], writes=[rB[bk]])
                    kb.op("act", lambda e, bk=bk, jj=jj, j=j: e.activation(
                        out=G4[0][:, jj, 0:T], in_=PS[:, bk, 0:T], func=AF.Sigmoid,
                        bias=PF[:, PF_BG + 8 + j:PF_BG + 8 + j + 1]), reads=[rB[bk], rPF], writes=[G4[1]])
                pan0, rpan0 = next_panel()
                pan1, rpan1 = next_panel()
                for jj in range(4):
                    j = hf * 4 + jj
                    bk = acc_bank()
                    pairs = [(pan0[:, kc, jj * 128:(jj + 1) * 128], YNF[0][:, kc, 0:T]) for kc in range(8)] + \
                            [(pan1[:, kc, jj * 128:(jj + 1) * 128], YNF[0][:, 8 + kc, 0:T]) for kc in range(8)]
                    mm_run(PS[:, bk, 0:T], pairs, reads=[YNF[1], rpan0, rpan1], writes=[rB[bk]])
                    kb.op("dve", lambda e, bk=bk, jj=jj: e.tensor_tensor(out=G4[0][:, jj, 0:T], in0=PS[:, bk, 0:T],
                                                                        in1=G4[0][:, jj, 0:T], op=ALU.mult),
                          reads=[rB[bk], G4[1]], writes=[G4[1]])
                    kb.op("pool", lambda e, jj=jj, j=j: e.tensor_tensor(out=MG[0][:, j, 0:T], in0=G4[0][:, jj, 0:T],
                                                                       in1=PA[0][:, j, 0:T], op=ALU.add),
                          reads=[G4[1], PA[1]], writes=[MG[1]])
            gate(6)
            dump("mg", MG[0][:, :, 0:T], MG[1], [128, 8, T], BF16)
            for hf in range(2):
                pan, rpan = next_panel()
                for b in range(nb):
                    bk = acc_bank()
                    mm_run(PS[0:tb, bk, :], [(MG[0][:, kc, b * 128:b * 128 + tb], pan[:, kc, :]) for kc in range(8)],
                           reads=[MG[1], rpan], writes=[rB[bk]])
                    kb.op("act", lambda e, bk=bk, b=b, hf=hf: e.copy(out=MO[b][0][0:tb, hf * 512:(hf + 1) * 512],
                                                                    in_=PS[0:tb, bk, :]), reads=[rB[bk]], writes=[MO[b][1]])
            for b in range(nb):
                mo, rmo = MO[b]
                xt, rxt = XT2[b % 2]
                kb.dma("sp", xt[0:tb, :], xd[b * 128:b * 128 + tb, :], rxt, writes=[rxt])
                kb.op("act", lambda e, mo=mo: e.activation(out=JUNK[0][0:tb, :], in_=mo[0:tb, :], func=AF.Square,
                                                           accum_out=STAT[0:tb, 0:1]), reads=[rmo], writes=[JUNK[1], rSTAT])
                rstd_from_ss(0, 1, 1024, tb)
                kb.op("dve", lambda e, mo=mo: e.scalar_tensor_tensor(
                    out=mo[0:tb, :], in0=mo[0:tb, :], scalar=STAT[0:tb, 1:2], in1=PF[0:tb, PF_NMPOST:PF_NMPOST + 1024],
                    op0=ALU.mult, op1=ALU.mult), reads=[rmo, rSTAT, rPF], writes=[rmo])
                kb.op("pool", lambda e, mo=mo, xt=xt: e.tensor_tensor(out=mo[0:tb, :], in0=mo[0:tb, :], in1=xt[0:tb, :],
                                                                     op=ALU.add), reads=[rmo, rxt], writes=[rmo])
                norm_to_fm(mo, rmo, b, tb, HS16[b % 2], XN, rXN)
            gate(7)
            dump("h", MO[0][0][0:tb, :], MO[0][1], [tb, 1024], F32)
            for q in range(8):
                pan, rpan = next_panel()
                for jj in range(4):
                    c = q * 4 + jj
                    bk = acc_bank()
                    mm_run(PS[:, bk, 0:T], [(pan[:, kc, jj * 128:(jj + 1) * 128], XN[:, kc, 0:T]) for kc in range(8)],
                           reads=[rXN, rpan], writes=[rB[bk]])
                    rt, rrt = RT[c % 2]
                    kb.op("act", lambda e, bk=bk, rt=rt: e.activation(out=rt[:, 0:T], in_=PS[:, bk, 0:T], func=AF.Relu),
                          reads=[rB[bk]], writes=[rrt])
                    kb.op("pool", lambda e, rt=rt, c=c: e.tensor_tensor(out=FF[0][:, c, 0:T], in0=rt[:, 0:T], in1=rt[:, 0:T],
                                                                       op=ALU.mult), reads=[rrt], writes=[FF[1]])
            for hf in range(2):
                for kg in range(4):
                    pan, rpan = next_panel()
                    for b in range(nb):
                        mm_run(PS[0:tb, b, :],
                               [(FF[0][:, kg * 8 + kc, b * 128:b * 128 + tb], pan[:, kc, :]) for kc in range(8)],
                               reads=[FF[1], rpan], writes=[rB[b]], first=(kg == 0), last=(kg == 3))
                for b in range(nb):
                    kb.op("act", lambda e, b=b, hf=hf: e.copy(out=FO[b][0][0:tb, hf * 512:(hf + 1) * 512], in_=PS[0:tb, b, :]),
                          reads=[rB[b]], writes=[FO[b][1]])
            dump("ff", FF[0][:, :, 0:T], FF[1], [128, 32, T], BF16)
            dump("fo", FO[0][0][0:tb, :], FO[0][1], [tb, 1024], F32)
            for b in range(nb):
                fo, rfo = FO[b]
                mo, rmo = MO[b]
                kb.op("act", lambda e, fo=fo: e.activation(out=JUNK[0][0:tb, :], in_=fo[0:tb, :], func=AF.Square,
                                                           accum_out=STAT[0:tb, 0:1]), reads=[rfo], writes=[JUNK[1], rSTAT])
                rstd_from_ss(0, 1, 1024, tb)
                kb.op("dve", lambda e, fo=fo: e.scalar_tensor_tensor(
                    out=fo[0:tb, :], in0=fo[0:tb, :], scalar=STAT[0:tb, 1:2], in1=PF[0:tb, PF_NFPOST:PF_NFPOST + 1024],
                    op0=ALU.mult, op1=ALU.mult), reads=[rfo, rSTAT, rPF], writes=[rfo])
                kb.op("pool", lambda e, fo=fo, mo=mo: e.tensor_tensor(out=fo[0:tb, :], in0=fo[0:tb, :], in1=mo[0:tb, :],
                                                                     op=ALU.add), reads=[rfo, rmo], writes=[rfo])
                kb.dma("sp", yd[b * 128:b * 128 + tb, :], fo[0:tb, :], rfo, reads=[rfo])
        for tl in tiles:
            try:
                run_tile(*tl)
            except _Stop:
                pass
        kb.finish()
        block = es.enter_context(nc.Block())
        kb.replay(block)
    return nc


def _host_tables(p):
    pf = np.zeros((128, PF_COLS), np.float32)
    pf[:, PF_CAW:PF_CAW + 24] = p["conv_a_w"][0].reshape(3, 8, 128).transpose(2, 1, 0).reshape(128, 24)
    pf[:, PF_CSW:PF_CSW + 128] = p["conv_ssm_w"][0].reshape(4, 32, 128).transpose(2, 1, 0).reshape(128, 128)
    pf[:, PF_CSB:PF_CSB + 32] = p["conv_ssm_b"][0].reshape(32, 128).T
    pf[:, PF_BG:PF_BG + 16] = p["b_gate"][0].reshape(16, 128).T
    pf[:, PF_NMP:PF_NMP + 8] = p["norm_mix_pre"][0].reshape(8, 128).T
    pf[:, PF_NFP:PF_NFP + 8] = p["norm_ffn_pre"][0].reshape(8, 128).T
    pf[:, PF_SSN:PF_SSN + 16] = p["ssm_norm"][0].reshape(16, 128).T
    pf[:, PF_DTB:PF_DTB + 32] = p["dt_bias"][0][None, :]
    pf[:, PF_ALOG:PF_ALOG + 32] = p["a_log"][0][None, :]
    pf[:, PF_DSK:PF_DSK + 32] = p["d_skip"][0][None, :]
    pf[:, PF_NMPOST:PF_NMPOST + 1024] = p["norm_mix_post"][0][None, :]
    pf[:, PF_NFPOST:PF_NFPOST + 1024] = p["norm_ffn_post"][0][None, :]
    cb = np.zeros((128, CB_COLS), np.float32)
    k = np.arange(128)
    cb[:, CB_ID:CB_ID + 128] = np.eye(128)
    l = np.arange(64)
    cb[:, CB_TRI2:CB_TRI2 + 64] = ((k[:, None] % 64) <= l[None, :])
    same = (k[:, None] // 64) == (k[None, :] // 64)
    cb[:, CB_BONES:CB_BONES + 128] = same
    cb[:, CB_ONESA:CB_ONESA + 128] = (k[:, None] < 64) * np.ones((1, 128))
    cb[:, CB_ONESB:CB_ONESB + 128] = (k[:, None] >= 64) * np.ones((1, 128))
    cb[:, CB_TRI128:CB_TRI128 + 128] = same & ((k[:, None] % 64) <= (k[None, :] % 64))
    neg = np.where((k[:, None] % 64) > l[None, :], -30000.0, 0.0)
    cb[:, CB_NEG:CB_NEG + 1024] = np.tile(neg, (1, 16))
    return pf, cb.astype(ml_dtypes.bfloat16)


_NC_CACHE = {}


def kernel(x_prompt, x_sample, state_conv_a, state_conv_ssm, state_ssm, norm_mix_pre, w_in, b_gate, conv_a_w, w_a_out,
           conv_ssm_w, conv_ssm_b, dt_bias, a_log, d_skip, ssm_norm, w_ssm_out, w_o, norm_mix_post, norm_ffn_pre,
           w_ff1, w_ff2, norm_ffn_post, _tiles_limit=None, _cores=None, _dbg=False, _stop=None):
    p = dict(norm_mix_pre=norm_mix_pre, b_gate=b_gate, conv_a_w=conv_a_w, conv_ssm_w=conv_ssm_w, conv_ssm_b=conv_ssm_b,
             dt_bias=dt_bias, a_log=a_log, d_skip=d_skip, ssm_norm=ssm_norm, norm_mix_post=norm_mix_post,
             norm_ffn_pre=norm_ffn_pre, norm_ffn_post=norm_ffn_post)
    p = {k: np.asarray(v, np.float32) for k, v in p.items()}
    pf, cb = _host_tables(p)
    x_prompt = np.asarray(x_prompt, np.float32)
    x_sample = np.asarray(x_sample, np.float32)
    sca_all = np.asarray(state_conv_a, np.float32)[0]
    scm_all = np.asarray(state_conv_ssm, np.float32)[0]
    sst_all = np.asarray(state_ssm, np.float32)[0]
    key = (None if _tiles_limit is None else tuple(_tiles_limit), _dbg, _stop)
    if key not in _NC_CACHE:
        _NC_CACHE[key] = build_program(_tiles_limit, _dbg, _stop)
    nc = _NC_CACHE[key]
    cores = list(range(NCORES)) if _cores is None else _cores
    shared = dict(pf=pf, cb=cb, w_in=np.ascontiguousarray(np.asarray(w_in, np.float32)[0]),
                  w_a_out=np.ascontiguousarray(np.asarray(w_a_out, np.float32)[0]),
                  w_ssm_out=np.ascontiguousarray(np.asarray(w_ssm_out, np.float32)[0]),
                  w_o=np.ascontiguousarray(np.asarray(w_o, np.float32)[0]),
                  w_ff1=np.ascontiguousarray(np.asarray(w_ff1, np.float32)[0]),
                  w_ff2=np.ascontiguousarray(np.asarray(w_ff2, np.float32)[0]))
    in_maps = []
    for c in cores:
        s2 = slice(2 * c, 2 * c + 2)
        m = dict(shared)
        m["xp"] = np.ascontiguousarray(x_prompt[s2])
        m["xs"] = np.ascontiguousarray(x_sample[s2])
        m["sca"] = np.ascontiguousarray(sca_all[s2].reshape(2, 2, 8, 128).transpose(0, 3, 2, 1).reshape(2, 128, 16))
        m["scm"] = np.ascontiguousarray(scm_all[s2].reshape(2, 3, 32, 128).transpose(0, 3, 2, 1).reshape(2, 128, 96))
        m["sst"] = np.ascontiguousarray(sst_all[s2].reshape(2, 2048, 128).transpose(0, 2, 1))
        in_maps.append(m)
    res = run_bass_kernel_spmd(nc, in_maps, core_ids=list(range(len(cores))))
    R = res.results
    if _dbg:
        kernel.dbg = {k: v for k, v in R[0].items() if k.startswith("dbg_")}
    yp = np.concatenate([r["yp"] for r in R], axis=0)
    ys_ = np.concatenate([r["ys"] for r in R], axis=0)
    oca = np.stack([r["oca"] for r in R])
    ocm = np.stack([r["ocm"] for r in R])
    ost = np.stack([r["ost"] for r in R])
    n = len(cores)

    def un_ca(a):
        return a.reshape(n * 2, 128, 8, 2).transpose(0, 3, 2, 1).reshape(n * 2, 2, 1024)

    def un_cm(a):
        return a.reshape(n * 2, 128, 32, 3).transpose(0, 3, 2, 1).reshape(n * 2, 3, 4096)

    def un_st(a):
        return a.reshape(n * 2, 128, 2048).transpose(0, 2, 1).reshape(n * 2, 32, 64, 128)

    outs = (yp, ys_,
            un_ca(oca[:, 0:2])[None], un_cm(ocm[:, 0:2])[None], un_st(ost[:, 0:2])[None],
            un_ca(oca[:, 2:4])[None], un_cm(ocm[:, 2:4])[None], un_st(ost[:, 2:4])[None])
    return tuple(np.ascontiguousarray(o, dtype=np.float32) for o in outs)
```

```python
import numpy as np
import ml_dtypes
from contextlib import ExitStack
import concourse.bass as bass
import concourse.mybir as mybir
from concourse.bass_utils import run_bass_kernel_spmd

F32 = mybir.dt.float32
BF16 = mybir.dt.bfloat16
AF = mybir.ActivationFunctionType
ALU = mybir.AluOpType
AX = mybir.AxisListType

NCORES = 8
D = 1024
SEQ = 2048
DSEQ = 64
PROJ = 11296
OFF_AB, OFF_AC, OFF_AH, OFF_Z, OFF_XBC, OFF_DT, OFF_GATE = 0, 1024, 2048, 3072, 5120, 9216, 9248
EPS = 1e-6
NS = 4

PF_CAW = 0
PF_CSW = 24
PF_CSB = 152
PF_BG = 184
PF_NMP = 200
PF_NFP = 208
PF_SSN = 216
PF_DTB = 232
PF_ALOG = 264
PF_DSK = 296
PF_NMPOST = 328
PF_NFPOST = 1352
PF_COLS = 2376
CB_ID = 0
CB_TRI2 = 128
CB_BONES = 192
CB_ONESA = 320
CB_ONESB = 448
CB_TRI128 = 576
CB_NEG = 704
CB_COLS = 1728


class Res:
    __slots__ = ("name", "lw", "rd", "ov", "dsem", "dcnt", "rng", "excl")

    def __init__(self, name):
        self.name = name
        self.lw = None
        self.rd = {}
        self.ov = []
        self.dsem = None
        self.dcnt = 0
        self.rng = None
        self.excl = False


class KB:
    ENG = ("pe", "act", "dve", "pool")

    def __init__(self, nc, es):
        self.nc = nc
        self.es = es
        self.streams = {k: [] for k in ("pe", "act", "dve", "pool", "sp")}
        self.sem = {e: es.enter_context(nc.semaphore("s_" + e)) for e in self.ENG}
        self.tick = {e: 0 for e in self.ENG}
        self.waited = {k: {} for k in self.streams}
        self.dsems = []
        self.all_res = []

    def res(self, name):
        r = Res(name)
        self.all_res.append(r)
        return r

    def _dsem(self, r):
        if r.dsem is None:
            r.dsem = self.es.enter_context(self.nc.semaphore("d%d" % len(self.dsems)))
            self.dsems.append(r)
        return r.dsem

    def _deps(self, eng, reads, writes):
        deps = []
        for r in reads:
            if r.lw is not None:
                deps.append(r.lw)
            if r.excl:
                for q, t in r.rd.items():
                    if q != eng:
                        deps.append((q, t))
        for w in writes:
            for x in [w] + w.ov:
                if x.lw is not None and x.lw[0] != eng:
                    deps.append(x.lw)
                for q, t in x.rd.items():
                    if q != eng:
                        deps.append((q, t))
        wd = self.waited[eng]
        for q, t in deps:
            if wd.get(q, 0) < t:
                wd[q] = t
                self.streams[eng].append(("w", q, t))

    def wait(self, eng, q, t):
        wd = self.waited[eng]
        if wd.get(q, 0) < t:
            wd[q] = t
            self.streams[eng].append(("w", q, t))

    def op(self, eng, fn, reads=(), writes=(), inc=True):
        self._deps(eng, reads, writes)
        t = self.tick[eng] + 1
        if inc:
            self.tick[eng] = t
        self.streams[eng].append(("op", fn, inc))
        for r in reads:
            r.rd[eng] = t
        for w in writes:
            w.lw = (eng, t)
            w.rd = {}
        return t

    def dma(self, eng, out, in_, dres, reads=(), writes=()):
        self._deps(eng, reads, writes)
        self._dsem(dres)
        dres.dcnt += 16
        key = (id(dres), dres.dcnt)
        q = ("d", id(dres))
        self._dq = getattr(self, "_dq", {})
        self._dq[q] = dres
        self.streams[eng].append(("dma", out, in_, dres))
        for r in reads:
            r.rd[q] = dres.dcnt
        for w in writes:
            w.lw = (q, dres.dcnt)
            w.rd = {}

    def semof(self, q):
        if isinstance(q, tuple):
            return self._dq[q].dsem
        return self.sem[q]

    def finish(self):
        for r in self.dsems:
            self.wait("sp", ("d", id(r)), r.dcnt)

    def replay(self, block):
        handles = {"pe": block.tensor, "act": block.scalar, "dve": block.vector, "pool": block.gpsimd,
                   "sp": block.sync}
        for eng, deco in handles.items():
            stream = self.streams[eng]
            own = self.sem.get(eng)

            def body(e, stream=stream, own=own):
                for ent in stream:
                    k = ent[0]
                    if k == "w":
                        e.wait_ge(self.semof(ent[1]), ent[2])
                    elif k == "op":
                        ins = ent[1](e)
                        if ent[2]:
                            ins.then_inc(own, 1)
                    else:
                        e.dma_start(out=ent[1], in_=ent[2]).then_inc(ent[3].dsem, 16)
            deco(body)


class Arena:
    def __init__(self, kb, tensor, nbytes):
        self.kb = kb
        self.t = tensor
        self.nbytes = nbytes
        self.views = []

    def view(self, name, off, shape, dt):
        esz = 2 if dt == BF16 else 4
        n = int(np.prod(shape))
        nb = n * esz
        assert off % 4 == 0 and off + nb <= self.nbytes, (name, off, nb, self.nbytes)
        w0, w1 = off // 4, (off + nb + 3) // 4
        ap = self.t[:, w0:w1]
        if dt == BF16:
            ap = ap.bitcast(BF16)[:, 0:n]
        if len(shape) == 2:
            ap = ap.rearrange("p (a b) -> p a b", a=shape[0])
        elif len(shape) == 3:
            ap = ap.rearrange("p (a b c) -> p a b c", a=shape[0], b=shape[1])
        r = self.kb.res(name)
        r.rng = (off, off + nb)
        for (o, rr) in self.views:
            if o.rng[0] < r.rng[1] and r.rng[0] < o.rng[1]:
                r.ov.append(o)
                o.ov.append(r)
        self.views.append((r, ap))
        return ap, r


import os
VARP = int(os.environ.get('VARP', '128'))


class _Stop(Exception):
    pass


def build_program(tiles_limit=None, dbg=False, stop_after=None):
    nc = bass.Bass("TRN2", target_bir_lowering=False)
    dram_in = lambda name, shape, dt=F32: nc.dram_tensor(name, shape, dt, kind="ExternalInput").ap()
    dram_out = lambda name, shape, dt=F32: nc.dram_tensor(name, shape, dt, kind="ExternalOutput").ap()
    xp = dram_in("xp", [2, SEQ, D])
    xs = dram_in("xs", [2, DSEQ, D])
    sca = dram_in("sca", [2, 128, 16])
    scm = dram_in("scm", [2, 128, 96])
    sst = dram_in("sst", [2, 128, 2048])
    pf_d = dram_in("pf", [128, PF_COLS])
    cb_d = dram_in("cb", [128, CB_COLS], BF16)
    w_in = dram_in("w_in", [D, PROJ])
    w_a = dram_in("w_a_out", [D, D])
    w_s = dram_in("w_ssm_out", [2 * D, D])
    w_o = dram_in("w_o", [D, D])
    w_f1 = dram_in("w_ff1", [D, 4 * D])
    w_f2 = dram_in("w_ff2", [4 * D, D])
    yp = dram_out("yp", [2, SEQ, D])
    ys = dram_out("ys", [2, DSEQ, D])
    oca = dram_out("oca", [4, 128, 16])
    ocm = dram_out("ocm", [4, 128, 96])
    ost = dram_out("ost", [4, 128, 2048])
    wi_b = nc.dram_tensor("wi_b", [D, PROJ], BF16).ap()
    wa_b = nc.dram_tensor("wa_b", [D, D], BF16).ap()
    ws_b = nc.dram_tensor("ws_b", [2 * D, D], BF16).ap()
    wo_b = nc.dram_tensor("wo_b", [D, D], BF16).ap()
    wf1_b = nc.dram_tensor("wf1_b", [D, 4 * D], BF16).ap()
    wf2_b = nc.dram_tensor("wf2_b", [4 * D, D], BF16).ap()

    with ExitStack() as es:
        kb = KB(nc, es)
        sbt = lambda name, shape, dt: es.enter_context(nc.sbuf_tensor(name, shape, dt))
        PF = sbt("PF", [128, PF_COLS], F32); rPF = kb.res("PF")
        CB = sbt("CB", [128, CB_COLS], BF16); rCB = kb.res("CB")
        PAN = [sbt("PAN%d" % i, [128, 8, 512], BF16) for i in range(NS)]
        rPAN = [kb.res("PAN%d" % i) for i in range(NS)]
        XN = sbt("XN", [128, 8, 512], BF16); rXN = kb.res("XN")
        DTW = sbt("DTW", [128, 8, 32], BF16); rDTW = kb.res("DTW")
        HISTA = sbt("HISTA", [128, 8, 2], F32); rHISTA = kb.res("HISTA")
        HISTM = sbt("HISTM", [128, 32, 3], F32); rHISTM = kb.res("HISTM")
        ST = sbt("ST", [128, 2048], F32); rST = kb.res("ST")
        SBF = [sbt("SBF%d" % i, [128, 2048], BF16) for i in range(2)]
        rSBF = [kb.res("SBF%d" % i) for i in range(2)]
        ANEG = sbt("ANEG", [128, 32], F32); rANEG = kb.res("ANEG")
        SM = sbt("SM", [128, 4, 320], F32)
        rSM = [kb.res("SM%d" % b) for b in range(4)]
        STAT = sbt("STAT", [128, 16], F32); rSTAT = kb.res("STAT")
        EPSC = sbt("EPSC", [128, 2], F32); rEPSC = kb.res("EPSC")
        C_DT, C_ADT, C_ACUM, C_EA, C_CD, C_DEC, C_T0, C_RS = 0, 32, 64, 96, 128, 192, 224, 256
        SMH = sbt("SMH", [128, 4, 64], BF16)
        rSMH = [kb.res("SMH%d" % b) for b in range(4)]
        KBY = 1024
        AR_BYTES = 114 * KBY
        ARt = sbt("ARENA", [128, AR_BYTES // 4], F32)
        ar = Arena(kb, ARt, AR_BYTES)
        XT = [ar.view("XT%d" % i, i * 4 * KBY, [1024], F32) for i in range(2)]
        XS16 = [ar.view("XS16_%d" % i, 8 * KBY + i * 2 * KBY, [1024], BF16) for i in range(2)]
        JUNK = ar.view("JUNK", 12 * KBY, [1024], BF16)
        SZ = ar.view("SZ", 0, [4, 2048], BF16)
        XSF = ar.view("XSF", 16 * KBY, [16, 512], BF16)
        BF_ = ar.view("BF", 32 * KBY, [8, 512], BF16)
        CF = ar.view("CF", 40 * KBY, [8, 512], BF16)
        XB4 = ar.view("XB4", 48 * KBY, [4, 516], F32)
        CV4 = ar.view("CV4", 57 * KBY, [4, 512], F32)
        XSt = ar.view("XSt", 48 * KBY, [2048], BF16)
        XC = ar.view("XC", 52 * KBY, [2048], BF16)
        XCD = ar.view("XCD", 56 * KBY, [2048], BF16)
        Bt = ar.view("Bt", 60 * KBY, [1024], BF16)
        RHI = ar.view("RHI", 62 * KBY, [32, 64], BF16)
        RLO = ar.view("RLO", 66 * KBY, [32, 64], BF16)
        Dh = ar.view("Dh", 70 * KBY, [16, 64], F32)
        Eh = ar.view("Eh", 74 * KBY, [16, 64], F32)
        Mh = ar.view("Mh", 78 * KBY, [16, 64], BF16)
        T1 = ar.view("T1", 80 * KBY, [1024], F32)
        Y1 = ar.view("Y1", 84 * KBY, [1024], F32)
        Y3 = ar.view("Y3", 88 * KBY, [1024], F32)
        YG = ar.view("YG", 92 * KBY, [1024], F32)
        YN = ar.view("YN", 96 * KBY, [1024], BF16)
        YNF = ar.view("YNF", 98 * KBY, [16, 512], BF16)
        AC4 = ar.view("AC4", 0, [4, 512], F32)
        U4 = ar.view("U4", 8 * KBY, [4, 516], F32)
        TT4 = ar.view("TT4", 17 * KBY, [4, 512], F32)
        V = ar.view("V", 25 * KBY, [8, 512], BF16)
        G4 = ar.view("G4", 33 * KBY, [4, 512], F32)
        PA = ar.view("PA", 41 * KBY, [8, 512], F32)
        MG = ar.view("MG", 57 * KBY, [8, 512], BF16)
        MO = [ar.view("MO%d" % b, 65 * KBY + b * 4 * KBY, [1024], F32) for b in range(4)]
        XT2 = [ar.view("XT2_%d" % i, 81 * KBY + i * 4 * KBY, [1024], F32) for i in range(2)]
        HS16 = [ar.view("HS16_%d" % i, 89 * KBY + i * 2 * KBY, [1024], BF16) for i in range(2)]
        FF = ar.view("FF", 0, [32, 512], BF16)
        RT = [ar.view("RT%d" % i, 32 * KBY + i * 2 * KBY, [512], F32) for i in range(2)]
        FO = [ar.view("FO%d" % b, 36 * KBY + b * 4 * KBY, [1024], F32) for b in range(4)]
        SGW = 2824
        NSG = 4
        SGI = [ar.view("SGI%d" % i, i * 12 * KBY, [SGW], F32) for i in range(NSG)]
        SGO = [ar.view("SGO%d" % i, 48 * KBY + i * 6 * KBY, [SGW], BF16) for i in range(NSG)]
        PS = es.enter_context(nc.psum_tensor("PS", [128, 8, 512], F32))
        rB = [kb.res("bank%d" % i) for i in range(8)]
        for r_ in rB:
            r_.excl = True
        PSb = PS[:].bitcast(BF16)

        ident = CB[:, CB_ID:CB_ID + 128]
        tri2 = CB[:, CB_TRI2:CB_TRI2 + 64]
        bones = CB[:, CB_BONES:CB_BONES + 128]
        onesA = CB[:, CB_ONESA:CB_ONESA + 128]
        onesB = CB[:, CB_ONESB:CB_ONESB + 128]
        tri128 = CB[:, CB_TRI128:CB_TRI128 + 128]
        negm = CB[:, CB_NEG:CB_NEG + 1024]

        dbg_names = []

        def dump(name, ap, res, shape, dt):
            if not dbg:
                return
            d = nc.dram_tensor("dbg_" + name, shape, dt, kind="ExternalOutput").ap()
            kb.dma("sp", d, ap, res, reads=[res])
            dbg_names.append("dbg_" + name)

        kb.dma("sp", PF[:], pf_d, rPF, writes=[rPF])
        kb.dma("sp", CB[:], cb_d, rCB, writes=[rCB])
        kb.op("pool", lambda e: e.memset(EPSC[:, 0:1], EPS), writes=[rEPSC])
        kb.op("pool", lambda e: e.memset(EPSC[:, 1:2], 1.0), writes=[rEPSC])
        kb.op("act", lambda e: e.activation(out=ANEG[:], in_=PF[:, PF_ALOG:PF_ALOG + 32], func=AF.Exp),
              reads=[rPF], writes=[rANEG])
        kb.op("dve", lambda e: e.tensor_scalar(out=ANEG[:], in0=ANEG[:], scalar1=-1.0, scalar2=None, op0=ALU.mult),
              reads=[rANEG], writes=[rANEG])

        conv_jobs = []
        def add_mat(src, dst, K, N, piece, scol):
            for rc in range(K // 128):
                for c0 in range(0, N, piece):
                    w = min(piece, N - c0)
                    conv_jobs.append((src[rc * 128:(rc + 1) * 128, c0:c0 + w], dst[rc * 128:(rc + 1) * 128, c0:c0 + w],
                                      None if scol is None else scol + rc, w))
        add_mat(w_in, wi_b, D, PROJ, SGW, PF_NMP)
        add_mat(w_a, wa_b, D, D, 1024, None)
        add_mat(w_s, ws_b, 2 * D, D, 1024, PF_SSN)
        add_mat(w_o, wo_b, D, D, 1024, None)
        add_mat(w_f1, wf1_b, D, 4 * D, 2048, PF_NFP)
        add_mat(w_f2, wf2_b, 4 * D, D, 1024, None)
        def conv_store(i):
            (src, dst, scol, w) = conv_jobs[i]
            so, ro = SGO[i % NSG]
            kb.dma("sp", dst, so[:, 0:w], ro, reads=[ro])
        for i, (src, dst, scol, w) in enumerate(conv_jobs):
            si, ri = SGI[i % NSG]
            so, ro = SGO[i % NSG]
            if i >= NSG - 1:
                conv_store(i - (NSG - 1))
            kb.dma("sp", si[:, 0:w], src, ri, writes=[ri])
            eng = "dve" if (i % 5) in (0, 2, 4) else "act"
            if eng == "dve":
                if scol is None:
                    kb.op(eng, lambda e, so=so, si=si, w=w: e.tensor_copy(out=so[:, 0:w], in_=si[:, 0:w]),
                          reads=[ri], writes=[ro])
                else:
                    kb.op(eng, lambda e, so=so, si=si, w=w, scol=scol: e.tensor_scalar(
                        out=so[:, 0:w], in0=si[:, 0:w], scalar1=PF[:, scol:scol + 1], scalar2=None, op0=ALU.mult),
                        reads=[ri, rPF], writes=[ro])
            else:
                if scol is None:
                    kb.op(eng, lambda e, so=so, si=si, w=w: e.copy(out=so[:, 0:w], in_=si[:, 0:w]),
                          reads=[ri], writes=[ro])
                else:
                    kb.op(eng, lambda e, so=so, si=si, w=w, scol=scol: e.activation(
                        out=so[:, 0:w], in_=si[:, 0:w], func=AF.Identity, scale=PF[:, scol:scol + 1]),
                        reads=[ri, rPF], writes=[ro])
        for i in range(max(0, len(conv_jobs) - (NSG - 1)), len(conv_jobs)):
            conv_store(i)
        for so, ro in SGO:
            kb.wait("sp", ("d", id(ro)), ro.dcnt)
        kb.dma("sp", DTW[:], wi_b[:, OFF_DT:OFF_DT + 32].rearrange("(kc p) n -> p kc n", p=128), rDTW, writes=[rDTW])

        tiles = []
        for s in range(2):
            for ti in range(4):
                tiles.append((s, xp[s, ti * 512:(ti + 1) * 512, :], yp[s, ti * 512:(ti + 1) * 512, :], 512,
                              ti == 0, ti == 3, None))
        for s in range(2):
            tiles.append((2 + s, xs[s], ys[s], 64, True, True, s))
        if tiles_limit is not None:
            tiles = [tiles[i] for i in tiles_limit]

        def panel_list(T):
            pl = []
            for q in range(4):
                pl.append((wi_b, 0, OFF_Z + q * 512))
            for q in range(8):
                pl.append((wi_b, 0, OFF_XBC + q * 512))
            for hf in range(2):
                pl.append((wi_b, 0, OFF_AC + hf * 512))
                pl.append((wi_b, 0, OFF_AH + hf * 512))
                pl.append((wi_b, 0, OFF_AB + hf * 512))
            for hf in range(2):
                pl.append((wi_b, 0, OFF_GATE + hf * 512))
                pl.append((wa_b, 0, hf * 512))
            for hf in range(2):
                pl.append((wi_b, 0, OFF_GATE + 1024 + hf * 512))
                pl.append((ws_b, 0, hf * 512))
                pl.append((ws_b, 1, hf * 512))
            for hf in range(2):
                pl.append((wo_b, 0, hf * 512))
            for q in range(8):
                pl.append((wf1_b, 0, q * 512))
            for hf in range(2):
                for kg in range(4):
                    pl.append((wf2_b, kg, hf * 512))
            return pl
        all_panels = []
        for tl in tiles:
            all_panels += panel_list(tl[3])
        pstate = {"issued": 0, "used": 0}

        def issue_panels(upto):
            while pstate["issued"] < min(upto, len(all_panels)):
                i = pstate["issued"]
                mat, kg, c0 = all_panels[i]
                slot = i % NS
                src = mat[kg * 1024:(kg + 1) * 1024, c0:c0 + 512].rearrange("(kc p) n -> p kc n", p=128)
                kb.dma("sp", PAN[slot][:], src, rPAN[slot], writes=[rPAN[slot]])
                pstate["issued"] += 1

        def next_panel():
            i = pstate["used"]
            issue_panels(i + NS - 1)
            pstate["used"] += 1
            return PAN[i % NS], rPAN[i % NS]

        bank_rr = {"i": 0}

        def acc_bank():
            i = bank_rr["i"]
            bank_rr["i"] = (i + 1) % 4
            return i

        def mm_run(out, pairs, reads, writes, first=True, last=True):
            n = len(pairs)
            for i, (l, r) in enumerate(pairs):
                kb.op("pe", lambda e, l=l, r=r, st=(first and i == 0), sp_=(last and i == n - 1):
                      e.matmul(out, lhsT=l, rhs=r, start=st, stop=sp_),
                      reads=reads if i == 0 else (), writes=writes if i == 0 else (), inc=(i == n - 1))

        def tr_run(outs_ins, reads, writes, idn):
            n = len(outs_ins)
            for i, (o, a) in enumerate(outs_ins):
                kb.op("pe", lambda e, o=o, a=a: e.transpose(o, a, idn),
                      reads=reads if i == 0 else (), writes=writes if i == 0 else (), inc=(i == n - 1))

        def rstd_from_ss(ss_col, rs_col, n, tb, w=1):
            kb.op("act", lambda e: e.activation(out=STAT[0:tb, rs_col:rs_col + w], in_=STAT[0:tb, ss_col:ss_col + w],
                                                func=AF.Ln, scale=1.0 / n, bias=EPSC[0:tb, 0:1]),
                  reads=[rSTAT, rEPSC], writes=[rSTAT])
            kb.op("act", lambda e: e.activation(out=STAT[0:tb, rs_col:rs_col + w], in_=STAT[0:tb, rs_col:rs_col + w],
                                                func=AF.Exp, scale=-0.5),
                  reads=[rSTAT], writes=[rSTAT])

        def norm_to_fm(src_ap, src_res, b, tb, s16, dst, dst_res):
            (s16ap, s16res) = s16
            kb.op("act", lambda e: e.activation(out=JUNK[0][0:tb, :], in_=src_ap[0:tb, :], func=AF.Square,
                                                accum_out=STAT[0:tb, 0:1]),
                  reads=[src_res], writes=[JUNK[1], rSTAT])
            rstd_from_ss(0, 1, 1024, tb)
            kb.op("dve", lambda e: e.tensor_scalar(out=s16ap[0:tb, :], in0=src_ap[0:tb, :], scalar1=STAT[0:tb, 1:2],
                                                   scalar2=None, op0=ALU.mult),
                  reads=[src_res, rSTAT], writes=[s16res])
            tpb = 4
            tr_run([(PSb[:, tpb, kc * 128:kc * 128 + tb], s16ap[0:tb, kc * 128:(kc + 1) * 128]) for kc in range(8)],
                   reads=[s16res, rCB], writes=[rB[tpb]], idn=ident[0:tb, 0:tb])
            kb.op("act", lambda e: e.copy(out=dst[:, :, b * 128:b * 128 + tb],
                                          in_=PSb[:, tpb, :].rearrange("p (k t) -> p k t", k=8)[:, :, 0:tb]),
                  reads=[rB[tpb]], writes=[dst_res])

        def gate(n):
            if stop_after is not None and n > stop_after:
                raise _Stop()

        def run_tile(sidx, xd, yd, T, first, last, samp):
            tb = min(T, 128)
            nb = T // tb
            nch = tb // 64
            if first:
                if samp is None:
                    kb.op("pool", lambda e: e.memset(HISTA[:], 0.0), writes=[rHISTA])
                    kb.op("pool", lambda e: e.memset(HISTM[:], 0.0), writes=[rHISTM])
                    kb.op("pool", lambda e: e.memset(ST[:], 0.0), writes=[rST])
                else:
                    kb.dma("sp", HISTA[:], sca[samp].rearrange("p (j i) -> p j i", i=2), rHISTA, writes=[rHISTA])
                    kb.dma("sp", HISTM[:], scm[samp].rearrange("p (c i) -> p c i", i=3), rHISTM, writes=[rHISTM])
                    kb.dma("sp", ST[:], sst[samp], rST, writes=[rST])
                kb.op("act", lambda e: e.copy(out=SBF[0][:], in_=ST[:]), reads=[rST], writes=[rSBF[0]])

            for b in range(nb):
                xt, rxt = XT[b % 2]
                kb.dma("sp", xt[0:tb, :], xd[b * 128:b * 128 + tb, :], rxt, writes=[rxt])
                norm_to_fm(xt, rxt, b, tb, XS16[b % 2], XN, rXN)

            gate(1)
            for q in range(4):
                pan, rpan = next_panel()
                for b in range(nb):
                    bk = acc_bank()
                    mm_run(PS[0:tb, bk, :], [(XN[:, kc, b * 128:b * 128 + tb], pan[:, kc, :]) for kc in range(8)],
                           reads=[rXN, rpan], writes=[rB[bk]])
                    kb.op("act", lambda e, bk=bk, b=b, q=q: e.activation(
                        out=SZ[0][0:tb, b, q * 512:(q + 1) * 512], in_=PS[0:tb, bk, :], func=AF.Silu),
                        reads=[rB[bk]], writes=[SZ[1]])
            gate(2)
            for q in range(8):
                pan, rpan = next_panel()
                kb.op("pool", lambda e, q=q: e.tensor_copy(out=XB4[0][:, :, 0:3], in_=HISTM[:, q * 4:(q + 1) * 4, :]),
                      reads=[rHISTM], writes=[XB4[1]])
                for jj in range(4):
                    c = q * 4 + jj
                    bk = acc_bank()
                    mm_run(PS[:, bk, 0:T], [(pan[:, kc, jj * 128:(jj + 1) * 128], XN[:, kc, 0:T]) for kc in range(8)],
                           reads=[rXN, rpan], writes=[rB[bk]])
                    kb.op("act", lambda e, bk=bk, jj=jj: e.copy(out=XB4[0][:, jj, 3:3 + T], in_=PS[:, bk, 0:T]),
                          reads=[rB[bk]], writes=[XB4[1]])
                kb.op("pool", lambda e, q=q: e.tensor_copy(out=HISTM[:, q * 4:(q + 1) * 4, :], in_=XB4[0][:, :, T:T + 3]),
                      reads=[XB4[1]], writes=[rHISTM])
                for jj in range(4):
                    c = q * 4 + jj
                    kb.op("act", lambda e, jj=jj, c=c: e.activation(
                        out=CV4[0][:, jj, 0:T], in_=XB4[0][:, jj, 0:T], func=AF.Identity,
                        scale=PF[:, PF_CSW + c * 4:PF_CSW + c * 4 + 1]),
                        reads=[XB4[1], rPF], writes=[CV4[1]])
                    for i in (1, 2, 3):
                        kb.op("dve", lambda e, jj=jj, c=c, i=i: e.scalar_tensor_tensor(
                            out=CV4[0][:, jj, 0:T], in0=XB4[0][:, jj, i:i + T],
                            scalar=PF[:, PF_CSW + c * 4 + i:PF_CSW + c * 4 + i + 1], in1=CV4[0][:, jj, 0:T],
                            op0=ALU.mult, op1=ALU.add),
                            reads=[XB4[1], CV4[1], rPF], writes=[CV4[1]])
                    if c < 16:
                        dst, dres = XSF[0][:, c, 0:T], XSF[1]
                    elif c < 24:
                        dst, dres = BF_[0][:, c - 16, 0:T], BF_[1]
                    else:
                        dst, dres = CF[0][:, c - 24, 0:T], CF[1]
                    kb.op("act", lambda e, jj=jj, c=c, dst=dst: e.activation(
                        out=dst, in_=CV4[0][:, jj, 0:T], func=AF.Silu, bias=PF[:, PF_CSB + c:PF_CSB + c + 1]),
                        reads=[CV4[1], rPF], writes=[dres])
            if last:
                kb.dma("sp", ocm[sidx].rearrange("p (c i) -> p c i", i=3), HISTM[:], rHISTM, reads=[rHISTM])

            gate(3)
            for b in range(nb):
                sm = SM[:, b, :]
                rsm = rSM[b]
                smh = SMH[:, b, :]
                rsmh = rSMH[b]
                tok = slice(b * 128, b * 128 + tb)
                bk = acc_bank()
                mm_run(PS[0:tb, bk, 0:32], [(XN[:, kc, tok], DTW[:, kc, :]) for kc in range(8)],
                       reads=[rXN, rDTW], writes=[rB[bk]])
                gate(3.05)
                kb.op("dve", lambda e, bk=bk, sm=sm: e.tensor_tensor(out=sm[0:tb, C_T0:C_T0 + 32], in0=PS[0:tb, bk, 0:32],
                                                                    in1=PF[0:tb, PF_DTB:PF_DTB + 32], op=ALU.add),
                      reads=[rB[bk], rPF], writes=[rsm])
                kb.op("act", lambda e, sm=sm: e.activation(out=sm[0:tb, C_T0:C_T0 + 32], in_=sm[0:tb, C_T0:C_T0 + 32],
                                                           func=AF.Exp), reads=[rsm], writes=[rsm])
                kb.op("act", lambda e, sm=sm: e.activation(out=sm[0:tb, C_DT:C_DT + 32], in_=sm[0:tb, C_T0:C_T0 + 32],
                                                           func=AF.Ln, bias=EPSC[0:tb, 1:2]), reads=[rsm, rEPSC], writes=[rsm])
                kb.op("dve", lambda e, sm=sm: e.tensor_tensor(out=sm[0:tb, C_ADT:C_ADT + 32], in0=sm[0:tb, C_DT:C_DT + 32],
                                                              in1=ANEG[0:tb, :], op=ALU.mult),
                      reads=[rsm, rANEG], writes=[rsm])
                kb.op("dve", lambda e, sm=sm, smh=smh: e.tensor_copy(out=smh[0:tb, 0:32], in_=sm[0:tb, C_ADT:C_ADT + 32]),
                      reads=[rsm], writes=[rsmh])
                kb.op("dve", lambda e, sm=sm, smh=smh: e.tensor_tensor(out=smh[0:tb, 32:64], in0=sm[0:tb, C_ADT:C_ADT + 32],
                                                                      in1=smh[0:tb, 0:32], op=ALU.subtract),
                      reads=[rsm, rsmh], writes=[rsmh])
                gate(3.1)
                bk = acc_bank()
                mm_run(PS[0:tb, bk, 0:32], [(tri128[0:tb, 0:tb], smh[0:tb, 0:32]), (tri128[0:tb, 0:tb], smh[0:tb, 32:64])],
                       reads=[rsmh, rCB], writes=[rB[bk]])
                gate(3.11)
                mm_run(PS[:, bk, 32:64], [(onesA[0:tb, :], smh[0:tb, 0:32]), (onesA[0:tb, :], smh[0:tb, 32:64])],
                       reads=[rsmh, rCB], writes=[rB[bk]])
                if nch == 2:
                    mm_run(PS[:, bk, 64:96], [(onesB[0:tb, :], smh[0:tb, 0:32]), (onesB[0:tb, :], smh[0:tb, 32:64])],
                           reads=[rsmh, rCB], writes=[rB[bk]])
                gate(3.12)
                kb.op("dve", lambda e, bk=bk, sm=sm: e.tensor_copy(out=sm[0:tb, C_ACUM:C_ACUM + 32], in_=PS[0:tb, bk, 0:32]),
                      reads=[rB[bk]], writes=[rsm])
                gate(3.13)
                kb.op("act", lambda e, bk=bk, sm=sm: e.activation(out=sm[0:tb, C_EA:C_EA + 32], in_=PS[0:tb, bk, 0:32],
                                                                 func=AF.Exp), reads=[rB[bk]], writes=[rsm])
                gate(3.14)
                kb.op("act", lambda e, bk=bk, sm=sm: e.activation(out=sm[0:VARP, C_CD:C_CD + 32 * nch],
                                                                 in_=PS[0:VARP, bk, 32:32 + 32 * nch], func=AF.Exp),
                      reads=[rB[bk]], writes=[rsm])
                gate(3.2)
                tr_run([(PSb[0:tb, 5 + c // 8, (c % 8) * 128:(c % 8 + 1) * 128], XSF[0][:, c, tok]) for c in range(16)],
                       reads=[XSF[1], rCB], writes=[rB[5], rB[6]], idn=ident)
                kb.op("act", lambda e: e.copy(out=XSt[0][0:tb, :], in_=PSb[0:tb, 5:7, :].rearrange("p a n -> p (a n)")),
                      reads=[rB[5], rB[6]], writes=[XSt[1]])
                kb.op("dve", lambda e, sm=sm: e.tensor_tensor(
                    out=XC[0][0:tb, :].rearrange("p (h d) -> p h d", h=32),
                    in0=PSb[0:tb, 5:7, :].rearrange("p a (h d) -> p (a h) d", d=64),
                    in1=sm[0:tb, C_DT:C_DT + 32].unsqueeze(2).broadcast_to([tb, 32, 64]), op=ALU.mult),
                    reads=[rB[5], rB[6], rsm], writes=[XC[1]])
                tr_run([(PSb[0:tb, 7, g * 128:(g + 1) * 128], BF_[0][:, g, tok]) for g in range(8)],
                       reads=[BF_[1], rCB], writes=[rB[7]], idn=ident)
                kb.op("act", lambda e: e.copy(out=Bt[0][0:tb, :], in_=PSb[0:tb, 7, :]), reads=[rB[7]], writes=[Bt[1]])
                gate(3.3)
                for (rr, lo) in ((RHI, 0), (RLO, 32)):
                    kb.op("dve", lambda e, rr=rr, lo=lo, smh=smh: e.tensor_tensor(
                        out=rr[0][0:tb, :, :], in0=smh[0:tb, lo:lo + 32].unsqueeze(2).broadcast_to([tb, 32, 64]),
                        in1=tri2[0:tb, :].unsqueeze(1).broadcast_to([tb, 32, 64]), op=ALU.mult),
                        reads=[rsmh, rCB], writes=[rr[1]])
                gate(3.4)
                for hf in range(2):
                    hs = slice(hf * 16, hf * 16 + 16)
                    for nn in range(2):
                        hh = slice(hf * 16 + nn * 8, hf * 16 + nn * 8 + 8)
                        mm_run(PS[0:tb, nn, :],
                               [(bones[0:tb, 0:tb], RHI[0][0:tb, hh, :]), (bones[0:tb, 0:tb], RLO[0][0:tb, hh, :]),
                                (ident[0:tb, 0:tb], negm[0:tb, nn * 512:(nn + 1) * 512])],
                               reads=[RHI[1], RLO[1], rCB], writes=[rB[nn]])
                    kb.op("dve", lambda e, hs=hs, sm=sm: e.tensor_tensor(
                        out=Dh[0][0:tb, :, :], in0=PS[0:tb, 0:2, :].rearrange("p a (h l) -> p (a h) l", l=64),
                        in1=sm[0:tb, C_ACUM + hs.start:C_ACUM + hs.stop].unsqueeze(2).broadcast_to([tb, 16, 64]),
                        op=ALU.subtract), reads=[rB[0], rB[1], rsm], writes=[Dh[1]])
                    kb.op("act", lambda e: e.activation(out=Eh[0][0:tb, :, :], in_=Dh[0][0:tb, :, :], func=AF.Exp),
                          reads=[Dh[1]], writes=[Eh[1]])
                    kb.op("pool", lambda e, hs=hs, sm=sm: e.tensor_copy(
                        out=sm[0:tb, C_DEC + hs.start:C_DEC + hs.stop].unsqueeze(2), in_=Eh[0][0:tb, :, 63:64]),
                        reads=[Eh[1]], writes=[rsm])
                    gate(3.5)
                    for g in range(4):
                        gg = hf * 4 + g
                        for ch in range(nch):
                            r0 = ch * 64
                            cs = slice(b * 128 + r0, b * 128 + r0 + 64)
                            mm_run(PS[r0:r0 + 64, 4, g * 64:(g + 1) * 64], [(BF_[0][:, gg, cs], CF[0][:, gg, cs])],
                                   reads=[BF_[1], CF[1]], writes=[rB[4]])
                    kb.op("dve", lambda e: e.tensor_tensor(
                        out=Mh[0][0:tb, :, :].rearrange("p (g j) l -> p g j l", g=4),
                        in0=Eh[0][0:tb, :, :].rearrange("p (g j) l -> p g j l", g=4),
                        in1=PS[0:tb, 4, 0:256].rearrange("p (g l) -> p g l", g=4).unsqueeze(2).broadcast_to([tb, 4, 4, 64]),
                        op=ALU.mult), reads=[Eh[1], rB[4]], writes=[Mh[1]])
                    kb.op("dve", lambda e, hs=hs, sm=sm, hf=hf: e.tensor_tensor(
                        out=XCD[0][0:tb, hf * 1024:(hf + 1) * 1024].rearrange("p (h d) -> p h d", h=16),
                        in0=XC[0][0:tb, hf * 1024:(hf + 1) * 1024].rearrange("p (h d) -> p h d", h=16),
                        in1=sm[0:tb, C_DEC + hs.start:C_DEC + hs.stop].unsqueeze(2).broadcast_to([tb, 16, 64]),
                        op=ALU.mult), reads=[XC[1], rsm], writes=[XCD[1]])
                    gate(3.6)
                    for ch in range(nch):
                        r0 = ch * 64
                        cs = slice(b * 128 + r0, b * 128 + r0 + 64)
                        sb_in = SBF[ch % 2]
                        rsb_in = rSBF[ch % 2]
                        firstmm = True
                        for g in range(4):
                            gg = hf * 4 + g
                            for j in range(4):
                                hl = g * 4 + j
                                h = hf * 16 + hl
                                mm_run(PS[r0:r0 + 64, 2 + hl // 8, (hl % 8) * 64:(hl % 8 + 1) * 64],
                                       [(Mh[0][r0:r0 + 64, hl, :], XC[0][r0:r0 + 64, h * 64:(h + 1) * 64])],
                                       reads=[Mh[1], XC[1]], writes=[rB[2], rB[3]])
                                firstmm = False
                        for g in range(4):
                            gg = hf * 4 + g
                            mm_run(PS[r0:r0 + 64, 5 + g // 2, (g % 2) * 256:(g % 2 + 1) * 256],
                                   [(CF[0][:, gg, cs], sb_in[:, gg * 256:(gg + 1) * 256])],
                                   reads=[CF[1], rsb_in], writes=[rB[5], rB[6]])
                        for g in range(4):
                            gg = hf * 4 + g
                            mm_run(PS[:, g // 2, (g % 2) * 256:(g % 2 + 1) * 256],
                                   [(Bt[0][r0:r0 + 64, gg * 128:(gg + 1) * 128], XCD[0][r0:r0 + 64, gg * 256:(gg + 1) * 256])],
                                   reads=[Bt[1], XCD[1]], writes=[rB[0], rB[1]])
                        gate(3.7)
                        stv = ST[:, hf * 1024:(hf + 1) * 1024]
                        kb.op("dve", lambda e, stv=stv, sm=sm, ch=ch, hs=hs: e.tensor_tensor(
                            out=T1[0][:, :].rearrange("p (h d) -> p h d", h=16), in0=stv.rearrange("p (h d) -> p h d", h=16),
                            in1=sm[:, C_CD + ch * 32 + hs.start:C_CD + ch * 32 + hs.stop].unsqueeze(2).broadcast_to([128, 16, 64]),
                            op=ALU.mult), reads=[rST, rsm], writes=[T1[1]])
                        kb.op("dve", lambda e, stv=stv: e.tensor_tensor(
                            out=stv, in0=PS[:, 0:2, :].rearrange("p a n -> p (a n)"), in1=T1[0][:, :], op=ALU.add),
                            reads=[rB[0], rB[1], T1[1]], writes=[rST])
                        sb_out = SBF[(ch + 1) % 2] if nch == 2 else SBF[0]
                        rsb_out = rSBF[(ch + 1) % 2] if nch == 2 else rSBF[0]
                        kb.op("act", lambda e, stv=stv, sb_out=sb_out, hf=hf: e.copy(
                            out=sb_out[:, hf * 1024:(hf + 1) * 1024], in_=stv), reads=[rST], writes=[rsb_out])
                    gate(3.8)
                    kb.op("dve", lambda e, hs=hs, sm=sm: e.tensor_tensor(
                        out=Y1[0][0:tb, :].rearrange("p (h d) -> p h d", h=16),
                        in0=PS[0:tb, 5:7, :].rearrange("p a (h d) -> p (a h) d", d=64),
                        in1=sm[0:tb, C_EA + hs.start:C_EA + hs.stop].unsqueeze(2).broadcast_to([tb, 16, 64]),
                        op=ALU.mult), reads=[rB[5], rB[6], rsm], writes=[Y1[1]])
                    kb.op("dve", lambda e: e.tensor_tensor(
                        out=Y1[0][0:tb, :], in0=PS[0:tb, 2:4, :].rearrange("p a n -> p (a n)"), in1=Y1[0][0:tb, :],
                        op=ALU.add), reads=[rB[2], rB[3], Y1[1]], writes=[Y1[1]])
                    kb.op("dve", lambda e, hs=hs, hf=hf: e.tensor_tensor(
                        out=Y3[0][0:tb, :].rearrange("p (h d) -> p h d", h=16),
                        in0=XSt[0][0:tb, hf * 1024:(hf + 1) * 1024].rearrange("p (h d) -> p h d", h=16),
                        in1=PF[0:tb, PF_DSK + hs.start:PF_DSK + hs.stop].unsqueeze(2).broadcast_to([tb, 16, 64]),
                        op=ALU.mult), reads=[XSt[1], rPF], writes=[Y3[1]])
                    kb.op("dve", lambda e: e.tensor_tensor(out=Y3[0][0:tb, :], in0=Y3[0][0:tb, :], in1=Y1[0][0:tb, :],
                                                          op=ALU.add), reads=[Y3[1], Y1[1]], writes=[Y3[1]])
                    kb.op("dve", lambda e, hf=hf, b=b: e.tensor_tensor(
                        out=YG[0][0:tb, :], in0=Y3[0][0:tb, :], in1=SZ[0][0:tb, b, hf * 1024:(hf + 1) * 1024], op=ALU.mult),
                        reads=[Y3[1], SZ[1]], writes=[YG[1]])
                    for g in range(4):
                        kb.op("act", lambda e, g=g: e.activation(out=Y3[0][0:tb, g * 256:(g + 1) * 256], in_=YG[0][0:tb, g * 256:(g + 1) * 256],
                                                                 func=AF.Square, accum_out=STAT[0:tb, 4 + g:5 + g]),
                              reads=[YG[1]], writes=[Y3[1], rSTAT])
                    rstd_from_ss(4, 8, 256, tb, w=4)
                    for g in range(4):
                        kb.op("act", lambda e, g=g: e.activation(out=YN[0][0:tb, g * 256:(g + 1) * 256],
                                                                 in_=YG[0][0:tb, g * 256:(g + 1) * 256], func=AF.Identity,
                                                                 scale=STAT[0:tb, 8 + g:9 + g]),
                              reads=[YG[1], rSTAT], writes=[YN[1]])
                    tr_run([(PSb[:, 7, c * 128:c * 128 + tb], YN[0][0:tb, c * 128:(c + 1) * 128]) for c in range(8)],
                           reads=[YN[1], rCB], writes=[rB[7]], idn=ident[0:tb, 0:tb])
                    kb.op("act", lambda e, hf=hf, b=b: e.copy(
                        out=YNF[0][:, hf * 8:(hf + 1) * 8, b * 128:b * 128 + tb],
                        in_=PSb[:, 7, :].rearrange("p (k t) -> p k t", k=8)[:, :, 0:tb]), reads=[rB[7]], writes=[YNF[1]])
                if nch == 2:
                    pass
            if last:
                kb.dma("sp", ost[sidx], ST[:], rST, reads=[rST])
            dump("xn", XN[:, :, 0:T], rXN, [128, 8, T], BF16)
            dump("sz", SZ[0][0:tb, 0:nb, :], SZ[1], [tb, nb, 2048], BF16)
            dump("xsf", XSF[0][:, :, 0:T], XSF[1], [128, 16, T], BF16)
            dump("bf", BF_[0][:, :, 0:T], BF_[1], [128, 8, T], BF16)
            dump("cf", CF[0][:, :, 0:T], CF[1], [128, 8, T], BF16)
            dump("ynf", YNF[0][:, :, 0:T], YNF[1], [128, 16, T], BF16)
            dump("sm", SM[0:tb, 0, 0:C_CD], rSM[0], [tb, C_CD], F32)
            dump("y1", Y1[0][0:tb, :], Y1[1], [tb, 1024], F32)
            dump("yg", YG[0][0:tb, :], YG[1], [tb, 1024], F32)

            gate(4)
            for hf in range(2):
                pan, rpan = next_panel()
                for jj in range(4):
                    bk = acc_bank()
                    mm_run(PS[:, bk, 0:T], [(pan[:, kc, jj * 128:(jj + 1) * 128], XN[:, kc, 0:T]) for kc in range(8)],
                           reads=[rXN, rpan], writes=[rB[bk]])
                    kb.op("act", lambda e, bk=bk, jj=jj: e.copy(out=AC4[0][:, jj, 0:T], in_=PS[:, bk, 0:T]),
                          reads=[rB[bk]], writes=[AC4[1]])
                pan, rpan = next_panel()
                kb.op("pool", lambda e, hf=hf: e.tensor_copy(out=U4[0][:, :, 0:2], in_=HISTA[:, hf * 4:(hf + 1) * 4, :]),
                      reads=[rHISTA], writes=[U4[1]])
                for jj in range(4):
                    bk = acc_bank()
                    mm_run(PS[:, bk, 0:T], [(pan[:, kc, jj * 128:(jj + 1) * 128], XN[:, kc, 0:T]) for kc in range(8)],
                           reads=[rXN, rpan], writes=[rB[bk]])
                    kb.op("dve", lambda e, bk=bk, jj=jj: e.tensor_tensor(out=U4[0][:, jj, 2:2 + T], in0=PS[:, bk, 0:T],
                                                                        in1=AC4[0][:, jj, 0:T], op=ALU.mult),
                          reads=[rB[bk], AC4[1]], writes=[U4[1]])
                kb.op("pool", lambda e, hf=hf: e.tensor_copy(out=HISTA[:, hf * 4:(hf + 1) * 4, :], in_=U4[0][:, :, T:T + 2]),
                      reads=[U4[1]], writes=[rHISTA])
                for jj in range(4):
                    j = hf * 4 + jj
                    kb.op("act", lambda e, jj=jj, j=j: e.activation(
                        out=TT4[0][:, jj, 0:T], in_=U4[0][:, jj, 0:T], func=AF.Identity,
                        scale=PF[:, PF_CAW + j * 3:PF_CAW + j * 3 + 1]), reads=[U4[1], rPF], writes=[TT4[1]])
                    for i in (1, 2):
                        kb.op("dve", lambda e, jj=jj, j=j, i=i: e.scalar_tensor_tensor(
                            out=TT4[0][:, jj, 0:T], in0=U4[0][:, jj, i:i + T],
                            scalar=PF[:, PF_CAW + j * 3 + i:PF_CAW + j * 3 + i + 1], in1=TT4[0][:, jj, 0:T],
                            op0=ALU.mult, op1=ALU.add), reads=[U4[1], TT4[1], rPF], writes=[TT4[1]])
                pan, rpan = next_panel()
                for jj in range(4):
                    j = hf * 4 + jj
                    bk = acc_bank()
                    mm_run(PS[:, bk, 0:T], [(pan[:, kc, jj * 128:(jj + 1) * 128], XN[:, kc, 0:T]) for kc in range(8)],
                           reads=[rXN, rpan], writes=[rB[bk]])
                    kb.op("dve", lambda e, bk=bk, jj=jj, j=j: e.tensor_tensor(out=V[0][:, j, 0:T], in0=PS[:, bk, 0:T],
                                                                             in1=TT4[0][:, jj, 0:T], op=ALU.mult),
                          reads=[rB[bk], TT4[1]], writes=[V[1]])
            if last:
                kb.dma("sp", oca[sidx].rearrange("p (j i) -> p j i", i=2), HISTA[:], rHISTA, reads=[rHISTA])
            for hf in range(2):
                pan, rpan = next_panel()
                for jj in range(4):
                    j = hf * 4 + jj
                    bk = acc_bank()
                    mm_run(PS[:, bk, 0:T], [(pan[:, kc, jj * 128:(jj + 1) * 128], XN[:, kc, 0:T]) for kc in range(8)],
                           reads=[rXN, rpan], writes=[rB[bk]])
                    kb.op("act", lambda e, bk=bk, jj=jj, j=j: e.activation(
                        out=G4[0][:, jj, 0:T], in_=PS[:, bk, 0:T], func=AF.Sigmoid, bias=PF[:, PF_BG + j:PF_BG + j + 1]),
                        reads=[rB[bk], rPF], writes=[G4[1]])
                pan, rpan = next_panel()
                for jj in range(4):
                    j = hf * 4 + jj
                    bk = acc_bank()
                    mm_run(PS[:, bk, 0:T], [(pan[:, kc, jj * 128:(jj + 1) * 128], V[0][:, kc, 0:T]) for kc in range(8)],
                           reads=[V[1], rpan], writes=[rB[bk]])
                    kb.op("dve", lambda e, bk=bk, jj=jj, j=j: e.tensor_tensor(out=PA[0][:, j, 0:T], in0=PS[:, bk, 0:T],
                                                                             in1=G4[0][:, jj, 0:T], op=ALU.mult),
                          reads=[rB[bk], G4[1]], writes=[PA[1]])
            gate(5)
            dump("v", V[0][:, :, 0:T], V[1], [128, 8, T], BF16)
            dump("pa", PA[0][:, :, 0:T], PA[1], [128, 8, T], F32)
            for hf in range(2):
                pan, rpan = next_panel()
                for jj in range(4):
                    j = hf * 4 + jj
                    bk = acc_bank()
                    mm_run(PS[:, bk, 0:T], [(pan[:, kc, jj * 128:(jj + 1) * 128], XN[:, kc, 0:T]) for kc in range(8)],
                           reads=[rXN, rpan], writes=[rB[bk]])
                    kb.op("act", lambda e, bk=bk, jj=jj, j=j: e.activation(
                        out=G4[0][:, jj, 0:T], in_=PS[:, bk, 0:T], func=AF.Sigmoid,
                        bias=PF[:, PF_BG + 8 + j:PF_BG + 8 + j + 1]), reads=[rB[bk], rPF], writes=[G4[1]])
                pan0, rpan0 = next_panel()
                pan1, rpan1 = next_panel()
                for jj in range(4):
                    j = hf * 4 + jj
                    bk = acc_bank()
                    pairs = [(pan0[:, kc, jj * 128:(jj + 1) * 128], YNF[0][:, kc, 0:T]) for kc in range(8)] + \
                            [(pan1[:, kc, jj * 128:(jj + 1) * 128], YNF[0][:, 8 + kc, 0:T]) for kc in range(8)]
                    mm_run(PS[:, bk, 0:T], pairs, reads=[YNF[1], rpan0, rpan1], writes=[rB[bk]])
                    kb.op("dve", lambda e, bk=bk, jj=jj: e.tensor_tensor(out=G4[0][:, jj, 0:T], in0=PS[:, bk, 0:T],
                                                                        in1=G4[0][:, jj, 0:T], op=ALU.mult),
                          reads=[rB[bk], G4[1]], writes=[G4[1]])
                    kb.op("dve", lambda e, jj=jj, j=j: e.tensor_tensor(out=MG[0][:, j, 0:T], in0=G4[0][:, jj, 0:T],
                                                                       in1=PA[0][:, j, 0:T], op=ALU.add),
                          reads=[G4[1], PA[1]], writes=[MG[1]])
            gate(6)
            dump("mg", MG[0][:, :, 0:T], MG[1], [128, 8, T], BF16)
            for hf in range(2):
                pan, rpan = next_panel()
                for b in range(nb):
                    bk = acc_bank()
                    mm_run(PS[0:tb, bk, :], [(MG[0][:, kc, b * 128:b * 128 + tb], pan[:, kc, :]) for kc in range(8)],
                           reads=[MG[1], rpan], writes=[rB[bk]])
                    kb.op("act", lambda e, bk=bk, b=b, hf=hf: e.copy(out=MO[b][0][0:tb, hf * 512:(hf + 1) * 512],
                                                                    in_=PS[0:tb, bk, :]), reads=[rB[bk]], writes=[MO[b][1]])
            for b in range(nb):
                mo, rmo = MO[b]
                xt, rxt = XT2[b % 2]
                kb.dma("sp", xt[0:tb, :], xd[b * 128:b * 128 + tb, :], rxt, writes=[rxt])
                kb.op("act", lambda e, mo=mo: e.activation(out=JUNK[0][0:tb, :], in_=mo[0:tb, :], func=AF.Square,
                                                           accum_out=STAT[0:tb, 0:1]), reads=[rmo], writes=[JUNK[1], rSTAT])
                rstd_from_ss(0, 1, 1024, tb)
                kb.op("dve", lambda e, mo=mo: e.scalar_tensor_tensor(
                    out=mo[0:tb, :], in0=mo[0:tb, :], scalar=STAT[0:tb, 1:2], in1=PF[0:tb, PF_NMPOST:PF_NMPOST + 1024],
                    op0=ALU.mult, op1=ALU.mult), reads=[rmo, rSTAT, rPF], writes=[rmo])
                kb.op("dve", lambda e, mo=mo, xt=xt: e.tensor_tensor(out=mo[0:tb, :], in0=mo[0:tb, :], in1=xt[0:tb, :],
                                                                     op=ALU.add), reads=[rmo, rxt], writes=[rmo])
                norm_to_fm(mo, rmo, b, tb, HS16[b % 2], XN, rXN)
            gate(7)
            dump("h", MO[0][0][0:tb, :], MO[0][1], [tb, 1024], F32)
            for q in range(8):
                pan, rpan = next_panel()
                for jj in range(4):
                    c = q * 4 + jj
                    bk = acc_bank()
                    mm_run(PS[:, bk, 0:T], [(pan[:, kc, jj * 128:(jj + 1) * 128], XN[:, kc, 0:T]) for kc in range(8)],
                           reads=[rXN, rpan], writes=[rB[bk]])
                    rt, rrt = RT[c % 2]
                    kb.op("act", lambda e, bk=bk, rt=rt: e.activation(out=rt[:, 0:T], in_=PS[:, bk, 0:T], func=AF.Relu),
                          reads=[rB[bk]], writes=[rrt])
                    kb.op("act", lambda e, rt=rt, c=c: e.activation(out=FF[0][:, c, 0:T], in_=rt[:, 0:T], func=AF.Square),
                          reads=[rrt], writes=[FF[1]])
            for hf in range(2):
                for kg in range(4):
                    pan, rpan = next_panel()
                    for b in range(nb):
                        mm_run(PS[0:tb, b, :],
                               [(FF[0][:, kg * 8 + kc, b * 128:b * 128 + tb], pan[:, kc, :]) for kc in range(8)],
                               reads=[FF[1], rpan], writes=[rB[b]], first=(kg == 0), last=(kg == 3))
                for b in range(nb):
                    kb.op("act", lambda e, b=b, hf=hf: e.copy(out=FO[b][0][0:tb, hf * 512:(hf + 1) * 512], in_=PS[0:tb, b, :]),
                          reads=[rB[b]], writes=[FO[b][1]])
            dump("ff", FF[0][:, :, 0:T], FF[1], [128, 32, T], BF16)
            dump("fo", FO[0][0][0:tb, :], FO[0][1], [tb, 1024], F32)
            for b in range(nb):
                fo, rfo = FO[b]
                mo, rmo = MO[b]
                kb.op("act", lambda e, fo=fo: e.activation(out=JUNK[0][0:tb, :], in_=fo[0:tb, :], func=AF.Square,
                                                           accum_out=STAT[0:tb, 0:1]), reads=[rfo], writes=[JUNK[1], rSTAT])
                rstd_from_ss(0, 1, 1024, tb)
                kb.op("dve", lambda e, fo=fo: e.scalar_tensor_tensor(
                    out=fo[0:tb, :], in0=fo[0:tb, :], scalar=STAT[0:tb, 1:2], in1=PF[0:tb, PF_NFPOST:PF_NFPOST + 1024],
                    op0=ALU.mult, op1=ALU.mult), reads=[rfo, rSTAT, rPF], writes=[rfo])
                kb.op("dve", lambda e, fo=fo, mo=mo: e.tensor_tensor(out=fo[0:tb, :], in0=fo[0:tb, :], in1=mo[0:tb, :],
                                                                     op=ALU.add), reads=[rfo, rmo], writes=[rfo])
                kb.dma("sp", yd[b * 128:b * 128 + tb, :], fo[0:tb, :], rfo, reads=[rfo])
        for tl in tiles:
            try:
                run_tile(*tl)
            except _Stop:
                pass
        kb.finish()
        block = es.enter_context(nc.Block())
        kb.replay(block)
    return nc


def _host_tables(p):
    pf = np.zeros((128, PF_COLS), np.float32)
    pf[:, PF_CAW:PF_CAW + 24] = p["conv_a_w"][0].reshape(3, 8, 128).transpose(2, 1, 0).reshape(128, 24)
    pf[:, PF_CSW:PF_CSW + 128] = p["conv_ssm_w"][0].reshape(4, 32, 128).transpose(2, 1, 0).reshape(128, 128)
    pf[:, PF_CSB:PF_CSB + 32] = p["conv_ssm_b"][0].reshape(32, 128).T
    pf[:, PF_BG:PF_BG + 16] = p["b_gate"][0].reshape(16, 128).T
    pf[:, PF_NMP:PF_NMP + 8] = p["norm_mix_pre"][0].reshape(8, 128).T
    pf[:, PF_NFP:PF_NFP + 8] = p["norm_ffn_pre"][0].reshape(8, 128).T
    pf[:, PF_SSN:PF_SSN + 16] = p["ssm_norm"][0].reshape(16, 128).T
    pf[:, PF_DTB:PF_DTB + 32] = p["dt_bias"][0][None, :]
    pf[:, PF_ALOG:PF_ALOG + 32] = p["a_log"][0][None, :]
    pf[:, PF_DSK:PF_DSK + 32] = p["d_skip"][0][None, :]
    pf[:, PF_NMPOST:PF_NMPOST + 1024] = p["norm_mix_post"][0][None, :]
    pf[:, PF_NFPOST:PF_NFPOST + 1024] = p["norm_ffn_post"][0][None, :]
    cb = np.zeros((128, CB_COLS), np.float32)
    k = np.arange(128)
    cb[:, CB_ID:CB_ID + 128] = np.eye(128)
    l = np.arange(64)
    cb[:, CB_TRI2:CB_TRI2 + 64] = ((k[:, None] % 64) <= l[None, :])
    same = (k[:, None] // 64) == (k[None, :] // 64)
    cb[:, CB_BONES:CB_BONES + 128] = same
    cb[:, CB_ONESA:CB_ONESA + 128] = (k[:, None] < 64) * np.ones((1, 128))
    cb[:, CB_ONESB:CB_ONESB + 128] = (k[:, None] >= 64) * np.ones((1, 128))
    cb[:, CB_TRI128:CB_TRI128 + 128] = same & ((k[:, None] % 64) <= (k[None, :] % 64))
    neg = np.where((k[:, None] % 64) > l[None, :], -30000.0, 0.0)
    cb[:, CB_NEG:CB_NEG + 1024] = np.tile(neg, (1, 16))
    return pf, cb.astype(ml_dtypes.bfloat16)


_NC_CACHE = {}


def kernel(x_prompt, x_sample, state_conv_a, state_conv_ssm, state_ssm, norm_mix_pre, w_in, b_gate, conv_a_w, w_a_out,
           conv_ssm_w, conv_ssm_b, dt_bias, a_log, d_skip, ssm_norm, w_ssm_out, w_o, norm_mix_post, norm_ffn_pre,
           w_ff1, w_ff2, norm_ffn_post, _tiles_limit=None, _cores=None, _dbg=False, _stop=None):
    p = dict(norm_mix_pre=norm_mix_pre, b_gate=b_gate, conv_a_w=conv_a_w, conv_ssm_w=conv_ssm_w, conv_ssm_b=conv_ssm_b,
             dt_bias=dt_bias, a_log=a_log, d_skip=d_skip, ssm_norm=ssm_norm, norm_mix_post=norm_mix_post,
             norm_ffn_pre=norm_ffn_pre, norm_ffn_post=norm_ffn_post)
    p = {k: np.asarray(v, np.float32) for k, v in p.items()}
    pf, cb = _host_tables(p)
    x_prompt = np.asarray(x_prompt, np.float32)
    x_sample = np.asarray(x_sample, np.float32)
    sca_all = np.asarray(state_conv_a, np.float32)[0]
    scm_all = np.asarray(state_conv_ssm, np.float32)[0]
    sst_all = np.asarray(state_ssm, np.float32)[0]
    key = (None if _tiles_limit is None else tuple(_tiles_limit), _dbg, _stop)
    if key not in _NC_CACHE:
        _NC_CACHE[key] = build_program(_tiles_limit, _dbg, _stop)
    nc = _NC_CACHE[key]
    cores = list(range(NCORES)) if _cores is None else _cores
    shared = dict(pf=pf, cb=cb, w_in=np.ascontiguousarray(np.asarray(w_in, np.float32)[0]),
                  w_a_out=np.ascontiguousarray(np.asarray(w_a_out, np.float32)[0]),
                  w_ssm_out=np.ascontiguousarray(np.asarray(w_ssm_out, np.float32)[0]),
                  w_o=np.ascontiguousarray(np.asarray(w_o, np.float32)[0]),
                  w_ff1=np.ascontiguousarray(np.asarray(w_ff1, np.float32)[0]),
                  w_ff2=np.ascontiguousarray(np.asarray(w_ff2, np.float32)[0]))
    in_maps = []
    for c in cores:
        s2 = slice(2 * c, 2 * c + 2)
        m = dict(shared)
        m["xp"] = np.ascontiguousarray(x_prompt[s2])
        m["xs"] = np.ascontiguousarray(x_sample[s2])
        m["sca"] = np.ascontiguousarray(sca_all[s2].reshape(2, 2, 8, 128).transpose(0, 3, 2, 1).reshape(2, 128, 16))
        m["scm"] = np.ascontiguousarray(scm_all[s2].reshape(2, 3, 32, 128).transpose(0, 3, 2, 1).reshape(2, 128, 96))
        m["sst"] = np.ascontiguousarray(sst_all[s2].reshape(2, 2048, 128).transpose(0, 2, 1))
        in_maps.append(m)
    res = run_bass_kernel_spmd(nc, in_maps, core_ids=list(range(len(cores))))
    R = res.results
    if _dbg:
        kernel.dbg = {k: v for k, v in R[0].items() if k.startswith("dbg_")}
    yp = np.concatenate([r["yp"] for r in R], axis=0)
    ys_ = np.concatenate([r["ys"] for r in R], axis=0)
    oca = np.stack([r["oca"] for r in R])
    ocm = np.stack([r["ocm"] for r in R])
    ost = np.stack([r["ost"] for r in R])
    n = len(cores)

    def un_ca(a):
        return a.reshape(n * 2, 128, 8, 2).transpose(0, 3, 2, 1).reshape(n * 2, 2, 1024)

    def un_cm(a):
        return a.reshape(n * 2, 128, 32, 3).transpose(0, 3, 2, 1).reshape(n * 2, 3, 4096)

    def un_st(a):
        return a.reshape(n * 2, 128, 2048).transpose(0, 2, 1).reshape(n * 2, 32, 64, 128)

    outs = (yp, ys_,
            un_ca(oca[:, 0:2])[None], un_cm(ocm[:, 0:2])[None], un_st(ost[:, 0:2])[None],
            un_ca(oca[:, 2:4])[None], un_cm(ocm[:, 2:4])[None], un_st(ost[:, 2:4])[None])
    return tuple(np.ascontiguousarray(o, dtype=np.float32) for o in outs)
```

```python
import numpy as np
import ml_dtypes
from contextlib import ExitStack
import concourse.bass as bass
import concourse.mybir as mybir
from concourse.bass_utils import run_bass_kernel_spmd

F32 = mybir.dt.float32
BF16 = mybir.dt.bfloat16
AF = mybir.ActivationFunctionType
ALU = mybir.AluOpType
AX = mybir.AxisListType

NCORES = 8
D = 1024
SEQ = 2048
DSEQ = 64
PROJ = 11296
OFF_AB, OFF_AC, OFF_AH, OFF_Z, OFF_XBC, OFF_DT, OFF_GATE = 0, 1024, 2048, 3072, 5120, 9216, 9248
EPS = 1e-6
NS = 4

PF_CAW = 0
PF_CSW = 24
PF_CSB = 152
PF_BG = 184
PF_NMP = 200
PF_NFP = 208
PF_SSN = 216
PF_DTB = 232
PF_ALOG = 264
PF_DSK = 296
PF_NMPOST = 328
PF_NFPOST = 1352
PF_COLS = 2376
CB_ID = 0
CB_TRI2 = 128
CB_BONES = 192
CB_ONESA = 320
CB_ONESB = 448
CB_TRI128 = 576
CB_NEG = 704
CB_COLS = 1728


class Res:
    __slots__ = ("name", "lw", "rd", "ov", "dsem", "dcnt", "rng", "excl")

    def __init__(self, name):
        self.name = name
        self.lw = None
        self.rd = {}
        self.ov = []
        self.dsem = None
        self.dcnt = 0
        self.rng = None
        self.excl = False


class KB:
    ENG = ("pe", "act", "dve", "pool")

    def __init__(self, nc, es):
        self.nc = nc
        self.es = es
        self.streams = {k: [] for k in ("pe", "act", "dve", "pool", "sp")}
        self.sem = {e: es.enter_context(nc.semaphore("s_" + e)) for e in self.ENG}
        self.tick = {e: 0 for e in self.ENG}
        self.waited = {k: {} for k in self.streams}
        self.dsems = []
        self.all_res = []

    def res(self, name):
        r = Res(name)
        self.all_res.append(r)
        return r

    def _dsem(self, r):
        if r.dsem is None:
            r.dsem = self.es.enter_context(self.nc.semaphore("d%d" % len(self.dsems)))
            self.dsems.append(r)
        return r.dsem

    def _deps(self, eng, reads, writes):
        deps = []
        for r in reads:
            if r.lw is not None:
                deps.append(r.lw)
            if r.excl:
                for q, t in r.rd.items():
                    if q != eng:
                        deps.append((q, t))
        same_ok = (eng == "pe")
        for w in writes:
            for x in [w] + w.ov:
                if x.lw is not None and (x.lw[0] != eng or not same_ok):
                    deps.append(x.lw)
                for q, t in x.rd.items():
                    if q != eng or not same_ok:
                        deps.append((q, t))
        wd = self.waited[eng]
        for q, t in deps:
            if wd.get(q, 0) < t:
                wd[q] = t
                self.streams[eng].append(("w", q, t))

    def wait(self, eng, q, t):
        wd = self.waited[eng]
        if wd.get(q, 0) < t:
            wd[q] = t
            self.streams[eng].append(("w", q, t))

    def op(self, eng, fn, reads=(), writes=(), inc=True):
        self._deps(eng, reads, writes)
        t = self.tick[eng] + 1
        if inc:
            self.tick[eng] = t
        self.streams[eng].append(("op", fn, inc))
        for r in reads:
            r.rd[eng] = t
        for w in writes:
            w.lw = (eng, t)
            w.rd = {}
        return t

    def dma(self, eng, out, in_, dres, reads=(), writes=()):
        self._deps(eng, reads, writes)
        self._dsem(dres)
        dres.dcnt += 16
        key = (id(dres), dres.dcnt)
        q = ("d", id(dres))
        self._dq = getattr(self, "_dq", {})
        self._dq[q] = dres
        self.streams[eng].append(("dma", out, in_, dres))
        for r in reads:
            r.rd[q] = dres.dcnt
        for w in writes:
            w.lw = (q, dres.dcnt)
            w.rd = {}

    def semof(self, q):
        if isinstance(q, tuple):
            return self._dq[q].dsem
        return self.sem[q]

    def finish(self):
        for r in self.dsems:
            self.wait("sp", ("d", id(r)), r.dcnt)

    def replay(self, block):
        handles = {"pe": block.tensor, "act": block.scalar, "dve": block.vector, "pool": block.gpsimd,
                   "sp": block.sync}
        for eng, deco in handles.items():
            stream = self.streams[eng]
            own = self.sem.get(eng)

            def body(e, stream=stream, own=own):
                for ent in stream:
                    k = ent[0]
                    if k == "w":
                        e.wait_ge(self.semof(ent[1]), ent[2])
                    elif k == "op":
                        ins = ent[1](e)
                        if ent[2]:
                            ins.then_inc(own, 1)
                    else:
                        e.dma_start(out=ent[1], in_=ent[2]).then_inc(ent[3].dsem, 16)
            deco(body)


class Arena:
    def __init__(self, kb, tensor, nbytes):
        self.kb = kb
        self.t = tensor
        self.nbytes = nbytes
        self.views = []

    def view(self, name, off, shape, dt):
        esz = 2 if dt == BF16 else 4
        n = int(np.prod(shape))
        nb = n * esz
        assert off % 4 == 0 and off + nb <= self.nbytes, (name, off, nb, self.nbytes)
        w0, w1 = off // 4, (off + nb + 3) // 4
        ap = self.t[:, w0:w1]
        if dt == BF16:
            ap = ap.bitcast(BF16)[:, 0:n]
        if len(shape) == 2:
            ap = ap.rearrange("p (a b) -> p a b", a=shape[0])
        elif len(shape) == 3:
            ap = ap.rearrange("p (a b c) -> p a b c", a=shape[0], b=shape[1])
        r = self.kb.res(name)
        r.rng = (off, off + nb)
        for (o, rr) in self.views:
            if o.rng[0] < r.rng[1] and r.rng[0] < o.rng[1]:
                r.ov.append(o)
                o.ov.append(r)
        self.views.append((r, ap))
        return ap, r


import os
VARP = int(os.environ.get('VARP', '128'))


class _Stop(Exception):
    pass


def build_program(tiles_limit=None, dbg=False, stop_after=None):
    nc = bass.Bass("TRN2", target_bir_lowering=False)
    dram_in = lambda name, shape, dt=F32: nc.dram_tensor(name, shape, dt, kind="ExternalInput").ap()
    dram_out = lambda name, shape, dt=F32: nc.dram_tensor(name, shape, dt, kind="ExternalOutput").ap()
    xp = dram_in("xp", [2, SEQ, D])
    xs = dram_in("xs", [2, DSEQ, D])
    sca = dram_in("sca", [2, 128, 16])
    scm = dram_in("scm", [2, 128, 96])
    sst = dram_in("sst", [2, 128, 2048])
    pf_d = dram_in("pf", [128, PF_COLS])
    cb_d = dram_in("cb", [128, CB_COLS], BF16)
    w_in = dram_in("w_in", [D, PROJ])
    w_a = dram_in("w_a_out", [D, D])
    w_s = dram_in("w_ssm_out", [2 * D, D])
    w_o = dram_in("w_o", [D, D])
    w_f1 = dram_in("w_ff1", [D, 4 * D])
    w_f2 = dram_in("w_ff2", [4 * D, D])
    yp = dram_out("yp", [2, SEQ, D])
    ys = dram_out("ys", [2, DSEQ, D])
    oca = dram_out("oca", [4, 128, 16])
    ocm = dram_out("ocm", [4, 128, 96])
    ost = dram_out("ost", [4, 128, 2048])
    wi_b = nc.dram_tensor("wi_b", [D, PROJ], BF16).ap()
    wa_b = nc.dram_tensor("wa_b", [D, D], BF16).ap()
    ws_b = nc.dram_tensor("ws_b", [2 * D, D], BF16).ap()
    wo_b = nc.dram_tensor("wo_b", [D, D], BF16).ap()
    wf1_b = nc.dram_tensor("wf1_b", [D, 4 * D], BF16).ap()
    wf2_b = nc.dram_tensor("wf2_b", [4 * D, D], BF16).ap()

    with ExitStack() as es:
        kb = KB(nc, es)
        sbt = lambda name, shape, dt: es.enter_context(nc.sbuf_tensor(name, shape, dt))
        PF = sbt("PF", [128, PF_COLS], F32); rPF = kb.res("PF")
        CB = sbt("CB", [128, CB_COLS], BF16); rCB = kb.res("CB")
        PAN = [sbt("PAN%d" % i, [128, 8, 512], BF16) for i in range(NS)]
        rPAN = [kb.res("PAN%d" % i) for i in range(NS)]
        XN = sbt("XN", [128, 8, 512], BF16); rXN = kb.res("XN")
        DTW = sbt("DTW", [128, 8, 32], BF16); rDTW = kb.res("DTW")
        HISTA = sbt("HISTA", [128, 8, 2], F32); rHISTA = kb.res("HISTA")
        HISTM = sbt("HISTM", [128, 32, 3], F32); rHISTM = kb.res("HISTM")
        ST = sbt("ST", [128, 2048], F32); rST = kb.res("ST")
        SBF = [sbt("SBF%d" % i, [128, 2048], BF16) for i in range(2)]
        rSBF = [kb.res("SBF%d" % i) for i in range(2)]
        ANEG = sbt("ANEG", [128, 32], F32); rANEG = kb.res("ANEG")
        SM = sbt("SM", [128, 4, 320], F32)
        rSM = [kb.res("SM%d" % b) for b in range(4)]
        STAT = sbt("STAT", [128, 64], F32)
        rSTATb = [kb.res("STAT%d" % b) for b in range(4)]
        EPSC = sbt("EPSC", [128, 2], F32); rEPSC = kb.res("EPSC")
        C_DT, C_ADT, C_ACUM, C_EA, C_CD, C_DEC, C_T0, C_RS = 0, 32, 64, 96, 128, 192, 224, 256
        SMH = sbt("SMH", [128, 4, 64], BF16)
        rSMH = [kb.res("SMH%d" % b) for b in range(4)]
        KBY = 1024
        AR_BYTES = 114 * KBY
        ARt = sbt("ARENA", [128, AR_BYTES // 4], F32)
        ar = Arena(kb, ARt, AR_BYTES)
        XT = [ar.view("XT%d" % i, i * 4 * KBY, [1024], F32) for i in range(4)]
        XS16 = [ar.view("XS16_%d" % i, 16 * KBY + i * 2 * KBY, [1024], BF16) for i in range(4)]
        JUNK4 = [ar.view("JUNK%d" % i, 24 * KBY + i * 2 * KBY, [1024], BF16) for i in range(4)]
        SZ = ar.view("SZ", 0, [4, 2048], BF16)
        XSF = ar.view("XSF", 16 * KBY, [16, 512], BF16)
        BF_ = ar.view("BF", 32 * KBY, [8, 512], BF16)
        CF = ar.view("CF", 40 * KBY, [8, 512], BF16)
        XBJ = [[ar.view("XB%d_%d" % (p_, jj), 48 * KBY + (p_ * 4 + jj) * 2064, [516], F32) for jj in range(4)]
               for p_ in range(3)]
        CVJ = [[ar.view("CV%d_%d" % (p_, jj), 73 * KBY + (p_ * 4 + jj) * 2048, [512], F32) for jj in range(4)]
               for p_ in range(2)]
        XSt = ar.view("XSt", 48 * KBY, [2048], BF16)
        XC = ar.view("XC", 52 * KBY, [2048], BF16)
        XCD = ar.view("XCD", 56 * KBY, [2048], BF16)
        Bt = ar.view("Bt", 60 * KBY, [1024], BF16)
        RHI = ar.view("RHI", 62 * KBY, [32, 64], BF16)
        RLO = ar.view("RLO", 66 * KBY, [32, 64], BF16)
        Dh = ar.view("Dh", 70 * KBY, [16, 64], F32)
        Eh = ar.view("Eh", 74 * KBY, [16, 64], F32)
        Mh = ar.view("Mh", 78 * KBY, [16, 64], BF16)
        T1 = ar.view("T1", 80 * KBY, [1024], F32)
        Y1 = ar.view("Y1", 84 * KBY, [1024], F32)
        Y3 = ar.view("Y3", 88 * KBY, [1024], F32)
        YG = ar.view("YG", 92 * KBY, [1024], F32)
        YN = ar.view("YN", 96 * KBY, [1024], BF16)
        YNF = ar.view("YNF", 98 * KBY, [16, 512], BF16)
        ACJ = [ar.view("AC_%d" % jj, jj * 2048, [512], F32) for jj in range(4)]
        UJ = [ar.view("U_%d" % jj, 8 * KBY + jj * 2064, [516], F32) for jj in range(4)]
        TTJ = [ar.view("TT_%d" % jj, 17 * KBY + jj * 2048, [512], F32) for jj in range(4)]
        V = ar.view("V", 25 * KBY, [8, 512], BF16)
        GJ = [ar.view("G_%d" % jj, 33 * KBY + jj * 2048, [512], F32) for jj in range(4)]
        PA = ar.view("PA", 41 * KBY, [8, 512], F32)
        MG = ar.view("MG", 57 * KBY, [8, 512], BF16)
        MO = [ar.view("MO%d" % b, 65 * KBY + b * 4 * KBY, [1024], F32) for b in range(4)]
        XT2 = [ar.view("XT2_%d" % i, 81 * KBY + i * 4 * KBY, [1024], F32) for i in range(4)]
        HS16 = [ar.view("HS16_%d" % i, 97 * KBY + i * 2 * KBY, [1024], BF16) for i in range(4)]
        FF = ar.view("FF", 0, [32, 512], BF16)
        RT = [ar.view("RT%d" % i, 32 * KBY + i * 2 * KBY, [512], F32) for i in range(4)]
        FO = [ar.view("FO%d" % b, 40 * KBY + b * 4 * KBY, [1024], F32) for b in range(4)]
        SGW = 2824
        NSG = 4
        SGI = [ar.view("SGI%d" % i, i * 12 * KBY, [SGW], F32) for i in range(NSG)]
        SGO = [ar.view("SGO%d" % i, 48 * KBY + i * 6 * KBY, [SGW], BF16) for i in range(NSG)]
        PS = es.enter_context(nc.psum_tensor("PS", [128, 8, 512], F32))
        rB = [kb.res("bank%d" % i) for i in range(8)]
        for r_ in rB:
            r_.excl = True
        PSb = PS[:].bitcast(BF16)

        ident = CB[:, CB_ID:CB_ID + 128]
        tri2 = CB[:, CB_TRI2:CB_TRI2 + 64]
        bones = CB[:, CB_BONES:CB_BONES + 128]
        onesA = CB[:, CB_ONESA:CB_ONESA + 128]
        onesB = CB[:, CB_ONESB:CB_ONESB + 128]
        tri128 = CB[:, CB_TRI128:CB_TRI128 + 128]
        negm = CB[:, CB_NEG:CB_NEG + 1024]

        dbg_names = []

        def dump(name, ap, res, shape, dt):
            if not dbg:
                return
            d = nc.dram_tensor("dbg_" + name, shape, dt, kind="ExternalOutput").ap()
            kb.dma("sp", d, ap, res, reads=[res])
            dbg_names.append("dbg_" + name)

        kb.dma("sp", PF[:], pf_d, rPF, writes=[rPF])
        kb.dma("sp", CB[:], cb_d, rCB, writes=[rCB])
        kb.op("pool", lambda e: e.memset(EPSC[:, 0:1], EPS), writes=[rEPSC])
        kb.op("pool", lambda e: e.memset(EPSC[:, 1:2], 1.0), writes=[rEPSC])
        kb.op("act", lambda e: e.activation(out=ANEG[:], in_=PF[:, PF_ALOG:PF_ALOG + 32], func=AF.Exp),
              reads=[rPF], writes=[rANEG])
        kb.op("dve", lambda e: e.tensor_scalar(out=ANEG[:], in0=ANEG[:], scalar1=-1.0, scalar2=None, op0=ALU.mult),
              reads=[rANEG], writes=[rANEG])

        conv_jobs = []
        def add_mat(src, dst, K, N, piece, scol):
            for rc in range(K // 128):
                for c0 in range(0, N, piece):
                    w = min(piece, N - c0)
                    conv_jobs.append((src[rc * 128:(rc + 1) * 128, c0:c0 + w], dst[rc * 128:(rc + 1) * 128, c0:c0 + w],
                                      None if scol is None else scol + rc, w))
        add_mat(w_in, wi_b, D, PROJ, SGW, PF_NMP)
        add_mat(w_a, wa_b, D, D, 1024, None)
        add_mat(w_s, ws_b, 2 * D, D, 1024, PF_SSN)
        add_mat(w_o, wo_b, D, D, 1024, None)
        add_mat(w_f1, wf1_b, D, 4 * D, 2048, PF_NFP)
        add_mat(w_f2, wf2_b, 4 * D, D, 1024, None)
        def conv_store(i):
            (src, dst, scol, w) = conv_jobs[i]
            so, ro = SGO[i % NSG]
            kb.dma("sp", dst, so[:, 0:w], ro, reads=[ro])
        for i, (src, dst, scol, w) in enumerate(conv_jobs):
            si, ri = SGI[i % NSG]
            so, ro = SGO[i % NSG]
            if i >= NSG - 1:
                conv_store(i - (NSG - 1))
            kb.dma("sp", si[:, 0:w], src, ri, writes=[ri])
            eng = "dve" if (i % 5) in (0, 2, 4) else "act"
            if eng == "dve":
                if scol is None:
                    kb.op(eng, lambda e, so=so, si=si, w=w: e.tensor_copy(out=so[:, 0:w], in_=si[:, 0:w]),
                          reads=[ri], writes=[ro])
                else:
                    kb.op(eng, lambda e, so=so, si=si, w=w, scol=scol: e.tensor_scalar(
                        out=so[:, 0:w], in0=si[:, 0:w], scalar1=PF[:, scol:scol + 1], scalar2=None, op0=ALU.mult),
                        reads=[ri, rPF], writes=[ro])
            else:
                if scol is None:
                    kb.op(eng, lambda e, so=so, si=si, w=w: e.copy(out=so[:, 0:w], in_=si[:, 0:w]),
                          reads=[ri], writes=[ro])
                else:
                    kb.op(eng, lambda e, so=so, si=si, w=w, scol=scol: e.activation(
                        out=so[:, 0:w], in_=si[:, 0:w], func=AF.Identity, scale=PF[:, scol:scol + 1]),
                        reads=[ri, rPF], writes=[ro])
        for i in range(max(0, len(conv_jobs) - (NSG - 1)), len(conv_jobs)):
            conv_store(i)
        for so, ro in SGO:
            kb.wait("sp", ("d", id(ro)), ro.dcnt)
        kb.dma("sp", DTW[:], wi_b[:, OFF_DT:OFF_DT + 32].rearrange("(kc p) n -> p kc n", p=128), rDTW, writes=[rDTW])

        tiles = []
        for s in range(2):
            for ti in range(4):
                tiles.append((s, xp[s, ti * 512:(ti + 1) * 512, :], yp[s, ti * 512:(ti + 1) * 512, :], 512,
                              ti == 0, ti == 3, None))
        for s in range(2):
            tiles.append((2 + s, xs[s], ys[s], 64, True, True, s))
        if tiles_limit is not None:
            tiles = [tiles[i] for i in tiles_limit]

        def panel_list(T):
            pl = []
            for q in range(4):
                pl.append((wi_b, 0, OFF_Z + q * 512))
            for q in range(8):
                pl.append((wi_b, 0, OFF_XBC + q * 512))
            for hf in range(2):
                pl.append((wi_b, 0, OFF_AC + hf * 512))
                pl.append((wi_b, 0, OFF_AH + hf * 512))
                pl.append((wi_b, 0, OFF_AB + hf * 512))
            for hf in range(2):
                pl.append((wi_b, 0, OFF_GATE + hf * 512))
                pl.append((wa_b, 0, hf * 512))
            for hf in range(2):
                pl.append((wi_b, 0, OFF_GATE + 1024 + hf * 512))
                pl.append((ws_b, 0, hf * 512))
                pl.append((ws_b, 1, hf * 512))
            for hf in range(2):
                pl.append((wo_b, 0, hf * 512))
            for q in range(8):
                pl.append((wf1_b, 0, q * 512))
            for hf in range(2):
                for kg in range(4):
                    pl.append((wf2_b, kg, hf * 512))
            return pl
        all_panels = []
        for tl in tiles:
            all_panels += panel_list(tl[3])
        pstate = {"issued": 0, "used": 0}

        def issue_panels(upto):
            while pstate["issued"] < min(upto, len(all_panels)):
                i = pstate["issued"]
                mat, kg, c0 = all_panels[i]
                slot = i % NS
                src = mat[kg * 1024:(kg + 1) * 1024, c0:c0 + 512].rearrange("(kc p) n -> p kc n", p=128)
                kb.dma("sp", PAN[slot][:], src, rPAN[slot], writes=[rPAN[slot]])
                pstate["issued"] += 1

        def next_panel():
            i = pstate["used"]
            issue_panels(i + NS - 1)
            pstate["used"] += 1
            return PAN[i % NS], rPAN[i % NS]

        bank_rr = {"i": 0}

        def acc_bank():
            i = bank_rr["i"]
            bank_rr["i"] = (i + 1) % 4
            return i

        def mm_run(out, pairs, reads, writes, first=True, last=True):
            n = len(pairs)
            for i, (l, r) in enumerate(pairs):
                kb.op("pe", lambda e, l=l, r=r, st=(first and i == 0), sp_=(last and i == n - 1):
                      e.matmul(out, lhsT=l, rhs=r, start=st, stop=sp_),
                      reads=reads if i == 0 else (), writes=writes if i == 0 else (), inc=(i == n - 1))

        def tr_run(outs_ins, reads, writes, idn):
            n = len(outs_ins)
            for i, (o, a) in enumerate(outs_ins):
                kb.op("pe", lambda e, o=o, a=a: e.transpose(o, a, idn),
                      reads=reads if i == 0 else (), writes=writes if i == 0 else (), inc=(i == n - 1))

        def round_robin(gens):
            gens = list(gens)
            while gens:
                for g_ in list(gens):
                    try:
                        next(g_)
                    except StopIteration:
                        gens.remove(g_)

        def g_rstd(sb_, ss_col, rs_col, n, tb, w=1):
            o = 16 * sb_
            rst = rSTATb[sb_]
            kb.op("act", lambda e: e.activation(out=STAT[0:tb, o + rs_col:o + rs_col + w], in_=STAT[0:tb, o + ss_col:o + ss_col + w],
                                                func=AF.Ln, scale=1.0 / n, bias=EPSC[0:tb, 0:1]),
                  reads=[rst, rEPSC], writes=[rst])
            yield
            kb.op("act", lambda e: e.activation(out=STAT[0:tb, o + rs_col:o + rs_col + w], in_=STAT[0:tb, o + rs_col:o + rs_col + w],
                                                func=AF.Exp, scale=-0.5),
                  reads=[rst], writes=[rst])
            yield

        def g_norm_to_fm(src_ap, src_res, b, tb, s16, dst, dst_res):
            (s16ap, s16res) = s16
            o = 16 * b
            rst = rSTATb[b]
            jk, rjk = JUNK4[b]
            kb.op("act", lambda e: e.activation(out=jk[0:tb, :], in_=src_ap[0:tb, :], func=AF.Square,
                                                accum_out=STAT[0:tb, o:o + 1]),
                  reads=[src_res], writes=[rjk, rst])
            yield
            yield from g_rstd(b, 0, 1, 1024, tb)
            kb.op("dve", lambda e: e.tensor_scalar(out=s16ap[0:tb, :], in0=src_ap[0:tb, :], scalar1=STAT[0:tb, o + 1:o + 2],
                                                   scalar2=None, op0=ALU.mult),
                  reads=[src_res, rst], writes=[s16res])
            yield
            tpb = 4 + b
            tr_run([(PSb[:, tpb, kc * 128:kc * 128 + tb], s16ap[0:tb, kc * 128:(kc + 1) * 128]) for kc in range(8)],
                   reads=[s16res, rCB], writes=[rB[tpb]], idn=ident[0:tb, 0:tb])
            yield
            kb.op("act", lambda e: e.copy(out=dst[:, :, b * 128:b * 128 + tb],
                                          in_=PSb[:, tpb, :].rearrange("p (k t) -> p k t", k=8)[:, :, 0:tb]),
                  reads=[rB[tpb]], writes=[dst_res])
            yield

        def gate(n):
            if stop_after is not None and n > stop_after:
                raise _Stop()

        def run_tile(sidx, xd, yd, T, first, last, samp):
            tb = min(T, 128)
            nb = T // tb
            nch = tb // 64
            if first:
                if samp is None:
                    kb.op("pool", lambda e: e.memset(HISTA[:], 0.0), writes=[rHISTA])
                    kb.op("pool", lambda e: e.memset(HISTM[:], 0.0), writes=[rHISTM])
                    kb.op("pool", lambda e: e.memset(ST[:], 0.0), writes=[rST])
                else:
                    kb.dma("sp", HISTA[:], sca[samp].rearrange("p (j i) -> p j i", i=2), rHISTA, writes=[rHISTA])
                    kb.dma("sp", HISTM[:], scm[samp].rearrange("p (c i) -> p c i", i=3), rHISTM, writes=[rHISTM])
                    kb.dma("sp", ST[:], sst[samp], rST, writes=[rST])
                kb.op("act", lambda e: e.copy(out=SBF[0][:], in_=ST[:]), reads=[rST], writes=[rSBF[0]])

            def g_p0(b):
                xt, rxt = XT[b]
                kb.dma("sp", xt[0:tb, :], xd[b * 128:b * 128 + tb, :], rxt, writes=[rxt])
                yield from g_norm_to_fm(xt, rxt, b, tb, XS16[b], XN, rXN)
            round_robin([g_p0(b) for b in range(nb)])

            gate(1)
            for q in range(4):
                pan, rpan = next_panel()
                for b in range(nb):
                    bk = acc_bank()
                    mm_run(PS[0:tb, bk, :], [(XN[:, kc, b * 128:b * 128 + tb], pan[:, kc, :]) for kc in range(8)],
                           reads=[rXN, rpan], writes=[rB[bk]])
                    kb.op("act", lambda e, bk=bk, b=b, q=q: e.activation(
                        out=SZ[0][0:tb, b, q * 512:(q + 1) * 512], in_=PS[0:tb, bk, :], func=AF.Silu),
                        reads=[rB[bk]], writes=[SZ[1]])
            gate(2)
            def b2_s1(q):
                pan, rpan = next_panel()
                xb = XBJ[q % 3]
                for jj in range(4):
                    c = q * 4 + jj
                    kb.op("pool", lambda e, jj=jj, c=c, xb=xb: e.tensor_copy(out=xb[jj][0][:, 0:3], in_=HISTM[:, c, :]),
                          reads=[rHISTM], writes=[xb[jj][1]])
                    bk = acc_bank()
                    mm_run(PS[:, bk, 0:T], [(pan[:, kc, jj * 128:(jj + 1) * 128], XN[:, kc, 0:T]) for kc in range(8)],
                           reads=[rXN, rpan], writes=[rB[bk]])
                    kb.op("act", lambda e, bk=bk, jj=jj, xb=xb: e.copy(out=xb[jj][0][:, 3:3 + T], in_=PS[:, bk, 0:T]),
                          reads=[rB[bk]], writes=[xb[jj][1]])
                    kb.op("pool", lambda e, jj=jj, c=c, xb=xb: e.tensor_copy(out=HISTM[:, c, :], in_=xb[jj][0][:, T:T + 3]),
                          reads=[xb[jj][1]], writes=[rHISTM])

            def b2_s2(q):
                xb = XBJ[q % 3]
                cv = CVJ[q % 2]
                for jj in range(4):
                    c = q * 4 + jj
                    kb.op("act", lambda e, jj=jj, c=c, xb=xb, cv=cv: e.activation(
                        out=cv[jj][0][:, 0:T], in_=xb[jj][0][:, 0:T], func=AF.Identity,
                        scale=PF[:, PF_CSW + c * 4:PF_CSW + c * 4 + 1]),
                        reads=[xb[jj][1], rPF], writes=[cv[jj][1]])
                for i in (1, 2, 3):
                    for jj in range(4):
                        c = q * 4 + jj
                        kb.op("dve", lambda e, jj=jj, c=c, i=i, xb=xb, cv=cv: e.scalar_tensor_tensor(
                            out=cv[jj][0][:, 0:T], in0=xb[jj][0][:, i:i + T],
                            scalar=PF[:, PF_CSW + c * 4 + i:PF_CSW + c * 4 + i + 1], in1=cv[jj][0][:, 0:T],
                            op0=ALU.mult, op1=ALU.add),
                            reads=[xb[jj][1], cv[jj][1], rPF], writes=[cv[jj][1]])

            def b2_s3(q):
                cv = CVJ[q % 2]
                for jj in range(4):
                    c = q * 4 + jj
                    if c < 16:
                        dst, dres = XSF[0][:, c, 0:T], XSF[1]
                    elif c < 24:
                        dst, dres = BF_[0][:, c - 16, 0:T], BF_[1]
                    else:
                        dst, dres = CF[0][:, c - 24, 0:T], CF[1]
                    kb.op("act", lambda e, jj=jj, c=c, dst=dst, cv=cv: e.activation(
                        out=dst, in_=cv[jj][0][:, 0:T], func=AF.Silu, bias=PF[:, PF_CSB + c:PF_CSB + c + 1]),
                        reads=[cv[jj][1], rPF], writes=[dres])

            for q in range(8 + 2):
                if q < 8:
                    b2_s1(q)
                if 1 <= q <= 8:
                    b2_s2(q - 1)
                if q >= 2:
                    b2_s3(q - 2)
            if last:
                kb.dma("sp", ocm[sidx].rearrange("p (c i) -> p c i", i=3), HISTM[:], rHISTM, reads=[rHISTM])

            gate(3)
            for b in range(nb):
                sm = SM[:, b, :]
                rsm = rSM[b]
                smh = SMH[:, b, :]
                rsmh = rSMH[b]
                tok = slice(b * 128, b * 128 + tb)
                bk = acc_bank()
                mm_run(PS[0:tb, bk, 0:32], [(XN[:, kc, tok], DTW[:, kc, :]) for kc in range(8)],
                       reads=[rXN, rDTW], writes=[rB[bk]])
                gate(3.05)
                kb.op("dve", lambda e, bk=bk, sm=sm: e.tensor_tensor(out=sm[0:tb, C_T0:C_T0 + 32], in0=PS[0:tb, bk, 0:32],
                                                                    in1=PF[0:tb, PF_DTB:PF_DTB + 32], op=ALU.add),
                      reads=[rB[bk], rPF], writes=[rsm])
                kb.op("act", lambda e, sm=sm: e.activation(out=sm[0:tb, C_T0:C_T0 + 32], in_=sm[0:tb, C_T0:C_T0 + 32],
                                                           func=AF.Exp), reads=[rsm], writes=[rsm])
                kb.op("act", lambda e, sm=sm: e.activation(out=sm[0:tb, C_DT:C_DT + 32], in_=sm[0:tb, C_T0:C_T0 + 32],
                                                           func=AF.Ln, bias=EPSC[0:tb, 1:2]), reads=[rsm, rEPSC], writes=[rsm])
                kb.op("dve", lambda e, sm=sm: e.tensor_tensor(out=sm[0:tb, C_ADT:C_ADT + 32], in0=sm[0:tb, C_DT:C_DT + 32],
                                                              in1=ANEG[0:tb, :], op=ALU.mult),
                      reads=[rsm, rANEG], writes=[rsm])
                kb.op("dve", lambda e, sm=sm, smh=smh: e.tensor_copy(out=smh[0:tb, 0:32], in_=sm[0:tb, C_ADT:C_ADT + 32]),
                      reads=[rsm], writes=[rsmh])
                kb.op("dve", lambda e, sm=sm, smh=smh: e.tensor_tensor(out=smh[0:tb, 32:64], in0=sm[0:tb, C_ADT:C_ADT + 32],
                                                                      in1=smh[0:tb, 0:32], op=ALU.subtract),
                      reads=[rsm, rsmh], writes=[rsmh])
                gate(3.1)
                bk = acc_bank()
                mm_run(PS[0:tb, bk, 0:32], [(tri128[0:tb, 0:tb], smh[0:tb, 0:32]), (tri128[0:tb, 0:tb], smh[0:tb, 32:64])],
                       reads=[rsmh, rCB], writes=[rB[bk]])
                gate(3.11)
                mm_run(PS[:, bk, 32:64], [(onesA[0:tb, :], smh[0:tb, 0:32]), (onesA[0:tb, :], smh[0:tb, 32:64])],
                       reads=[rsmh, rCB], writes=[rB[bk]])
                if nch == 2:
                    mm_run(PS[:, bk, 64:96], [(onesB[0:tb, :], smh[0:tb, 0:32]), (onesB[0:tb, :], smh[0:tb, 32:64])],
                           reads=[rsmh, rCB], writes=[rB[bk]])
                gate(3.12)
                kb.op("dve", lambda e, bk=bk, sm=sm: e.tensor_copy(out=sm[0:tb, C_ACUM:C_ACUM + 32], in_=PS[0:tb, bk, 0:32]),
                      reads=[rB[bk]], writes=[rsm])
                gate(3.13)
                kb.op("act", lambda e, bk=bk, sm=sm: e.activation(out=sm[0:tb, C_EA:C_EA + 32], in_=PS[0:tb, bk, 0:32],
                                                                 func=AF.Exp), reads=[rB[bk]], writes=[rsm])
                gate(3.14)
                kb.op("act", lambda e, bk=bk, sm=sm: e.activation(out=sm[0:VARP, C_CD:C_CD + 32 * nch],
                                                                 in_=PS[0:VARP, bk, 32:32 + 32 * nch], func=AF.Exp),
                      reads=[rB[bk]], writes=[rsm])
                gate(3.2)
                tr_run([(PSb[0:tb, 5 + c // 8, (c % 8) * 128:(c % 8 + 1) * 128], XSF[0][:, c, tok]) for c in range(16)],
                       reads=[XSF[1], rCB], writes=[rB[5], rB[6]], idn=ident)
                kb.op("act", lambda e: e.copy(out=XSt[0][0:tb, :], in_=PSb[0:tb, 5:7, :].rearrange("p a n -> p (a n)")),
                      reads=[rB[5], rB[6]], writes=[XSt[1]])
                kb.op("dve", lambda e, sm=sm: e.tensor_tensor(
                    out=XC[0][0:tb, :].rearrange("p (h d) -> p h d", h=32),
                    in0=PSb[0:tb, 5:7, :].rearrange("p a (h d) -> p (a h) d", d=64),
                    in1=sm[0:tb, C_DT:C_DT + 32].unsqueeze(2).broadcast_to([tb, 32, 64]), op=ALU.mult),
                    reads=[rB[5], rB[6], rsm], writes=[XC[1]])
                tr_run([(PSb[0:tb, 7, g * 128:(g + 1) * 128], BF_[0][:, g, tok]) for g in range(8)],
                       reads=[BF_[1], rCB], writes=[rB[7]], idn=ident)
                kb.op("act", lambda e: e.copy(out=Bt[0][0:tb, :], in_=PSb[0:tb, 7, :]), reads=[rB[7]], writes=[Bt[1]])
                gate(3.3)
                for (rr, lo) in ((RHI, 0), (RLO, 32)):
                    kb.op("dve", lambda e, rr=rr, lo=lo, smh=smh: e.tensor_tensor(
                        out=rr[0][0:tb, :, :], in0=smh[0:tb, lo:lo + 32].unsqueeze(2).broadcast_to([tb, 32, 64]),
                        in1=tri2[0:tb, :].unsqueeze(1).broadcast_to([tb, 32, 64]), op=ALU.mult),
                        reads=[rsmh, rCB], writes=[rr[1]])
                gate(3.4)
                for hf in range(2):
                    hs = slice(hf * 16, hf * 16 + 16)
                    for nn in range(2):
                        hh = slice(hf * 16 + nn * 8, hf * 16 + nn * 8 + 8)
                        mm_run(PS[0:tb, nn, :],
                               [(bones[0:tb, 0:tb], RHI[0][0:tb, hh, :]), (bones[0:tb, 0:tb], RLO[0][0:tb, hh, :]),
                                (ident[0:tb, 0:tb], negm[0:tb, nn * 512:(nn + 1) * 512])],
                               reads=[RHI[1], RLO[1], rCB], writes=[rB[nn]])
                    kb.op("dve", lambda e, hs=hs, sm=sm: e.tensor_tensor(
                        out=Dh[0][0:tb, :, :], in0=PS[0:tb, 0:2, :].rearrange("p a (h l) -> p (a h) l", l=64),
                        in1=sm[0:tb, C_ACUM + hs.start:C_ACUM + hs.stop].unsqueeze(2).broadcast_to([tb, 16, 64]),
                        op=ALU.subtract), reads=[rB[0], rB[1], rsm], writes=[Dh[1]])
                    kb.op("act", lambda e: e.activation(out=Eh[0][0:tb, :, :], in_=Dh[0][0:tb, :, :], func=AF.Exp),
                          reads=[Dh[1]], writes=[Eh[1]])
                    kb.op("pool", lambda e, hs=hs, sm=sm: e.tensor_copy(
                        out=sm[0:tb, C_DEC + hs.start:C_DEC + hs.stop].unsqueeze(2), in_=Eh[0][0:tb, :, 63:64]),
                        reads=[Eh[1]], writes=[rsm])
                    gate(3.5)
                    for g in range(4):
                        gg = hf * 4 + g
                        for ch in range(nch):
                            r0 = ch * 64
                            cs = slice(b * 128 + r0, b * 128 + r0 + 64)
                            mm_run(PS[r0:r0 + 64, 4, g * 64:(g + 1) * 64], [(BF_[0][:, gg, cs], CF[0][:, gg, cs])],
                                   reads=[BF_[1], CF[1]], writes=[rB[4]])
                    kb.op("dve", lambda e: e.tensor_tensor(
                        out=Mh[0][0:tb, :, :].rearrange("p (g j) l -> p g j l", g=4),
                        in0=Eh[0][0:tb, :, :].rearrange("p (g j) l -> p g j l", g=4),
                        in1=PS[0:tb, 4, 0:256].rearrange("p (g l) -> p g l", g=4).unsqueeze(2).broadcast_to([tb, 4, 4, 64]),
                        op=ALU.mult), reads=[Eh[1], rB[4]], writes=[Mh[1]])
                    kb.op("dve", lambda e, hs=hs, sm=sm, hf=hf: e.tensor_tensor(
                        out=XCD[0][0:tb, hf * 1024:(hf + 1) * 1024].rearrange("p (h d) -> p h d", h=16),
                        in0=XC[0][0:tb, hf * 1024:(hf + 1) * 1024].rearrange("p (h d) -> p h d", h=16),
                        in1=sm[0:tb, C_DEC + hs.start:C_DEC + hs.stop].unsqueeze(2).broadcast_to([tb, 16, 64]),
                        op=ALU.mult), reads=[XC[1], rsm], writes=[XCD[1]])
                    gate(3.6)
                    for ch in range(nch):
                        r0 = ch * 64
                        cs = slice(b * 128 + r0, b * 128 + r0 + 64)
                        sb_in = SBF[ch % 2]
                        rsb_in = rSBF[ch % 2]
                        firstmm = True
                        for g in range(4):
                            gg = hf * 4 + g
                            for j in range(4):
                                hl = g * 4 + j
                                h = hf * 16 + hl
                                mm_run(PS[r0:r0 + 64, 2 + hl // 8, (hl % 8) * 64:(hl % 8 + 1) * 64],
                                       [(Mh[0][r0:r0 + 64, hl, :], XC[0][r0:r0 + 64, h * 64:(h + 1) * 64])],
                                       reads=[Mh[1], XC[1]], writes=[rB[2], rB[3]])
                                firstmm = False
                        for g in range(4):
                            gg = hf * 4 + g
                            mm_run(PS[r0:r0 + 64, 5 + g // 2, (g % 2) * 256:(g % 2 + 1) * 256],
                                   [(CF[0][:, gg, cs], sb_in[:, gg * 256:(gg + 1) * 256])],
                                   reads=[CF[1], rsb_in], writes=[rB[5], rB[6]])
                        for g in range(4):
                            gg = hf * 4 + g
                            mm_run(PS[:, g // 2, (g % 2) * 256:(g % 2 + 1) * 256],
                                   [(Bt[0][r0:r0 + 64, gg * 128:(gg + 1) * 128], XCD[0][r0:r0 + 64, gg * 256:(gg + 1) * 256])],
                                   reads=[Bt[1], XCD[1]], writes=[rB[0], rB[1]])
                        gate(3.7)
                        stv = ST[:, hf * 1024:(hf + 1) * 1024]
                        kb.op("dve", lambda e, stv=stv, sm=sm, ch=ch, hs=hs: e.tensor_tensor(
                            out=T1[0][:, :].rearrange("p (h d) -> p h d", h=16), in0=stv.rearrange("p (h d) -> p h d", h=16),
                            in1=sm[:, C_CD + ch * 32 + hs.start:C_CD + ch * 32 + hs.stop].unsqueeze(2).broadcast_to([128, 16, 64]),
                            op=ALU.mult), reads=[rST, rsm], writes=[T1[1]])
                        kb.op("dve", lambda e, stv=stv: e.tensor_tensor(
                            out=stv, in0=PS[:, 0:2, :].rearrange("p a n -> p (a n)"), in1=T1[0][:, :], op=ALU.add),
                            reads=[rB[0], rB[1], T1[1]], writes=[rST])
                        sb_out = SBF[(ch + 1) % 2] if nch == 2 else SBF[0]
                        rsb_out = rSBF[(ch + 1) % 2] if nch == 2 else rSBF[0]
                        kb.op("act", lambda e, stv=stv, sb_out=sb_out, hf=hf: e.copy(
                            out=sb_out[:, hf * 1024:(hf + 1) * 1024], in_=stv), reads=[rST], writes=[rsb_out])
                    gate(3.8)
                    kb.op("dve", lambda e, hs=hs, sm=sm: e.tensor_tensor(
                        out=Y1[0][0:tb, :].rearrange("p (h d) -> p h d", h=16),
                        in0=PS[0:tb, 5:7, :].rearrange("p a (h d) -> p (a h) d", d=64),
                        in1=sm[0:tb, C_EA + hs.start:C_EA + hs.stop].unsqueeze(2).broadcast_to([tb, 16, 64]),
                        op=ALU.mult), reads=[rB[5], rB[6], rsm], writes=[Y1[1]])
                    kb.op("dve", lambda e: e.tensor_tensor(
                        out=Y1[0][0:tb, :], in0=PS[0:tb, 2:4, :].rearrange("p a n -> p (a n)"), in1=Y1[0][0:tb, :],
                        op=ALU.add), reads=[rB[2], rB[3], Y1[1]], writes=[Y1[1]])
                    kb.op("dve", lambda e, hs=hs, hf=hf: e.tensor_tensor(
                        out=Y3[0][0:tb, :].rearrange("p (h d) -> p h d", h=16),
                        in0=XSt[0][0:tb, hf * 1024:(hf + 1) * 1024].rearrange("p (h d) -> p h d", h=16),
                        in1=PF[0:tb, PF_DSK + hs.start:PF_DSK + hs.stop].unsqueeze(2).broadcast_to([tb, 16, 64]),
                        op=ALU.mult), reads=[XSt[1], rPF], writes=[Y3[1]])
                    kb.op("dve", lambda e: e.tensor_tensor(out=Y3[0][0:tb, :], in0=Y3[0][0:tb, :], in1=Y1[0][0:tb, :],
                                                          op=ALU.add), reads=[Y3[1], Y1[1]], writes=[Y3[1]])
                    kb.op("dve", lambda e, hf=hf, b=b: e.tensor_tensor(
                        out=YG[0][0:tb, :], in0=Y3[0][0:tb, :], in1=SZ[0][0:tb, b, hf * 1024:(hf + 1) * 1024], op=ALU.mult),
                        reads=[Y3[1], SZ[1]], writes=[YG[1]])
                    so_ = 16 * b
                    for g in range(4):
                        kb.op("act", lambda e, g=g, so_=so_: e.activation(
                            out=Y3[0][0:tb, g * 256:(g + 1) * 256], in_=YG[0][0:tb, g * 256:(g + 1) * 256],
                            func=AF.Square, accum_out=STAT[0:tb, so_ + 4 + g:so_ + 5 + g]),
                            reads=[YG[1]], writes=[Y3[1], rSTATb[b]])
                    for _ in g_rstd(b, 4, 8, 256, tb, w=4):
                        pass
                    for g in range(4):
                        kb.op("act", lambda e, g=g, so_=so_: e.activation(
                            out=YN[0][0:tb, g * 256:(g + 1) * 256], in_=YG[0][0:tb, g * 256:(g + 1) * 256],
                            func=AF.Identity, scale=STAT[0:tb, so_ + 8 + g:so_ + 9 + g]),
                            reads=[YG[1], rSTATb[b]], writes=[YN[1]])
                    tr_run([(PSb[:, 7, c * 128:c * 128 + tb], YN[0][0:tb, c * 128:(c + 1) * 128]) for c in range(8)],
                           reads=[YN[1], rCB], writes=[rB[7]], idn=ident[0:tb, 0:tb])
                    kb.op("act", lambda e, hf=hf, b=b: e.copy(
                        out=YNF[0][:, hf * 8:(hf + 1) * 8, b * 128:b * 128 + tb],
                        in_=PSb[:, 7, :].rearrange("p (k t) -> p k t", k=8)[:, :, 0:tb]), reads=[rB[7]], writes=[YNF[1]])
                if nch == 2:
                    pass
            if last:
                kb.dma("sp", ost[sidx], ST[:], rST, reads=[rST])
            dump("xn", XN[:, :, 0:T], rXN, [128, 8, T], BF16)
            dump("sz", SZ[0][0:tb, 0:nb, :], SZ[1], [tb, nb, 2048], BF16)
            dump("xsf", XSF[0][:, :, 0:T], XSF[1], [128, 16, T], BF16)
            dump("bf", BF_[0][:, :, 0:T], BF_[1], [128, 8, T], BF16)
            dump("cf", CF[0][:, :, 0:T], CF[1], [128, 8, T], BF16)
            dump("ynf", YNF[0][:, :, 0:T], YNF[1], [128, 16, T], BF16)
            dump("sm", SM[0:tb, 0, 0:C_CD], rSM[0], [tb, C_CD], F32)
            dump("y1", Y1[0][0:tb, :], Y1[1], [tb, 1024], F32)
            dump("yg", YG[0][0:tb, :], YG[1], [tb, 1024], F32)

            gate(4)
            def fm_group(pan, rpan, jj, act, ract):
                bk = acc_bank()
                mm_run(PS[:, bk, 0:T], [(pan[:, kc, jj * 128:(jj + 1) * 128], act[:, kc, 0:T]) for kc in range(8)],
                       reads=[ract, rpan], writes=[rB[bk]])
                return bk

            for hf in range(2):
                pan, rpan = next_panel()
                for jj in range(4):
                    bk = fm_group(pan, rpan, jj, XN, rXN)
                    kb.op("act", lambda e, bk=bk, jj=jj: e.copy(out=ACJ[jj][0][:, 0:T], in_=PS[:, bk, 0:T]),
                          reads=[rB[bk]], writes=[ACJ[jj][1]])
                pan, rpan = next_panel()
                for jj in range(4):
                    j = hf * 4 + jj
                    kb.op("pool", lambda e, jj=jj, j=j: e.tensor_copy(out=UJ[jj][0][:, 0:2], in_=HISTA[:, j, :]),
                          reads=[rHISTA], writes=[UJ[jj][1]])
                    bk = fm_group(pan, rpan, jj, XN, rXN)
                    kb.op("dve", lambda e, bk=bk, jj=jj: e.tensor_tensor(out=UJ[jj][0][:, 2:2 + T], in0=PS[:, bk, 0:T],
                                                                        in1=ACJ[jj][0][:, 0:T], op=ALU.mult),
                          reads=[rB[bk], ACJ[jj][1]], writes=[UJ[jj][1]])
                    kb.op("pool", lambda e, jj=jj, j=j: e.tensor_copy(out=HISTA[:, j, :], in_=UJ[jj][0][:, T:T + 2]),
                          reads=[UJ[jj][1]], writes=[rHISTA])
                for jj in range(4):
                    j = hf * 4 + jj
                    kb.op("act", lambda e, jj=jj, j=j: e.activation(
                        out=TTJ[jj][0][:, 0:T], in_=UJ[jj][0][:, 0:T], func=AF.Identity,
                        scale=PF[:, PF_CAW + j * 3:PF_CAW + j * 3 + 1]), reads=[UJ[jj][1], rPF], writes=[TTJ[jj][1]])
                pan, rpan = next_panel()
                bks = [fm_group(pan, rpan, jj, XN, rXN) for jj in range(4)]
                for i in (1, 2):
                    for jj in range(4):
                        j = hf * 4 + jj
                        kb.op("dve", lambda e, jj=jj, j=j, i=i: e.scalar_tensor_tensor(
                            out=TTJ[jj][0][:, 0:T], in0=UJ[jj][0][:, i:i + T],
                            scalar=PF[:, PF_CAW + j * 3 + i:PF_CAW + j * 3 + i + 1], in1=TTJ[jj][0][:, 0:T],
                            op0=ALU.mult, op1=ALU.add), reads=[UJ[jj][1], TTJ[jj][1], rPF], writes=[TTJ[jj][1]])
                for jj in range(4):
                    j = hf * 4 + jj
                    bk = bks[jj]
                    kb.op("dve", lambda e, bk=bk, jj=jj, j=j: e.tensor_tensor(out=V[0][:, j, 0:T], in0=PS[:, bk, 0:T],
                                                                             in1=TTJ[jj][0][:, 0:T], op=ALU.mult),
                          reads=[rB[bk], TTJ[jj][1]], writes=[V[1]])
            if last:
                kb.dma("sp", oca[sidx].rearrange("p (j i) -> p j i", i=2), HISTA[:], rHISTA, reads=[rHISTA])
            gate(5)
            dump("v", V[0][:, :, 0:T], V[1], [128, 8, T], BF16)
            for hf in range(2):
                pan, rpan = next_panel()
                for jj in range(4):
                    j = hf * 4 + jj
                    bk = fm_group(pan, rpan, jj, XN, rXN)
                    kb.op("act", lambda e, bk=bk, jj=jj, j=j: e.activation(
                        out=GJ[jj][0][:, 0:T], in_=PS[:, bk, 0:T], func=AF.Sigmoid, bias=PF[:, PF_BG + j:PF_BG + j + 1]),
                        reads=[rB[bk], rPF], writes=[GJ[jj][1]])
                pan, rpan = next_panel()
                for jj in range(4):
                    j = hf * 4 + jj
                    bk = fm_group(pan, rpan, jj, V[0], V[1])
                    kb.op("dve", lambda e, bk=bk, jj=jj, j=j: e.tensor_tensor(out=PA[0][:, j, 0:T], in0=PS[:, bk, 0:T],
                                                                             in1=GJ[jj][0][:, 0:T], op=ALU.mult),
                          reads=[rB[bk], GJ[jj][1]], writes=[PA[1]])
            dump("pa", PA[0][:, :, 0:T], PA[1], [128, 8, T], F32)
            for hf in range(2):
                pan, rpan = next_panel()
                for jj in range(4):
                    j = hf * 4 + jj
                    bk = fm_group(pan, rpan, jj, XN, rXN)
                    kb.op("act", lambda e, bk=bk, jj=jj, j=j: e.activation(
                        out=GJ[jj][0][:, 0:T], in_=PS[:, bk, 0:T], func=AF.Sigmoid,
                        bias=PF[:, PF_BG + 8 + j:PF_BG + 8 + j + 1]), reads=[rB[bk], rPF], writes=[GJ[jj][1]])
                pan0, rpan0 = next_panel()
                pan1, rpan1 = next_panel()
                for jj in range(4):
                    j = hf * 4 + jj
                    bk = acc_bank()
                    pairs = [(pan0[:, kc, jj * 128:(jj + 1) * 128], YNF[0][:, kc, 0:T]) for kc in range(8)] + \
                            [(pan1[:, kc, jj * 128:(jj + 1) * 128], YNF[0][:, 8 + kc, 0:T]) for kc in range(8)]
                    mm_run(PS[:, bk, 0:T], pairs, reads=[YNF[1], rpan0, rpan1], writes=[rB[bk]])
                    kb.op("dve", lambda e, bk=bk, jj=jj: e.tensor_tensor(out=GJ[jj][0][:, 0:T], in0=PS[:, bk, 0:T],
                                                                        in1=GJ[jj][0][:, 0:T], op=ALU.mult),
                          reads=[rB[bk], GJ[jj][1]], writes=[GJ[jj][1]])
                    kb.op("dve", lambda e, jj=jj, j=j: e.tensor_tensor(out=MG[0][:, j, 0:T], in0=GJ[jj][0][:, 0:T],
                                                                      in1=PA[0][:, j, 0:T], op=ALU.add),
                          reads=[GJ[jj][1], PA[1]], writes=[MG[1]])
            gate(6)
            dump("mg", MG[0][:, :, 0:T], MG[1], [128, 8, T], BF16)
            for hf in range(2):
                pan, rpan = next_panel()
                for b in range(nb):
                    bk = acc_bank()
                    mm_run(PS[0:tb, bk, :], [(MG[0][:, kc, b * 128:b * 128 + tb], pan[:, kc, :]) for kc in range(8)],
                           reads=[MG[1], rpan], writes=[rB[bk]])
                    kb.op("act", lambda e, bk=bk, b=b, hf=hf: e.copy(out=MO[b][0][0:tb, hf * 512:(hf + 1) * 512],
                                                                    in_=PS[0:tb, bk, :]), reads=[rB[bk]], writes=[MO[b][1]])
            def g_post(b, src, wcol, addsrc):
                (sap, sres) = src
                (aap, ares) = addsrc
                o = 16 * b
                rst = rSTATb[b]
                jk, rjk = JUNK4[b]
                kb.op("act", lambda e: e.activation(out=jk[0:tb, :], in_=sap[0:tb, :], func=AF.Square,
                                                    accum_out=STAT[0:tb, o:o + 1]), reads=[sres], writes=[rjk, rst])
                yield
                yield from g_rstd(b, 0, 1, 1024, tb)
                kb.op("dve", lambda e: e.scalar_tensor_tensor(
                    out=sap[0:tb, :], in0=sap[0:tb, :], scalar=STAT[0:tb, o + 1:o + 2], in1=PF[0:tb, wcol:wcol + 1024],
                    op0=ALU.mult, op1=ALU.mult), reads=[sres, rst, rPF], writes=[sres])
                yield
                kb.op("dve", lambda e: e.tensor_tensor(out=sap[0:tb, :], in0=sap[0:tb, :], in1=aap[0:tb, :], op=ALU.add),
                      reads=[sres, ares], writes=[sres])
                yield

            def g_wo(b):
                xt, rxt = XT2[b]
                kb.dma("sp", xt[0:tb, :], xd[b * 128:b * 128 + tb, :], rxt, writes=[rxt])
                yield from g_post(b, MO[b], PF_NMPOST, XT2[b])
                yield from g_norm_to_fm(MO[b][0], MO[b][1], b, tb, HS16[b], XN, rXN)
            round_robin([g_wo(b) for b in range(nb)])
            gate(7)
            dump("h", MO[0][0][0:tb, :], MO[0][1], [tb, 1024], F32)
            for q in range(8):
                pan, rpan = next_panel()
                for jj in range(4):
                    c = q * 4 + jj
                    bk = acc_bank()
                    mm_run(PS[:, bk, 0:T], [(pan[:, kc, jj * 128:(jj + 1) * 128], XN[:, kc, 0:T]) for kc in range(8)],
                           reads=[rXN, rpan], writes=[rB[bk]])
                    rt, rrt = RT[c % 4]
                    kb.op("act", lambda e, bk=bk, rt=rt: e.activation(out=rt[:, 0:T], in_=PS[:, bk, 0:T], func=AF.Relu),
                          reads=[rB[bk]], writes=[rrt])
                    kb.op("dve", lambda e, rt=rt, c=c: e.tensor_tensor(out=FF[0][:, c, 0:T], in0=rt[:, 0:T], in1=rt[:, 0:T],
                                                                      op=ALU.mult), reads=[rrt], writes=[FF[1]])
            for hf in range(2):
                for kg in range(4):
                    pan, rpan = next_panel()
                    for b in range(nb):
                        mm_run(PS[0:tb, b, :],
                               [(FF[0][:, kg * 8 + kc, b * 128:b * 128 + tb], pan[:, kc, :]) for kc in range(8)],
                               reads=[FF[1], rpan], writes=[rB[b]], first=(kg == 0), last=(kg == 3))
                for b in range(nb):
                    kb.op("act", lambda e, b=b, hf=hf: e.copy(out=FO[b][0][0:tb, hf * 512:(hf + 1) * 512], in_=PS[0:tb, b, :]),
                          reads=[rB[b]], writes=[FO[b][1]])
            dump("ff", FF[0][:, :, 0:T], FF[1], [128, 32, T], BF16)
            dump("fo", FO[0][0][0:tb, :], FO[0][1], [tb, 1024], F32)
            def g_fin(b):
                yield from g_post(b, FO[b], PF_NFPOST, MO[b])
                kb.dma("sp", yd[b * 128:b * 128 + tb, :], FO[b][0][0:tb, :], FO[b][1], reads=[FO[b][1]])
            round_robin([g_fin(b) for b in range(nb)])
        for tl in tiles:
            try:
                run_tile(*tl)
            except _Stop:
                pass
        kb.finish()
        block = es.enter_context(nc.Block())
        kb.replay(block)
    return nc


def _host_tables(p):
    pf = np.zeros((128, PF_COLS), np.float32)
    pf[:, PF_CAW:PF_CAW + 24] = p["conv_a_w"][0].reshape(3, 8, 128).transpose(2, 1, 0).reshape(128, 24)
    pf[:, PF_CSW:PF_CSW + 128] = p["conv_ssm_w"][0].reshape(4, 32, 128).transpose(2, 1, 0).reshape(128, 128)
    pf[:, PF_CSB:PF_CSB + 32] = p["conv_ssm_b"][0].reshape(32, 128).T
    pf[:, PF_BG:PF_BG + 16] = p["b_gate"][0].reshape(16, 128).T
    pf[:, PF_NMP:PF_NMP + 8] = p["norm_mix_pre"][0].reshape(8, 128).T
    pf[:, PF_NFP:PF_NFP + 8] = p["norm_ffn_pre"][0].reshape(8, 128).T
    pf[:, PF_SSN:PF_SSN + 16] = p["ssm_norm"][0].reshape(16, 128).T
    pf[:, PF_DTB:PF_DTB + 32] = p["dt_bias"][0][None, :]
    pf[:, PF_ALOG:PF_ALOG + 32] = p["a_log"][0][None, :]
    pf[:, PF_DSK:PF_DSK + 32] = p["d_skip"][0][None, :]
    pf[:, PF_NMPOST:PF_NMPOST + 1024] = p["norm_mix_post"][0][None, :]
    pf[:, PF_NFPOST:PF_NFPOST + 1024] = p["norm_ffn_post"][0][None, :]
    cb = np.zeros((128, CB_COLS), np.float32)
    k = np.arange(128)
    cb[:, CB_ID:CB_ID + 128] = np.eye(128)
    l = np.arange(64)
    cb[:, CB_TRI2:CB_TRI2 + 64] = ((k[:, None] % 64) <= l[None, :])
    same = (k[:, None] // 64) == (k[None, :] // 64)
    cb[:, CB_BONES:CB_BONES + 128] = same
    cb[:, CB_ONESA:CB_ONESA + 128] = (k[:, None] < 64) * np.ones((1, 128))
    cb[:, CB_ONESB:CB_ONESB + 128] = (k[:, None] >= 64) * np.ones((1, 128))
    cb[:, CB_TRI128:CB_TRI128 + 128] = same & ((k[:, None] % 64) <= (k[None, :] % 64))
    neg = np.where((k[:, None] % 64) > l[None, :], -30000.0, 0.0)
    cb[:, CB_NEG:CB_NEG + 1024] = np.tile(neg, (1, 16))
    return pf, cb.astype(ml_dtypes.bfloat16)


_NC_CACHE = {}


def kernel(x_prompt, x_sample, state_conv_a, state_conv_ssm, state_ssm, norm_mix_pre, w_in, b_gate, conv_a_w, w_a_out,
           conv_ssm_w, conv_ssm_b, dt_bias, a_log, d_skip, ssm_norm, w_ssm_out, w_o, norm_mix_post, norm_ffn_pre,
           w_ff1, w_ff2, norm_ffn_post, _tiles_limit=None, _cores=None, _dbg=False, _stop=None):
    p = dict(norm_mix_pre=norm_mix_pre, b_gate=b_gate, conv_a_w=conv_a_w, conv_ssm_w=conv_ssm_w, conv_ssm_b=conv_ssm_b,
             dt_bias=dt_bias, a_log=a_log, d_skip=d_skip, ssm_norm=ssm_norm, norm_mix_post=norm_mix_post,
             norm_ffn_pre=norm_ffn_pre, norm_ffn_post=norm_ffn_post)
    p = {k: np.asarray(v, np.float32) for k, v in p.items()}
    pf, cb = _host_tables(p)
    x_prompt = np.asarray(x_prompt, np.float32)
    x_sample = np.asarray(x_sample, np.float32)
    sca_all = np.asarray(state_conv_a, np.float32)[0]
    scm_all = np.asarray(state_conv_ssm, np.float32)[0]
    sst_all = np.asarray(state_ssm, np.float32)[0]
    key = (None if _tiles_limit is None else tuple(_tiles_limit), _dbg, _stop)
    if key not in _NC_CACHE:
        _NC_CACHE[key] = build_program(_tiles_limit, _dbg, _stop)
    nc = _NC_CACHE[key]
    cores = list(range(NCORES)) if _cores is None else _cores
    shared = dict(pf=pf, cb=cb, w_in=np.ascontiguousarray(np.asarray(w_in, np.float32)[0]),
                  w_a_out=np.ascontiguousarray(np.asarray(w_a_out, np.float32)[0]),
                  w_ssm_out=np.ascontiguousarray(np.asarray(w_ssm_out, np.float32)[0]),
                  w_o=np.ascontiguousarray(np.asarray(w_o, np.float32)[0]),
                  w_ff1=np.ascontiguousarray(np.asarray(w_ff1, np.float32)[0]),
                  w_ff2=np.ascontiguousarray(np.asarray(w_ff2, np.float32)[0]))
    in_maps = []
    for c in cores:
        s2 = slice(2 * c, 2 * c + 2)
        m = dict(shared)
        m["xp"] = np.ascontiguousarray(x_prompt[s2])
        m["xs"] = np.ascontiguousarray(x_sample[s2])
        m["sca"] = np.ascontiguousarray(sca_all[s2].reshape(2, 2, 8, 128).transpose(0, 3, 2, 1).reshape(2, 128, 16))
        m["scm"] = np.ascontiguousarray(scm_all[s2].reshape(2, 3, 32, 128).transpose(0, 3, 2, 1).reshape(2, 128, 96))
        m["sst"] = np.ascontiguousarray(sst_all[s2].reshape(2, 2048, 128).transpose(0, 2, 1))
        in_maps.append(m)
    res = run_bass_kernel_spmd(nc, in_maps, core_ids=list(range(len(cores))))
    R = res.results
    if _dbg:
        kernel.dbg = {k: v for k, v in R[0].items() if k.startswith("dbg_")}
    yp = np.concatenate([r["yp"] for r in R], axis=0)
    ys_ = np.concatenate([r["ys"] for r in R], axis=0)
    oca = np.stack([r["oca"] for r in R])
    ocm = np.stack([r["ocm"] for r in R])
    ost = np.stack([r["ost"] for r in R])
    n = len(cores)

    def un_ca(a):
        return a.reshape(n * 2, 128, 8, 2).transpose(0, 3, 2, 1).reshape(n * 2, 2, 1024)

    def un_cm(a):
        return a.reshape(n * 2, 128, 32, 3).transpose(0, 3, 2, 1).reshape(n * 2, 3, 4096)

    def un_st(a):
        return a.reshape(n * 2, 128, 2048).transpose(0, 2, 1).reshape(n * 2, 32, 64, 128)

    outs = (yp, ys_,
            un_ca(oca[:, 0:2])[None], un_cm(ocm[:, 0:2])[None], un_st(ost[:, 0:2])[None],
            un_ca(oca[:, 2:4])[None], un_cm(ocm[:, 2:4])[None], un_st(ost[:, 2:4])[None])
    return tuple(np.ascontiguousarray(o, dtype=np.float32) for o in outs)
```

```python
import numpy as np
import ml_dtypes
from contextlib import ExitStack
import concourse.bass as bass
import concourse.mybir as mybir
from concourse.bass_utils import run_bass_kernel_spmd

F32 = mybir.dt.float32
BF16 = mybir.dt.bfloat16
AF = mybir.ActivationFunctionType
ALU = mybir.AluOpType
AX = mybir.AxisListType

NCORES = 8
D = 1024
SEQ = 2048
DSEQ = 64
PROJ = 11296
OFF_AB, OFF_AC, OFF_AH, OFF_Z, OFF_XBC, OFF_DT, OFF_GATE = 0, 1024, 2048, 3072, 5120, 9216, 9248
EPS = 1e-6
NS = 4

PF_CAW = 0
PF_CSW = 24
PF_CSB = 152
PF_BG = 184
PF_NMP = 200
PF_NFP = 208
PF_SSN = 216
PF_DTB = 232
PF_ALOG = 264
PF_DSK = 296
PF_NMPOST = 328
PF_NFPOST = 1352
PF_COLS = 2376
CB_ID = 0
CB_TRI2 = 128
CB_BONES = 192
CB_ONESA = 320
CB_ONESB = 448
CB_TRI128 = 576
CB_NEG = 704
CB_COLS = 1728


class Res:
    __slots__ = ("name", "lw", "rd", "ov", "dsem", "dcnt", "rng", "excl", "disj")

    def __init__(self, name):
        self.name = name
        self.lw = None
        self.rd = {}
        self.ov = []
        self.dsem = None
        self.dcnt = 0
        self.rng = None
        self.disj = False
        self.excl = False


class KB:
    ENG = ("pe", "act", "dve", "pool")

    def __init__(self, nc, es):
        self.nc = nc
        self.es = es
        self.streams = {k: [] for k in ("pe", "act", "dve", "pool", "sp")}
        self.sem = {e: es.enter_context(nc.semaphore("s_" + e)) for e in self.ENG}
        self.tick = {e: 0 for e in self.ENG}
        self.waited = {k: {} for k in self.streams}
        self.dsems = []
        self.all_res = []

    def res(self, name):
        r = Res(name)
        self.all_res.append(r)
        return r

    def _dsem(self, r):
        if r.dsem is None:
            r.dsem = self.es.enter_context(self.nc.semaphore("d%d" % len(self.dsems)))
            self.dsems.append(r)
        return r.dsem

    def _deps(self, eng, reads, writes):
        deps = []
        for r in reads:
            if r.lw is not None:
                deps.append(r.lw)
            if r.excl:
                for q, t in r.rd.items():
                    if q != eng:
                        deps.append((q, t))
        same_ok = (eng == "pe")
        for w in writes:
            for x in [w] + w.ov:
                if x.lw is not None and (x.lw[0] != eng or not (same_ok or (x is w and w.disj))):
                    deps.append(x.lw)
                for q, t in x.rd.items():
                    if q != eng or not same_ok:
                        deps.append((q, t))
        wd = self.waited[eng]
        for q, t in deps:
            if wd.get(q, 0) < t:
                wd[q] = t
                self.streams[eng].append(("w", q, t))

    def wait(self, eng, q, t):
        wd = self.waited[eng]
        if wd.get(q, 0) < t:
            wd[q] = t
            self.streams[eng].append(("w", q, t))

    def op(self, eng, fn, reads=(), writes=(), inc=True):
        self._deps(eng, reads, writes)
        t = self.tick[eng] + 1
        if inc:
            self.tick[eng] = t
        self.streams[eng].append(("op", fn, inc))
        for r in reads:
            r.rd[eng] = t
        for w in writes:
            w.lw = (eng, t)
            w.rd = {}
        return t

    def dma(self, eng, out, in_, dres, reads=(), writes=()):
        self._deps(eng, reads, writes)
        self._dsem(dres)
        dres.dcnt += 16
        key = (id(dres), dres.dcnt)
        q = ("d", id(dres))
        self._dq = getattr(self, "_dq", {})
        self._dq[q] = dres
        self.streams[eng].append(("dma", out, in_, dres))
        for r in reads:
            r.rd[q] = dres.dcnt
        for w in writes:
            w.lw = (q, dres.dcnt)
            w.rd = {}

    def semof(self, q):
        if isinstance(q, tuple):
            return self._dq[q].dsem
        return self.sem[q]

    def finish(self):
        for r in self.dsems:
            self.wait("sp", ("d", id(r)), r.dcnt)

    def replay(self, block):
        handles = {"pe": block.tensor, "act": block.scalar, "dve": block.vector, "pool": block.gpsimd,
                   "sp": block.sync}
        for eng, deco in handles.items():
            stream = self.streams[eng]
            own = self.sem.get(eng)

            def body(e, stream=stream, own=own):
                for ent in stream:
                    k = ent[0]
                    if k == "w":
                        e.wait_ge(self.semof(ent[1]), ent[2])
                    elif k == "op":
                        ins = ent[1](e)
                        if ent[2]:
                            ins.then_inc(own, 1)
                    else:
                        e.dma_start(out=ent[1], in_=ent[2]).then_inc(ent[3].dsem, 16)
            deco(body)


class Arena:
    def __init__(self, kb, tensor, nbytes):
        self.kb = kb
        self.t = tensor
        self.nbytes = nbytes
        self.views = []

    def view(self, name, off, shape, dt):
        esz = 2 if dt == BF16 else 4
        n = int(np.prod(shape))
        nb = n * esz
        assert off % 4 == 0 and off + nb <= self.nbytes, (name, off, nb, self.nbytes)
        w0, w1 = off // 4, (off + nb + 3) // 4
        ap = self.t[:, w0:w1]
        if dt == BF16:
            ap = ap.bitcast(BF16)[:, 0:n]
        if len(shape) == 2:
            ap = ap.rearrange("p (a b) -> p a b", a=shape[0])
        elif len(shape) == 3:
            ap = ap.rearrange("p (a b c) -> p a b c", a=shape[0], b=shape[1])
        r = self.kb.res(name)
        r.rng = (off, off + nb)
        for (o, rr) in self.views:
            if o.rng[0] < r.rng[1] and r.rng[0] < o.rng[1]:
                r.ov.append(o)
                o.ov.append(r)
        self.views.append((r, ap))
        return ap, r


import os
VARP = int(os.environ.get('VARP', '128'))


class _Stop(Exception):
    pass


def build_program(tiles_limit=None, dbg=False, stop_after=None):
    nc = bass.Bass("TRN2", target_bir_lowering=False)
    dram_in = lambda name, shape, dt=F32: nc.dram_tensor(name, shape, dt, kind="ExternalInput").ap()
    dram_out = lambda name, shape, dt=F32: nc.dram_tensor(name, shape, dt, kind="ExternalOutput").ap()
    xp = dram_in("xp", [2, SEQ, D])
    xs = dram_in("xs", [2, DSEQ, D])
    sca = dram_in("sca", [2, 128, 16])
    scm = dram_in("scm", [2, 128, 96])
    sst = dram_in("sst", [2, 128, 2048])
    pf_d = dram_in("pf", [128, PF_COLS])
    cb_d = dram_in("cb", [128, CB_COLS], BF16)
    w_in = dram_in("w_in", [D, PROJ])
    w_a = dram_in("w_a_out", [D, D])
    w_s = dram_in("w_ssm_out", [2 * D, D])
    w_o = dram_in("w_o", [D, D])
    w_f1 = dram_in("w_ff1", [D, 4 * D])
    w_f2 = dram_in("w_ff2", [4 * D, D])
    yp = dram_out("yp", [2, SEQ, D])
    ys = dram_out("ys", [2, DSEQ, D])
    oca = dram_out("oca", [4, 128, 16])
    ocm = dram_out("ocm", [4, 128, 96])
    ost = dram_out("ost", [4, 128, 2048])
    wi_b = nc.dram_tensor("wi_b", [D, PROJ], BF16).ap()
    wa_b = nc.dram_tensor("wa_b", [D, D], BF16).ap()
    ws_b = nc.dram_tensor("ws_b", [2 * D, D], BF16).ap()
    wo_b = nc.dram_tensor("wo_b", [D, D], BF16).ap()
    wf1_b = nc.dram_tensor("wf1_b", [D, 4 * D], BF16).ap()
    wf2_b = nc.dram_tensor("wf2_b", [4 * D, D], BF16).ap()

    with ExitStack() as es:
        kb = KB(nc, es)
        sbt = lambda name, shape, dt: es.enter_context(nc.sbuf_tensor(name, shape, dt))
        PF = sbt("PF", [128, PF_COLS], F32); rPF = kb.res("PF")
        CB = sbt("CB", [128, CB_COLS], BF16); rCB = kb.res("CB")
        PAN = [sbt("PAN%d" % i, [128, 8, 512], BF16) for i in range(NS)]
        rPAN = [kb.res("PAN%d" % i) for i in range(NS)]
        XN = sbt("XN", [128, 8, 512], BF16); rXN = kb.res("XN")
        DTW = sbt("DTW", [128, 8, 32], BF16); rDTW = kb.res("DTW")
        HISTA = sbt("HISTA", [128, 8, 2], F32); rHISTA = kb.res("HISTA")
        HISTM = sbt("HISTM", [128, 32, 3], F32); rHISTM = kb.res("HISTM")
        ST = sbt("ST", [128, 2048], F32)
        rSTh = [kb.res("ST_h%d" % i) for i in range(2)]
        SBF = [sbt("SBF%d" % i, [128, 2048], BF16) for i in range(2)]
        rSBFh = [[kb.res("SBF%d_h%d" % (i, hf)) for hf in range(2)] for i in range(2)]
        DECT = sbt("DECT", [128, 4, 32], F32)
        rDECT = [[kb.res("DEC%d_%d" % (b, hf)) for hf in range(2)] for b in range(4)]
        STATC = 64
        rSTATc = [[kb.res("STATC%d_%d" % (b, c_)) for c_ in range(2)] for b in range(4)]
        ANEG = sbt("ANEG", [128, 32], F32); rANEG = kb.res("ANEG")
        SM = sbt("SM", [128, 4, 320], F32)
        rSM = [kb.res("SM%d" % b) for b in range(4)]
        STAT = sbt("STAT", [128, 256], F32)
        rSTATb = [kb.res("STAT%d" % b) for b in range(4)]
        EPSC = sbt("EPSC", [128, 2], F32); rEPSC = kb.res("EPSC")
        C_DT, C_ADT, C_ACUM, C_EA, C_CD, C_DEC, C_T0, C_RS = 0, 32, 64, 96, 128, 192, 224, 256
        SMH = sbt("SMH", [128, 4, 64], BF16)
        rSMH = [kb.res("SMH%d" % b) for b in range(4)]
        KBY = 1024
        AR_BYTES = 128 * KBY
        ARt = sbt("ARENA", [128, AR_BYTES // 4], F32)
        ar = Arena(kb, ARt, AR_BYTES)
        XT = [ar.view("XT%d" % i, i * 4 * KBY, [1024], F32) for i in range(4)]
        XS16 = [ar.view("XS16_%d" % i, 16 * KBY + i * 2 * KBY, [1024], BF16) for i in range(4)]
        JUNK4 = [ar.view("JUNK%d" % i, 24 * KBY + i * 2 * KBY, [1024], BF16) for i in range(4)]
        SZ = ar.view("SZ", 0, [4, 2048], BF16)
        XSF = ar.view("XSF", 16 * KBY, [16, 512], BF16)
        BF_ = ar.view("BF", 32 * KBY, [8, 512], BF16)
        CF = ar.view("CF", 40 * KBY, [8, 512], BF16)
        XBJ = [[ar.view("XB%d_%d" % (p_, jj), 48 * KBY + (p_ * 4 + jj) * 2064, [516], F32) for jj in range(4)]
               for p_ in range(3)]
        CVJ = [[ar.view("CV%d_%d" % (p_, jj), 73 * KBY + (p_ * 4 + jj) * 2048, [512], F32) for jj in range(4)]
               for p_ in range(2)]
        XStP = [ar.view("XSt%d" % i, 48 * KBY + i * 4 * KBY, [2048], BF16) for i in range(2)]
        BtP = [ar.view("Bt%d" % i, 56 * KBY + i * 2 * KBY, [1024], BF16) for i in range(2)]
        XCDh = [ar.view("XCD_%d" % hf, 60 * KBY + hf * 2 * KBY, [1024], BF16) for hf in range(2)]
        XCP = [ar.view("XC%d" % i, 64 * KBY + i * 4 * KBY, [2048], BF16) for i in range(2)]
        RHI = ar.view("RHI", 72 * KBY, [32, 64], BF16)
        RLO = ar.view("RLO", 76 * KBY, [32, 64], BF16)
        DEc = [ar.view("DE%d" % i, 80 * KBY + i * 4 * KBY, [16, 64], F32) for i in range(2)]
        MHc = [ar.view("MH%d" % i, 88 * KBY + i * 2 * KBY, [16, 64], BF16) for i in range(2)]
        T1c = [ar.view("T1_%d" % i, 92 * KBY + i * 4 * KBY, [1024], F32) for i in range(2)]
        Y1c = [ar.view("Y1_%d" % i, 100 * KBY + i * 4 * KBY, [1024], F32) for i in range(2)]
        YNc = [ar.view("YN_%d" % i, 108 * KBY + i * 2 * KBY, [1024], BF16) for i in range(2)]
        YNF = ar.view("YNF", 112 * KBY, [16, 512], BF16)
        ACJ = [ar.view("AC_%d" % jj, jj * 2048, [512], F32) for jj in range(4)]
        UJ = [ar.view("U_%d" % jj, 8 * KBY + jj * 2064, [516], F32) for jj in range(4)]
        TTJ = [ar.view("TT_%d" % jj, 17 * KBY + jj * 2048, [512], F32) for jj in range(4)]
        V = ar.view("V", 25 * KBY, [8, 512], BF16)
        GJ = [ar.view("G_%d" % jj, 33 * KBY + jj * 2048, [512], F32) for jj in range(4)]
        PA = ar.view("PA", 41 * KBY, [8, 512], F32)
        MG = ar.view("MG", 57 * KBY, [8, 512], BF16)
        MO = [ar.view("MO%d" % b, 65 * KBY + b * 4 * KBY, [1024], F32) for b in range(4)]
        XT2 = [ar.view("XT2_%d" % i, 81 * KBY + i * 4 * KBY, [1024], F32) for i in range(4)]
        HS16 = [ar.view("HS16_%d" % i, 97 * KBY + i * 2 * KBY, [1024], BF16) for i in range(4)]
        FF = ar.view("FF", 0, [32, 512], BF16)
        RT = [ar.view("RT%d" % i, 32 * KBY + i * 2 * KBY, [512], F32) for i in range(4)]
        FO = [ar.view("FO%d" % b, 40 * KBY + b * 4 * KBY, [1024], F32) for b in range(4)]
        SGW = 2824
        NSG = 4
        SGI = [ar.view("SGI%d" % i, i * 12 * KBY, [SGW], F32) for i in range(NSG)]
        SGO = [ar.view("SGO%d" % i, 48 * KBY + i * 6 * KBY, [SGW], BF16) for i in range(NSG)]
        PS = es.enter_context(nc.psum_tensor("PS", [128, 8, 512], F32))
        rB = [kb.res("bank%d" % i) for i in range(8)]
        for r_ in rB:
            r_.excl = True
        PSb = PS[:].bitcast(BF16)

        for r_ in [XSF[1], BF_[1], CF[1], V[1], PA[1], MG[1], FF[1], YNF[1], SZ[1], rXN] + [m_[1] for m_ in MO] + \
                [f_[1] for f_ in FO]:
            r_.disj = True
        ident = CB[:, CB_ID:CB_ID + 128]
        tri2 = CB[:, CB_TRI2:CB_TRI2 + 64]
        bones = CB[:, CB_BONES:CB_BONES + 128]
        onesA = CB[:, CB_ONESA:CB_ONESA + 128]
        onesB = CB[:, CB_ONESB:CB_ONESB + 128]
        tri128 = CB[:, CB_TRI128:CB_TRI128 + 128]
        negm = CB[:, CB_NEG:CB_NEG + 1024]

        dbg_names = []

        def dump(name, ap, res, shape, dt):
            if not dbg:
                return
            d = nc.dram_tensor("dbg_" + name, shape, dt, kind="ExternalOutput").ap()
            kb.dma("sp", d, ap, res, reads=[res])
            dbg_names.append("dbg_" + name)

        kb.dma("sp", PF[:], pf_d, rPF, writes=[rPF])
        kb.dma("sp", CB[:], cb_d, rCB, writes=[rCB])
        kb.op("pool", lambda e: e.memset(EPSC[:, 0:1], EPS), writes=[rEPSC])
        kb.op("pool", lambda e: e.memset(EPSC[:, 1:2], 1.0), writes=[rEPSC])
        kb.op("act", lambda e: e.activation(out=ANEG[:], in_=PF[:, PF_ALOG:PF_ALOG + 32], func=AF.Exp),
              reads=[rPF], writes=[rANEG])
        kb.op("dve", lambda e: e.tensor_scalar(out=ANEG[:], in0=ANEG[:], scalar1=-1.0, scalar2=None, op0=ALU.mult),
              reads=[rANEG], writes=[rANEG])

        conv_jobs = []
        def add_mat(src, dst, K, N, piece, scol):
            for rc in range(K // 128):
                for c0 in range(0, N, piece):
                    w = min(piece, N - c0)
                    conv_jobs.append((src[rc * 128:(rc + 1) * 128, c0:c0 + w], dst[rc * 128:(rc + 1) * 128, c0:c0 + w],
                                      None if scol is None else scol + rc, w))
        add_mat(w_in, wi_b, D, PROJ, SGW, PF_NMP)
        add_mat(w_a, wa_b, D, D, 1024, None)
        add_mat(w_s, ws_b, 2 * D, D, 1024, PF_SSN)
        add_mat(w_o, wo_b, D, D, 1024, None)
        add_mat(w_f1, wf1_b, D, 4 * D, 2048, PF_NFP)
        add_mat(w_f2, wf2_b, 4 * D, D, 1024, None)
        def conv_store(i):
            (src, dst, scol, w) = conv_jobs[i]
            so, ro = SGO[i % NSG]
            kb.dma("sp", dst, so[:, 0:w], ro, reads=[ro])
        for i, (src, dst, scol, w) in enumerate(conv_jobs):
            si, ri = SGI[i % NSG]
            so, ro = SGO[i % NSG]
            if i >= NSG - 1:
                conv_store(i - (NSG - 1))
            kb.dma("sp", si[:, 0:w], src, ri, writes=[ri])
            eng = "dve" if (i % 5) in (0, 2, 4) else "act"
            if eng == "dve":
                if scol is None:
                    kb.op(eng, lambda e, so=so, si=si, w=w: e.tensor_copy(out=so[:, 0:w], in_=si[:, 0:w]),
                          reads=[ri], writes=[ro])
                else:
                    kb.op(eng, lambda e, so=so, si=si, w=w, scol=scol: e.tensor_scalar(
                        out=so[:, 0:w], in0=si[:, 0:w], scalar1=PF[:, scol:scol + 1], scalar2=None, op0=ALU.mult),
                        reads=[ri, rPF], writes=[ro])
            else:
                if scol is None:
                    kb.op(eng, lambda e, so=so, si=si, w=w: e.copy(out=so[:, 0:w], in_=si[:, 0:w]),
                          reads=[ri], writes=[ro])
                else:
                    kb.op(eng, lambda e, so=so, si=si, w=w, scol=scol: e.activation(
                        out=so[:, 0:w], in_=si[:, 0:w], func=AF.Identity, scale=PF[:, scol:scol + 1]),
                        reads=[ri, rPF], writes=[ro])
        for i in range(max(0, len(conv_jobs) - (NSG - 1)), len(conv_jobs)):
            conv_store(i)
        for so, ro in SGO:
            kb.wait("sp", ("d", id(ro)), ro.dcnt)
        kb.dma("sp", DTW[:], wi_b[:, OFF_DT:OFF_DT + 32].rearrange("(kc p) n -> p kc n", p=128), rDTW, writes=[rDTW])

        tiles = []
        for s in range(2):
            for ti in range(4):
                tiles.append((s, xp[s, ti * 512:(ti + 1) * 512, :], yp[s, ti * 512:(ti + 1) * 512, :], 512,
                              ti == 0, ti == 3, None))
        for s in range(2):
            tiles.append((2 + s, xs[s], ys[s], 64, True, True, s))
        if tiles_limit is not None:
            tiles = [tiles[i] for i in tiles_limit]

        def panel_list(T):
            pl = []
            for q in range(4):
                pl.append((wi_b, 0, OFF_Z + q * 512))
            for q in range(8):
                pl.append((wi_b, 0, OFF_XBC + q * 512))
            for hf in range(2):
                pl.append((wi_b, 0, OFF_AC + hf * 512))
                pl.append((wi_b, 0, OFF_AH + hf * 512))
                pl.append((wi_b, 0, OFF_AB + hf * 512))
            for hf in range(2):
                pl.append((wi_b, 0, OFF_GATE + hf * 512))
                pl.append((wa_b, 0, hf * 512))
            for hf in range(2):
                pl.append((wi_b, 0, OFF_GATE + 1024 + hf * 512))
                pl.append((ws_b, 0, hf * 512))
                pl.append((ws_b, 1, hf * 512))
            for hf in range(2):
                pl.append((wo_b, 0, hf * 512))
            for q in range(8):
                pl.append((wf1_b, 0, q * 512))
            for hf in range(2):
                for kg in range(4):
                    pl.append((wf2_b, kg, hf * 512))
            return pl
        all_panels = []
        for tl in tiles:
            all_panels += panel_list(tl[3])
        pstate = {"issued": 0, "used": 0}

        def issue_panels(upto):
            while pstate["issued"] < min(upto, len(all_panels)):
                i = pstate["issued"]
                mat, kg, c0 = all_panels[i]
                slot = i % NS
                src = mat[kg * 1024:(kg + 1) * 1024, c0:c0 + 512].rearrange("(kc p) n -> p kc n", p=128)
                kb.dma("sp", PAN[slot][:], src, rPAN[slot], writes=[rPAN[slot]])
                pstate["issued"] += 1

        def next_panel():
            i = pstate["used"]
            issue_panels(i + NS - 1)
            pstate["used"] += 1
            return PAN[i % NS], rPAN[i % NS]

        bank_rr = {"i": 0}

        def acc_bank():
            i = bank_rr["i"]
            bank_rr["i"] = (i + 1) % 4
            return i

        def mm_run(out, pairs, reads, writes, first=True, last=True):
            n = len(pairs)
            for i, (l, r) in enumerate(pairs):
                kb.op("pe", lambda e, l=l, r=r, st=(first and i == 0), sp_=(last and i == n - 1):
                      e.matmul(out, lhsT=l, rhs=r, start=st, stop=sp_),
                      reads=reads if i == 0 else (), writes=writes if i == 0 else (), inc=(i == n - 1))

        def tr_run(outs_ins, reads, writes, idn):
            n = len(outs_ins)
            for i, (o, a) in enumerate(outs_ins):
                kb.op("pe", lambda e, o=o, a=a: e.transpose(o, a, idn),
                      reads=reads if i == 0 else (), writes=writes if i == 0 else (), inc=(i == n - 1))

        def round_robin(gens):
            gens = list(gens)
            while gens:
                for g_ in list(gens):
                    try:
                        next(g_)
                    except StopIteration:
                        gens.remove(g_)

        def g_rstd(o, rst, ss_col, rs_col, n, tb, w=1):
            kb.op("act", lambda e: e.activation(out=STAT[0:tb, o + rs_col:o + rs_col + w], in_=STAT[0:tb, o + ss_col:o + ss_col + w],
                                                func=AF.Ln, scale=1.0 / n, bias=EPSC[0:tb, 0:1]),
                  reads=[rst, rEPSC], writes=[rst])
            yield
            kb.op("act", lambda e: e.activation(out=STAT[0:tb, o + rs_col:o + rs_col + w], in_=STAT[0:tb, o + rs_col:o + rs_col + w],
                                                func=AF.Exp, scale=-0.5),
                  reads=[rst], writes=[rst])
            yield

        def round_robin_off(gens):
            live = list(gens)
            step = 0
            while live:
                for item in list(live):
                    if item[1] <= step:
                        try:
                            next(item[0])
                        except StopIteration:
                            live.remove(item)
                step += 1

        def g_norm_to_fm(src_ap, src_res, b, tb, s16, dst, dst_res):
            (s16ap, s16res) = s16
            o = 16 * b
            rst = rSTATb[b]
            jk, rjk = JUNK4[b]
            kb.op("act", lambda e: e.activation(out=jk[0:tb, :], in_=src_ap[0:tb, :], func=AF.Square,
                                                accum_out=STAT[0:tb, o:o + 1]),
                  reads=[src_res], writes=[rjk, rst])
            yield
            yield from g_rstd(o, rst, 0, 1, 1024, tb)
            kb.op("dve", lambda e: e.tensor_scalar(out=s16ap[0:tb, :], in0=src_ap[0:tb, :], scalar1=STAT[0:tb, o + 1:o + 2],
                                                   scalar2=None, op0=ALU.mult),
                  reads=[src_res, rst], writes=[s16res])
            yield
            tpb = 4 + b
            tr_run([(PSb[:, tpb, kc * 128:kc * 128 + tb], s16ap[0:tb, kc * 128:(kc + 1) * 128]) for kc in range(8)],
                   reads=[s16res, rCB], writes=[rB[tpb]], idn=ident[0:tb, 0:tb])
            yield
            kb.op("act", lambda e: e.copy(out=dst[:, :, b * 128:b * 128 + tb],
                                          in_=PSb[:, tpb, :].rearrange("p (k t) -> p k t", k=8)[:, :, 0:tb]),
                  reads=[rB[tpb]], writes=[dst_res])
            yield

        def gate(n):
            if stop_after is not None and n > stop_after:
                raise _Stop()

        def run_tile(sidx, xd, yd, T, first, last, samp):
            tb = min(T, 128)
            nb = T // tb
            nch = tb // 64
            if first:
                if samp is None:
                    kb.op("pool", lambda e: e.memset(HISTA[:], 0.0), writes=[rHISTA])
                    kb.op("pool", lambda e: e.memset(HISTM[:], 0.0), writes=[rHISTM])
                    kb.op("pool", lambda e: e.memset(ST[:], 0.0), writes=[rSTh[0], rSTh[1]])
                else:
                    kb.dma("sp", HISTA[:], sca[samp].rearrange("p (j i) -> p j i", i=2), rHISTA, writes=[rHISTA])
                    kb.dma("sp", HISTM[:], scm[samp].rearrange("p (c i) -> p c i", i=3), rHISTM, writes=[rHISTM])
                    kb.dma("sp", ST[:], sst[samp], rSTh[0], writes=[rSTh[0], rSTh[1]])
                kb.op("act", lambda e: e.copy(out=SBF[0][:], in_=ST[:]), reads=[rSTh[0], rSTh[1]],
                      writes=[rSBFh[0][0], rSBFh[0][1]])

            def g_p0(b):
                xt, rxt = XT[b]
                kb.dma("sp", xt[0:tb, :], xd[b * 128:b * 128 + tb, :], rxt, writes=[rxt])
                yield from g_norm_to_fm(xt, rxt, b, tb, XS16[b], XN, rXN)
            round_robin([g_p0(b) for b in range(nb)])

            gate(1)
            for q in range(4):
                pan, rpan = next_panel()
                for b in range(nb):
                    bk = acc_bank()
                    mm_run(PS[0:tb, bk, :], [(XN[:, kc, b * 128:b * 128 + tb], pan[:, kc, :]) for kc in range(8)],
                           reads=[rXN, rpan], writes=[rB[bk]])
                    kb.op("act", lambda e, bk=bk, b=b, q=q: e.activation(
                        out=SZ[0][0:tb, b, q * 512:(q + 1) * 512], in_=PS[0:tb, bk, :], func=AF.Silu),
                        reads=[rB[bk]], writes=[SZ[1]])
            gate(2)
            def b2_s1(q):
                pan, rpan = next_panel()
                xb = XBJ[q % 3]
                for jj in range(4):
                    c = q * 4 + jj
                    kb.op("pool", lambda e, jj=jj, c=c, xb=xb: e.tensor_copy(out=xb[jj][0][:, 0:3], in_=HISTM[:, c, :]),
                          reads=[rHISTM], writes=[xb[jj][1]])
                    bk = acc_bank()
                    mm_run(PS[:, bk, 0:T], [(pan[:, kc, jj * 128:(jj + 1) * 128], XN[:, kc, 0:T]) for kc in range(8)],
                           reads=[rXN, rpan], writes=[rB[bk]])
                    kb.op("act", lambda e, bk=bk, jj=jj, xb=xb: e.copy(out=xb[jj][0][:, 3:3 + T], in_=PS[:, bk, 0:T]),
                          reads=[rB[bk]], writes=[xb[jj][1]])
                    kb.op("pool", lambda e, jj=jj, c=c, xb=xb: e.tensor_copy(out=HISTM[:, c, :], in_=xb[jj][0][:, T:T + 3]),
                          reads=[xb[jj][1]], writes=[rHISTM])

            def b2_s2(q):
                xb = XBJ[q % 3]
                cv = CVJ[q % 2]
                for jj in range(4):
                    c = q * 4 + jj
                    kb.op("act", lambda e, jj=jj, c=c, xb=xb, cv=cv: e.activation(
                        out=cv[jj][0][:, 0:T], in_=xb[jj][0][:, 0:T], func=AF.Identity,
                        scale=PF[:, PF_CSW + c * 4:PF_CSW + c * 4 + 1]),
                        reads=[xb[jj][1], rPF], writes=[cv[jj][1]])
                for i in (1, 2, 3):
                    for jj in range(4):
                        c = q * 4 + jj
                        kb.op("dve", lambda e, jj=jj, c=c, i=i, xb=xb, cv=cv: e.scalar_tensor_tensor(
                            out=cv[jj][0][:, 0:T], in0=xb[jj][0][:, i:i + T],
                            scalar=PF[:, PF_CSW + c * 4 + i:PF_CSW + c * 4 + i + 1], in1=cv[jj][0][:, 0:T],
                            op0=ALU.mult, op1=ALU.add),
                            reads=[xb[jj][1], cv[jj][1], rPF], writes=[cv[jj][1]])

            def b2_s3(q):
                cv = CVJ[q % 2]
                for jj in range(4):
                    c = q * 4 + jj
                    if c < 16:
                        dst, dres = XSF[0][:, c, 0:T], XSF[1]
                    elif c < 24:
                        dst, dres = BF_[0][:, c - 16, 0:T], BF_[1]
                    else:
                        dst, dres = CF[0][:, c - 24, 0:T], CF[1]
                    kb.op("act", lambda e, jj=jj, c=c, dst=dst, cv=cv: e.activation(
                        out=dst, in_=cv[jj][0][:, 0:T], func=AF.Silu, bias=PF[:, PF_CSB + c:PF_CSB + c + 1]),
                        reads=[cv[jj][1], rPF], writes=[dres])

            for q in range(8 + 2):
                if q < 8:
                    b2_s1(q)
                if 1 <= q <= 8:
                    b2_s2(q - 1)
                if q >= 2:
                    b2_s3(q - 2)
            if last:
                kb.dma("sp", ocm[sidx].rearrange("p (c i) -> p c i", i=3), HISTM[:], rHISTM, reads=[rHISTM])

            gate(3)
            def g_dt(b):
                sm = SM[:, b, :]
                rsm = rSM[b]
                smh = SMH[:, b, :]
                rsmh = rSMH[b]
                tok = slice(b * 128, b * 128 + tb)
                bk = acc_bank()
                mm_run(PS[0:tb, bk, 0:32], [(XN[:, kc, tok], DTW[:, kc, :]) for kc in range(8)],
                       reads=[rXN, rDTW], writes=[rB[bk]])
                yield
                kb.op("dve", lambda e: e.tensor_tensor(out=sm[0:tb, C_T0:C_T0 + 32], in0=PS[0:tb, bk, 0:32],
                                                       in1=PF[0:tb, PF_DTB:PF_DTB + 32], op=ALU.add),
                      reads=[rB[bk], rPF], writes=[rsm])
                yield
                kb.op("act", lambda e: e.activation(out=sm[0:tb, C_T0:C_T0 + 32], in_=sm[0:tb, C_T0:C_T0 + 32],
                                                    func=AF.Exp), reads=[rsm], writes=[rsm])
                yield
                kb.op("act", lambda e: e.activation(out=sm[0:tb, C_DT:C_DT + 32], in_=sm[0:tb, C_T0:C_T0 + 32],
                                                    func=AF.Ln, bias=EPSC[0:tb, 1:2]), reads=[rsm, rEPSC], writes=[rsm])
                yield
                kb.op("dve", lambda e: e.tensor_tensor(out=sm[0:tb, C_ADT:C_ADT + 32], in0=sm[0:tb, C_DT:C_DT + 32],
                                                       in1=ANEG[0:tb, :], op=ALU.mult),
                      reads=[rsm, rANEG], writes=[rsm])
                yield
                kb.op("dve", lambda e: e.tensor_copy(out=smh[0:tb, 0:32], in_=sm[0:tb, C_ADT:C_ADT + 32]),
                      reads=[rsm], writes=[rsmh])
                yield
                kb.op("dve", lambda e: e.tensor_tensor(out=smh[0:tb, 32:64], in0=sm[0:tb, C_ADT:C_ADT + 32],
                                                       in1=smh[0:tb, 0:32], op=ALU.subtract),
                      reads=[rsm, rsmh], writes=[rsmh])
                yield
                bk2 = acc_bank()
                mm_run(PS[0:tb, bk2, 0:32], [(tri128[0:tb, 0:tb], smh[0:tb, 0:32]), (tri128[0:tb, 0:tb], smh[0:tb, 32:64])],
                       reads=[rsmh, rCB], writes=[rB[bk2]])
                mm_run(PS[:, bk2, 32:64], [(onesA[0:tb, :], smh[0:tb, 0:32]), (onesA[0:tb, :], smh[0:tb, 32:64])],
                       reads=[rsmh, rCB], writes=[rB[bk2]])
                if nch == 2:
                    mm_run(PS[:, bk2, 64:96], [(onesB[0:tb, :], smh[0:tb, 0:32]), (onesB[0:tb, :], smh[0:tb, 32:64])],
                           reads=[rsmh, rCB], writes=[rB[bk2]])
                yield
                kb.op("dve", lambda e: e.tensor_copy(out=sm[0:tb, C_ACUM:C_ACUM + 32], in_=PS[0:tb, bk2, 0:32]),
                      reads=[rB[bk2]], writes=[rsm])
                yield
                kb.op("act", lambda e: e.activation(out=sm[0:tb, C_EA:C_EA + 32], in_=PS[0:tb, bk2, 0:32],
                                                    func=AF.Exp), reads=[rB[bk2]], writes=[rsm])
                kb.op("act", lambda e: e.activation(out=sm[0:VARP, C_CD:C_CD + 32 * nch],
                                                    in_=PS[0:VARP, bk2, 32:32 + 32 * nch], func=AF.Exp),
                      reads=[rB[bk2]], writes=[rsm])
                yield
            round_robin([g_dt(b) for b in range(nb)])

            def g_prep_banks():
                return (4, 5), (6, 7)

            def g_prep(b):
                sm = SM[:, b, :]
                rsm = rSM[b]
                smh = SMH[:, b, :]
                rsmh = rSMH[b]
                tok = slice(b * 128, b * 128 + tb)
                xst, rxst = XStP[b % 2]
                btt, rbtt = BtP[b % 2]
                xc, rxc = XCP[b % 2]
                p, p2 = g_prep_banks()
                tr_run([(PSb[0:tb, p[c // 8], (c % 8) * 128:(c % 8 + 1) * 128], XSF[0][:, c, tok]) for c in range(16)],
                       reads=[XSF[1], rCB], writes=[rB[p[0]], rB[p[1]]], idn=ident)
                yield
                kb.op("act", lambda e: e.copy(out=xst[0:tb, :], in_=PSb[0:tb, p[0]:p[1] + 1, :].rearrange("p a n -> p (a n)")),
                      reads=[rB[p[0]], rB[p[1]]], writes=[rxst])
                yield
                kb.op("dve", lambda e: e.tensor_tensor(
                    out=xc[0:tb, :].rearrange("p (h d) -> p h d", h=32),
                    in0=PSb[0:tb, p[0]:p[1] + 1, :].rearrange("p a (h d) -> p (a h) d", d=64),
                    in1=sm[0:tb, C_DT:C_DT + 32].unsqueeze(2).broadcast_to([tb, 32, 64]), op=ALU.mult),
                    reads=[rB[p[0]], rB[p[1]], rsm], writes=[rxc])
                yield
                tr_run([(PSb[0:tb, p2[0], g * 128:(g + 1) * 128], BF_[0][:, g, tok]) for g in range(8)],
                       reads=[BF_[1], rCB], writes=[rB[p2[0]]], idn=ident)
                yield
                kb.op("act", lambda e: e.copy(out=btt[0:tb, :], in_=PSb[0:tb, p2[0], :]), reads=[rB[p2[0]]], writes=[rbtt])
                yield
                for (rr_, lo) in ((RHI, 0), (RLO, 32)):
                    kb.op("dve", lambda e, rr_=rr_, lo=lo: e.tensor_tensor(
                        out=rr_[0][0:tb, :, :], in0=smh[0:tb, lo:lo + 32].unsqueeze(2).broadcast_to([tb, 32, 64]),
                        in1=tri2[0:tb, :].unsqueeze(1).broadcast_to([tb, 32, 64]), op=ALU.mult),
                        reads=[rsmh, rCB], writes=[rr_[1]])
                    yield

            def g_chain(b, hf):
                c_ = hf
                sm = SM[:, b, :]
                rsm = rSM[b]
                tok0 = b * 128
                hs = slice(hf * 16, hf * 16 + 16)
                xst, rxst = XStP[b % 2]
                btt, rbtt = BtP[b % 2]
                xcd, rxcd = XCDh[hf]
                xc, rxc = XCP[b % 2]
                mypair = (2 * c_, 2 * c_ + 1)

                def scratch():
                    return mypair
                de, rde = DEc[c_]
                mh, rmh = MHc[c_]
                t1, rt1 = T1c[c_]
                y1, ry1 = Y1c[c_]
                yn, ryn = YNc[c_]
                dec = DECT[:, b, hs]
                rdec = rDECT[b][hf]
                yo = (4, 5) if c_ == 0 else (6, 7)
                stc = STATC + 32 * b + 8 * c_
                rstc = rSTATc[b][c_]
                p = scratch()
                for nn in range(2):
                    hh = slice(hf * 16 + nn * 8, hf * 16 + nn * 8 + 8)
                    mm_run(PS[0:tb, p[nn], :],
                           [(bones[0:tb, 0:tb], RHI[0][0:tb, hh, :]), (bones[0:tb, 0:tb], RLO[0][0:tb, hh, :]),
                            (ident[0:tb, 0:tb], negm[0:tb, nn * 512:(nn + 1) * 512])],
                           reads=[RHI[1], RLO[1], rCB], writes=[rB[p[nn]]])
                yield
                kb.op("dve", lambda e: e.tensor_tensor(
                    out=de[0:tb, :, :], in0=PS[0:tb, p[0]:p[1] + 1, :].rearrange("p a (h l) -> p (a h) l", l=64),
                    in1=sm[0:tb, C_ACUM + hs.start:C_ACUM + hs.stop].unsqueeze(2).broadcast_to([tb, 16, 64]),
                    op=ALU.subtract), reads=[rB[p[0]], rB[p[1]], rsm], writes=[rde])
                pc = scratch()
                for g in range(4):
                    gg = hf * 4 + g
                    for ch in range(nch):
                        r0 = ch * 64
                        cs = slice(tok0 + r0, tok0 + r0 + 64)
                        mm_run(PS[r0:r0 + 64, pc[0], g * 64:(g + 1) * 64], [(BF_[0][:, gg, cs], CF[0][:, gg, cs])],
                               reads=[BF_[1], CF[1]], writes=[rB[pc[0]]])
                yield
                kb.op("act", lambda e: e.activation(out=de[0:tb, :, :], in_=de[0:tb, :, :], func=AF.Exp),
                      reads=[rde], writes=[rde])
                yield
                kb.op("pool", lambda e: e.tensor_copy(out=dec[0:tb, :].unsqueeze(2), in_=de[0:tb, :, 63:64]),
                      reads=[rde], writes=[rdec])
                kb.op("dve", lambda e: e.tensor_tensor(
                    out=mh[0:tb, :, :].rearrange("p (g j) l -> p g j l", g=4),
                    in0=de[0:tb, :, :].rearrange("p (g j) l -> p g j l", g=4),
                    in1=PS[0:tb, pc[0], 0:256].rearrange("p (g l) -> p g l", g=4).unsqueeze(2).broadcast_to([tb, 4, 4, 64]),
                    op=ALU.mult), reads=[rde, rB[pc[0]]], writes=[rmh])
                yield
                kb.op("dve", lambda e: e.tensor_tensor(
                    out=xcd[0:tb, :].rearrange("p (h d) -> p h d", h=16),
                    in0=xc[0:tb, hf * 1024:(hf + 1) * 1024].rearrange("p (h d) -> p h d", h=16),
                    in1=dec[0:tb, :].unsqueeze(2).broadcast_to([tb, 16, 64]),
                    op=ALU.mult), reads=[rxc, rdec], writes=[rxcd])
                yield
                pd = scratch()
                for ch in range(nch):
                    r0 = ch * 64
                    for hl in range(16):
                        h = hf * 16 + hl
                        mm_run(PS[r0:r0 + 64, pd[hl // 8], (hl % 8) * 64:(hl % 8 + 1) * 64],
                               [(mh[r0:r0 + 64, hl, :], xc[r0:r0 + 64, h * 64:(h + 1) * 64])],
                               reads=[rmh, rxc], writes=[rB[pd[0]], rB[pd[1]]])
                yield
                yds = de[0:tb, :, :].rearrange("p h l -> p (h l)")
                kb.op("act", lambda e: e.copy(out=yds, in_=PS[0:tb, pd[0]:pd[1] + 1, :].rearrange("p a n -> p (a n)")),
                      reads=[rB[pd[0]], rB[pd[1]]], writes=[rde])
                yield
                for ch in range(nch):
                    r0 = ch * 64
                    cs = slice(tok0 + r0, tok0 + r0 + 64)
                    sb_in, rsb_in = SBF[ch % 2], rSBFh[ch % 2][hf]
                    if nch == 2:
                        sb_out, rsb_out = SBF[(ch + 1) % 2], rSBFh[(ch + 1) % 2][hf]
                    else:
                        sb_out, rsb_out = SBF[0], rSBFh[0][hf]
                    for g in range(4):
                        gg = hf * 4 + g
                        mm_run(PS[r0:r0 + 64, yo[g // 2], (g % 2) * 256:(g % 2 + 1) * 256],
                               [(CF[0][:, gg, cs], sb_in[:, gg * 256:(gg + 1) * 256])],
                               reads=[CF[1], rsb_in], writes=[rB[yo[0]], rB[yo[1]]])
                    ps_ = scratch()
                    for g in range(4):
                        gg = hf * 4 + g
                        mm_run(PS[:, ps_[g // 2], (g % 2) * 256:(g % 2 + 1) * 256],
                               [(btt[r0:r0 + 64, gg * 128:(gg + 1) * 128], xcd[r0:r0 + 64, g * 256:(g + 1) * 256])],
                               reads=[rbtt, rxcd], writes=[rB[ps_[0]], rB[ps_[1]]])
                    yield
                    stv = ST[:, hf * 1024:(hf + 1) * 1024]
                    kb.op("dve", lambda e, ch=ch: e.tensor_tensor(
                        out=t1[:, :].rearrange("p (h d) -> p h d", h=16), in0=stv.rearrange("p (h d) -> p h d", h=16),
                        in1=sm[:, C_CD + ch * 32 + hs.start:C_CD + ch * 32 + hs.stop].unsqueeze(2).broadcast_to([128, 16, 64]),
                        op=ALU.mult), reads=[rSTh[hf], rsm], writes=[rt1])
                    yield
                    kb.op("dve", lambda e, ps_=ps_: e.tensor_tensor(
                        out=stv, in0=PS[:, ps_[0]:ps_[1] + 1, :].rearrange("p a n -> p (a n)"), in1=t1[:, :], op=ALU.add),
                        reads=[rB[ps_[0]], rB[ps_[1]], rt1], writes=[rSTh[hf]])
                    yield
                    kb.op("act", lambda e, sb_out=sb_out: e.copy(out=sb_out[:, hf * 1024:(hf + 1) * 1024], in_=stv),
                          reads=[rSTh[hf]], writes=[rsb_out])
                    yield
                kb.op("dve", lambda e: e.tensor_tensor(
                    out=y1[0:tb, :].rearrange("p (h d) -> p h d", h=16),
                    in0=PS[0:tb, yo[0]:yo[1] + 1, :].rearrange("p a (h d) -> p (a h) d", d=64),
                    in1=sm[0:tb, C_EA + hs.start:C_EA + hs.stop].unsqueeze(2).broadcast_to([tb, 16, 64]),
                    op=ALU.mult), reads=[rB[yo[0]], rB[yo[1]], rsm], writes=[ry1])
                yield
                kb.op("dve", lambda e: e.tensor_tensor(out=y1[0:tb, :], in0=y1[0:tb, :], in1=yds, op=ALU.add),
                      reads=[ry1, rde], writes=[ry1])
                yield
                kb.op("dve", lambda e: e.tensor_tensor(
                    out=t1[0:tb, :].rearrange("p (h d) -> p h d", h=16),
                    in0=xst[0:tb, hf * 1024:(hf + 1) * 1024].rearrange("p (h d) -> p h d", h=16),
                    in1=PF[0:tb, PF_DSK + hs.start:PF_DSK + hs.stop].unsqueeze(2).broadcast_to([tb, 16, 64]),
                    op=ALU.mult), reads=[rxst, rPF], writes=[rt1])
                yield
                kb.op("dve", lambda e: e.tensor_tensor(out=t1[0:tb, :], in0=t1[0:tb, :], in1=y1[0:tb, :], op=ALU.add),
                      reads=[rt1, ry1], writes=[rt1])
                yield
                kb.op("dve", lambda e: e.tensor_tensor(
                    out=y1[0:tb, :], in0=t1[0:tb, :], in1=SZ[0][0:tb, b, hf * 1024:(hf + 1) * 1024], op=ALU.mult),
                    reads=[rt1, SZ[1]], writes=[ry1])
                yield
                if hf == 0 and b == 0:
                    dump("yg0", y1[0:tb, :], ry1, [tb, 1024], F32)
                    dump("yds0", yds, rde, [tb, 1024], F32)
                for g in range(4):
                    kb.op("act", lambda e, g=g: e.activation(
                        out=t1[0:tb, g * 256:(g + 1) * 256], in_=y1[0:tb, g * 256:(g + 1) * 256],
                        func=AF.Square, accum_out=STAT[0:tb, stc + g:stc + g + 1]),
                        reads=[ry1], writes=[rt1, rstc])
                yield
                yield from g_rstd(stc, rstc, 0, 4, 256, tb, w=4)
                for g in range(4):
                    kb.op("act", lambda e, g=g: e.activation(
                        out=yn[0:tb, g * 256:(g + 1) * 256], in_=y1[0:tb, g * 256:(g + 1) * 256],
                        func=AF.Identity, scale=STAT[0:tb, stc + 4 + g:stc + 5 + g]),
                        reads=[ry1, rstc], writes=[ryn])
                yield
                pt = scratch()
                tr_run([(PSb[:, pt[0], c * 128:c * 128 + tb], yn[0:tb, c * 128:(c + 1) * 128]) for c in range(8)],
                       reads=[ryn, rCB], writes=[rB[pt[0]]], idn=ident[0:tb, 0:tb])
                yield
                kb.op("act", lambda e: e.copy(
                    out=YNF[0][:, hf * 8:(hf + 1) * 8, b * 128:b * 128 + tb],
                    in_=PSb[:, pt[0], :].rearrange("p (k t) -> p k t", k=8)[:, :, 0:tb]), reads=[rB[pt[0]]], writes=[YNF[1]])
                yield

            for _ in g_prep(0):
                pass
            for b in range(nb):
                gens = []
                if b + 1 < nb:
                    gens.append((g_prep(b + 1), 0))
                gens += [(g_chain(b, 0), 0), (g_chain(b, 1), 0)]
                if os.environ.get('SEQ'):
                    for g_, _o in gens:
                        for _ in g_:
                            pass
                else:
                    round_robin_off(gens)
            if last:
                kb.dma("sp", ost[sidx], ST[:], rSTh[0], reads=[rSTh[0], rSTh[1]])
            dump("xn", XN[:, :, 0:T], rXN, [128, 8, T], BF16)
            dump("ynf", YNF[0][:, :, 0:T], YNF[1], [128, 16, T], BF16)

            gate(4)
            def fm_group(pan, rpan, jj, act, ract):
                bk = acc_bank()
                mm_run(PS[:, bk, 0:T], [(pan[:, kc, jj * 128:(jj + 1) * 128], act[:, kc, 0:T]) for kc in range(8)],
                       reads=[ract, rpan], writes=[rB[bk]])
                return bk

            for hf in range(2):
                pan, rpan = next_panel()
                for jj in range(4):
                    bk = fm_group(pan, rpan, jj, XN, rXN)
                    kb.op("act", lambda e, bk=bk, jj=jj: e.copy(out=ACJ[jj][0][:, 0:T], in_=PS[:, bk, 0:T]),
                          reads=[rB[bk]], writes=[ACJ[jj][1]])
                pan, rpan = next_panel()
                for jj in range(4):
                    j = hf * 4 + jj
                    kb.op("pool", lambda e, jj=jj, j=j: e.tensor_copy(out=UJ[jj][0][:, 0:2], in_=HISTA[:, j, :]),
                          reads=[rHISTA], writes=[UJ[jj][1]])
                    bk = fm_group(pan, rpan, jj, XN, rXN)
                    kb.op("dve", lambda e, bk=bk, jj=jj: e.tensor_tensor(out=UJ[jj][0][:, 2:2 + T], in0=PS[:, bk, 0:T],
                                                                        in1=ACJ[jj][0][:, 0:T], op=ALU.mult),
                          reads=[rB[bk], ACJ[jj][1]], writes=[UJ[jj][1]])
                    kb.op("pool", lambda e, jj=jj, j=j: e.tensor_copy(out=HISTA[:, j, :], in_=UJ[jj][0][:, T:T + 2]),
                          reads=[UJ[jj][1]], writes=[rHISTA])
                for jj in range(4):
                    j = hf * 4 + jj
                    kb.op("act", lambda e, jj=jj, j=j: e.activation(
                        out=TTJ[jj][0][:, 0:T], in_=UJ[jj][0][:, 0:T], func=AF.Identity,
                        scale=PF[:, PF_CAW + j * 3:PF_CAW + j * 3 + 1]), reads=[UJ[jj][1], rPF], writes=[TTJ[jj][1]])
                pan, rpan = next_panel()
                bks = [fm_group(pan, rpan, jj, XN, rXN) for jj in range(4)]
                for i in (1, 2):
                    for jj in range(4):
                        j = hf * 4 + jj
                        kb.op("dve", lambda e, jj=jj, j=j, i=i: e.scalar_tensor_tensor(
                            out=TTJ[jj][0][:, 0:T], in0=UJ[jj][0][:, i:i + T],
                            scalar=PF[:, PF_CAW + j * 3 + i:PF_CAW + j * 3 + i + 1], in1=TTJ[jj][0][:, 0:T],
                            op0=ALU.mult, op1=ALU.add), reads=[UJ[jj][1], TTJ[jj][1], rPF], writes=[TTJ[jj][1]])
                for jj in range(4):
                    j = hf * 4 + jj
                    bk = bks[jj]
                    kb.op("dve", lambda e, bk=bk, jj=jj, j=j: e.tensor_tensor(out=V[0][:, j, 0:T], in0=PS[:, bk, 0:T],
                                                                             in1=TTJ[jj][0][:, 0:T], op=ALU.mult),
                          reads=[rB[bk], TTJ[jj][1]], writes=[V[1]])
            if last:
                kb.dma("sp", oca[sidx].rearrange("p (j i) -> p j i", i=2), HISTA[:], rHISTA, reads=[rHISTA])
            gate(5)
            dump("v", V[0][:, :, 0:T], V[1], [128, 8, T], BF16)
            for hf in range(2):
                pan, rpan = next_panel()
                for jj in range(4):
                    j = hf * 4 + jj
                    bk = fm_group(pan, rpan, jj, XN, rXN)
                    kb.op("act", lambda e, bk=bk, jj=jj, j=j: e.activation(
                        out=GJ[jj][0][:, 0:T], in_=PS[:, bk, 0:T], func=AF.Sigmoid, bias=PF[:, PF_BG + j:PF_BG + j + 1]),
                        reads=[rB[bk], rPF], writes=[GJ[jj][1]])
                pan, rpan = next_panel()
                for jj in range(4):
                    j = hf * 4 + jj
                    bk = fm_group(pan, rpan, jj, V[0], V[1])
                    kb.op("dve", lambda e, bk=bk, jj=jj, j=j: e.tensor_tensor(out=PA[0][:, j, 0:T], in0=PS[:, bk, 0:T],
                                                                             in1=GJ[jj][0][:, 0:T], op=ALU.mult),
                          reads=[rB[bk], GJ[jj][1]], writes=[PA[1]])
            dump("pa", PA[0][:, :, 0:T], PA[1], [128, 8, T], F32)
            for hf in range(2):
                pan, rpan = next_panel()
                for jj in range(4):
                    j = hf * 4 + jj
                    bk = fm_group(pan, rpan, jj, XN, rXN)
                    kb.op("act", lambda e, bk=bk, jj=jj, j=j: e.activation(
                        out=GJ[jj][0][:, 0:T], in_=PS[:, bk, 0:T], func=AF.Sigmoid,
                        bias=PF[:, PF_BG + 8 + j:PF_BG + 8 + j + 1]), reads=[rB[bk], rPF], writes=[GJ[jj][1]])
                pan0, rpan0 = next_panel()
                pan1, rpan1 = next_panel()
                for jj in range(4):
                    j = hf * 4 + jj
                    bk = acc_bank()
                    pairs = [(pan0[:, kc, jj * 128:(jj + 1) * 128], YNF[0][:, kc, 0:T]) for kc in range(8)] + \
                            [(pan1[:, kc, jj * 128:(jj + 1) * 128], YNF[0][:, 8 + kc, 0:T]) for kc in range(8)]
                    mm_run(PS[:, bk, 0:T], pairs, reads=[YNF[1], rpan0, rpan1], writes=[rB[bk]])
                    kb.op("dve", lambda e, bk=bk, jj=jj: e.tensor_tensor(out=GJ[jj][0][:, 0:T], in0=PS[:, bk, 0:T],
                                                                        in1=GJ[jj][0][:, 0:T], op=ALU.mult),
                          reads=[rB[bk], GJ[jj][1]], writes=[GJ[jj][1]])
                    kb.op("dve", lambda e, jj=jj, j=j: e.tensor_tensor(out=MG[0][:, j, 0:T], in0=GJ[jj][0][:, 0:T],
                                                                      in1=PA[0][:, j, 0:T], op=ALU.add),
                          reads=[GJ[jj][1], PA[1]], writes=[MG[1]])
            gate(6)
            dump("mg", MG[0][:, :, 0:T], MG[1], [128, 8, T], BF16)
            for hf in range(2):
                pan, rpan = next_panel()
                for b in range(nb):
                    bk = acc_bank()
                    mm_run(PS[0:tb, bk, :], [(MG[0][:, kc, b * 128:b * 128 + tb], pan[:, kc, :]) for kc in range(8)],
                           reads=[MG[1], rpan], writes=[rB[bk]])
                    kb.op("act", lambda e, bk=bk, b=b, hf=hf: e.copy(out=MO[b][0][0:tb, hf * 512:(hf + 1) * 512],
                                                                    in_=PS[0:tb, bk, :]), reads=[rB[bk]], writes=[MO[b][1]])
            def g_post(b, src, wcol, addsrc):
                (sap, sres) = src
                (aap, ares) = addsrc
                o = 16 * b
                rst = rSTATb[b]
                jk, rjk = JUNK4[b]
                kb.op("act", lambda e: e.activation(out=jk[0:tb, :], in_=sap[0:tb, :], func=AF.Square,
                                                    accum_out=STAT[0:tb, o:o + 1]), reads=[sres], writes=[rjk, rst])
                yield
                yield from g_rstd(o, rst, 0, 1, 1024, tb)
                kb.op("dve", lambda e: e.scalar_tensor_tensor(
                    out=sap[0:tb, :], in0=sap[0:tb, :], scalar=STAT[0:tb, o + 1:o + 2], in1=PF[0:tb, wcol:wcol + 1024],
                    op0=ALU.mult, op1=ALU.mult), reads=[sres, rst, rPF], writes=[sres])
                yield
                kb.op("dve", lambda e: e.tensor_tensor(out=sap[0:tb, :], in0=sap[0:tb, :], in1=aap[0:tb, :], op=ALU.add),
                      reads=[sres, ares], writes=[sres])
                yield

            def g_wo(b):
                xt, rxt = XT2[b]
                kb.dma("sp", xt[0:tb, :], xd[b * 128:b * 128 + tb, :], rxt, writes=[rxt])
                yield from g_post(b, MO[b], PF_NMPOST, XT2[b])
                yield from g_norm_to_fm(MO[b][0], MO[b][1], b, tb, HS16[b], XN, rXN)
            round_robin([g_wo(b) for b in range(nb)])
            gate(7)
            dump("h", MO[0][0][0:tb, :], MO[0][1], [tb, 1024], F32)
            for q in range(8):
                pan, rpan = next_panel()
                for jj in range(4):
                    c = q * 4 + jj
                    bk = acc_bank()
                    mm_run(PS[:, bk, 0:T], [(pan[:, kc, jj * 128:(jj + 1) * 128], XN[:, kc, 0:T]) for kc in range(8)],
                           reads=[rXN, rpan], writes=[rB[bk]])
                    rt, rrt = RT[c % 4]
                    kb.op("act", lambda e, bk=bk, rt=rt: e.activation(out=rt[:, 0:T], in_=PS[:, bk, 0:T], func=AF.Relu),
                          reads=[rB[bk]], writes=[rrt])
                    kb.op("dve", lambda e, rt=rt, c=c: e.tensor_tensor(out=FF[0][:, c, 0:T], in0=rt[:, 0:T], in1=rt[:, 0:T],
                                                                      op=ALU.mult), reads=[rrt], writes=[FF[1]])
            for hf in range(2):
                for kg in range(4):
                    pan, rpan = next_panel()
                    for b in range(nb):
                        mm_run(PS[0:tb, b, :],
                               [(FF[0][:, kg * 8 + kc, b * 128:b * 128 + tb], pan[:, kc, :]) for kc in range(8)],
                               reads=[FF[1], rpan], writes=[rB[b]], first=(kg == 0), last=(kg == 3))
                for b in range(nb):
                    kb.op("act", lambda e, b=b, hf=hf: e.copy(out=FO[b][0][0:tb, hf * 512:(hf + 1) * 512], in_=PS[0:tb, b, :]),
                          reads=[rB[b]], writes=[FO[b][1]])
            dump("ff", FF[0][:, :, 0:T], FF[1], [128, 32, T], BF16)
            dump("fo", FO[0][0][0:tb, :], FO[0][1], [tb, 1024], F32)
            def g_fin(b):
                yield from g_post(b, FO[b], PF_NFPOST, MO[b])
                kb.dma("sp", yd[b * 128:b * 128 + tb, :], FO[b][0][0:tb, :], FO[b][1], reads=[FO[b][1]])
            round_robin([g_fin(b) for b in range(nb)])
        for tl in tiles:
            try:
                run_tile(*tl)
            except _Stop:
                pass
        kb.finish()
        block = es.enter_context(nc.Block())
        kb.replay(block)
    return nc


def _host_tables(p):
    pf = np.zeros((128, PF_COLS), np.float32)
    pf[:, PF_CAW:PF_CAW + 24] = p["conv_a_w"][0].reshape(3, 8, 128).transpose(2, 1, 0).reshape(128, 24)
    pf[:, PF_CSW:PF_CSW + 128] = p["conv_ssm_w"][0].reshape(4, 32, 128).transpose(2, 1, 0).reshape(128, 128)
    pf[:, PF_CSB:PF_CSB + 32] = p["conv_ssm_b"][0].reshape(32, 128).T
    pf[:, PF_BG:PF_BG + 16] = p["b_gate"][0].reshape(16, 128).T
    pf[:, PF_NMP:PF_NMP + 8] = p["norm_mix_pre"][0].reshape(8, 128).T
    pf[:, PF_NFP:PF_NFP + 8] = p["norm_ffn_pre"][0].reshape(8, 128).T
    pf[:, PF_SSN:PF_SSN + 16] = p["ssm_norm"][0].reshape(16, 128).T
    pf[:, PF_DTB:PF_DTB + 32] = p["dt_bias"][0][None, :]
    pf[:, PF_ALOG:PF_ALOG + 32] = p["a_log"][0][None, :]
    pf[:, PF_DSK:PF_DSK + 32] = p["d_skip"][0][None, :]
    pf[:, PF_NMPOST:PF_NMPOST + 1024] = p["norm_mix_post"][0][None, :]
    pf[:, PF_NFPOST:PF_NFPOST + 1024] = p["norm_ffn_post"][0][None, :]
    cb = np.zeros((128, CB_COLS), np.float32)
    k = np.arange(128)
    cb[:, CB_ID:CB_ID + 128] = np.eye(128)
    l = np.arange(64)
    cb[:, CB_TRI2:CB_TRI2 + 64] = ((k[:, None] % 64) <= l[None, :])
    same = (k[:, None] // 64) == (k[None, :] // 64)
    cb[:, CB_BONES:CB_BONES + 128] = same
    cb[:, CB_ONESA:CB_ONESA + 128] = (k[:, None] < 64) * np.ones((1, 128))
    cb[:, CB_ONESB:CB_ONESB + 128] = (k[:, None] >= 64) * np.ones((1, 128))
    cb[:, CB_TRI128:CB_TRI128 + 128] = same & ((k[:, None] % 64) <= (k[None, :] % 64))
    neg = np.where((k[:, None] % 64) > l[None, :], -30000.0, 0.0)
    cb[:, CB_NEG:CB_NEG + 1024] = np.tile(neg, (1, 16))
    return pf, cb.astype(ml_dtypes.bfloat16)


_NC_CACHE = {}


def kernel(x_prompt, x_sample, state_conv_a, state_conv_ssm, state_ssm, norm_mix_pre, w_in, b_gate, conv_a_w, w_a_out,
           conv_ssm_w, conv_ssm_b, dt_bias, a_log, d_skip, ssm_norm, w_ssm_out, w_o, norm_mix_post, norm_ffn_pre,
           w_ff1, w_ff2, norm_ffn_post, _tiles_limit=None, _cores=None, _dbg=False, _stop=None):
    p = dict(norm_mix_pre=norm_mix_pre, b_gate=b_gate, conv_a_w=conv_a_w, conv_ssm_w=conv_ssm_w, conv_ssm_b=conv_ssm_b,
             dt_bias=dt_bias, a_log=a_log, d_skip=d_skip, ssm_norm=ssm_norm, norm_mix_post=norm_mix_post,
             norm_ffn_pre=norm_ffn_pre, norm_ffn_post=norm_ffn_post)
    p = {k: np.asarray(v, np.float32) for k, v in p.items()}
    pf, cb = _host_tables(p)
    x_prompt = np.asarray(x_prompt, np.float32)
    x_sample = np.asarray(x_sample, np.float32)
    sca_all = np.asarray(state_conv_a, np.float32)[0]
    scm_all = np.asarray(state_conv_ssm, np.float32)[0]
    sst_all = np.asarray(state_ssm, np.float32)[0]
    key = (None if _tiles_limit is None else tuple(_tiles_limit), _dbg, _stop)
    if key not in _NC_CACHE:
        _NC_CACHE[key] = build_program(_tiles_limit, _dbg, _stop)
    nc = _NC_CACHE[key]
    cores = list(range(NCORES)) if _cores is None else _cores
    shared = dict(pf=pf, cb=cb, w_in=np.ascontiguousarray(np.asarray(w_in, np.float32)[0]),
                  w_a_out=np.ascontiguousarray(np.asarray(w_a_out, np.float32)[0]),
                  w_ssm_out=np.ascontiguousarray(np.asarray(w_ssm_out, np.float32)[0]),
                  w_o=np.ascontiguousarray(np.asarray(w_o, np.float32)[0]),
                  w_ff1=np.ascontiguousarray(np.asarray(w_ff1, np.float32)[0]),
                  w_ff2=np.ascontiguousarray(np.asarray(w_ff2, np.float32)[0]))
    in_maps = []
    for c in cores:
        s2 = slice(2 * c, 2 * c + 2)
        m = dict(shared)
        m["xp"] = np.ascontiguousarray(x_prompt[s2])
        m["xs"] = np.ascontiguousarray(x_sample[s2])
        m["sca"] = np.ascontiguousarray(sca_all[s2].reshape(2, 2, 8, 128).transpose(0, 3, 2, 1).reshape(2, 128, 16))
        m["scm"] = np.ascontiguousarray(scm_all[s2].reshape(2, 3, 32, 128).transpose(0, 3, 2, 1).reshape(2, 128, 96))
        m["sst"] = np.ascontiguousarray(sst_all[s2].reshape(2, 2048, 128).transpose(0, 2, 1))
        in_maps.append(m)
    res = run_bass_kernel_spmd(nc, in_maps, core_ids=list(range(len(cores))))
    R = res.results
    if _dbg:
        kernel.dbg = {k: v for k, v in R[0].items() if k.startswith("dbg_")}
    yp = np.concatenate([r["yp"] for r in R], axis=0)
    ys_ = np.concatenate([r["ys"] for r in R], axis=0)
    oca = np.stack([r["oca"] for r in R])
    ocm = np.stack([r["ocm"] for r in R])
    ost = np.stack([r["ost"] for r in R])
    n = len(cores)

    def un_ca(a):
        return a.reshape(n * 2, 128, 8, 2).transpose(0, 3, 2, 1).reshape(n * 2, 2, 1024)

    def un_cm(a):
        return a.reshape(n * 2, 128, 32, 3).transpose(0, 3, 2, 1).reshape(n * 2, 3, 4096)

    def un_st(a):
        return a.reshape(n * 2, 128, 2048).transpose(0, 2, 1).reshape(n * 2, 32, 64, 128)

    outs = (yp, ys_,
            un_ca(oca[:, 0:2])[None], un_cm(ocm[:, 0:2])[None], un_st(ost[:, 0:2])[None],
            un_ca(oca[:, 2:4])[None], un_cm(ocm[:, 2:4])[None], un_st(ost[:, 2:4])[None])
    return tuple(np.ascontiguousarray(o, dtype=np.float32) for o in outs)
```

```python
import numpy as np
import ml_dtypes
from contextlib import ExitStack
import concourse.bass as bass
import concourse.mybir as mybir
from concourse.bass_utils import run_bass_kernel_spmd

F32 = mybir.dt.float32
BF16 = mybir.dt.bfloat16
AF = mybir.ActivationFunctionType
ALU = mybir.AluOpType
AX = mybir.AxisListType

NCORES = 8
D = 1024
SEQ = 2048
DSEQ = 64
PROJ = 11296
OFF_AB, OFF_AC, OFF_AH, OFF_Z, OFF_XBC, OFF_DT, OFF_GATE = 0, 1024, 2048, 3072, 5120, 9216, 9248
EPS = 1e-6
NS = 4

PF_CAW = 0
PF_CSW = 24
PF_CSB = 152
PF_BG = 184
PF_NMP = 200
PF_NFP = 208
PF_SSN = 216
PF_DTB = 232
PF_ALOG = 264
PF_DSK = 296
PF_NMPOST = 328
PF_NFPOST = 1352
PF_COLS = 2376
CB_ID = 0
CB_TRI2 = 128
CB_BONES = 192
CB_ONESA = 320
CB_ONESB = 448
CB_TRI128 = 576
CB_NEG = 704
CB_COLS = 1728


class Res:
    __slots__ = ("name", "lw", "rd", "ov", "dsem", "dcnt", "rng", "excl", "disj")

    def __init__(self, name):
        self.name = name
        self.lw = None
        self.rd = {}
        self.ov = []
        self.dsem = None
        self.dcnt = 0
        self.rng = None
        self.disj = False
        self.excl = False


class KB:
    ENG = ("pe", "act", "dve", "pool")

    def __init__(self, nc, es):
        self.nc = nc
        self.es = es
        self.streams = {k: [] for k in ("pe", "act", "dve", "pool", "sp")}
        self.sem = {e: es.enter_context(nc.semaphore("s_" + e)) for e in self.ENG}
        self.tick = {e: 0 for e in self.ENG}
        self.waited = {k: {} for k in self.streams}
        self.dsems = []
        self.all_res = []

    def res(self, name):
        r = Res(name)
        self.all_res.append(r)
        return r

    def _dsem(self, r):
        if r.dsem is None:
            r.dsem = self.es.enter_context(self.nc.semaphore("d%d" % len(self.dsems)))
            self.dsems.append(r)
        return r.dsem

    def _deps(self, eng, reads, writes):
        deps = []
        for r in reads:
            if r.lw is not None:
                deps.append(r.lw)
            if r.excl:
                for q, t in r.rd.items():
                    if q != eng:
                        deps.append((q, t))
        same_ok = (eng == "pe")
        for w in writes:
            for x in [w] + w.ov:
                if x.lw is not None and (x.lw[0] != eng or not (same_ok or (x is w and w.disj))):
                    deps.append(x.lw)
                for q, t in x.rd.items():
                    if q != eng or not same_ok:
                        deps.append((q, t))
        wd = self.waited[eng]
        for q, t in deps:
            if wd.get(q, 0) < t:
                wd[q] = t
                self.streams[eng].append(("w", q, t))

    def wait(self, eng, q, t):
        wd = self.waited[eng]
        if wd.get(q, 0) < t:
            wd[q] = t
            self.streams[eng].append(("w", q, t))

    def op(self, eng, fn, reads=(), writes=(), inc=True):
        self._deps(eng, reads, writes)
        t = self.tick[eng] + 1
        if inc:
            self.tick[eng] = t
        self.streams[eng].append(("op", fn, inc))
        for r in reads:
            r.rd[eng] = t
        for w in writes:
            w.lw = (eng, t)
            w.rd = {}
        return t

    def dma(self, eng, out, in_, dres, reads=(), writes=()):
        self._deps(eng, reads, writes)
        self._dsem(dres)
        dres.dcnt += 16
        key = (id(dres), dres.dcnt)
        q = ("d", id(dres))
        self._dq = getattr(self, "_dq", {})
        self._dq[q] = dres
        self.streams[eng].append(("dma", out, in_, dres))
        for r in reads:
            r.rd[q] = dres.dcnt
        for w in writes:
            w.lw = (q, dres.dcnt)
            w.rd = {}

    def semof(self, q):
        if isinstance(q, tuple):
            return self._dq[q].dsem
        return self.sem[q]

    def finish(self):
        for r in self.dsems:
            self.wait("sp", ("d", id(r)), r.dcnt)

    def replay(self, block):
        handles = {"pe": block.tensor, "act": block.scalar, "dve": block.vector, "pool": block.gpsimd,
                   "sp": block.sync}
        for eng, deco in handles.items():
            stream = self.streams[eng]
            own = self.sem.get(eng)

            def body(e, stream=stream, own=own):
                for ent in stream:
                    k = ent[0]
                    if k == "w":
                        e.wait_ge(self.semof(ent[1]), ent[2])
                    elif k == "op":
                        ins = ent[1](e)
                        if ent[2]:
                            ins.then_inc(own, 1)
                    else:
                        e.dma_start(out=ent[1], in_=ent[2]).then_inc(ent[3].dsem, 16)
            deco(body)


class Arena:
    def __init__(self, kb, tensor, nbytes):
        self.kb = kb
        self.t = tensor
        self.nbytes = nbytes
        self.views = []

    def view(self, name, off, shape, dt):
        esz = 2 if dt == BF16 else 4
        n = int(np.prod(shape))
        nb = n * esz
        assert off % 4 == 0 and off + nb <= self.nbytes, (name, off, nb, self.nbytes)
        w0, w1 = off // 4, (off + nb + 3) // 4
        ap = self.t[:, w0:w1]
        if dt == BF16:
            ap = ap.bitcast(BF16)[:, 0:n]
        if len(shape) == 2:
            ap = ap.rearrange("p (a b) -> p a b", a=shape[0])
        elif len(shape) == 3:
            ap = ap.rearrange("p (a b c) -> p a b c", a=shape[0], b=shape[1])
        r = self.kb.res(name)
        r.rng = (off, off + nb)
        for (o, rr) in self.views:
            if o.rng[0] < r.rng[1] and r.rng[0] < o.rng[1]:
                r.ov.append(o)
                o.ov.append(r)
        self.views.append((r, ap))
        return ap, r


import os
VARP = int(os.environ.get('VARP', '128'))
CH_OFF = int(os.environ.get('CH_OFF', '0'))


class _Stop(Exception):
    pass


def build_program(tiles_limit=None, dbg=False, stop_after=None):
    nc = bass.Bass("TRN2", target_bir_lowering=False)
    dram_in = lambda name, shape, dt=F32: nc.dram_tensor(name, shape, dt, kind="ExternalInput").ap()
    dram_out = lambda name, shape, dt=F32: nc.dram_tensor(name, shape, dt, kind="ExternalOutput").ap()
    xp = dram_in("xp", [2, SEQ, D])
    xs = dram_in("xs", [2, DSEQ, D])
    sca = dram_in("sca", [2, 128, 16])
    scm = dram_in("scm", [2, 128, 96])
    sst = dram_in("sst", [2, 128, 2048])
    pf_d = dram_in("pf", [128, PF_COLS])
    cb_d = dram_in("cb", [128, CB_COLS], BF16)
    w_in = dram_in("w_in", [D, PROJ])
    w_a = dram_in("w_a_out", [D, D])
    w_s = dram_in("w_ssm_out", [2 * D, D])
    w_o = dram_in("w_o", [D, D])
    w_f1 = dram_in("w_ff1", [D, 4 * D])
    w_f2 = dram_in("w_ff2", [4 * D, D])
    yp = dram_out("yp", [2, SEQ, D])
    ys = dram_out("ys", [2, DSEQ, D])
    oca = dram_out("oca", [4, 128, 16])
    ocm = dram_out("ocm", [4, 128, 96])
    ost = dram_out("ost", [4, 128, 2048])
    wi_b = nc.dram_tensor("wi_b", [D, PROJ], BF16).ap()
    wa_b = nc.dram_tensor("wa_b", [D, D], BF16).ap()
    ws_b = nc.dram_tensor("ws_b", [2 * D, D], BF16).ap()
    wo_b = nc.dram_tensor("wo_b", [D, D], BF16).ap()
    wf1_b = nc.dram_tensor("wf1_b", [D, 4 * D], BF16).ap()
    wf2_b = nc.dram_tensor("wf2_b", [4 * D, D], BF16).ap()

    with ExitStack() as es:
        kb = KB(nc, es)
        sbt = lambda name, shape, dt: es.enter_context(nc.sbuf_tensor(name, shape, dt))
        PF = sbt("PF", [128, PF_COLS], F32); rPF = kb.res("PF")
        CB = sbt("CB", [128, CB_COLS], BF16); rCB = kb.res("CB")
        PAN = [sbt("PAN%d" % i, [128, 8, 512], BF16) for i in range(NS)]
        rPAN = [kb.res("PAN%d" % i) for i in range(NS)]
        XN = sbt("XN", [128, 8, 512], BF16); rXN = kb.res("XN")
        DTW = sbt("DTW", [128, 8, 32], BF16); rDTW = kb.res("DTW")
        HISTA = sbt("HISTA", [128, 8, 2], F32); rHISTA = kb.res("HISTA")
        HISTM = sbt("HISTM", [128, 32, 3], F32); rHISTM = kb.res("HISTM")
        ST = sbt("ST", [128, 2048], F32)
        rSTh = [kb.res("ST_h%d" % i) for i in range(2)]
        SBF = [sbt("SBF%d" % i, [128, 2048], BF16) for i in range(2)]
        rSBFh = [[kb.res("SBF%d_h%d" % (i, hf)) for hf in range(2)] for i in range(2)]
        DECT = sbt("DECT", [128, 4, 32], F32)
        rDECT = [[kb.res("DEC%d_%d" % (b, hf)) for hf in range(2)] for b in range(4)]
        STATC = 64
        rSTATc = [[kb.res("STATC%d_%d" % (b, c_)) for c_ in range(2)] for b in range(4)]
        ANEG = sbt("ANEG", [128, 32], F32); rANEG = kb.res("ANEG")
        SM = sbt("SM", [128, 4, 320], F32)
        rSM = [kb.res("SM%d" % b) for b in range(4)]
        STAT = sbt("STAT", [128, 256], F32)
        rSTATb = [kb.res("STAT%d" % b) for b in range(4)]
        EPSC = sbt("EPSC", [128, 2], F32); rEPSC = kb.res("EPSC")
        C_DT, C_ADT, C_ACUM, C_EA, C_CD, C_DEC, C_T0, C_RS = 0, 32, 64, 96, 128, 192, 224, 256
        SMH = sbt("SMH", [128, 4, 64], BF16)
        rSMH = [kb.res("SMH%d" % b) for b in range(4)]
        KBY = 1024
        AR_BYTES = 128 * KBY
        ARt = sbt("ARENA", [128, AR_BYTES // 4], F32)
        ar = Arena(kb, ARt, AR_BYTES)
        XT = [ar.view("XT%d" % i, i * 4 * KBY, [1024], F32) for i in range(4)]
        XS16 = [ar.view("XS16_%d" % i, 16 * KBY + i * 2 * KBY, [1024], BF16) for i in range(4)]
        JUNK4 = [ar.view("JUNK%d" % i, 24 * KBY + i * 2 * KBY, [1024], BF16) for i in range(4)]
        SZ = ar.view("SZ", 0, [4, 2048], BF16)
        XSF = ar.view("XSF", 16 * KBY, [16, 512], BF16)
        BF_ = ar.view("BF", 32 * KBY, [8, 512], BF16)
        CF = ar.view("CF", 40 * KBY, [8, 512], BF16)
        XBJ = [[ar.view("XB%d_%d" % (p_, jj), 48 * KBY + (p_ * 4 + jj) * 2064, [516], F32) for jj in range(4)]
               for p_ in range(3)]
        CVJ = [[ar.view("CV%d_%d" % (p_, jj), 73 * KBY + (p_ * 4 + jj) * 2048, [512], F32) for jj in range(4)]
               for p_ in range(2)]
        XStP = [ar.view("XSt%d" % i, 48 * KBY + i * 4 * KBY, [2048], BF16) for i in range(2)]
        BtP = [ar.view("Bt%d" % i, 56 * KBY + i * 2 * KBY, [1024], BF16) for i in range(2)]
        XCDh = [ar.view("XCD_%d" % hf, 60 * KBY + hf * 2 * KBY, [1024], BF16) for hf in range(2)]
        XCP = [ar.view("XC%d" % i, 64 * KBY + i * 4 * KBY, [2048], BF16) for i in range(2)]
        RHI = ar.view("RHI", 72 * KBY, [32, 64], BF16)
        RLO = ar.view("RLO", 76 * KBY, [32, 64], BF16)
        DEc = [ar.view("DE%d" % i, 80 * KBY + i * 4 * KBY, [16, 64], F32) for i in range(2)]
        MHc = [ar.view("MH%d" % i, 88 * KBY + i * 2 * KBY, [16, 64], BF16) for i in range(2)]
        T1c = [ar.view("T1_%d" % i, 92 * KBY + i * 4 * KBY, [1024], F32) for i in range(2)]
        Y1c = [ar.view("Y1_%d" % i, 100 * KBY + i * 4 * KBY, [1024], F32) for i in range(2)]
        YNc = [ar.view("YN_%d" % i, 108 * KBY + i * 2 * KBY, [1024], BF16) for i in range(2)]
        YNF = ar.view("YNF", 112 * KBY, [16, 512], BF16)
        ACJ = [ar.view("AC_%d" % jj, jj * 2048, [512], F32) for jj in range(4)]
        UJ = [ar.view("U_%d" % jj, 8 * KBY + jj * 2064, [516], F32) for jj in range(4)]
        TTJ = [ar.view("TT_%d" % jj, 17 * KBY + jj * 2048, [512], F32) for jj in range(4)]
        V = ar.view("V", 25 * KBY, [8, 512], BF16)
        GJ = [ar.view("G_%d" % jj, 33 * KBY + jj * 2048, [512], F32) for jj in range(4)]
        PA = ar.view("PA", 41 * KBY, [8, 512], F32)
        MG = ar.view("MG", 57 * KBY, [8, 512], BF16)
        MO = [ar.view("MO%d" % b, 65 * KBY + b * 4 * KBY, [1024], F32) for b in range(4)]
        XT2 = [ar.view("XT2_%d" % i, 81 * KBY + i * 4 * KBY, [1024], F32) for i in range(4)]
        HS16 = [ar.view("HS16_%d" % i, 97 * KBY + i * 2 * KBY, [1024], BF16) for i in range(4)]
        FF = ar.view("FF", 0, [32, 512], BF16)
        RT = [ar.view("RT%d" % i, 32 * KBY + i * 2 * KBY, [512], F32) for i in range(4)]
        FO = [ar.view("FO%d" % b, 40 * KBY + b * 4 * KBY, [1024], F32) for b in range(4)]
        SGW = 2824
        NSG = 4
        SGI = [ar.view("SGI%d" % i, i * 12 * KBY, [SGW], F32) for i in range(NSG)]
        SGO = [ar.view("SGO%d" % i, 48 * KBY + i * 6 * KBY, [SGW], BF16) for i in range(NSG)]
        PS = es.enter_context(nc.psum_tensor("PS", [128, 8, 512], F32))
        rB = [kb.res("bank%d" % i) for i in range(8)]
        for r_ in rB:
            r_.excl = True
        PSb = PS[:].bitcast(BF16)

        for r_ in [XSF[1], BF_[1], CF[1], V[1], PA[1], MG[1], FF[1], YNF[1], SZ[1], rXN] + [m_[1] for m_ in MO] + \
                [f_[1] for f_ in FO]:
            r_.disj = True
        ident = CB[:, CB_ID:CB_ID + 128]
        tri2 = CB[:, CB_TRI2:CB_TRI2 + 64]
        bones = CB[:, CB_BONES:CB_BONES + 128]
        onesA = CB[:, CB_ONESA:CB_ONESA + 128]
        onesB = CB[:, CB_ONESB:CB_ONESB + 128]
        tri128 = CB[:, CB_TRI128:CB_TRI128 + 128]
        negm = CB[:, CB_NEG:CB_NEG + 1024]

        dbg_names = []

        def dump(name, ap, res, shape, dt):
            if not dbg:
                return
            d = nc.dram_tensor("dbg_" + name, shape, dt, kind="ExternalOutput").ap()
            kb.dma("sp", d, ap, res, reads=[res])
            dbg_names.append("dbg_" + name)

        kb.dma("sp", PF[:], pf_d, rPF, writes=[rPF])
        kb.dma("sp", CB[:], cb_d, rCB, writes=[rCB])
        kb.op("pool", lambda e: e.memset(EPSC[:, 0:1], EPS), writes=[rEPSC])
        kb.op("pool", lambda e: e.memset(EPSC[:, 1:2], 1.0), writes=[rEPSC])
        kb.op("act", lambda e: e.activation(out=ANEG[:], in_=PF[:, PF_ALOG:PF_ALOG + 32], func=AF.Exp),
              reads=[rPF], writes=[rANEG])
        kb.op("dve", lambda e: e.tensor_scalar(out=ANEG[:], in0=ANEG[:], scalar1=-1.0, scalar2=None, op0=ALU.mult),
              reads=[rANEG], writes=[rANEG])

        conv_jobs = []
        def add_mat(src, dst, K, N, piece, scol):
            for rc in range(K // 128):
                for c0 in range(0, N, piece):
                    w = min(piece, N - c0)
                    conv_jobs.append((src[rc * 128:(rc + 1) * 128, c0:c0 + w], dst[rc * 128:(rc + 1) * 128, c0:c0 + w],
                                      None if scol is None else scol + rc, w))
        add_mat(w_in, wi_b, D, PROJ, SGW, PF_NMP)
        add_mat(w_a, wa_b, D, D, 1024, None)
        add_mat(w_s, ws_b, 2 * D, D, 1024, PF_SSN)
        add_mat(w_o, wo_b, D, D, 1024, None)
        add_mat(w_f1, wf1_b, D, 4 * D, 2048, PF_NFP)
        add_mat(w_f2, wf2_b, 4 * D, D, 1024, None)
        def conv_store(i):
            (src, dst, scol, w) = conv_jobs[i]
            so, ro = SGO[i % NSG]
            kb.dma("sp", dst, so[:, 0:w], ro, reads=[ro])
        for i, (src, dst, scol, w) in enumerate(conv_jobs):
            si, ri = SGI[i % NSG]
            so, ro = SGO[i % NSG]
            if i >= NSG - 1:
                conv_store(i - (NSG - 1))
            kb.dma("sp", si[:, 0:w], src, ri, writes=[ri])
            eng = "dve" if (i % 5) in (0, 2, 4) else "act"
            if eng == "dve":
                if scol is None:
                    kb.op(eng, lambda e, so=so, si=si, w=w: e.tensor_copy(out=so[:, 0:w], in_=si[:, 0:w]),
                          reads=[ri], writes=[ro])
                else:
                    kb.op(eng, lambda e, so=so, si=si, w=w, scol=scol: e.tensor_scalar(
                        out=so[:, 0:w], in0=si[:, 0:w], scalar1=PF[:, scol:scol + 1], scalar2=None, op0=ALU.mult),
                        reads=[ri, rPF], writes=[ro])
            else:
                if scol is None:
                    kb.op(eng, lambda e, so=so, si=si, w=w: e.copy(out=so[:, 0:w], in_=si[:, 0:w]),
                          reads=[ri], writes=[ro])
                else:
                    kb.op(eng, lambda e, so=so, si=si, w=w, scol=scol: e.activation(
                        out=so[:, 0:w], in_=si[:, 0:w], func=AF.Identity, scale=PF[:, scol:scol + 1]),
                        reads=[ri, rPF], writes=[ro])
        for i in range(max(0, len(conv_jobs) - (NSG - 1)), len(conv_jobs)):
            conv_store(i)
        for so, ro in SGO:
            kb.wait("sp", ("d", id(ro)), ro.dcnt)
        kb.dma("sp", DTW[:], wi_b[:, OFF_DT:OFF_DT + 32].rearrange("(kc p) n -> p kc n", p=128), rDTW, writes=[rDTW])

        tiles = []
        for s in range(2):
            for ti in range(4):
                tiles.append((s, xp[s, ti * 512:(ti + 1) * 512, :], yp[s, ti * 512:(ti + 1) * 512, :], 512,
                              ti == 0, ti == 3, None))
        for s in range(2):
            tiles.append((2 + s, xs[s], ys[s], 64, True, True, s))
        if tiles_limit is not None:
            tiles = [tiles[i] for i in tiles_limit]

        def panel_list(T):
            pl = []
            for q in range(8):
                pl.append((wi_b, 0, OFF_XBC + q * 512))
                if q % 2 == 1:
                    pl.append((wi_b, 0, OFF_Z + (q // 2) * 512))
            for hf in range(2):
                pl.append((wi_b, 0, OFF_AC + hf * 512))
                pl.append((wi_b, 0, OFF_AH + hf * 512))
                pl.append((wi_b, 0, OFF_AB + hf * 512))
            for hf in range(2):
                pl.append((wi_b, 0, OFF_GATE + hf * 512))
                pl.append((wa_b, 0, hf * 512))
            for hf in range(2):
                pl.append((wi_b, 0, OFF_GATE + 1024 + hf * 512))
                pl.append((ws_b, 0, hf * 512))
                pl.append((ws_b, 1, hf * 512))
            for hf in range(2):
                pl.append((wo_b, 0, hf * 512))
            for q in range(8):
                pl.append((wf1_b, 0, q * 512))
            for hf in range(2):
                for kg in range(4):
                    pl.append((wf2_b, kg, hf * 512))
            return pl
        all_panels = []
        for tl in tiles:
            all_panels += panel_list(tl[3])
        pstate = {"issued": 0, "used": 0}

        def issue_panels(upto):
            while pstate["issued"] < min(upto, len(all_panels)):
                i = pstate["issued"]
                mat, kg, c0 = all_panels[i]
                slot = i % NS
                src = mat[kg * 1024:(kg + 1) * 1024, c0:c0 + 512].rearrange("(kc p) n -> p kc n", p=128)
                kb.dma("sp", PAN[slot][:], src, rPAN[slot], writes=[rPAN[slot]])
                pstate["issued"] += 1

        def next_panel():
            i = pstate["used"]
            issue_panels(i + NS - 1)
            pstate["used"] += 1
            return PAN[i % NS], rPAN[i % NS]

        bank_rr = {"i": 0}

        def acc_bank():
            i = bank_rr["i"]
            bank_rr["i"] = (i + 1) % 4
            return i

        def mm_run(out, pairs, reads, writes, first=True, last=True):
            n = len(pairs)
            for i, (l, r) in enumerate(pairs):
                kb.op("pe", lambda e, l=l, r=r, st=(first and i == 0), sp_=(last and i == n - 1):
                      e.matmul(out, lhsT=l, rhs=r, start=st, stop=sp_),
                      reads=reads if i == 0 else (), writes=writes if i == 0 else (), inc=(i == n - 1))

        def tr_run(outs_ins, reads, writes, idn):
            n = len(outs_ins)
            for i, (o, a) in enumerate(outs_ins):
                kb.op("pe", lambda e, o=o, a=a: e.transpose(o, a, idn),
                      reads=reads if i == 0 else (), writes=writes if i == 0 else (), inc=(i == n - 1))

        def round_robin(gens):
            gens = list(gens)
            while gens:
                for g_ in list(gens):
                    try:
                        next(g_)
                    except StopIteration:
                        gens.remove(g_)

        def g_rstd(o, rst, ss_col, rs_col, n, tb, w=1):
            kb.op("act", lambda e: e.activation(out=STAT[0:tb, o + rs_col:o + rs_col + w], in_=STAT[0:tb, o + ss_col:o + ss_col + w],
                                                func=AF.Ln, scale=1.0 / n, bias=EPSC[0:tb, 0:1]),
                  reads=[rst, rEPSC], writes=[rst])
            yield
            kb.op("act", lambda e: e.activation(out=STAT[0:tb, o + rs_col:o + rs_col + w], in_=STAT[0:tb, o + rs_col:o + rs_col + w],
                                                func=AF.Exp, scale=-0.5),
                  reads=[rst], writes=[rst])
            yield

        def round_robin_off(gens):
            live = list(gens)
            step = 0
            while live:
                for item in list(live):
                    if item[1] <= step:
                        try:
                            next(item[0])
                        except StopIteration:
                            live.remove(item)
                step += 1

        def g_norm_to_fm(src_ap, src_res, b, tb, s16, dst, dst_res):
            (s16ap, s16res) = s16
            o = 16 * b
            rst = rSTATb[b]
            jk, rjk = JUNK4[b]
            kb.op("act", lambda e: e.activation(out=jk[0:tb, :], in_=src_ap[0:tb, :], func=AF.Square,
                                                accum_out=STAT[0:tb, o:o + 1]),
                  reads=[src_res], writes=[rjk, rst])
            yield
            yield from g_rstd(o, rst, 0, 1, 1024, tb)
            kb.op("dve", lambda e: e.tensor_scalar(out=s16ap[0:tb, :], in0=src_ap[0:tb, :], scalar1=STAT[0:tb, o + 1:o + 2],
                                                   scalar2=None, op0=ALU.mult),
                  reads=[src_res, rst], writes=[s16res])
            yield
            tpb = 4 + b
            tr_run([(PSb[:, tpb, kc * 128:kc * 128 + tb], s16ap[0:tb, kc * 128:(kc + 1) * 128]) for kc in range(8)],
                   reads=[s16res, rCB], writes=[rB[tpb]], idn=ident[0:tb, 0:tb])
            yield
            kb.op("act", lambda e: e.copy(out=dst[:, :, b * 128:b * 128 + tb],
                                          in_=PSb[:, tpb, :].rearrange("p (k t) -> p k t", k=8)[:, :, 0:tb]),
                  reads=[rB[tpb]], writes=[dst_res])
            yield

        def gate(n):
            if stop_after is not None and n > stop_after:
                raise _Stop()

        def run_tile(sidx, xd, yd, T, first, last, samp):
            tb = min(T, 128)
            nb = T // tb
            nch = tb // 64
            if first:
                if samp is None:
                    kb.op("pool", lambda e: e.memset(HISTA[:], 0.0), writes=[rHISTA])
                    kb.op("pool", lambda e: e.memset(HISTM[:], 0.0), writes=[rHISTM])
                    kb.op("pool", lambda e: e.memset(ST[:], 0.0), writes=[rSTh[0], rSTh[1]])
                else:
                    kb.dma("sp", HISTA[:], sca[samp].rearrange("p (j i) -> p j i", i=2), rHISTA, writes=[rHISTA])
                    kb.dma("sp", HISTM[:], scm[samp].rearrange("p (c i) -> p c i", i=3), rHISTM, writes=[rHISTM])
                    kb.dma("sp", ST[:], sst[samp], rSTh[0], writes=[rSTh[0], rSTh[1]])
                kb.op("act", lambda e: e.copy(out=SBF[0][:], in_=ST[:]), reads=[rSTh[0], rSTh[1]],
                      writes=[rSBFh[0][0], rSBFh[0][1]])

            def g_p0(b):
                xt, rxt = XT[b]
                kb.dma("sp", xt[0:tb, :], xd[b * 128:b * 128 + tb, :], rxt, writes=[rxt])
                yield from g_norm_to_fm(xt, rxt, b, tb, XS16[b], XN, rXN)
            round_robin([g_p0(b) for b in range(nb)])

            gate(1)
            def b1_panel(q):
                pan, rpan = next_panel()
                for b in range(nb):
                    bk = acc_bank()
                    mm_run(PS[0:tb, bk, :], [(XN[:, kc, b * 128:b * 128 + tb], pan[:, kc, :]) for kc in range(8)],
                           reads=[rXN, rpan], writes=[rB[bk]])
                    kb.op("act", lambda e, bk=bk, b=b, q=q: e.activation(
                        out=SZ[0][0:tb, b, q * 512:(q + 1) * 512], in_=PS[0:tb, bk, :], func=AF.Silu),
                        reads=[rB[bk]], writes=[SZ[1]])
            gate(2)
            def b2_s1(q):
                pan, rpan = next_panel()
                xb = XBJ[q % 3]
                for jj in range(4):
                    c = q * 4 + jj
                    kb.op("pool", lambda e, jj=jj, c=c, xb=xb: e.tensor_copy(out=xb[jj][0][:, 0:3], in_=HISTM[:, c, :]),
                          reads=[rHISTM], writes=[xb[jj][1]])
                    bk = acc_bank()
                    mm_run(PS[:, bk, 0:T], [(pan[:, kc, jj * 128:(jj + 1) * 128], XN[:, kc, 0:T]) for kc in range(8)],
                           reads=[rXN, rpan], writes=[rB[bk]])
                    kb.op("act", lambda e, bk=bk, jj=jj, xb=xb: e.copy(out=xb[jj][0][:, 3:3 + T], in_=PS[:, bk, 0:T]),
                          reads=[rB[bk]], writes=[xb[jj][1]])
                    kb.op("pool", lambda e, jj=jj, c=c, xb=xb: e.tensor_copy(out=HISTM[:, c, :], in_=xb[jj][0][:, T:T + 3]),
                          reads=[xb[jj][1]], writes=[rHISTM])

            def b2_s2(q):
                xb = XBJ[q % 3]
                cv = CVJ[q % 2]
                for jj in range(4):
                    c = q * 4 + jj
                    kb.op("act", lambda e, jj=jj, c=c, xb=xb, cv=cv: e.activation(
                        out=cv[jj][0][:, 0:T], in_=xb[jj][0][:, 0:T], func=AF.Identity,
                        scale=PF[:, PF_CSW + c * 4:PF_CSW + c * 4 + 1]),
                        reads=[xb[jj][1], rPF], writes=[cv[jj][1]])
                for i in (1, 2, 3):
                    for jj in range(4):
                        c = q * 4 + jj
                        kb.op("dve", lambda e, jj=jj, c=c, i=i, xb=xb, cv=cv: e.scalar_tensor_tensor(
                            out=cv[jj][0][:, 0:T], in0=xb[jj][0][:, i:i + T],
                            scalar=PF[:, PF_CSW + c * 4 + i:PF_CSW + c * 4 + i + 1], in1=cv[jj][0][:, 0:T],
                            op0=ALU.mult, op1=ALU.add),
                            reads=[xb[jj][1], cv[jj][1], rPF], writes=[cv[jj][1]])

            def b2_s3(q):
                cv = CVJ[q % 2]
                for jj in range(4):
                    c = q * 4 + jj
                    if c < 16:
                        dst, dres = XSF[0][:, c, 0:T], XSF[1]
                    elif c < 24:
                        dst, dres = BF_[0][:, c - 16, 0:T], BF_[1]
                    else:
                        dst, dres = CF[0][:, c - 24, 0:T], CF[1]
                    kb.op("act", lambda e, jj=jj, c=c, dst=dst, cv=cv: e.activation(
                        out=dst, in_=cv[jj][0][:, 0:T], func=AF.Silu, bias=PF[:, PF_CSB + c:PF_CSB + c + 1]),
                        reads=[cv[jj][1], rPF], writes=[dres])

            for q in range(8 + 2):
                if q < 8:
                    b2_s1(q)
                if 1 <= q <= 8:
                    b2_s2(q - 1)
                if q < 8 and q % 2 == 1:
                    b1_panel(q // 2)
                if q >= 2:
                    b2_s3(q - 2)
            if last:
                kb.dma("sp", ocm[sidx].rearrange("p (c i) -> p c i", i=3), HISTM[:], rHISTM, reads=[rHISTM])

            gate(3)
            def g_dt(b):
                sm = SM[:, b, :]
                rsm = rSM[b]
                smh = SMH[:, b, :]
                rsmh = rSMH[b]
                tok = slice(b * 128, b * 128 + tb)
                bk = acc_bank()
                mm_run(PS[0:tb, bk, 0:32], [(XN[:, kc, tok], DTW[:, kc, :]) for kc in range(8)],
                       reads=[rXN, rDTW], writes=[rB[bk]])
                yield
                kb.op("dve", lambda e: e.tensor_tensor(out=sm[0:tb, C_T0:C_T0 + 32], in0=PS[0:tb, bk, 0:32],
                                                       in1=PF[0:tb, PF_DTB:PF_DTB + 32], op=ALU.add),
                      reads=[rB[bk], rPF], writes=[rsm])
                yield
                kb.op("act", lambda e: e.activation(out=sm[0:tb, C_T0:C_T0 + 32], in_=sm[0:tb, C_T0:C_T0 + 32],
                                                    func=AF.Exp), reads=[rsm], writes=[rsm])
                yield
                kb.op("act", lambda e: e.activation(out=sm[0:tb, C_DT:C_DT + 32], in_=sm[0:tb, C_T0:C_T0 + 32],
                                                    func=AF.Ln, bias=EPSC[0:tb, 1:2]), reads=[rsm, rEPSC], writes=[rsm])
                yield
                kb.op("dve", lambda e: e.tensor_tensor(out=sm[0:tb, C_ADT:C_ADT + 32], in0=sm[0:tb, C_DT:C_DT + 32],
                                                       in1=ANEG[0:tb, :], op=ALU.mult),
                      reads=[rsm, rANEG], writes=[rsm])
                yield
                kb.op("dve", lambda e: e.tensor_copy(out=smh[0:tb, 0:32], in_=sm[0:tb, C_ADT:C_ADT + 32]),
                      reads=[rsm], writes=[rsmh])
                yield
                kb.op("dve", lambda e: e.tensor_tensor(out=smh[0:tb, 32:64], in0=sm[0:tb, C_ADT:C_ADT + 32],
                                                       in1=smh[0:tb, 0:32], op=ALU.subtract),
                      reads=[rsm, rsmh], writes=[rsmh])
                yield
                bk2 = acc_bank()
                mm_run(PS[0:tb, bk2, 0:32], [(tri128[0:tb, 0:tb], smh[0:tb, 0:32]), (tri128[0:tb, 0:tb], smh[0:tb, 32:64])],
                       reads=[rsmh, rCB], writes=[rB[bk2]])
                mm_run(PS[:, bk2, 32:64], [(onesA[0:tb, :], smh[0:tb, 0:32]), (onesA[0:tb, :], smh[0:tb, 32:64])],
                       reads=[rsmh, rCB], writes=[rB[bk2]])
                if nch == 2:
                    mm_run(PS[:, bk2, 64:96], [(onesB[0:tb, :], smh[0:tb, 0:32]), (onesB[0:tb, :], smh[0:tb, 32:64])],
                           reads=[rsmh, rCB], writes=[rB[bk2]])
                yield
                kb.op("dve", lambda e: e.tensor_copy(out=sm[0:tb, C_ACUM:C_ACUM + 32], in_=PS[0:tb, bk2, 0:32]),
                      reads=[rB[bk2]], writes=[rsm])
                yield
                kb.op("act", lambda e: e.activation(out=sm[0:tb, C_EA:C_EA + 32], in_=PS[0:tb, bk2, 0:32],
                                                    func=AF.Exp), reads=[rB[bk2]], writes=[rsm])
                kb.op("act", lambda e: e.activation(out=sm[0:VARP, C_CD:C_CD + 32 * nch],
                                                    in_=PS[0:VARP, bk2, 32:32 + 32 * nch], func=AF.Exp),
                      reads=[rB[bk2]], writes=[rsm])
                yield
            round_robin([g_dt(b) for b in range(nb)])

            def g_prep_banks():
                return (4, 5), (6, 7)

            def g_prep(b):
                sm = SM[:, b, :]
                rsm = rSM[b]
                smh = SMH[:, b, :]
                rsmh = rSMH[b]
                tok = slice(b * 128, b * 128 + tb)
                xst, rxst = XStP[b % 2]
                btt, rbtt = BtP[b % 2]
                xc, rxc = XCP[b % 2]
                p, p2 = g_prep_banks()
                tr_run([(PSb[0:tb, p[c // 8], (c % 8) * 128:(c % 8 + 1) * 128], XSF[0][:, c, tok]) for c in range(16)],
                       reads=[XSF[1], rCB], writes=[rB[p[0]], rB[p[1]]], idn=ident)
                yield
                kb.op("act", lambda e: e.copy(out=xst[0:tb, :], in_=PSb[0:tb, p[0]:p[1] + 1, :].rearrange("p a n -> p (a n)")),
                      reads=[rB[p[0]], rB[p[1]]], writes=[rxst])
                yield
                kb.op("dve", lambda e: e.tensor_tensor(
                    out=xc[0:tb, :].rearrange("p (h d) -> p h d", h=32),
                    in0=PSb[0:tb, p[0]:p[1] + 1, :].rearrange("p a (h d) -> p (a h) d", d=64),
                    in1=sm[0:tb, C_DT:C_DT + 32].unsqueeze(2).broadcast_to([tb, 32, 64]), op=ALU.mult),
                    reads=[rB[p[0]], rB[p[1]], rsm], writes=[rxc])
                yield
                tr_run([(PSb[0:tb, p2[0], g * 128:(g + 1) * 128], BF_[0][:, g, tok]) for g in range(8)],
                       reads=[BF_[1], rCB], writes=[rB[p2[0]]], idn=ident)
                yield
                kb.op("act", lambda e: e.copy(out=btt[0:tb, :], in_=PSb[0:tb, p2[0], :]), reads=[rB[p2[0]]], writes=[rbtt])
                yield
                for (rr_, lo) in ((RHI, 0), (RLO, 32)):
                    kb.op("dve", lambda e, rr_=rr_, lo=lo: e.tensor_tensor(
                        out=rr_[0][0:tb, :, :], in0=smh[0:tb, lo:lo + 32].unsqueeze(2).broadcast_to([tb, 32, 64]),
                        in1=tri2[0:tb, :].unsqueeze(1).broadcast_to([tb, 32, 64]), op=ALU.mult),
                        reads=[rsmh, rCB], writes=[rr_[1]])
                    yield

            def g_chain(b, hf):
                c_ = hf
                sm = SM[:, b, :]
                rsm = rSM[b]
                tok0 = b * 128
                hs = slice(hf * 16, hf * 16 + 16)
                xst, rxst = XStP[b % 2]
                btt, rbtt = BtP[b % 2]
                xcd, rxcd = XCDh[hf]
                xc, rxc = XCP[b % 2]
                mypair = (2 * c_, 2 * c_ + 1)

                def scratch():
                    return mypair
                de, rde = DEc[c_]
                mh, rmh = MHc[c_]
                t1, rt1 = T1c[c_]
                y1, ry1 = Y1c[c_]
                yn, ryn = YNc[c_]
                dec = DECT[:, b, hs]
                rdec = rDECT[b][hf]
                yo = (4, 5) if c_ == 0 else (6, 7)
                stc = STATC + 32 * b + 8 * c_
                rstc = rSTATc[b][c_]
                p = scratch()
                for nn in range(2):
                    hh = slice(hf * 16 + nn * 8, hf * 16 + nn * 8 + 8)
                    mm_run(PS[0:tb, p[nn], :],
                           [(bones[0:tb, 0:tb], RHI[0][0:tb, hh, :]), (bones[0:tb, 0:tb], RLO[0][0:tb, hh, :]),
                            (ident[0:tb, 0:tb], negm[0:tb, nn * 512:(nn + 1) * 512])],
                           reads=[RHI[1], RLO[1], rCB], writes=[rB[p[nn]]])
                yield
                kb.op("dve", lambda e: e.tensor_tensor(
                    out=de[0:tb, :, :], in0=PS[0:tb, p[0]:p[1] + 1, :].rearrange("p a (h l) -> p (a h) l", l=64),
                    in1=sm[0:tb, C_ACUM + hs.start:C_ACUM + hs.stop].unsqueeze(2).broadcast_to([tb, 16, 64]),
                    op=ALU.subtract), reads=[rB[p[0]], rB[p[1]], rsm], writes=[rde])
                pc = scratch()
                for g in range(4):
                    gg = hf * 4 + g
                    for ch in range(nch):
                        r0 = ch * 64
                        cs = slice(tok0 + r0, tok0 + r0 + 64)
                        mm_run(PS[r0:r0 + 64, pc[0], g * 64:(g + 1) * 64], [(BF_[0][:, gg, cs], CF[0][:, gg, cs])],
                               reads=[BF_[1], CF[1]], writes=[rB[pc[0]]])
                yield
                kb.op("act", lambda e: e.activation(out=de[0:tb, :, :], in_=de[0:tb, :, :], func=AF.Exp),
                      reads=[rde], writes=[rde])
                yield
                kb.op("pool", lambda e: e.tensor_copy(out=dec[0:tb, :].unsqueeze(2), in_=de[0:tb, :, 63:64]),
                      reads=[rde], writes=[rdec])
                kb.op("dve", lambda e: e.tensor_tensor(
                    out=mh[0:tb, :, :].rearrange("p (g j) l -> p g j l", g=4),
                    in0=de[0:tb, :, :].rearrange("p (g j) l -> p g j l", g=4),
                    in1=PS[0:tb, pc[0], 0:256].rearrange("p (g l) -> p g l", g=4).unsqueeze(2).broadcast_to([tb, 4, 4, 64]),
                    op=ALU.mult), reads=[rde, rB[pc[0]]], writes=[rmh])
                yield
                kb.op("dve", lambda e: e.tensor_tensor(
                    out=xcd[0:tb, :].rearrange("p (h d) -> p h d", h=16),
                    in0=xc[0:tb, hf * 1024:(hf + 1) * 1024].rearrange("p (h d) -> p h d", h=16),
                    in1=dec[0:tb, :].unsqueeze(2).broadcast_to([tb, 16, 64]),
                    op=ALU.mult), reads=[rxc, rdec], writes=[rxcd])
                yield
                pd = scratch()
                for ch in range(nch):
                    r0 = ch * 64
                    for hl in range(16):
                        h = hf * 16 + hl
                        mm_run(PS[r0:r0 + 64, pd[hl // 8], (hl % 8) * 64:(hl % 8 + 1) * 64],
                               [(mh[r0:r0 + 64, hl, :], xc[r0:r0 + 64, h * 64:(h + 1) * 64])],
                               reads=[rmh, rxc], writes=[rB[pd[0]], rB[pd[1]]])
                yield
                yds = de[0:tb, :, :].rearrange("p h l -> p (h l)")
                kb.op("act", lambda e: e.copy(out=yds, in_=PS[0:tb, pd[0]:pd[1] + 1, :].rearrange("p a n -> p (a n)")),
                      reads=[rB[pd[0]], rB[pd[1]]], writes=[rde])
                yield
                for ch in range(nch):
                    r0 = ch * 64
                    cs = slice(tok0 + r0, tok0 + r0 + 64)
                    sb_in, rsb_in = SBF[ch % 2], rSBFh[ch % 2][hf]
                    if nch == 2:
                        sb_out, rsb_out = SBF[(ch + 1) % 2], rSBFh[(ch + 1) % 2][hf]
                    else:
                        sb_out, rsb_out = SBF[0], rSBFh[0][hf]
                    for g in range(4):
                        gg = hf * 4 + g
                        mm_run(PS[r0:r0 + 64, yo[g // 2], (g % 2) * 256:(g % 2 + 1) * 256],
                               [(CF[0][:, gg, cs], sb_in[:, gg * 256:(gg + 1) * 256])],
                               reads=[CF[1], rsb_in], writes=[rB[yo[0]], rB[yo[1]]])
                    ps_ = scratch()
                    for g in range(4):
                        gg = hf * 4 + g
                        mm_run(PS[:, ps_[g // 2], (g % 2) * 256:(g % 2 + 1) * 256],
                               [(btt[r0:r0 + 64, gg * 128:(gg + 1) * 128], xcd[r0:r0 + 64, g * 256:(g + 1) * 256])],
                               reads=[rbtt, rxcd], writes=[rB[ps_[0]], rB[ps_[1]]])
                    yield
                    stv = ST[:, hf * 1024:(hf + 1) * 1024]
                    kb.op("dve", lambda e, ch=ch: e.tensor_tensor(
                        out=t1[:, :].rearrange("p (h d) -> p h d", h=16), in0=stv.rearrange("p (h d) -> p h d", h=16),
                        in1=sm[:, C_CD + ch * 32 + hs.start:C_CD + ch * 32 + hs.stop].unsqueeze(2).broadcast_to([128, 16, 64]),
                        op=ALU.mult), reads=[rSTh[hf], rsm], writes=[rt1])
                    yield
                    kb.op("dve", lambda e, ps_=ps_: e.tensor_tensor(
                        out=stv, in0=PS[:, ps_[0]:ps_[1] + 1, :].rearrange("p a n -> p (a n)"), in1=t1[:, :], op=ALU.add),
                        reads=[rB[ps_[0]], rB[ps_[1]], rt1], writes=[rSTh[hf]])
                    yield
                    kb.op("act", lambda e, sb_out=sb_out: e.copy(out=sb_out[:, hf * 1024:(hf + 1) * 1024], in_=stv),
                          reads=[rSTh[hf]], writes=[rsb_out])
                    yield
                kb.op("dve", lambda e: e.tensor_tensor(
                    out=y1[0:tb, :].rearrange("p (h d) -> p h d", h=16),
                    in0=PS[0:tb, yo[0]:yo[1] + 1, :].rearrange("p a (h d) -> p (a h) d", d=64),
                    in1=sm[0:tb, C_EA + hs.start:C_EA + hs.stop].unsqueeze(2).broadcast_to([tb, 16, 64]),
                    op=ALU.mult), reads=[rB[yo[0]], rB[yo[1]], rsm], writes=[ry1])
                yield
                kb.op("dve", lambda e: e.tensor_tensor(out=y1[0:tb, :], in0=y1[0:tb, :], in1=yds, op=ALU.add),
                      reads=[ry1, rde], writes=[ry1])
                yield
                kb.op("dve", lambda e: e.tensor_tensor(
                    out=t1[0:tb, :].rearrange("p (h d) -> p h d", h=16),
                    in0=xst[0:tb, hf * 1024:(hf + 1) * 1024].rearrange("p (h d) -> p h d", h=16),
                    in1=PF[0:tb, PF_DSK + hs.start:PF_DSK + hs.stop].unsqueeze(2).broadcast_to([tb, 16, 64]),
                    op=ALU.mult), reads=[rxst, rPF], writes=[rt1])
                yield
                kb.op("dve", lambda e: e.tensor_tensor(out=t1[0:tb, :], in0=t1[0:tb, :], in1=y1[0:tb, :], op=ALU.add),
                      reads=[rt1, ry1], writes=[rt1])
                yield
                kb.op("dve", lambda e: e.tensor_tensor(
                    out=y1[0:tb, :], in0=t1[0:tb, :], in1=SZ[0][0:tb, b, hf * 1024:(hf + 1) * 1024], op=ALU.mult),
                    reads=[rt1, SZ[1]], writes=[ry1])
                yield
                if hf == 0 and b == 0:
                    dump("yg0", y1[0:tb, :], ry1, [tb, 1024], F32)
                    dump("yds0", yds, rde, [tb, 1024], F32)
                for g in range(4):
                    kb.op("act", lambda e, g=g: e.activation(
                        out=t1[0:tb, g * 256:(g + 1) * 256], in_=y1[0:tb, g * 256:(g + 1) * 256],
                        func=AF.Square, accum_out=STAT[0:tb, stc + g:stc + g + 1]),
                        reads=[ry1], writes=[rt1, rstc])
                yield
                yield from g_rstd(stc, rstc, 0, 4, 256, tb, w=4)
                for g in range(4):
                    kb.op("act", lambda e, g=g: e.activation(
                        out=yn[0:tb, g * 256:(g + 1) * 256], in_=y1[0:tb, g * 256:(g + 1) * 256],
                        func=AF.Identity, scale=STAT[0:tb, stc + 4 + g:stc + 5 + g]),
                        reads=[ry1, rstc], writes=[ryn])
                yield
                pt = scratch()
                tr_run([(PSb[:, pt[0], c * 128:c * 128 + tb], yn[0:tb, c * 128:(c + 1) * 128]) for c in range(8)],
                       reads=[ryn, rCB], writes=[rB[pt[0]]], idn=ident[0:tb, 0:tb])
                yield
                kb.op("act", lambda e: e.copy(
                    out=YNF[0][:, hf * 8:(hf + 1) * 8, b * 128:b * 128 + tb],
                    in_=PSb[:, pt[0], :].rearrange("p (k t) -> p k t", k=8)[:, :, 0:tb]), reads=[rB[pt[0]]], writes=[YNF[1]])
                yield

            for _ in g_prep(0):
                pass
            for b in range(nb):
                gens = []
                if b + 1 < nb:
                    gens.append((g_prep(b + 1), 0))
                gens += [(g_chain(b, 0), 0), (g_chain(b, 1), CH_OFF)]
                if os.environ.get('SEQ'):
                    for g_, _o in gens:
                        for _ in g_:
                            pass
                else:
                    round_robin_off(gens)
            if last:
                kb.dma("sp", ost[sidx], ST[:], rSTh[0], reads=[rSTh[0], rSTh[1]])
            dump("xn", XN[:, :, 0:T], rXN, [128, 8, T], BF16)
            dump("ynf", YNF[0][:, :, 0:T], YNF[1], [128, 16, T], BF16)

            gate(4)
            def fm_group(pan, rpan, jj, act, ract):
                bk = acc_bank()
                mm_run(PS[:, bk, 0:T], [(pan[:, kc, jj * 128:(jj + 1) * 128], act[:, kc, 0:T]) for kc in range(8)],
                       reads=[ract, rpan], writes=[rB[bk]])
                return bk

            for hf in range(2):
                pan, rpan = next_panel()
                for jj in range(4):
                    bk = fm_group(pan, rpan, jj, XN, rXN)
                    kb.op("act", lambda e, bk=bk, jj=jj: e.copy(out=ACJ[jj][0][:, 0:T], in_=PS[:, bk, 0:T]),
                          reads=[rB[bk]], writes=[ACJ[jj][1]])
                pan, rpan = next_panel()
                for jj in range(4):
                    j = hf * 4 + jj
                    kb.op("pool", lambda e, jj=jj, j=j: e.tensor_copy(out=UJ[jj][0][:, 0:2], in_=HISTA[:, j, :]),
                          reads=[rHISTA], writes=[UJ[jj][1]])
                    bk = fm_group(pan, rpan, jj, XN, rXN)
                    kb.op("dve", lambda e, bk=bk, jj=jj: e.tensor_tensor(out=UJ[jj][0][:, 2:2 + T], in0=PS[:, bk, 0:T],
                                                                        in1=ACJ[jj][0][:, 0:T], op=ALU.mult),
                          reads=[rB[bk], ACJ[jj][1]], writes=[UJ[jj][1]])
                    kb.op("pool", lambda e, jj=jj, j=j: e.tensor_copy(out=HISTA[:, j, :], in_=UJ[jj][0][:, T:T + 2]),
                          reads=[UJ[jj][1]], writes=[rHISTA])
                for jj in range(4):
                    j = hf * 4 + jj
                    kb.op("act", lambda e, jj=jj, j=j: e.activation(
                        out=TTJ[jj][0][:, 0:T], in_=UJ[jj][0][:, 0:T], func=AF.Identity,
                        scale=PF[:, PF_CAW + j * 3:PF_CAW + j * 3 + 1]), reads=[UJ[jj][1], rPF], writes=[TTJ[jj][1]])
                pan, rpan = next_panel()
                bks = [fm_group(pan, rpan, jj, XN, rXN) for jj in range(4)]
                for i in (1, 2):
                    for jj in range(4):
                        j = hf * 4 + jj
                        kb.op("dve", lambda e, jj=jj, j=j, i=i: e.scalar_tensor_tensor(
                            out=TTJ[jj][0][:, 0:T], in0=UJ[jj][0][:, i:i + T],
                            scalar=PF[:, PF_CAW + j * 3 + i:PF_CAW + j * 3 + i + 1], in1=TTJ[jj][0][:, 0:T],
                            op0=ALU.mult, op1=ALU.add), reads=[UJ[jj][1], TTJ[jj][1], rPF], writes=[TTJ[jj][1]])
                for jj in range(4):
                    j = hf * 4 + jj
                    bk = bks[jj]
                    kb.op("dve", lambda e, bk=bk, jj=jj, j=j: e.tensor_tensor(out=V[0][:, j, 0:T], in0=PS[:, bk, 0:T],
                                                                             in1=TTJ[jj][0][:, 0:T], op=ALU.mult),
                          reads=[rB[bk], TTJ[jj][1]], writes=[V[1]])
            if last:
                kb.dma("sp", oca[sidx].rearrange("p (j i) -> p j i", i=2), HISTA[:], rHISTA, reads=[rHISTA])
            gate(5)
            dump("v", V[0][:, :, 0:T], V[1], [128, 8, T], BF16)
            for hf in range(2):
                pan, rpan = next_panel()
                for jj in range(4):
                    j = hf * 4 + jj
                    bk = fm_group(pan, rpan, jj, XN, rXN)
                    kb.op("act", lambda e, bk=bk, jj=jj, j=j: e.activation(
                        out=GJ[jj][0][:, 0:T], in_=PS[:, bk, 0:T], func=AF.Sigmoid, bias=PF[:, PF_BG + j:PF_BG + j + 1]),
                        reads=[rB[bk], rPF], writes=[GJ[jj][1]])
                pan, rpan = next_panel()
                for jj in range(4):
                    j = hf * 4 + jj
                    bk = fm_group(pan, rpan, jj, V[0], V[1])
                    kb.op("dve", lambda e, bk=bk, jj=jj, j=j: e.tensor_tensor(out=PA[0][:, j, 0:T], in0=PS[:, bk, 0:T],
                                                                             in1=GJ[jj][0][:, 0:T], op=ALU.mult),
                          reads=[rB[bk], GJ[jj][1]], writes=[PA[1]])
            dump("pa", PA[0][:, :, 0:T], PA[1], [128, 8, T], F32)
            for hf in range(2):
                pan, rpan = next_panel()
                for jj in range(4):
                    j = hf * 4 + jj
                    bk = fm_group(pan, rpan, jj, XN, rXN)
                    kb.op("act", lambda e, bk=bk, jj=jj, j=j: e.activation(
                        out=GJ[jj][0][:, 0:T], in_=PS[:, bk, 0:T], func=AF.Sigmoid,
                        bias=PF[:, PF_BG + 8 + j:PF_BG + 8 + j + 1]), reads=[rB[bk], rPF], writes=[GJ[jj][1]])
                pan0, rpan0 = next_panel()
                pan1, rpan1 = next_panel()
                for jj in range(4):
                    j = hf * 4 + jj
                    bk = acc_bank()
                    pairs = [(pan0[:, kc, jj * 128:(jj + 1) * 128], YNF[0][:, kc, 0:T]) for kc in range(8)] + \
                            [(pan1[:, kc, jj * 128:(jj + 1) * 128], YNF[0][:, 8 + kc, 0:T]) for kc in range(8)]
                    mm_run(PS[:, bk, 0:T], pairs, reads=[YNF[1], rpan0, rpan1], writes=[rB[bk]])
                    kb.op("dve", lambda e, bk=bk, jj=jj: e.tensor_tensor(out=GJ[jj][0][:, 0:T], in0=PS[:, bk, 0:T],
                                                                        in1=GJ[jj][0][:, 0:T], op=ALU.mult),
                          reads=[rB[bk], GJ[jj][1]], writes=[GJ[jj][1]])
                    kb.op("dve", lambda e, jj=jj, j=j: e.tensor_tensor(out=MG[0][:, j, 0:T], in0=GJ[jj][0][:, 0:T],
                                                                      in1=PA[0][:, j, 0:T], op=ALU.add),
                          reads=[GJ[jj][1], PA[1]], writes=[MG[1]])
            gate(6)
            dump("mg", MG[0][:, :, 0:T], MG[1], [128, 8, T], BF16)
            for hf in range(2):
                pan, rpan = next_panel()
                for b in range(nb):
                    bk = acc_bank()
                    mm_run(PS[0:tb, bk, :], [(MG[0][:, kc, b * 128:b * 128 + tb], pan[:, kc, :]) for kc in range(8)],
                           reads=[MG[1], rpan], writes=[rB[bk]])
                    kb.op("act", lambda e, bk=bk, b=b, hf=hf: e.copy(out=MO[b][0][0:tb, hf * 512:(hf + 1) * 512],
                                                                    in_=PS[0:tb, bk, :]), reads=[rB[bk]], writes=[MO[b][1]])
            def g_post(b, src, wcol, addsrc):
                (sap, sres) = src
                (aap, ares) = addsrc
                o = 16 * b
                rst = rSTATb[b]
                jk, rjk = JUNK4[b]
                kb.op("act", lambda e: e.activation(out=jk[0:tb, :], in_=sap[0:tb, :], func=AF.Square,
                                                    accum_out=STAT[0:tb, o:o + 1]), reads=[sres], writes=[rjk, rst])
                yield
                yield from g_rstd(o, rst, 0, 1, 1024, tb)
                kb.op("dve", lambda e: e.scalar_tensor_tensor(
                    out=sap[0:tb, :], in0=sap[0:tb, :], scalar=STAT[0:tb, o + 1:o + 2], in1=PF[0:tb, wcol:wcol + 1024],
                    op0=ALU.mult, op1=ALU.mult), reads=[sres, rst, rPF], writes=[sres])
                yield
                kb.op("dve", lambda e: e.tensor_tensor(out=sap[0:tb, :], in0=sap[0:tb, :], in1=aap[0:tb, :], op=ALU.add),
                      reads=[sres, ares], writes=[sres])
                yield

            def g_wo(b):
                xt, rxt = XT2[b]
                kb.dma("sp", xt[0:tb, :], xd[b * 128:b * 128 + tb, :], rxt, writes=[rxt])
                yield from g_post(b, MO[b], PF_NMPOST, XT2[b])
                yield from g_norm_to_fm(MO[b][0], MO[b][1], b, tb, HS16[b], XN, rXN)
            round_robin([g_wo(b) for b in range(nb)])
            gate(7)
            dump("h", MO[0][0][0:tb, :], MO[0][1], [tb, 1024], F32)
            for q in range(8):
                pan, rpan = next_panel()
                for jj in range(4):
                    c = q * 4 + jj
                    bk = acc_bank()
                    mm_run(PS[:, bk, 0:T], [(pan[:, kc, jj * 128:(jj + 1) * 128], XN[:, kc, 0:T]) for kc in range(8)],
                           reads=[rXN, rpan], writes=[rB[bk]])
                    rt, rrt = RT[c % 4]
                    kb.op("act", lambda e, bk=bk, rt=rt: e.activation(out=rt[:, 0:T], in_=PS[:, bk, 0:T], func=AF.Relu),
                          reads=[rB[bk]], writes=[rrt])
                    kb.op("dve", lambda e, rt=rt, c=c: e.tensor_tensor(out=FF[0][:, c, 0:T], in0=rt[:, 0:T], in1=rt[:, 0:T],
                                                                      op=ALU.mult), reads=[rrt], writes=[FF[1]])
            for hf in range(2):
                for kg in range(4):
                    pan, rpan = next_panel()
                    for b in range(nb):
                        mm_run(PS[0:tb, b, :],
                               [(FF[0][:, kg * 8 + kc, b * 128:b * 128 + tb], pan[:, kc, :]) for kc in range(8)],
                               reads=[FF[1], rpan], writes=[rB[b]], first=(kg == 0), last=(kg == 3))
                for b in range(nb):
                    kb.op("act", lambda e, b=b, hf=hf: e.copy(out=FO[b][0][0:tb, hf * 512:(hf + 1) * 512], in_=PS[0:tb, b, :]),
                          reads=[rB[b]], writes=[FO[b][1]])
            dump("ff", FF[0][:, :, 0:T], FF[1], [128, 32, T], BF16)
            dump("fo", FO[0][0][0:tb, :], FO[0][1], [tb, 1024], F32)
            def g_fin(b):
                yield from g_post(b, FO[b], PF_NFPOST, MO[b])
                kb.dma("sp", yd[b * 128:b * 128 + tb, :], FO[b][0][0:tb, :], FO[b][1], reads=[FO[b][1]])
            round_robin([g_fin(b) for b in range(nb)])
        for tl in tiles:
            try:
                run_tile(*tl)
            except _Stop:
                pass
        kb.finish()
        block = es.enter_context(nc.Block())
        kb.replay(block)
    return nc


def _host_tables(p):
    pf = np.zeros((128, PF_COLS), np.float32)
    pf[:, PF_CAW:PF_CAW + 24] = p["conv_a_w"][0].reshape(3, 8, 128).transpose(2, 1, 0).reshape(128, 24)
    pf[:, PF_CSW:PF_CSW + 128] = p["conv_ssm_w"][0].reshape(4, 32, 128).transpose(2, 1, 0).reshape(128, 128)
    pf[:, PF_CSB:PF_CSB + 32] = p["conv_ssm_b"][0].reshape(32, 128).T
    pf[:, PF_BG:PF_BG + 16] = p["b_gate"][0].reshape(16, 128).T
    pf[:, PF_NMP:PF_NMP + 8] = p["norm_mix_pre"][0].reshape(8, 128).T
    pf[:, PF_NFP:PF_NFP + 8] = p["norm_ffn_pre"][0].reshape(8, 128).T
    pf[:, PF_SSN:PF_SSN + 16] = p["ssm_norm"][0].reshape(16, 128).T
    pf[:, PF_DTB:PF_DTB + 32] = p["dt_bias"][0][None, :]
    pf[:, PF_ALOG:PF_ALOG + 32] = p["a_log"][0][None, :]
    pf[:, PF_DSK:PF_DSK + 32] = p["d_skip"][0][None, :]
    pf[:, PF_NMPOST:PF_NMPOST + 1024] = p["norm_mix_post"][0][None, :]
    pf[:, PF_NFPOST:PF_NFPOST + 1024] = p["norm_ffn_post"][0][None, :]
    cb = np.zeros((128, CB_COLS), np.float32)
    k = np.arange(128)
    cb[:, CB_ID:CB_ID + 128] = np.eye(128)
    l = np.arange(64)
    cb[:, CB_TRI2:CB_TRI2 + 64] = ((k[:, None] % 64) <= l[None, :])
    same = (k[:, None] // 64) == (k[None, :] // 64)
    cb[:, CB_BONES:CB_BONES + 128] = same
    cb[:, CB_ONESA:CB_ONESA + 128] = (k[:, None] < 64) * np.ones((1, 128))
    cb[:, CB_ONESB:CB_ONESB + 128] = (k[:, None] >= 64) * np.ones((1, 128))
    cb[:, CB_TRI128:CB_TRI128 + 128] = same & ((k[:, None] % 64) <= (k[None, :] % 64))
    neg = np.where((k[:, None] % 64) > l[None, :], -30000.0, 0.0)
    cb[:, CB_NEG:CB_NEG + 1024] = np.tile(neg, (1, 16))
    return pf, cb.astype(ml_dtypes.bfloat16)


_NC_CACHE = {}


def kernel(x_prompt, x_sample, state_conv_a, state_conv_ssm, state_ssm, norm_mix_pre, w_in, b_gate, conv_a_w, w_a_out,
           conv_ssm_w, conv_ssm_b, dt_bias, a_log, d_skip, ssm_norm, w_ssm_out, w_o, norm_mix_post, norm_ffn_pre,
           w_ff1, w_ff2, norm_ffn_post, _tiles_limit=None, _cores=None, _dbg=False, _stop=None):
    p = dict(norm_mix_pre=norm_mix_pre, b_gate=b_gate, conv_a_w=conv_a_w, conv_ssm_w=conv_ssm_w, conv_ssm_b=conv_ssm_b,
             dt_bias=dt_bias, a_log=a_log, d_skip=d_skip, ssm_norm=ssm_norm, norm_mix_post=norm_mix_post,
             norm_ffn_pre=norm_ffn_pre, norm_ffn_post=norm_ffn_post)
    p = {k: np.asarray(v, np.float32) for k, v in p.items()}
    pf, cb = _host_tables(p)
    x_prompt = np.asarray(x_prompt, np.float32)
    x_sample = np.asarray(x_sample, np.float32)
    sca_all = np.asarray(state_conv_a, np.float32)[0]
    scm_all = np.asarray(state_conv_ssm, np.float32)[0]
    sst_all = np.asarray(state_ssm, np.float32)[0]
    key = (None if _tiles_limit is None else tuple(_tiles_limit), _dbg, _stop)
    if key not in _NC_CACHE:
        _NC_CACHE[key] = build_program(_tiles_limit, _dbg, _stop)
    nc = _NC_CACHE[key]
    cores = list(range(NCORES)) if _cores is None else _cores
    shared = dict(pf=pf, cb=cb, w_in=np.ascontiguousarray(np.asarray(w_in, np.float32)[0]),
                  w_a_out=np.ascontiguousarray(np.asarray(w_a_out, np.float32)[0]),
                  w_ssm_out=np.ascontiguousarray(np.asarray(w_ssm_out, np.float32)[0]),
                  w_o=np.ascontiguousarray(np.asarray(w_o, np.float32)[0]),
                  w_ff1=np.ascontiguousarray(np.asarray(w_ff1, np.float32)[0]),
                  w_ff2=np.ascontiguousarray(np.asarray(w_ff2, np.float32)[0]))
    in_maps = []
    for c in cores:
        s2 = slice(2 * c, 2 * c + 2)
        m = dict(shared)
        m["xp"] = np.ascontiguousarray(x_prompt[s2])
        m["xs"] = np.ascontiguousarray(x_sample[s2])
        m["sca"] = np.ascontiguousarray(sca_all[s2].reshape(2, 2, 8, 128).transpose(0, 3, 2, 1).reshape(2, 128, 16))
        m["scm"] = np.ascontiguousarray(scm_all[s2].reshape(2, 3, 32, 128).transpose(0, 3, 2, 1).reshape(2, 128, 96))
        m["sst"] = np.ascontiguousarray(sst_all[s2].reshape(2, 2048, 128).transpose(0, 2, 1))
        in_maps.append(m)
    res = run_bass_kernel_spmd(nc, in_maps, core_ids=list(range(len(cores))))
    R = res.results
    if _dbg:
        kernel.dbg = {k: v for k, v in R[0].items() if k.startswith("dbg_")}
    yp = np.concatenate([r["yp"] for r in R], axis=0)
    ys_ = np.concatenate([r["ys"] for r in R], axis=0)
    oca = np.stack([r["oca"] for r in R])
    ocm = np.stack([r["ocm"] for r in R])
    ost = np.stack([r["ost"] for r in R])
    n = len(cores)

    def un_ca(a):
        return a.reshape(n * 2, 128, 8, 2).transpose(0, 3, 2, 1).reshape(n * 2, 2, 1024)

    def un_cm(a):
        return a.reshape(n * 2, 128, 32, 3).transpose(0, 3, 2, 1).reshape(n * 2, 3, 4096)

    def un_st(a):
        return a.reshape(n * 2, 128, 2048).transpose(0, 2, 1).reshape(n * 2, 32, 64, 128)

    outs = (yp, ys_,
            un_ca(oca[:, 0:2])[None], un_cm(ocm[:, 0:2])[None], un_st(ost[:, 0:2])[None],
            un_ca(oca[:, 2:4])[None], un_cm(ocm[:, 2:4])[None], un_st(ost[:, 2:4])[None])
    return tuple(np.ascontiguousarray(o, dtype=np.float32) for o in outs)
```

```python
import numpy as np
import ml_dtypes
from contextlib import ExitStack
import concourse.bass as bass
import concourse.mybir as mybir
from concourse.bass_utils import run_bass_kernel_spmd

F32 = mybir.dt.float32
BF16 = mybir.dt.bfloat16
AF = mybir.ActivationFunctionType
ALU = mybir.AluOpType
AX = mybir.AxisListType

NCORES = 8
D = 1024
SEQ = 2048
DSEQ = 64
PROJ = 11296
OFF_AB, OFF_AC, OFF_AH, OFF_Z, OFF_XBC, OFF_DT, OFF_GATE = 0, 1024, 2048, 3072, 5120, 9216, 9248
EPS = 1e-6
NS = 4

PF_CAW = 0
PF_CSW = 24
PF_CSB = 152
PF_BG = 184
PF_NMP = 200
PF_NFP = 208
PF_SSN = 216
PF_DTB = 232
PF_ALOG = 264
PF_DSK = 296
PF_NMPOST = 328
PF_NFPOST = 1352
PF_COLS = 2376
CB_ID = 0
CB_TRI2 = 128
CB_BONES = 192
CB_ONESA = 320
CB_ONESB = 448
CB_TRI128 = 576
CB_NEG = 704
CB_COLS = 1728


class Res:
    __slots__ = ("name", "lw", "rd", "ov", "dsem", "dcnt", "rng", "excl", "disj")

    def __init__(self, name):
        self.name = name
        self.lw = None
        self.rd = {}
        self.ov = []
        self.dsem = None
        self.dcnt = 0
        self.rng = None
        self.disj = False
        self.excl = False


class KB:
    ENG = ("pe", "act", "dve", "pool")

    def __init__(self, nc, es):
        self.nc = nc
        self.es = es
        self.streams = {k: [] for k in ("pe", "act", "dve", "pool", "sp")}
        self.sem = {e: es.enter_context(nc.semaphore("s_" + e)) for e in self.ENG}
        self.tick = {e: 0 for e in self.ENG}
        self.waited = {k: {} for k in self.streams}
        self.dsems = []
        self.all_res = []

    def res(self, name):
        r = Res(name)
        self.all_res.append(r)
        return r

    def _dsem(self, r):
        if r.dsem is None:
            r.dsem = self.es.enter_context(self.nc.semaphore("d%d" % len(self.dsems)))
            self.dsems.append(r)
        return r.dsem

    def _deps(self, eng, reads, writes):
        deps = []
        for r in reads:
            if r.lw is not None:
                deps.append(r.lw)
            if r.excl:
                for q, t in r.rd.items():
                    if q != eng:
                        deps.append((q, t))
        same_ok = (eng == "pe")
        for w in writes:
            for x in [w] + w.ov:
                if x.lw is not None and (x.lw[0] != eng or not (same_ok or (x is w and w.disj))):
                    deps.append(x.lw)
                for q, t in x.rd.items():
                    if q != eng or not same_ok:
                        deps.append((q, t))
        wd = self.waited[eng]
        for q, t in deps:
            if wd.get(q, 0) < t:
                wd[q] = t
                self.streams[eng].append(("w", q, t))

    def wait(self, eng, q, t):
        wd = self.waited[eng]
        if wd.get(q, 0) < t:
            wd[q] = t
            self.streams[eng].append(("w", q, t))

    def op(self, eng, fn, reads=(), writes=(), inc=True):
        self._deps(eng, reads, writes)
        t = self.tick[eng] + 1
        if inc:
            self.tick[eng] = t
        self.streams[eng].append(("op", fn, inc))
        for r in reads:
            r.rd[eng] = t
        for w in writes:
            w.lw = (eng, t)
            w.rd = {}
        return t

    def dma(self, eng, out, in_, dres, reads=(), writes=()):
        self._deps(eng, reads, writes)
        self._dsem(dres)
        dres.dcnt += 16
        key = (id(dres), dres.dcnt)
        q = ("d", id(dres))
        self._dq = getattr(self, "_dq", {})
        self._dq[q] = dres
        self.streams[eng].append(("dma", out, in_, dres))
        for r in reads:
            r.rd[q] = dres.dcnt
        for w in writes:
            w.lw = (q, dres.dcnt)
            w.rd = {}

    def semof(self, q):
        if isinstance(q, tuple):
            return self._dq[q].dsem
        return self.sem[q]

    def finish(self):
        for r in self.dsems:
            self.wait("sp", ("d", id(r)), r.dcnt)

    def replay(self, block):
        handles = {"pe": block.tensor, "act": block.scalar, "dve": block.vector, "pool": block.gpsimd,
                   "sp": block.sync}
        for eng, deco in handles.items():
            stream = self.streams[eng]
            own = self.sem.get(eng)

            def body(e, stream=stream, own=own):
                for ent in stream:
                    k = ent[0]
                    if k == "w":
                        e.wait_ge(self.semof(ent[1]), ent[2])
                    elif k == "op":
                        ins = ent[1](e)
                        if ent[2]:
                            ins.then_inc(own, 1)
                    else:
                        e.dma_start(out=ent[1], in_=ent[2]).then_inc(ent[3].dsem, 16)
            deco(body)


class Arena:
    def __init__(self, kb, tensor, nbytes):
        self.kb = kb
        self.t = tensor
        self.nbytes = nbytes
        self.views = []

    def view(self, name, off, shape, dt):
        esz = 2 if dt == BF16 else 4
        n = int(np.prod(shape))
        nb = n * esz
        assert off % 4 == 0 and off + nb <= self.nbytes, (name, off, nb, self.nbytes)
        w0, w1 = off // 4, (off + nb + 3) // 4
        ap = self.t[:, w0:w1]
        if dt == BF16:
            ap = ap.bitcast(BF16)[:, 0:n]
        if len(shape) == 2:
            ap = ap.rearrange("p (a b) -> p a b", a=shape[0])
        elif len(shape) == 3:
            ap = ap.rearrange("p (a b c) -> p a b c", a=shape[0], b=shape[1])
        r = self.kb.res(name)
        r.rng = (off, off + nb)
        for (o, rr) in self.views:
            if o.rng[0] < r.rng[1] and r.rng[0] < o.rng[1]:
                r.ov.append(o)
                o.ov.append(r)
        self.views.append((r, ap))
        return ap, r


import os
VARP = int(os.environ.get('VARP', '128'))
CH_OFF = int(os.environ.get('CH_OFF', '0'))


class _Stop(Exception):
    pass


def build_program(tiles_limit=None, dbg=False, stop_after=None):
    nc = bass.Bass("TRN2", target_bir_lowering=False)
    dram_in = lambda name, shape, dt=F32: nc.dram_tensor(name, shape, dt, kind="ExternalInput").ap()
    dram_out = lambda name, shape, dt=F32: nc.dram_tensor(name, shape, dt, kind="ExternalOutput").ap()
    xp = dram_in("xp", [2, SEQ, D])
    xs = dram_in("xs", [2, DSEQ, D])
    sca = dram_in("sca", [2, 128, 16])
    scm = dram_in("scm", [2, 128, 96])
    sst = dram_in("sst", [2, 128, 2048])
    pf_d = dram_in("pf", [128, PF_COLS])
    cb_d = dram_in("cb", [128, CB_COLS], BF16)
    w_in = dram_in("w_in", [D, PROJ])
    w_a = dram_in("w_a_out", [D, D])
    w_s = dram_in("w_ssm_out", [2 * D, D])
    w_o = dram_in("w_o", [D, D])
    w_f1 = dram_in("w_ff1", [D, 4 * D])
    w_f2 = dram_in("w_ff2", [4 * D, D])
    yp = dram_out("yp", [2, SEQ, D])
    ys = dram_out("ys", [2, DSEQ, D])
    oca = dram_out("oca", [4, 128, 16])
    ocm = dram_out("ocm", [4, 128, 96])
    ost = dram_out("ost", [4, 128, 2048])
    wi_b = nc.dram_tensor("wi_b", [D, PROJ], BF16).ap()
    wa_b = nc.dram_tensor("wa_b", [D, D], BF16).ap()
    ws_b = nc.dram_tensor("ws_b", [2 * D, D], BF16).ap()
    wo_b = nc.dram_tensor("wo_b", [D, D], BF16).ap()
    wf1_b = nc.dram_tensor("wf1_b", [D, 4 * D], BF16).ap()
    wf2_b = nc.dram_tensor("wf2_b", [4 * D, D], BF16).ap()

    with ExitStack() as es:
        kb = KB(nc, es)
        sbt = lambda name, shape, dt: es.enter_context(nc.sbuf_tensor(name, shape, dt))
        PF = sbt("PF", [128, PF_COLS], F32); rPF = kb.res("PF")
        CB = sbt("CB", [128, CB_COLS], BF16); rCB = kb.res("CB")
        PAN = [sbt("PAN%d" % i, [128, 8, 512], BF16) for i in range(NS)]
        rPAN = [kb.res("PAN%d" % i) for i in range(NS)]
        XN = sbt("XN", [128, 8, 512], BF16); rXN = kb.res("XN")
        DTW = sbt("DTW", [128, 8, 32], BF16); rDTW = kb.res("DTW")
        HISTA = sbt("HISTA", [128, 8, 2], F32); rHISTA = kb.res("HISTA")
        HISTM = sbt("HISTM", [128, 32, 3], F32); rHISTM = kb.res("HISTM")
        ST = sbt("ST", [128, 2048], F32)
        rSTh = [kb.res("ST_h%d" % i) for i in range(2)]
        SBF = [sbt("SBF%d" % i, [128, 2048], BF16) for i in range(2)]
        rSBFh = [[kb.res("SBF%d_h%d" % (i, hf)) for hf in range(2)] for i in range(2)]
        DECT = sbt("DECT", [128, 4, 32], F32)
        rDECT = [[kb.res("DEC%d_%d" % (b, hf)) for hf in range(2)] for b in range(4)]
        STATC = 64
        rSTATc = [[kb.res("STATC%d_%d" % (b, c_)) for c_ in range(2)] for b in range(4)]
        ANEG = sbt("ANEG", [128, 32], F32); rANEG = kb.res("ANEG")
        SM = sbt("SM", [128, 4, 320], F32)
        rSM = [kb.res("SM%d" % b) for b in range(4)]
        STAT = sbt("STAT", [128, 256], F32)
        rSTATb = [kb.res("STAT%d" % b) for b in range(4)]
        EPSC = sbt("EPSC", [128, 2], F32); rEPSC = kb.res("EPSC")
        C_DT, C_ADT, C_ACUM, C_EA, C_CD, C_DEC, C_T0, C_RS = 0, 32, 64, 96, 128, 192, 224, 256
        SMH = sbt("SMH", [128, 4, 64], BF16)
        rSMH = [kb.res("SMH%d" % b) for b in range(4)]
        KBY = 1024
        AR_BYTES = 128 * KBY
        ARt = sbt("ARENA", [128, AR_BYTES // 4], F32)
        ar = Arena(kb, ARt, AR_BYTES)
        XT = [ar.view("XT%d" % i, 81 * KBY + i * 4 * KBY, [1024], F32) for i in range(4)]
        XS16 = [ar.view("XS16_%d" % i, 97 * KBY + i * 2 * KBY, [1024], BF16) for i in range(4)]
        JUNK4 = [ar.view("JUNK%d" % i, 105 * KBY + i * 2 * KBY, [1024], BF16) for i in range(4)]
        SZ = ar.view("SZ", 0, [4, 2048], BF16)
        XSF = ar.view("XSF", 16 * KBY, [16, 512], BF16)
        BF_ = ar.view("BF", 32 * KBY, [8, 512], BF16)
        CF = ar.view("CF", 40 * KBY, [8, 512], BF16)
        XBJ = [[ar.view("XB%d_%d" % (p_, jj), 48 * KBY + (p_ * 4 + jj) * 2064, [516], F32) for jj in range(4)]
               for p_ in range(3)]
        CVJ = [[ar.view("CV%d_%d" % (p_, jj), 73 * KBY + (p_ * 4 + jj) * 2048, [512], F32) for jj in range(4)]
               for p_ in range(2)]
        XStP = [ar.view("XSt%d" % i, 48 * KBY + i * 4 * KBY, [2048], BF16) for i in range(2)]
        BtP = [ar.view("Bt%d" % i, 56 * KBY + i * 2 * KBY, [1024], BF16) for i in range(2)]
        XCDh = [ar.view("XCD_%d" % hf, 60 * KBY + hf * 2 * KBY, [1024], BF16) for hf in range(2)]
        XCP = [ar.view("XC%d" % i, 64 * KBY + i * 4 * KBY, [2048], BF16) for i in range(2)]
        RHI = ar.view("RHI", 72 * KBY, [32, 64], BF16)
        RLO = ar.view("RLO", 76 * KBY, [32, 64], BF16)
        DEc = [ar.view("DE%d" % i, 80 * KBY + i * 4 * KBY, [16, 64], F32) for i in range(2)]
        MHc = [ar.view("MH%d" % i, 88 * KBY + i * 2 * KBY, [16, 64], BF16) for i in range(2)]
        T1c = [ar.view("T1_%d" % i, 92 * KBY + i * 4 * KBY, [1024], F32) for i in range(2)]
        Y1c = [ar.view("Y1_%d" % i, 100 * KBY + i * 4 * KBY, [1024], F32) for i in range(2)]
        YNc = [ar.view("YN_%d" % i, 108 * KBY + i * 2 * KBY, [1024], BF16) for i in range(2)]
        YNF = ar.view("YNF", 112 * KBY, [16, 512], BF16)
        ACJ = [ar.view("AC_%d" % jj, jj * 2048, [512], F32) for jj in range(4)]
        UJ = [ar.view("U_%d" % jj, 8 * KBY + jj * 2064, [516], F32) for jj in range(4)]
        TTJ = [ar.view("TT_%d" % jj, 17 * KBY + jj * 2048, [512], F32) for jj in range(4)]
        V = ar.view("V", 25 * KBY, [8, 512], BF16)
        GJ = [ar.view("G_%d" % jj, 33 * KBY + jj * 2048, [512], F32) for jj in range(4)]
        PA = ar.view("PA", 41 * KBY, [8, 512], F32)
        MG = ar.view("MG", 57 * KBY, [8, 512], BF16)
        MO = [ar.view("MO%d" % b, 65 * KBY + b * 4 * KBY, [1024], F32) for b in range(4)]
        XT2 = [ar.view("XT2_%d" % i, 81 * KBY + i * 4 * KBY, [1024], F32) for i in range(4)]
        HS16 = [ar.view("HS16_%d" % i, 97 * KBY + i * 2 * KBY, [1024], BF16) for i in range(4)]
        FF = ar.view("FF", 0, [32, 512], BF16)
        RT = [ar.view("RT%d" % i, 32 * KBY + i * 2 * KBY, [512], F32) for i in range(4)]
        FO = [ar.view("FO%d" % b, 40 * KBY + b * 4 * KBY, [1024], F32) for b in range(4)]
        SGW = 2824
        NSG = 4
        SGI = [ar.view("SGI%d" % i, i * 12 * KBY, [SGW], F32) for i in range(NSG)]
        SGO = [ar.view("SGO%d" % i, 48 * KBY + i * 6 * KBY, [SGW], BF16) for i in range(NSG)]
        PS = es.enter_context(nc.psum_tensor("PS", [128, 8, 512], F32))
        rB = [kb.res("bank%d" % i) for i in range(8)]
        for r_ in rB:
            r_.excl = True
        PSb = PS[:].bitcast(BF16)

        for r_ in [XSF[1], BF_[1], CF[1], V[1], PA[1], MG[1], FF[1], YNF[1], SZ[1], rXN] + [m_[1] for m_ in MO] + \
                [f_[1] for f_ in FO]:
            r_.disj = True
        ident = CB[:, CB_ID:CB_ID + 128]
        tri2 = CB[:, CB_TRI2:CB_TRI2 + 64]
        bones = CB[:, CB_BONES:CB_BONES + 128]
        onesA = CB[:, CB_ONESA:CB_ONESA + 128]
        onesB = CB[:, CB_ONESB:CB_ONESB + 128]
        tri128 = CB[:, CB_TRI128:CB_TRI128 + 128]
        negm = CB[:, CB_NEG:CB_NEG + 1024]

        dbg_names = []

        def dump(name, ap, res, shape, dt):
            if not dbg:
                return
            d = nc.dram_tensor("dbg_" + name, shape, dt, kind="ExternalOutput").ap()
            kb.dma("sp", d, ap, res, reads=[res])
            dbg_names.append("dbg_" + name)

        kb.dma("sp", PF[:], pf_d, rPF, writes=[rPF])
        kb.dma("sp", CB[:], cb_d, rCB, writes=[rCB])
        kb.op("pool", lambda e: e.memset(EPSC[:, 0:1], EPS), writes=[rEPSC])
        kb.op("pool", lambda e: e.memset(EPSC[:, 1:2], 1.0), writes=[rEPSC])
        kb.op("act", lambda e: e.activation(out=ANEG[:], in_=PF[:, PF_ALOG:PF_ALOG + 32], func=AF.Exp),
              reads=[rPF], writes=[rANEG])
        kb.op("dve", lambda e: e.tensor_scalar(out=ANEG[:], in0=ANEG[:], scalar1=-1.0, scalar2=None, op0=ALU.mult),
              reads=[rANEG], writes=[rANEG])

        conv_jobs = []
        def add_mat(src, dst, K, N, piece, scol):
            for rc in range(K // 128):
                for c0 in range(0, N, piece):
                    w = min(piece, N - c0)
                    conv_jobs.append((src[rc * 128:(rc + 1) * 128, c0:c0 + w], dst[rc * 128:(rc + 1) * 128, c0:c0 + w],
                                      None if scol is None else scol + rc, w))
        add_mat(w_in, wi_b, D, PROJ, SGW, PF_NMP)
        add_mat(w_a, wa_b, D, D, 1024, None)
        add_mat(w_s, ws_b, 2 * D, D, 1024, PF_SSN)
        add_mat(w_o, wo_b, D, D, 1024, None)
        add_mat(w_f1, wf1_b, D, 4 * D, 2048, PF_NFP)
        add_mat(w_f2, wf2_b, 4 * D, D, 1024, None)
        def conv_store(i):
            (src, dst, scol, w) = conv_jobs[i]
            so, ro = SGO[i % NSG]
            kb.dma("sp", dst, so[:, 0:w], ro, reads=[ro])
        for i, (src, dst, scol, w) in enumerate(conv_jobs):
            si, ri = SGI[i % NSG]
            so, ro = SGO[i % NSG]
            if i >= NSG - 1:
                conv_store(i - (NSG - 1))
            kb.dma("sp", si[:, 0:w], src, ri, writes=[ri])
            eng = "dve" if (i % 5) in (0, 2, 4) else "act"
            if eng == "dve":
                if scol is None:
                    kb.op(eng, lambda e, so=so, si=si, w=w: e.tensor_copy(out=so[:, 0:w], in_=si[:, 0:w]),
                          reads=[ri], writes=[ro])
                else:
                    kb.op(eng, lambda e, so=so, si=si, w=w, scol=scol: e.tensor_scalar(
                        out=so[:, 0:w], in0=si[:, 0:w], scalar1=PF[:, scol:scol + 1], scalar2=None, op0=ALU.mult),
                        reads=[ri, rPF], writes=[ro])
            else:
                if scol is None:
                    kb.op(eng, lambda e, so=so, si=si, w=w: e.copy(out=so[:, 0:w], in_=si[:, 0:w]),
                          reads=[ri], writes=[ro])
                else:
                    kb.op(eng, lambda e, so=so, si=si, w=w, scol=scol: e.activation(
                        out=so[:, 0:w], in_=si[:, 0:w], func=AF.Identity, scale=PF[:, scol:scol + 1]),
                        reads=[ri, rPF], writes=[ro])
        for i in range(max(0, len(conv_jobs) - (NSG - 1)), len(conv_jobs)):
            conv_store(i)
        for so, ro in SGO:
            kb.wait("sp", ("d", id(ro)), ro.dcnt)
        kb.dma("sp", DTW[:], wi_b[:, OFF_DT:OFF_DT + 32].rearrange("(kc p) n -> p kc n", p=128), rDTW, writes=[rDTW])

        tiles = []
        for s in range(2):
            for ti in range(4):
                tiles.append((s, xp[s, ti * 512:(ti + 1) * 512, :], yp[s, ti * 512:(ti + 1) * 512, :], 512,
                              ti == 0, ti == 3, None))
        for s in range(2):
            tiles.append((2 + s, xs[s], ys[s], 64, True, True, s))
        if tiles_limit is not None:
            tiles = [tiles[i] for i in tiles_limit]

        def panel_list(T):
            pl = []
            for q in range(8):
                pl.append((wi_b, 0, OFF_XBC + q * 512))
                if q % 2 == 1:
                    pl.append((wi_b, 0, OFF_Z + (q // 2) * 512))
            for hf in range(2):
                pl.append((wi_b, 0, OFF_AC + hf * 512))
                pl.append((wi_b, 0, OFF_AH + hf * 512))
                pl.append((wi_b, 0, OFF_AB + hf * 512))
            for hf in range(2):
                pl.append((wi_b, 0, OFF_GATE + hf * 512))
                pl.append((wa_b, 0, hf * 512))
            for hf in range(2):
                pl.append((wi_b, 0, OFF_GATE + 1024 + hf * 512))
                pl.append((ws_b, 0, hf * 512))
                pl.append((ws_b, 1, hf * 512))
            for hf in range(2):
                pl.append((wo_b, 0, hf * 512))
            for q in range(8):
                pl.append((wf1_b, 0, q * 512))
            for hf in range(2):
                for kg in range(4):
                    pl.append((wf2_b, kg, hf * 512))
            return pl
        all_panels = []
        for tl in tiles:
            all_panels += panel_list(tl[3])
        pstate = {"issued": 0, "used": 0}

        def issue_panels(upto):
            while pstate["issued"] < min(upto, len(all_panels)):
                i = pstate["issued"]
                mat, kg, c0 = all_panels[i]
                slot = i % NS
                src = mat[kg * 1024:(kg + 1) * 1024, c0:c0 + 512].rearrange("(kc p) n -> p kc n", p=128)
                kb.dma("sp", PAN[slot][:], src, rPAN[slot], writes=[rPAN[slot]])
                pstate["issued"] += 1

        def next_panel():
            i = pstate["used"]
            issue_panels(i + NS - 1)
            pstate["used"] += 1
            return PAN[i % NS], rPAN[i % NS]

        bank_rr = {"i": 0}

        def acc_bank():
            i = bank_rr["i"]
            bank_rr["i"] = (i + 1) % 4
            return i

        def mm_run(out, pairs, reads, writes, first=True, last=True):
            n = len(pairs)
            for i, (l, r) in enumerate(pairs):
                kb.op("pe", lambda e, l=l, r=r, st=(first and i == 0), sp_=(last and i == n - 1):
                      e.matmul(out, lhsT=l, rhs=r, start=st, stop=sp_),
                      reads=reads if i == 0 else (), writes=writes if i == 0 else (), inc=(i == n - 1))

        def tr_run(outs_ins, reads, writes, idn):
            n = len(outs_ins)
            for i, (o, a) in enumerate(outs_ins):
                kb.op("pe", lambda e, o=o, a=a: e.transpose(o, a, idn),
                      reads=reads if i == 0 else (), writes=writes if i == 0 else (), inc=(i == n - 1))

        def round_robin(gens):
            gens = list(gens)
            while gens:
                for g_ in list(gens):
                    try:
                        next(g_)
                    except StopIteration:
                        gens.remove(g_)

        def g_rstd(o, rst, ss_col, rs_col, n, tb, w=1):
            kb.op("act", lambda e: e.activation(out=STAT[0:tb, o + rs_col:o + rs_col + w], in_=STAT[0:tb, o + ss_col:o + ss_col + w],
                                                func=AF.Ln, scale=1.0 / n, bias=EPSC[0:tb, 0:1]),
                  reads=[rst, rEPSC], writes=[rst])
            yield
            kb.op("act", lambda e: e.activation(out=STAT[0:tb, o + rs_col:o + rs_col + w], in_=STAT[0:tb, o + rs_col:o + rs_col + w],
                                                func=AF.Exp, scale=-0.5),
                  reads=[rst], writes=[rst])
            yield

        def round_robin_off(gens):
            live = list(gens)
            step = 0
            while live:
                for item in list(live):
                    if item[1] <= step:
                        try:
                            next(item[0])
                        except StopIteration:
                            live.remove(item)
                step += 1

        def g_norm_to_fm(src_ap, src_res, b, tb, s16, dst, dst_res):
            (s16ap, s16res) = s16
            o = 16 * b
            rst = rSTATb[b]
            jk, rjk = JUNK4[b]
            kb.op("act", lambda e: e.activation(out=jk[0:tb, :], in_=src_ap[0:tb, :], func=AF.Square,
                                                accum_out=STAT[0:tb, o:o + 1]),
                  reads=[src_res], writes=[rjk, rst])
            yield
            yield from g_rstd(o, rst, 0, 1, 1024, tb)
            kb.op("dve", lambda e: e.tensor_scalar(out=s16ap[0:tb, :], in0=src_ap[0:tb, :], scalar1=STAT[0:tb, o + 1:o + 2],
                                                   scalar2=None, op0=ALU.mult),
                  reads=[src_res, rst], writes=[s16res])
            yield
            tpb = 4 + b
            tr_run([(PSb[:, tpb, kc * 128:kc * 128 + tb], s16ap[0:tb, kc * 128:(kc + 1) * 128]) for kc in range(8)],
                   reads=[s16res, rCB], writes=[rB[tpb]], idn=ident[0:tb, 0:tb])
            yield
            kb.op("act", lambda e: e.copy(out=dst[:, :, b * 128:b * 128 + tb],
                                          in_=PSb[:, tpb, :].rearrange("p (k t) -> p k t", k=8)[:, :, 0:tb]),
                  reads=[rB[tpb]], writes=[dst_res])
            yield

        def gate(n):
            if stop_after is not None and n > stop_after:
                raise _Stop()

        p0_done = set()

        def p0_gens(ti):
            (_sidx, xd_, _yd, T_, _f, _l, _s) = tiles[ti]
            tb_ = min(T_, 128)

            def g_p0(b):
                xt, rxt = XT[b]
                kb.dma("sp", xt[0:tb_, :], xd_[b * 128:b * 128 + tb_, :], rxt, writes=[rxt])
                yield from g_norm_to_fm(xt, rxt, b, tb_, XS16[b], XN, rXN)
            p0_done.add(ti)
            return [g_p0(b) for b in range(T_ // tb_)]

        def run_tile(ti, sidx, xd, yd, T, first, last, samp):
            tb = min(T, 128)
            nb = T // tb
            nch = tb // 64
            if first:
                if samp is None:
                    kb.op("pool", lambda e: e.memset(HISTA[:], 0.0), writes=[rHISTA])
                    kb.op("pool", lambda e: e.memset(HISTM[:], 0.0), writes=[rHISTM])
                    kb.op("pool", lambda e: e.memset(ST[:], 0.0), writes=[rSTh[0], rSTh[1]])
                else:
                    kb.dma("sp", HISTA[:], sca[samp].rearrange("p (j i) -> p j i", i=2), rHISTA, writes=[rHISTA])
                    kb.dma("sp", HISTM[:], scm[samp].rearrange("p (c i) -> p c i", i=3), rHISTM, writes=[rHISTM])
                    kb.dma("sp", ST[:], sst[samp], rSTh[0], writes=[rSTh[0], rSTh[1]])
                kb.op("act", lambda e: e.copy(out=SBF[0][:], in_=ST[:]), reads=[rSTh[0], rSTh[1]],
                      writes=[rSBFh[0][0], rSBFh[0][1]])

            if ti not in p0_done:
                round_robin(p0_gens(ti))

            gate(1)
            def b1_panel(q):
                pan, rpan = next_panel()
                for b in range(nb):
                    bk = acc_bank()
                    mm_run(PS[0:tb, bk, :], [(XN[:, kc, b * 128:b * 128 + tb], pan[:, kc, :]) for kc in range(8)],
                           reads=[rXN, rpan], writes=[rB[bk]])
                    kb.op("act", lambda e, bk=bk, b=b, q=q: e.activation(
                        out=SZ[0][0:tb, b, q * 512:(q + 1) * 512], in_=PS[0:tb, bk, :], func=AF.Silu),
                        reads=[rB[bk]], writes=[SZ[1]])
            gate(2)
            def b2_s1(q):
                pan, rpan = next_panel()
                xb = XBJ[q % 3]
                for jj in range(4):
                    c = q * 4 + jj
                    kb.op("pool", lambda e, jj=jj, c=c, xb=xb: e.tensor_copy(out=xb[jj][0][:, 0:3], in_=HISTM[:, c, :]),
                          reads=[rHISTM], writes=[xb[jj][1]])
                    bk = acc_bank()
                    mm_run(PS[:, bk, 0:T], [(pan[:, kc, jj * 128:(jj + 1) * 128], XN[:, kc, 0:T]) for kc in range(8)],
                           reads=[rXN, rpan], writes=[rB[bk]])
                    kb.op("act", lambda e, bk=bk, jj=jj, xb=xb: e.copy(out=xb[jj][0][:, 3:3 + T], in_=PS[:, bk, 0:T]),
                          reads=[rB[bk]], writes=[xb[jj][1]])
                    kb.op("pool", lambda e, jj=jj, c=c, xb=xb: e.tensor_copy(out=HISTM[:, c, :], in_=xb[jj][0][:, T:T + 3]),
                          reads=[xb[jj][1]], writes=[rHISTM])

            def b2_s2(q):
                xb = XBJ[q % 3]
                cv = CVJ[q % 2]
                for jj in range(4):
                    c = q * 4 + jj
                    kb.op("act", lambda e, jj=jj, c=c, xb=xb, cv=cv: e.activation(
                        out=cv[jj][0][:, 0:T], in_=xb[jj][0][:, 0:T], func=AF.Identity,
                        scale=PF[:, PF_CSW + c * 4:PF_CSW + c * 4 + 1]),
                        reads=[xb[jj][1], rPF], writes=[cv[jj][1]])
                for i in (1, 2, 3):
                    for jj in range(4):
                        c = q * 4 + jj
                        kb.op("dve", lambda e, jj=jj, c=c, i=i, xb=xb, cv=cv: e.scalar_tensor_tensor(
                            out=cv[jj][0][:, 0:T], in0=xb[jj][0][:, i:i + T],
                            scalar=PF[:, PF_CSW + c * 4 + i:PF_CSW + c * 4 + i + 1], in1=cv[jj][0][:, 0:T],
                            op0=ALU.mult, op1=ALU.add),
                            reads=[xb[jj][1], cv[jj][1], rPF], writes=[cv[jj][1]])

            def b2_s3(q):
                cv = CVJ[q % 2]
                for jj in range(4):
                    c = q * 4 + jj
                    if c < 16:
                        dst, dres = XSF[0][:, c, 0:T], XSF[1]
                    elif c < 24:
                        dst, dres = BF_[0][:, c - 16, 0:T], BF_[1]
                    else:
                        dst, dres = CF[0][:, c - 24, 0:T], CF[1]
                    kb.op("act", lambda e, jj=jj, c=c, dst=dst, cv=cv: e.activation(
                        out=dst, in_=cv[jj][0][:, 0:T], func=AF.Silu, bias=PF[:, PF_CSB + c:PF_CSB + c + 1]),
                        reads=[cv[jj][1], rPF], writes=[dres])

            for q in range(8 + 2):
                if q < 8:
                    b2_s1(q)
                if 1 <= q <= 8:
                    b2_s2(q - 1)
                if q < 8 and q % 2 == 1:
                    b1_panel(q // 2)
                if q >= 2:
                    b2_s3(q - 2)
            if last:
                kb.dma("sp", ocm[sidx].rearrange("p (c i) -> p c i", i=3), HISTM[:], rHISTM, reads=[rHISTM])

            gate(3)
            def g_dt(b):
                sm = SM[:, b, :]
                rsm = rSM[b]
                smh = SMH[:, b, :]
                rsmh = rSMH[b]
                tok = slice(b * 128, b * 128 + tb)
                bk = acc_bank()
                mm_run(PS[0:tb, bk, 0:32], [(XN[:, kc, tok], DTW[:, kc, :]) for kc in range(8)],
                       reads=[rXN, rDTW], writes=[rB[bk]])
                yield
                kb.op("dve", lambda e: e.tensor_tensor(out=sm[0:tb, C_T0:C_T0 + 32], in0=PS[0:tb, bk, 0:32],
                                                       in1=PF[0:tb, PF_DTB:PF_DTB + 32], op=ALU.add),
                      reads=[rB[bk], rPF], writes=[rsm])
                yield
                kb.op("act", lambda e: e.activation(out=sm[0:tb, C_T0:C_T0 + 32], in_=sm[0:tb, C_T0:C_T0 + 32],
                                                    func=AF.Exp), reads=[rsm], writes=[rsm])
                yield
                kb.op("act", lambda e: e.activation(out=sm[0:tb, C_DT:C_DT + 32], in_=sm[0:tb, C_T0:C_T0 + 32],
                                                    func=AF.Ln, bias=EPSC[0:tb, 1:2]), reads=[rsm, rEPSC], writes=[rsm])
                yield
                kb.op("dve", lambda e: e.tensor_tensor(out=sm[0:tb, C_ADT:C_ADT + 32], in0=sm[0:tb, C_DT:C_DT + 32],
                                                       in1=ANEG[0:tb, :], op=ALU.mult),
                      reads=[rsm, rANEG], writes=[rsm])
                yield
                kb.op("dve", lambda e: e.tensor_copy(out=smh[0:tb, 0:32], in_=sm[0:tb, C_ADT:C_ADT + 32]),
                      reads=[rsm], writes=[rsmh])
                yield
                kb.op("dve", lambda e: e.tensor_tensor(out=smh[0:tb, 32:64], in0=sm[0:tb, C_ADT:C_ADT + 32],
                                                       in1=smh[0:tb, 0:32], op=ALU.subtract),
                      reads=[rsm, rsmh], writes=[rsmh])
                yield
                bk2 = acc_bank()
                mm_run(PS[0:tb, bk2, 0:32], [(tri128[0:tb, 0:tb], smh[0:tb, 0:32]), (tri128[0:tb, 0:tb], smh[0:tb, 32:64])],
                       reads=[rsmh, rCB], writes=[rB[bk2]])
                mm_run(PS[:, bk2, 32:64], [(onesA[0:tb, :], smh[0:tb, 0:32]), (onesA[0:tb, :], smh[0:tb, 32:64])],
                       reads=[rsmh, rCB], writes=[rB[bk2]])
                if nch == 2:
                    mm_run(PS[:, bk2, 64:96], [(onesB[0:tb, :], smh[0:tb, 0:32]), (onesB[0:tb, :], smh[0:tb, 32:64])],
                           reads=[rsmh, rCB], writes=[rB[bk2]])
                yield
                kb.op("dve", lambda e: e.tensor_copy(out=sm[0:tb, C_ACUM:C_ACUM + 32], in_=PS[0:tb, bk2, 0:32]),
                      reads=[rB[bk2]], writes=[rsm])
                yield
                kb.op("act", lambda e: e.activation(out=sm[0:tb, C_EA:C_EA + 32], in_=PS[0:tb, bk2, 0:32],
                                                    func=AF.Exp), reads=[rB[bk2]], writes=[rsm])
                kb.op("act", lambda e: e.activation(out=sm[0:VARP, C_CD:C_CD + 32 * nch],
                                                    in_=PS[0:VARP, bk2, 32:32 + 32 * nch], func=AF.Exp),
                      reads=[rB[bk2]], writes=[rsm])
                yield
            round_robin([g_dt(b) for b in range(nb)])

            def g_prep_banks():
                return (4, 5), (6, 7)

            def g_prep(b):
                sm = SM[:, b, :]
                rsm = rSM[b]
                smh = SMH[:, b, :]
                rsmh = rSMH[b]
                tok = slice(b * 128, b * 128 + tb)
                xst, rxst = XStP[b % 2]
                btt, rbtt = BtP[b % 2]
                xc, rxc = XCP[b % 2]
                p, p2 = g_prep_banks()
                tr_run([(PSb[0:tb, p[c // 8], (c % 8) * 128:(c % 8 + 1) * 128], XSF[0][:, c, tok]) for c in range(16)],
                       reads=[XSF[1], rCB], writes=[rB[p[0]], rB[p[1]]], idn=ident)
                yield
                kb.op("act", lambda e: e.copy(out=xst[0:tb, :], in_=PSb[0:tb, p[0]:p[1] + 1, :].rearrange("p a n -> p (a n)")),
                      reads=[rB[p[0]], rB[p[1]]], writes=[rxst])
                yield
                kb.op("dve", lambda e: e.tensor_tensor(
                    out=xc[0:tb, :].rearrange("p (h d) -> p h d", h=32),
                    in0=PSb[0:tb, p[0]:p[1] + 1, :].rearrange("p a (h d) -> p (a h) d", d=64),
                    in1=sm[0:tb, C_DT:C_DT + 32].unsqueeze(2).broadcast_to([tb, 32, 64]), op=ALU.mult),
                    reads=[rB[p[0]], rB[p[1]], rsm], writes=[rxc])
                yield
                tr_run([(PSb[0:tb, p2[0], g * 128:(g + 1) * 128], BF_[0][:, g, tok]) for g in range(8)],
                       reads=[BF_[1], rCB], writes=[rB[p2[0]]], idn=ident)
                yield
                kb.op("act", lambda e: e.copy(out=btt[0:tb, :], in_=PSb[0:tb, p2[0], :]), reads=[rB[p2[0]]], writes=[rbtt])
                yield
                for (rr_, lo) in ((RHI, 0), (RLO, 32)):
                    kb.op("dve", lambda e, rr_=rr_, lo=lo: e.tensor_tensor(
                        out=rr_[0][0:tb, :, :], in0=smh[0:tb, lo:lo + 32].unsqueeze(2).broadcast_to([tb, 32, 64]),
                        in1=tri2[0:tb, :].unsqueeze(1).broadcast_to([tb, 32, 64]), op=ALU.mult),
                        reads=[rsmh, rCB], writes=[rr_[1]])
                    yield

            def g_chain(b, hf):
                c_ = hf
                sm = SM[:, b, :]
                rsm = rSM[b]
                tok0 = b * 128
                hs = slice(hf * 16, hf * 16 + 16)
                xst, rxst = XStP[b % 2]
                btt, rbtt = BtP[b % 2]
                xcd, rxcd = XCDh[hf]
                xc, rxc = XCP[b % 2]
                mypair = (2 * c_, 2 * c_ + 1)

                def scratch():
                    return mypair
                de, rde = DEc[c_]
                mh, rmh = MHc[c_]
                t1, rt1 = T1c[c_]
                y1, ry1 = Y1c[c_]
                yn, ryn = YNc[c_]
                dec = DECT[:, b, hs]
                rdec = rDECT[b][hf]
                yo = (4, 5) if c_ == 0 else (6, 7)
                stc = STATC + 32 * b + 8 * c_
                rstc = rSTATc[b][c_]
                p = scratch()
                for nn in range(2):
                    hh = slice(hf * 16 + nn * 8, hf * 16 + nn * 8 + 8)
                    mm_run(PS[0:tb, p[nn], :],
                           [(bones[0:tb, 0:tb], RHI[0][0:tb, hh, :]), (bones[0:tb, 0:tb], RLO[0][0:tb, hh, :]),
                            (ident[0:tb, 0:tb], negm[0:tb, nn * 512:(nn + 1) * 512])],
                           reads=[RHI[1], RLO[1], rCB], writes=[rB[p[nn]]])
                yield
                kb.op("dve", lambda e: e.tensor_tensor(
                    out=de[0:tb, :, :], in0=PS[0:tb, p[0]:p[1] + 1, :].rearrange("p a (h l) -> p (a h) l", l=64),
                    in1=sm[0:tb, C_ACUM + hs.start:C_ACUM + hs.stop].unsqueeze(2).broadcast_to([tb, 16, 64]),
                    op=ALU.subtract), reads=[rB[p[0]], rB[p[1]], rsm], writes=[rde])
                pc = scratch()
                for g in range(4):
                    gg = hf * 4 + g
                    for ch in range(nch):
                        r0 = ch * 64
                        cs = slice(tok0 + r0, tok0 + r0 + 64)
                        mm_run(PS[r0:r0 + 64, pc[0], g * 64:(g + 1) * 64], [(BF_[0][:, gg, cs], CF[0][:, gg, cs])],
                               reads=[BF_[1], CF[1]], writes=[rB[pc[0]]])
                yield
                kb.op("act", lambda e: e.activation(out=de[0:tb, :, :], in_=de[0:tb, :, :], func=AF.Exp),
                      reads=[rde], writes=[rde])
                yield
                kb.op("pool", lambda e: e.tensor_copy(out=dec[0:tb, :].unsqueeze(2), in_=de[0:tb, :, 63:64]),
                      reads=[rde], writes=[rdec])
                kb.op("dve", lambda e: e.tensor_tensor(
                    out=mh[0:tb, :, :].rearrange("p (g j) l -> p g j l", g=4),
                    in0=de[0:tb, :, :].rearrange("p (g j) l -> p g j l", g=4),
                    in1=PS[0:tb, pc[0], 0:256].rearrange("p (g l) -> p g l", g=4).unsqueeze(2).broadcast_to([tb, 4, 4, 64]),
                    op=ALU.mult), reads=[rde, rB[pc[0]]], writes=[rmh])
                yield
                kb.op("dve", lambda e: e.tensor_tensor(
                    out=xcd[0:tb, :].rearrange("p (h d) -> p h d", h=16),
                    in0=xc[0:tb, hf * 1024:(hf + 1) * 1024].rearrange("p (h d) -> p h d", h=16),
                    in1=dec[0:tb, :].unsqueeze(2).broadcast_to([tb, 16, 64]),
                    op=ALU.mult), reads=[rxc, rdec], writes=[rxcd])
                yield
                pd = scratch()
                for ch in range(nch):
                    r0 = ch * 64
                    for hl in range(16):
                        h = hf * 16 + hl
                        mm_run(PS[r0:r0 + 64, pd[hl // 8], (hl % 8) * 64:(hl % 8 + 1) * 64],
                               [(mh[r0:r0 + 64, hl, :], xc[r0:r0 + 64, h * 64:(h + 1) * 64])],
                               reads=[rmh, rxc], writes=[rB[pd[0]], rB[pd[1]]])
                yield
                yds = de[0:tb, :, :].rearrange("p h l -> p (h l)")
                kb.op("act", lambda e: e.copy(out=yds, in_=PS[0:tb, pd[0]:pd[1] + 1, :].rearrange("p a n -> p (a n)")),
                      reads=[rB[pd[0]], rB[pd[1]]], writes=[rde])
                yield
                for ch in range(nch):
                    r0 = ch * 64
                    cs = slice(tok0 + r0, tok0 + r0 + 64)
                    sb_in, rsb_in = SBF[ch % 2], rSBFh[ch % 2][hf]
                    if nch == 2:
                        sb_out, rsb_out = SBF[(ch + 1) % 2], rSBFh[(ch + 1) % 2][hf]
                    else:
                        sb_out, rsb_out = SBF[0], rSBFh[0][hf]
                    for g in range(4):
                        gg = hf * 4 + g
                        mm_run(PS[r0:r0 + 64, yo[g // 2], (g % 2) * 256:(g % 2 + 1) * 256],
                               [(CF[0][:, gg, cs], sb_in[:, gg * 256:(gg + 1) * 256])],
                               reads=[CF[1], rsb_in], writes=[rB[yo[0]], rB[yo[1]]])
                    ps_ = scratch()
                    for g in range(4):
                        gg = hf * 4 + g
                        mm_run(PS[:, ps_[g // 2], (g % 2) * 256:(g % 2 + 1) * 256],
                               [(btt[r0:r0 + 64, gg * 128:(gg + 1) * 128], xcd[r0:r0 + 64, g * 256:(g + 1) * 256])],
                               reads=[rbtt, rxcd], writes=[rB[ps_[0]], rB[ps_[1]]])
                    yield
                    stv = ST[:, hf * 1024:(hf + 1) * 1024]
                    kb.op("dve", lambda e, ch=ch: e.tensor_tensor(
                        out=t1[:, :].rearrange("p (h d) -> p h d", h=16), in0=stv.rearrange("p (h d) -> p h d", h=16),
                        in1=sm[:, C_CD + ch * 32 + hs.start:C_CD + ch * 32 + hs.stop].unsqueeze(2).broadcast_to([128, 16, 64]),
                        op=ALU.mult), reads=[rSTh[hf], rsm], writes=[rt1])
                    yield
                    kb.op("dve", lambda e, ps_=ps_: e.tensor_tensor(
                        out=stv, in0=PS[:, ps_[0]:ps_[1] + 1, :].rearrange("p a n -> p (a n)"), in1=t1[:, :], op=ALU.add),
                        reads=[rB[ps_[0]], rB[ps_[1]], rt1], writes=[rSTh[hf]])
                    yield
                    kb.op("act", lambda e, sb_out=sb_out: e.copy(out=sb_out[:, hf * 1024:(hf + 1) * 1024], in_=stv),
                          reads=[rSTh[hf]], writes=[rsb_out])
                    yield
                kb.op("dve", lambda e: e.tensor_tensor(
                    out=y1[0:tb, :].rearrange("p (h d) -> p h d", h=16),
                    in0=PS[0:tb, yo[0]:yo[1] + 1, :].rearrange("p a (h d) -> p (a h) d", d=64),
                    in1=sm[0:tb, C_EA + hs.start:C_EA + hs.stop].unsqueeze(2).broadcast_to([tb, 16, 64]),
                    op=ALU.mult), reads=[rB[yo[0]], rB[yo[1]], rsm], writes=[ry1])
                yield
                kb.op("dve", lambda e: e.tensor_tensor(out=y1[0:tb, :], in0=y1[0:tb, :], in1=yds, op=ALU.add),
                      reads=[ry1, rde], writes=[ry1])
                yield
                kb.op("dve", lambda e: e.tensor_tensor(
                    out=t1[0:tb, :].rearrange("p (h d) -> p h d", h=16),
                    in0=xst[0:tb, hf * 1024:(hf + 1) * 1024].rearrange("p (h d) -> p h d", h=16),
                    in1=PF[0:tb, PF_DSK + hs.start:PF_DSK + hs.stop].unsqueeze(2).broadcast_to([tb, 16, 64]),
                    op=ALU.mult), reads=[rxst, rPF], writes=[rt1])
                yield
                kb.op("dve", lambda e: e.tensor_tensor(out=t1[0:tb, :], in0=t1[0:tb, :], in1=y1[0:tb, :], op=ALU.add),
                      reads=[rt1, ry1], writes=[rt1])
                yield
                kb.op("dve", lambda e: e.tensor_tensor(
                    out=y1[0:tb, :], in0=t1[0:tb, :], in1=SZ[0][0:tb, b, hf * 1024:(hf + 1) * 1024], op=ALU.mult),
                    reads=[rt1, SZ[1]], writes=[ry1])
                yield
                if hf == 0 and b == 0:
                    dump("yg0", y1[0:tb, :], ry1, [tb, 1024], F32)
                    dump("yds0", yds, rde, [tb, 1024], F32)
                for g in range(4):
                    kb.op("act", lambda e, g=g: e.activation(
                        out=t1[0:tb, g * 256:(g + 1) * 256], in_=y1[0:tb, g * 256:(g + 1) * 256],
                        func=AF.Square, accum_out=STAT[0:tb, stc + g:stc + g + 1]),
                        reads=[ry1], writes=[rt1, rstc])
                yield
                yield from g_rstd(stc, rstc, 0, 4, 256, tb, w=4)
                for g in range(4):
                    kb.op("act", lambda e, g=g: e.activation(
                        out=yn[0:tb, g * 256:(g + 1) * 256], in_=y1[0:tb, g * 256:(g + 1) * 256],
                        func=AF.Identity, scale=STAT[0:tb, stc + 4 + g:stc + 5 + g]),
                        reads=[ry1, rstc], writes=[ryn])
                yield
                pt = scratch()
                tr_run([(PSb[:, pt[0], c * 128:c * 128 + tb], yn[0:tb, c * 128:(c + 1) * 128]) for c in range(8)],
                       reads=[ryn, rCB], writes=[rB[pt[0]]], idn=ident[0:tb, 0:tb])
                yield
                kb.op("act", lambda e: e.copy(
                    out=YNF[0][:, hf * 8:(hf + 1) * 8, b * 128:b * 128 + tb],
                    in_=PSb[:, pt[0], :].rearrange("p (k t) -> p k t", k=8)[:, :, 0:tb]), reads=[rB[pt[0]]], writes=[YNF[1]])
                yield

            for _ in g_prep(0):
                pass
            for b in range(nb):
                gens = []
                if b + 1 < nb:
                    gens.append((g_prep(b + 1), 0))
                gens += [(g_chain(b, 0), 0), (g_chain(b, 1), CH_OFF)]
                if os.environ.get('SEQ'):
                    for g_, _o in gens:
                        for _ in g_:
                            pass
                else:
                    round_robin_off(gens)
            if last:
                kb.dma("sp", ost[sidx], ST[:], rSTh[0], reads=[rSTh[0], rSTh[1]])
            dump("xn", XN[:, :, 0:T], rXN, [128, 8, T], BF16)
            dump("ynf", YNF[0][:, :, 0:T], YNF[1], [128, 16, T], BF16)

            gate(4)
            def fm_group(pan, rpan, jj, act, ract):
                bk = acc_bank()
                mm_run(PS[:, bk, 0:T], [(pan[:, kc, jj * 128:(jj + 1) * 128], act[:, kc, 0:T]) for kc in range(8)],
                       reads=[ract, rpan], writes=[rB[bk]])
                return bk

            for hf in range(2):
                pan, rpan = next_panel()
                for jj in range(4):
                    bk = fm_group(pan, rpan, jj, XN, rXN)
                    kb.op("act", lambda e, bk=bk, jj=jj: e.copy(out=ACJ[jj][0][:, 0:T], in_=PS[:, bk, 0:T]),
                          reads=[rB[bk]], writes=[ACJ[jj][1]])
                pan, rpan = next_panel()
                for jj in range(4):
                    j = hf * 4 + jj
                    kb.op("pool", lambda e, jj=jj, j=j: e.tensor_copy(out=UJ[jj][0][:, 0:2], in_=HISTA[:, j, :]),
                          reads=[rHISTA], writes=[UJ[jj][1]])
                    bk = fm_group(pan, rpan, jj, XN, rXN)
                    kb.op("dve", lambda e, bk=bk, jj=jj: e.tensor_tensor(out=UJ[jj][0][:, 2:2 + T], in0=PS[:, bk, 0:T],
                                                                        in1=ACJ[jj][0][:, 0:T], op=ALU.mult),
                          reads=[rB[bk], ACJ[jj][1]], writes=[UJ[jj][1]])
                    kb.op("pool", lambda e, jj=jj, j=j: e.tensor_copy(out=HISTA[:, j, :], in_=UJ[jj][0][:, T:T + 2]),
                          reads=[UJ[jj][1]], writes=[rHISTA])
                for jj in range(4):
                    j = hf * 4 + jj
                    kb.op("act", lambda e, jj=jj, j=j: e.activation(
                        out=TTJ[jj][0][:, 0:T], in_=UJ[jj][0][:, 0:T], func=AF.Identity,
                        scale=PF[:, PF_CAW + j * 3:PF_CAW + j * 3 + 1]), reads=[UJ[jj][1], rPF], writes=[TTJ[jj][1]])
                pan, rpan = next_panel()
                bks = [fm_group(pan, rpan, jj, XN, rXN) for jj in range(4)]
                for i in (1, 2):
                    for jj in range(4):
                        j = hf * 4 + jj
                        kb.op("dve", lambda e, jj=jj, j=j, i=i: e.scalar_tensor_tensor(
                            out=TTJ[jj][0][:, 0:T], in0=UJ[jj][0][:, i:i + T],
                            scalar=PF[:, PF_CAW + j * 3 + i:PF_CAW + j * 3 + i + 1], in1=TTJ[jj][0][:, 0:T],
                            op0=ALU.mult, op1=ALU.add), reads=[UJ[jj][1], TTJ[jj][1], rPF], writes=[TTJ[jj][1]])
                for jj in range(4):
                    j = hf * 4 + jj
                    bk = bks[jj]
                    kb.op("dve", lambda e, bk=bk, jj=jj, j=j: e.tensor_tensor(out=V[0][:, j, 0:T], in0=PS[:, bk, 0:T],
                                                                             in1=TTJ[jj][0][:, 0:T], op=ALU.mult),
                          reads=[rB[bk], TTJ[jj][1]], writes=[V[1]])
            if last:
                kb.dma("sp", oca[sidx].rearrange("p (j i) -> p j i", i=2), HISTA[:], rHISTA, reads=[rHISTA])
            gate(5)
            dump("v", V[0][:, :, 0:T], V[1], [128, 8, T], BF16)
            for hf in range(2):
                pan, rpan = next_panel()
                for jj in range(4):
                    j = hf * 4 + jj
                    bk = fm_group(pan, rpan, jj, XN, rXN)
                    kb.op("act", lambda e, bk=bk, jj=jj, j=j: e.activation(
                        out=GJ[jj][0][:, 0:T], in_=PS[:, bk, 0:T], func=AF.Sigmoid, bias=PF[:, PF_BG + j:PF_BG + j + 1]),
                        reads=[rB[bk], rPF], writes=[GJ[jj][1]])
                pan, rpan = next_panel()
                for jj in range(4):
                    j = hf * 4 + jj
                    bk = fm_group(pan, rpan, jj, V[0], V[1])
                    kb.op("dve", lambda e, bk=bk, jj=jj, j=j: e.tensor_tensor(out=PA[0][:, j, 0:T], in0=PS[:, bk, 0:T],
                                                                             in1=GJ[jj][0][:, 0:T], op=ALU.mult),
                          reads=[rB[bk], GJ[jj][1]], writes=[PA[1]])
            dump("pa", PA[0][:, :, 0:T], PA[1], [128, 8, T], F32)
            for hf in range(2):
                pan, rpan = next_panel()
                for jj in range(4):
                    j = hf * 4 + jj
                    bk = fm_group(pan, rpan, jj, XN, rXN)
                    kb.op("act", lambda e, bk=bk, jj=jj, j=j: e.activation(
                        out=GJ[jj][0][:, 0:T], in_=PS[:, bk, 0:T], func=AF.Sigmoid,
                        bias=PF[:, PF_BG + 8 + j:PF_BG + 8 + j + 1]), reads=[rB[bk], rPF], writes=[GJ[jj][1]])
                pan0, rpan0 = next_panel()
                pan1, rpan1 = next_panel()
                for jj in range(4):
                    j = hf * 4 + jj
                    bk = acc_bank()
                    pairs = [(pan0[:, kc, jj * 128:(jj + 1) * 128], YNF[0][:, kc, 0:T]) for kc in range(8)] + \
                            [(pan1[:, kc, jj * 128:(jj + 1) * 128], YNF[0][:, 8 + kc, 0:T]) for kc in range(8)]
                    mm_run(PS[:, bk, 0:T], pairs, reads=[YNF[1], rpan0, rpan1], writes=[rB[bk]])
                    kb.op("dve", lambda e, bk=bk, jj=jj: e.tensor_tensor(out=GJ[jj][0][:, 0:T], in0=PS[:, bk, 0:T],
                                                                        in1=GJ[jj][0][:, 0:T], op=ALU.mult),
                          reads=[rB[bk], GJ[jj][1]], writes=[GJ[jj][1]])
                    kb.op("dve", lambda e, jj=jj, j=j: e.tensor_tensor(out=MG[0][:, j, 0:T], in0=GJ[jj][0][:, 0:T],
                                                                      in1=PA[0][:, j, 0:T], op=ALU.add),
                          reads=[GJ[jj][1], PA[1]], writes=[MG[1]])
            gate(6)
            dump("mg", MG[0][:, :, 0:T], MG[1], [128, 8, T], BF16)
            for hf in range(2):
                pan, rpan = next_panel()
                for b in range(nb):
                    bk = acc_bank()
                    mm_run(PS[0:tb, bk, :], [(MG[0][:, kc, b * 128:b * 128 + tb], pan[:, kc, :]) for kc in range(8)],
                           reads=[MG[1], rpan], writes=[rB[bk]])
                    kb.op("act", lambda e, bk=bk, b=b, hf=hf: e.copy(out=MO[b][0][0:tb, hf * 512:(hf + 1) * 512],
                                                                    in_=PS[0:tb, bk, :]), reads=[rB[bk]], writes=[MO[b][1]])
            def g_post(b, src, wcol, addsrc):
                (sap, sres) = src
                (aap, ares) = addsrc
                o = 16 * b
                rst = rSTATb[b]
                jk, rjk = JUNK4[b]
                kb.op("act", lambda e: e.activation(out=jk[0:tb, :], in_=sap[0:tb, :], func=AF.Square,
                                                    accum_out=STAT[0:tb, o:o + 1]), reads=[sres], writes=[rjk, rst])
                yield
                yield from g_rstd(o, rst, 0, 1, 1024, tb)
                kb.op("dve", lambda e: e.scalar_tensor_tensor(
                    out=sap[0:tb, :], in0=sap[0:tb, :], scalar=STAT[0:tb, o + 1:o + 2], in1=PF[0:tb, wcol:wcol + 1024],
                    op0=ALU.mult, op1=ALU.mult), reads=[sres, rst, rPF], writes=[sres])
                yield
                kb.op("dve", lambda e: e.tensor_tensor(out=sap[0:tb, :], in0=sap[0:tb, :], in1=aap[0:tb, :], op=ALU.add),
                      reads=[sres, ares], writes=[sres])
                yield

            def g_wo(b):
                xt, rxt = XT2[b]
                kb.dma("sp", xt[0:tb, :], xd[b * 128:b * 128 + tb, :], rxt, writes=[rxt])
                yield from g_post(b, MO[b], PF_NMPOST, XT2[b])
                yield from g_norm_to_fm(MO[b][0], MO[b][1], b, tb, HS16[b], XN, rXN)
            round_robin([g_wo(b) for b in range(nb)])
            gate(7)
            dump("h", MO[0][0][0:tb, :], MO[0][1], [tb, 1024], F32)
            for q in range(8):
                pan, rpan = next_panel()
                for jj in range(4):
                    c = q * 4 + jj
                    bk = acc_bank()
                    mm_run(PS[:, bk, 0:T], [(pan[:, kc, jj * 128:(jj + 1) * 128], XN[:, kc, 0:T]) for kc in range(8)],
                           reads=[rXN, rpan], writes=[rB[bk]])
                    rt, rrt = RT[c % 4]
                    kb.op("act", lambda e, bk=bk, rt=rt: e.activation(out=rt[:, 0:T], in_=PS[:, bk, 0:T], func=AF.Relu),
                          reads=[rB[bk]], writes=[rrt])
                    kb.op("dve", lambda e, rt=rt, c=c: e.tensor_tensor(out=FF[0][:, c, 0:T], in0=rt[:, 0:T], in1=rt[:, 0:T],
                                                                      op=ALU.mult), reads=[rrt], writes=[FF[1]])
            nxt = p0_gens(ti + 1) if (ti + 1 < len(tiles) and stop_after is None) else []
            for hf in range(2):
                for kg in range(4):
                    pan, rpan = next_panel()
                    for b in range(nb):
                        mm_run(PS[0:tb, b, :],
                               [(FF[0][:, kg * 8 + kc, b * 128:b * 128 + tb], pan[:, kc, :]) for kc in range(8)],
                               reads=[FF[1], rpan], writes=[rB[b]], first=(kg == 0), last=(kg == 3))
                    for g_ in list(nxt):
                        try:
                            next(g_)
                        except StopIteration:
                            nxt.remove(g_)
                for b in range(nb):
                    kb.op("act", lambda e, b=b, hf=hf: e.copy(out=FO[b][0][0:tb, hf * 512:(hf + 1) * 512], in_=PS[0:tb, b, :]),
                          reads=[rB[b]], writes=[FO[b][1]])
            round_robin(nxt)
            dump("ff", FF[0][:, :, 0:T], FF[1], [128, 32, T], BF16)
            dump("fo", FO[0][0][0:tb, :], FO[0][1], [tb, 1024], F32)
            def g_fin(b):
                yield from g_post(b, FO[b], PF_NFPOST, MO[b])
                kb.dma("sp", yd[b * 128:b * 128 + tb, :], FO[b][0][0:tb, :], FO[b][1], reads=[FO[b][1]])
            round_robin([g_fin(b) for b in range(nb)])
        for ti, tl in enumerate(tiles):
            try:
                run_tile(ti, *tl)
            except _Stop:
                pass
        kb.finish()
        block = es.enter_context(nc.Block())
        kb.replay(block)
    return nc


def _host_tables(p):
    pf = np.zeros((128, PF_COLS), np.float32)
    pf[:, PF_CAW:PF_CAW + 24] = p["conv_a_w"][0].reshape(3, 8, 128).transpose(2, 1, 0).reshape(128, 24)
    pf[:, PF_CSW:PF_CSW + 128] = p["conv_ssm_w"][0].reshape(4, 32, 128).transpose(2, 1, 0).reshape(128, 128)
    pf[:, PF_CSB:PF_CSB + 32] = p["conv_ssm_b"][0].reshape(32, 128).T
    pf[:, PF_BG:PF_BG + 16] = p["b_gate"][0].reshape(16, 128).T
    pf[:, PF_NMP:PF_NMP + 8] = p["norm_mix_pre"][0].reshape(8, 128).T
    pf[:, PF_NFP:PF_NFP + 8] = p["norm_ffn_pre"][0].reshape(8, 128).T
    pf[:, PF_SSN:PF_SSN + 16] = p["ssm_norm"][0].reshape(16, 128).T
    pf[:, PF_DTB:PF_DTB + 32] = p["dt_bias"][0][None, :]
    pf[:, PF_ALOG:PF_ALOG + 32] = p["a_log"][0][None, :]
    pf[:, PF_DSK:PF_DSK + 32] = p["d_skip"][0][None, :]
    pf[:, PF_NMPOST:PF_NMPOST + 1024] = p["norm_mix_post"][0][None, :]
    pf[:, PF_NFPOST:PF_NFPOST + 1024] = p["norm_ffn_post"][0][None, :]
    cb = np.zeros((128, CB_COLS), np.float32)
    k = np.arange(128)
    cb[:, CB_ID:CB_ID + 128] = np.eye(128)
    l = np.arange(64)
    cb[:, CB_TRI2:CB_TRI2 + 64] = ((k[:, None] % 64) <= l[None, :])
    same = (k[:, None] // 64) == (k[None, :] // 64)
    cb[:, CB_BONES:CB_BONES + 128] = same
    cb[:, CB_ONESA:CB_ONESA + 128] = (k[:, None] < 64) * np.ones((1, 128))
    cb[:, CB_ONESB:CB_ONESB + 128] = (k[:, None] >= 64) * np.ones((1, 128))
    cb[:, CB_TRI128:CB_TRI128 + 128] = same & ((k[:, None] % 64) <= (k[None, :] % 64))
    neg = np.where((k[:, None] % 64) > l[None, :], -30000.0, 0.0)
    cb[:, CB_NEG:CB_NEG + 1024] = np.tile(neg, (1, 16))
    return pf, cb.astype(ml_dtypes.bfloat16)


_NC_CACHE = {}


def kernel(x_prompt, x_sample, state_conv_a, state_conv_ssm, state_ssm, norm_mix_pre, w_in, b_gate, conv_a_w, w_a_out,
           conv_ssm_w, conv_ssm_b, dt_bias, a_log, d_skip, ssm_norm, w_ssm_out, w_o, norm_mix_post, norm_ffn_pre,
           w_ff1, w_ff2, norm_ffn_post, _tiles_limit=None, _cores=None, _dbg=False, _stop=None):
    p = dict(norm_mix_pre=norm_mix_pre, b_gate=b_gate, conv_a_w=conv_a_w, conv_ssm_w=conv_ssm_w, conv_ssm_b=conv_ssm_b,
             dt_bias=dt_bias, a_log=a_log, d_skip=d_skip, ssm_norm=ssm_norm, norm_mix_post=norm_mix_post,
             norm_ffn_pre=norm_ffn_pre, norm_ffn_post=norm_ffn_post)
    p = {k: np.asarray(v, np.float32) for k, v in p.items()}
    pf, cb = _host_tables(p)
    x_prompt = np.asarray(x_prompt, np.float32)
    x_sample = np.asarray(x_sample, np.float32)
    sca_all = np.asarray(state_conv_a, np.float32)[0]
    scm_all = np.asarray(state_conv_ssm, np.float32)[0]
    sst_all = np.asarray(state_ssm, np.float32)[0]
    key = (None if _tiles_limit is None else tuple(_tiles_limit), _dbg, _stop)
    if key not in _NC_CACHE:
        _NC_CACHE[key] = build_program(_tiles_limit, _dbg, _stop)
    nc = _NC_CACHE[key]
    cores = list(range(NCORES)) if _cores is None else _cores
    shared = dict(pf=pf, cb=cb, w_in=np.ascontiguousarray(np.asarray(w_in, np.float32)[0]),
                  w_a_out=np.ascontiguousarray(np.asarray(w_a_out, np.float32)[0]),
                  w_ssm_out=np.ascontiguousarray(np.asarray(w_ssm_out, np.float32)[0]),
                  w_o=np.ascontiguousarray(np.asarray(w_o, np.float32)[0]),
                  w_ff1=np.ascontiguousarray(np.asarray(w_ff1, np.float32)[0]),
                  w_ff2=np.ascontiguousarray(np.asarray(w_ff2, np.float32)[0]))
    in_maps = []
    for c in cores:
        s2 = slice(2 * c, 2 * c + 2)
        m = dict(shared)
        m["xp"] = np.ascontiguousarray(x_prompt[s2])
        m["xs"] = np.ascontiguousarray(x_sample[s2])
        m["sca"] = np.ascontiguousarray(sca_all[s2].reshape(2, 2, 8, 128).transpose(0, 3, 2, 1).reshape(2, 128, 16))
        m["scm"] = np.ascontiguousarray(scm_all[s2].reshape(2, 3, 32, 128).transpose(0, 3, 2, 1).reshape(2, 128, 96))
        m["sst"] = np.ascontiguousarray(sst_all[s2].reshape(2, 2048, 128).transpose(0, 2, 1))
        in_maps.append(m)
    res = run_bass_kernel_spmd(nc, in_maps, core_ids=list(range(len(cores))))
    R = res.results
    if _dbg:
        kernel.dbg = {k: v for k, v in R[0].items() if k.startswith("dbg_")}
    yp = np.concatenate([r["yp"] for r in R], axis=0)
    ys_ = np.concatenate([r["ys"] for r in R], axis=0)
    oca = np.stack([r["oca"] for r in R])
    ocm = np.stack([r["ocm"] for r in R])
    ost = np.stack([r["ost"] for r in R])
    n = len(cores)

    def un_ca(a):
        return a.reshape(n * 2, 128, 8, 2).transpose(0, 3, 2, 1).reshape(n * 2, 2, 1024)

    def un_cm(a):
        return a.reshape(n * 2, 128, 32, 3).transpose(0, 3, 2, 1).reshape(n * 2, 3, 4096)

    def un_st(a):
        return a.reshape(n * 2, 128, 2048).transpose(0, 2, 1).reshape(n * 2, 32, 64, 128)

    outs = (yp, ys_,
            un_ca(oca[:, 0:2])[None], un_cm(ocm[:, 0:2])[None], un_st(ost[:, 0:2])[None],
            un_ca(oca[:, 2:4])[None], un_cm(ocm[:, 2:4])[None], un_st(ost[:, 2:4])[None])
    return tuple(np.ascontiguousarray(o, dtype=np.float32) for o in outs)
```

```python
import numpy as np
import ml_dtypes
from contextlib import ExitStack
import concourse.bass as bass
import concourse.mybir as mybir
from concourse.bass_utils import run_bass_kernel_spmd

F32 = mybir.dt.float32
BF16 = mybir.dt.bfloat16
AF = mybir.ActivationFunctionType
ALU = mybir.AluOpType
AX = mybir.AxisListType

NCORES = 8
D = 1024
SEQ = 2048
DSEQ = 64
PROJ = 11296
OFF_AB, OFF_AC, OFF_AH, OFF_Z, OFF_XBC, OFF_DT, OFF_GATE = 0, 1024, 2048, 3072, 5120, 9216, 9248
EPS = 1e-6
NS = 4

PF_CAW = 0
PF_CSW = 24
PF_CSB = 152
PF_BG = 184
PF_NMP = 200
PF_NFP = 208
PF_SSN = 216
PF_DTB = 232
PF_ALOG = 264
PF_DSK = 296
PF_NMPOST = 328
PF_NFPOST = 1352
PF_COLS = 2376
CB_ID = 0
CB_TRI2 = 128
CB_BONES = 192
CB_ONESA = 320
CB_ONESB = 448
CB_TRI128 = 576
CB_NEG = 704
CB_COLS = 1728


class Res:
    __slots__ = ("name", "lw", "rd", "ov", "dsem", "dcnt", "rng", "excl", "disj")

    def __init__(self, name):
        self.name = name
        self.lw = None
        self.rd = {}
        self.ov = []
        self.dsem = None
        self.dcnt = 0
        self.rng = None
        self.disj = False
        self.excl = False


class KB:
    ENG = ("pe", "act", "dve", "pool")

    def __init__(self, nc, es):
        self.nc = nc
        self.es = es
        self.streams = {k: [] for k in ("pe", "act", "dve", "pool", "sp")}
        self.sem = {e: es.enter_context(nc.semaphore("s_" + e)) for e in self.ENG}
        self.tick = {e: 0 for e in self.ENG}
        self.waited = {k: {} for k in self.streams}
        self.dsems = []
        self.all_res = []

    def res(self, name):
        r = Res(name)
        self.all_res.append(r)
        return r

    def _dsem(self, r):
        if r.dsem is None:
            r.dsem = self.es.enter_context(self.nc.semaphore("d%d" % len(self.dsems)))
            self.dsems.append(r)
        return r.dsem

    def _deps(self, eng, reads, writes):
        deps = []
        for r in reads:
            if r.lw is not None:
                deps.append(r.lw)
            if r.excl:
                for q, t in r.rd.items():
                    if q != eng:
                        deps.append((q, t))
        same_ok = (eng == "pe")
        for w in writes:
            for x in [w] + w.ov:
                if x.lw is not None and (x.lw[0] != eng or not (same_ok or (x is w and w.disj))):
                    deps.append(x.lw)
                for q, t in x.rd.items():
                    if q != eng or not same_ok:
                        deps.append((q, t))
        wd = self.waited[eng]
        for q, t in deps:
            if wd.get(q, 0) < t:
                wd[q] = t
                self.streams[eng].append(("w", q, t))

    def wait(self, eng, q, t):
        wd = self.waited[eng]
        if wd.get(q, 0) < t:
            wd[q] = t
            self.streams[eng].append(("w", q, t))

    def op(self, eng, fn, reads=(), writes=(), inc=True):
        self._deps(eng, reads, writes)
        t = self.tick[eng] + 1
        if inc:
            self.tick[eng] = t
        self.streams[eng].append(("op", fn, inc))
        for r in reads:
            r.rd[eng] = t
        for w in writes:
            w.lw = (eng, t)
            w.rd = {}
        return t

    def dma(self, eng, out, in_, dres, reads=(), writes=()):
        self._deps(eng, reads, writes)
        self._dsem(dres)
        dres.dcnt += 16
        key = (id(dres), dres.dcnt)
        q = ("d", id(dres))
        self._dq = getattr(self, "_dq", {})
        self._dq[q] = dres
        self.streams[eng].append(("dma", out, in_, dres))
        for r in reads:
            r.rd[q] = dres.dcnt
        for w in writes:
            w.lw = (q, dres.dcnt)
            w.rd = {}

    def semof(self, q):
        if isinstance(q, tuple):
            return self._dq[q].dsem
        return self.sem[q]

    def finish(self):
        for r in self.dsems:
            self.wait("sp", ("d", id(r)), r.dcnt)

    def replay(self, block):
        handles = {"pe": block.tensor, "act": block.scalar, "dve": block.vector, "pool": block.gpsimd,
                   "sp": block.sync}
        for eng, deco in handles.items():
            stream = self.streams[eng]
            own = self.sem.get(eng)

            def body(e, stream=stream, own=own):
                for ent in stream:
                    k = ent[0]
                    if k == "w":
                        e.wait_ge(self.semof(ent[1]), ent[2])
                    elif k == "op":
                        ins = ent[1](e)
                        if ent[2]:
                            ins.then_inc(own, 1)
                    else:
                        e.dma_start(out=ent[1], in_=ent[2]).then_inc(ent[3].dsem, 16)
            deco(body)


class Arena:
    def __init__(self, kb, tensor, nbytes):
        self.kb = kb
        self.t = tensor
        self.nbytes = nbytes
        self.views = []

    def view(self, name, off, shape, dt):
        esz = 2 if dt == BF16 else 4
        n = int(np.prod(shape))
        nb = n * esz
        assert off % 4 == 0 and off + nb <= self.nbytes, (name, off, nb, self.nbytes)
        w0, w1 = off // 4, (off + nb + 3) // 4
        ap = self.t[:, w0:w1]
        if dt == BF16:
            ap = ap.bitcast(BF16)[:, 0:n]
        if len(shape) == 2:
            ap = ap.rearrange("p (a b) -> p a b", a=shape[0])
        elif len(shape) == 3:
            ap = ap.rearrange("p (a b c) -> p a b c", a=shape[0], b=shape[1])
        r = self.kb.res(name)
        r.rng = (off, off + nb)
        for (o, rr) in self.views:
            if o.rng[0] < r.rng[1] and r.rng[0] < o.rng[1]:
                r.ov.append(o)
                o.ov.append(r)
        self.views.append((r, ap))
        return ap, r


import os
VARP = int(os.environ.get('VARP', '128'))
CH_OFF = int(os.environ.get('CH_OFF', '0'))
NACC = int(os.environ.get('NACC', '4'))


class _Stop(Exception):
    pass


def build_program(tiles_limit=None, dbg=False, stop_after=None):
    nc = bass.Bass("TRN2", target_bir_lowering=False)
    dram_in = lambda name, shape, dt=F32: nc.dram_tensor(name, shape, dt, kind="ExternalInput").ap()
    dram_out = lambda name, shape, dt=F32: nc.dram_tensor(name, shape, dt, kind="ExternalOutput").ap()
    xp = dram_in("xp", [2, SEQ, D])
    xs = dram_in("xs", [2, DSEQ, D])
    sca = dram_in("sca", [2, 128, 16])
    scm = dram_in("scm", [2, 128, 96])
    sst = dram_in("sst", [2, 128, 2048])
    pf_d = dram_in("pf", [128, PF_COLS])
    cb_d = dram_in("cb", [128, CB_COLS], BF16)
    w_in = dram_in("w_in", [D, PROJ])
    w_a = dram_in("w_a_out", [D, D])
    w_s = dram_in("w_ssm_out", [2 * D, D])
    w_o = dram_in("w_o", [D, D])
    w_f1 = dram_in("w_ff1", [D, 4 * D])
    w_f2 = dram_in("w_ff2", [4 * D, D])
    yp = dram_out("yp", [2, SEQ, D])
    ys = dram_out("ys", [2, DSEQ, D])
    oca = dram_out("oca", [4, 128, 16])
    ocm = dram_out("ocm", [4, 128, 96])
    ost = dram_out("ost", [4, 128, 2048])
    wi_b = nc.dram_tensor("wi_b", [D, PROJ], BF16).ap()
    wa_b = nc.dram_tensor("wa_b", [D, D], BF16).ap()
    ws_b = nc.dram_tensor("ws_b", [2 * D, D], BF16).ap()
    wo_b = nc.dram_tensor("wo_b", [D, D], BF16).ap()
    wf1_b = nc.dram_tensor("wf1_b", [D, 4 * D], BF16).ap()
    wf2_b = nc.dram_tensor("wf2_b", [4 * D, D], BF16).ap()

    with ExitStack() as es:
        kb = KB(nc, es)
        sbt = lambda name, shape, dt: es.enter_context(nc.sbuf_tensor(name, shape, dt))
        PF = sbt("PF", [128, PF_COLS], F32); rPF = kb.res("PF")
        CB = sbt("CB", [128, CB_COLS], BF16); rCB = kb.res("CB")
        PAN = [sbt("PAN%d" % i, [128, 8, 512], BF16) for i in range(NS)]
        rPAN = [kb.res("PAN%d" % i) for i in range(NS)]
        XN = sbt("XN", [128, 8, 512], BF16); rXN = kb.res("XN")
        DTW = sbt("DTW", [128, 8, 32], BF16); rDTW = kb.res("DTW")
        HISTA = sbt("HISTA", [128, 8, 2], F32); rHISTA = kb.res("HISTA")
        HISTM = sbt("HISTM", [128, 32, 3], F32); rHISTM = kb.res("HISTM")
        ST = sbt("ST", [128, 2048], F32)
        rSTh = [kb.res("ST_h%d" % i) for i in range(2)]
        SBF = [sbt("SBF%d" % i, [128, 2048], BF16) for i in range(2)]
        rSBFh = [[kb.res("SBF%d_h%d" % (i, hf)) for hf in range(2)] for i in range(2)]
        DECT = sbt("DECT", [128, 4, 32], F32)
        rDECT = [[kb.res("DEC%d_%d" % (b, hf)) for hf in range(2)] for b in range(4)]
        STATC = 64
        rSTATc = [[kb.res("STATC%d_%d" % (b, c_)) for c_ in range(2)] for b in range(4)]
        ANEG = sbt("ANEG", [128, 32], F32); rANEG = kb.res("ANEG")
        SM = sbt("SM", [128, 4, 320], F32)
        rSM = [kb.res("SM%d" % b) for b in range(4)]
        STAT = sbt("STAT", [128, 256], F32)
        rSTATb = [kb.res("STAT%d" % b) for b in range(4)]
        EPSC = sbt("EPSC", [128, 2], F32); rEPSC = kb.res("EPSC")
        C_DT, C_ADT, C_ACUM, C_EA, C_CD, C_DEC, C_T0, C_RS = 0, 32, 64, 96, 128, 192, 224, 256
        SMH = sbt("SMH", [128, 4, 64], BF16)
        rSMH = [kb.res("SMH%d" % b) for b in range(4)]
        KBY = 1024
        AR_BYTES = 128 * KBY
        ARt = sbt("ARENA", [128, AR_BYTES // 4], F32)
        ar = Arena(kb, ARt, AR_BYTES)
        XT = [ar.view("XT%d" % i, 81 * KBY + i * 4 * KBY, [1024], F32) for i in range(4)]
        XS16 = [ar.view("XS16_%d" % i, 97 * KBY + i * 2 * KBY, [1024], BF16) for i in range(4)]
        JUNK4 = [ar.view("JUNK%d" % i, 105 * KBY + i * 2 * KBY, [1024], BF16) for i in range(4)]
        SZ = ar.view("SZ", 0, [4, 2048], BF16)
        XSF = ar.view("XSF", 16 * KBY, [16, 512], BF16)
        BF_ = ar.view("BF", 32 * KBY, [8, 512], BF16)
        CF = ar.view("CF", 40 * KBY, [8, 512], BF16)
        XBJ = [[ar.view("XB%d_%d" % (p_, jj), 48 * KBY + (p_ * 4 + jj) * 2064, [516], F32) for jj in range(4)]
               for p_ in range(3)]
        CVJ = [[ar.view("CV%d_%d" % (p_, jj), 73 * KBY + (p_ * 4 + jj) * 2048, [512], F32) for jj in range(4)]
               for p_ in range(2)]
        XStP = [ar.view("XSt%d" % i, 48 * KBY + i * 4 * KBY, [2048], BF16) for i in range(2)]
        BtP = [ar.view("Bt%d" % i, 56 * KBY + i * 2 * KBY, [1024], BF16) for i in range(2)]
        XCDh = [ar.view("XCD_%d" % hf, 60 * KBY + hf * 2 * KBY, [1024], BF16) for hf in range(2)]
        XCP = [ar.view("XC%d" % i, 64 * KBY + i * 4 * KBY, [2048], BF16) for i in range(2)]
        RHI = ar.view("RHI", 72 * KBY, [32, 64], BF16)
        RLO = ar.view("RLO", 76 * KBY, [32, 64], BF16)
        DEc = [ar.view("DE%d" % i, 80 * KBY + i * 4 * KBY, [16, 64], F32) for i in range(2)]
        MHc = [ar.view("MH%d" % i, 88 * KBY + i * 2 * KBY, [16, 64], BF16) for i in range(2)]
        T1c = [ar.view("T1_%d" % i, 92 * KBY + i * 4 * KBY, [1024], F32) for i in range(2)]
        Y1c = [ar.view("Y1_%d" % i, 100 * KBY + i * 4 * KBY, [1024], F32) for i in range(2)]
        YNc = [ar.view("YN_%d" % i, 108 * KBY + i * 2 * KBY, [1024], BF16) for i in range(2)]
        YNF = ar.view("YNF", 112 * KBY, [16, 512], BF16)
        ACJ = [ar.view("AC_%d" % jj, jj * 2048, [512], F32) for jj in range(4)]
        UJ = [ar.view("U_%d" % jj, 8 * KBY + jj * 2064, [516], F32) for jj in range(4)]
        TTJ = [ar.view("TT_%d" % jj, 17 * KBY + jj * 2048, [512], F32) for jj in range(4)]
        V = ar.view("V", 25 * KBY, [8, 512], BF16)
        GJ = [ar.view("G_%d" % jj, 33 * KBY + jj * 2048, [512], F32) for jj in range(4)]
        PA = ar.view("PA", 41 * KBY, [8, 512], F32)
        MG = ar.view("MG", 57 * KBY, [8, 512], BF16)
        MO = [ar.view("MO%d" % b, 65 * KBY + b * 4 * KBY, [1024], F32) for b in range(4)]
        XT2 = [ar.view("XT2_%d" % i, 81 * KBY + i * 4 * KBY, [1024], F32) for i in range(4)]
        HS16 = [ar.view("HS16_%d" % i, 97 * KBY + i * 2 * KBY, [1024], BF16) for i in range(4)]
        FF = ar.view("FF", 0, [32, 512], BF16)
        RT = [ar.view("RT%d" % i, 32 * KBY + i * 2 * KBY, [512], F32) for i in range(4)]
        FO = [ar.view("FO%d" % b, 40 * KBY + b * 4 * KBY, [1024], F32) for b in range(4)]
        SGW = 2824
        NSG = 4
        SGI = [ar.view("SGI%d" % i, i * 12 * KBY, [SGW], F32) for i in range(NSG)]
        SGO = [ar.view("SGO%d" % i, 48 * KBY + i * 6 * KBY, [SGW], BF16) for i in range(NSG)]
        PS = es.enter_context(nc.psum_tensor("PS", [128, 8, 512], F32))
        rB = [kb.res("bank%d" % i) for i in range(8)]
        for r_ in rB:
            r_.excl = True
        PSb = PS[:].bitcast(BF16)

        for r_ in [XSF[1], BF_[1], CF[1], V[1], PA[1], MG[1], FF[1], YNF[1], SZ[1], rXN] + [m_[1] for m_ in MO] + \
                [f_[1] for f_ in FO]:
            r_.disj = True
        ident = CB[:, CB_ID:CB_ID + 128]
        tri2 = CB[:, CB_TRI2:CB_TRI2 + 64]
        bones = CB[:, CB_BONES:CB_BONES + 128]
        onesA = CB[:, CB_ONESA:CB_ONESA + 128]
        onesB = CB[:, CB_ONESB:CB_ONESB + 128]
        tri128 = CB[:, CB_TRI128:CB_TRI128 + 128]
        negm = CB[:, CB_NEG:CB_NEG + 1024]

        dbg_names = []

        def dump(name, ap, res, shape, dt):
            if not dbg:
                return
            d = nc.dram_tensor("dbg_" + name, shape, dt, kind="ExternalOutput").ap()
            kb.dma("sp", d, ap, res, reads=[res])
            dbg_names.append("dbg_" + name)

        kb.dma("sp", PF[:], pf_d, rPF, writes=[rPF])
        kb.dma("sp", CB[:], cb_d, rCB, writes=[rCB])
        kb.op("pool", lambda e: e.memset(EPSC[:, 0:1], EPS), writes=[rEPSC])
        kb.op("pool", lambda e: e.memset(EPSC[:, 1:2], 1.0), writes=[rEPSC])
        kb.op("act", lambda e: e.activation(out=ANEG[:], in_=PF[:, PF_ALOG:PF_ALOG + 32], func=AF.Exp),
              reads=[rPF], writes=[rANEG])
        kb.op("dve", lambda e: e.tensor_scalar(out=ANEG[:], in0=ANEG[:], scalar1=-1.0, scalar2=None, op0=ALU.mult),
              reads=[rANEG], writes=[rANEG])

        conv_jobs = []
        def add_mat(src, dst, K, N, piece, scol):
            for rc in range(K // 128):
                for c0 in range(0, N, piece):
                    w = min(piece, N - c0)
                    conv_jobs.append((src[rc * 128:(rc + 1) * 128, c0:c0 + w], dst[rc * 128:(rc + 1) * 128, c0:c0 + w],
                                      None if scol is None else scol + rc, w))
        add_mat(w_in, wi_b, D, PROJ, SGW, PF_NMP)
        add_mat(w_a, wa_b, D, D, 1024, None)
        add_mat(w_s, ws_b, 2 * D, D, 1024, PF_SSN)
        add_mat(w_o, wo_b, D, D, 1024, None)
        add_mat(w_f1, wf1_b, D, 4 * D, 2048, PF_NFP)
        add_mat(w_f2, wf2_b, 4 * D, D, 1024, None)
        def conv_store(i):
            (src, dst, scol, w) = conv_jobs[i]
            so, ro = SGO[i % NSG]
            kb.dma("sp", dst, so[:, 0:w], ro, reads=[ro])
        for i, (src, dst, scol, w) in enumerate(conv_jobs):
            si, ri = SGI[i % NSG]
            so, ro = SGO[i % NSG]
            if i >= NSG - 1:
                conv_store(i - (NSG - 1))
            kb.dma("sp", si[:, 0:w], src, ri, writes=[ri])
            eng = "dve" if (i % 5) in (0, 2, 4) else "act"
            if eng == "dve":
                if scol is None:
                    kb.op(eng, lambda e, so=so, si=si, w=w: e.tensor_copy(out=so[:, 0:w], in_=si[:, 0:w]),
                          reads=[ri], writes=[ro])
                else:
                    kb.op(eng, lambda e, so=so, si=si, w=w, scol=scol: e.tensor_scalar(
                        out=so[:, 0:w], in0=si[:, 0:w], scalar1=PF[:, scol:scol + 1], scalar2=None, op0=ALU.mult),
                        reads=[ri, rPF], writes=[ro])
            else:
                if scol is None:
                    kb.op(eng, lambda e, so=so, si=si, w=w: e.copy(out=so[:, 0:w], in_=si[:, 0:w]),
                          reads=[ri], writes=[ro])
                else:
                    kb.op(eng, lambda e, so=so, si=si, w=w, scol=scol: e.activation(
                        out=so[:, 0:w], in_=si[:, 0:w], func=AF.Identity, scale=PF[:, scol:scol + 1]),
                        reads=[ri, rPF], writes=[ro])
        for i in range(max(0, len(conv_jobs) - (NSG - 1)), len(conv_jobs)):
            conv_store(i)
        for so, ro in SGO:
            kb.wait("sp", ("d", id(ro)), ro.dcnt)
        kb.dma("sp", DTW[:], wi_b[:, OFF_DT:OFF_DT + 32].rearrange("(kc p) n -> p kc n", p=128), rDTW, writes=[rDTW])

        tiles = []
        for s in range(2):
            for ti in range(4):
                tiles.append((s, xp[s, ti * 512:(ti + 1) * 512, :], yp[s, ti * 512:(ti + 1) * 512, :], 512,
                              ti == 0, ti == 3, None))
        for s in range(2):
            tiles.append((2 + s, xs[s], ys[s], 64, True, True, s))
        if tiles_limit is not None:
            tiles = [tiles[i] for i in tiles_limit]

        def panel_list(T):
            pl = []
            for q in range(8):
                pl.append((wi_b, 0, OFF_XBC + q * 512))
                if q % 2 == 1:
                    pl.append((wi_b, 0, OFF_Z + (q // 2) * 512))
            for hf in range(2):
                pl.append((wi_b, 0, OFF_AC + hf * 512))
                pl.append((wi_b, 0, OFF_AH + hf * 512))
                pl.append((wi_b, 0, OFF_AB + hf * 512))
            for hf in range(2):
                pl.append((wi_b, 0, OFF_GATE + hf * 512))
                pl.append((wa_b, 0, hf * 512))
            for hf in range(2):
                pl.append((wi_b, 0, OFF_GATE + 1024 + hf * 512))
                pl.append((ws_b, 0, hf * 512))
                pl.append((ws_b, 1, hf * 512))
            for hf in range(2):
                pl.append((wo_b, 0, hf * 512))
            for q in range(8):
                pl.append((wf1_b, 0, q * 512))
            for hf in range(2):
                for kg in range(4):
                    pl.append((wf2_b, kg, hf * 512))
            return pl
        all_panels = []
        for tl in tiles:
            all_panels += panel_list(tl[3])
        pstate = {"issued": 0, "used": 0}

        def issue_panels(upto):
            while pstate["issued"] < min(upto, len(all_panels)):
                i = pstate["issued"]
                mat, kg, c0 = all_panels[i]
                slot = i % NS
                src = mat[kg * 1024:(kg + 1) * 1024, c0:c0 + 512].rearrange("(kc p) n -> p kc n", p=128)
                kb.dma("sp", PAN[slot][:], src, rPAN[slot], writes=[rPAN[slot]])
                pstate["issued"] += 1

        def next_panel():
            i = pstate["used"]
            issue_panels(i + NS - 1)
            pstate["used"] += 1
            return PAN[i % NS], rPAN[i % NS]

        bank_rr = {"i": 0}

        def acc_bank():
            i = bank_rr["i"]
            bank_rr["i"] = (i + 1) % NACC
            return i

        def mm_run(out, pairs, reads, writes, first=True, last=True):
            n = len(pairs)
            for i, (l, r) in enumerate(pairs):
                kb.op("pe", lambda e, l=l, r=r, st=(first and i == 0), sp_=(last and i == n - 1):
                      e.matmul(out, lhsT=l, rhs=r, start=st, stop=sp_),
                      reads=reads if i == 0 else (), writes=writes if i == 0 else (), inc=(i == n - 1))

        def tr_run(outs_ins, reads, writes, idn):
            n = len(outs_ins)
            for i, (o, a) in enumerate(outs_ins):
                kb.op("pe", lambda e, o=o, a=a: e.transpose(o, a, idn),
                      reads=reads if i == 0 else (), writes=writes if i == 0 else (), inc=(i == n - 1))

        def round_robin(gens):
            gens = list(gens)
            while gens:
                for g_ in list(gens):
                    try:
                        next(g_)
                    except StopIteration:
                        gens.remove(g_)

        def g_rstd(o, rst, ss_col, rs_col, n, tb, w=1):
            kb.op("act", lambda e: e.activation(out=STAT[0:tb, o + rs_col:o + rs_col + w], in_=STAT[0:tb, o + ss_col:o + ss_col + w],
                                                func=AF.Ln, scale=1.0 / n, bias=EPSC[0:tb, 0:1]),
                  reads=[rst, rEPSC], writes=[rst])
            yield
            kb.op("act", lambda e: e.activation(out=STAT[0:tb, o + rs_col:o + rs_col + w], in_=STAT[0:tb, o + rs_col:o + rs_col + w],
                                                func=AF.Exp, scale=-0.5),
                  reads=[rst], writes=[rst])
            yield

        def round_robin_off(gens):
            live = list(gens)
            step = 0
            while live:
                for item in list(live):
                    if item[1] <= step:
                        try:
                            next(item[0])
                        except StopIteration:
                            live.remove(item)
                step += 1

        def g_norm_to_fm(src_ap, src_res, b, tb, s16, dst, dst_res):
            (s16ap, s16res) = s16
            o = 16 * b
            rst = rSTATb[b]
            jk, rjk = JUNK4[b]
            kb.op("act", lambda e: e.activation(out=jk[0:tb, :], in_=src_ap[0:tb, :], func=AF.Square,
                                                accum_out=STAT[0:tb, o:o + 1]),
                  reads=[src_res], writes=[rjk, rst])
            yield
            yield from g_rstd(o, rst, 0, 1, 1024, tb)
            kb.op("dve", lambda e: e.tensor_scalar(out=s16ap[0:tb, :], in0=src_ap[0:tb, :], scalar1=STAT[0:tb, o + 1:o + 2],
                                                   scalar2=None, op0=ALU.mult),
                  reads=[src_res, rst], writes=[s16res])
            yield
            tpb = 4 + b
            tr_run([(PSb[:, tpb, kc * 128:kc * 128 + tb], s16ap[0:tb, kc * 128:(kc + 1) * 128]) for kc in range(8)],
                   reads=[s16res, rCB], writes=[rB[tpb]], idn=ident[0:tb, 0:tb])
            yield
            kb.op("act", lambda e: e.copy(out=dst[:, :, b * 128:b * 128 + tb],
                                          in_=PSb[:, tpb, :].rearrange("p (k t) -> p k t", k=8)[:, :, 0:tb]),
                  reads=[rB[tpb]], writes=[dst_res])
            yield

        def gate(n):
            if stop_after is not None and n > stop_after:
                raise _Stop()

        p0_done = set()

        def p0_gens(ti):
            (_sidx, xd_, _yd, T_, _f, _l, _s) = tiles[ti]
            tb_ = min(T_, 128)

            def g_p0(b):
                xt, rxt = XT[b]
                kb.dma("sp", xt[0:tb_, :], xd_[b * 128:b * 128 + tb_, :], rxt, writes=[rxt])
                yield from g_norm_to_fm(xt, rxt, b, tb_, XS16[b], XN, rXN)
            p0_done.add(ti)
            return [g_p0(b) for b in range(T_ // tb_)]

        def run_tile(ti, sidx, xd, yd, T, first, last, samp):
            tb = min(T, 128)
            nb = T // tb
            nch = tb // 64
            if first:
                if samp is None:
                    kb.op("pool", lambda e: e.memset(HISTA[:], 0.0), writes=[rHISTA])
                    kb.op("pool", lambda e: e.memset(HISTM[:], 0.0), writes=[rHISTM])
                    kb.op("pool", lambda e: e.memset(ST[:], 0.0), writes=[rSTh[0], rSTh[1]])
                else:
                    kb.dma("sp", HISTA[:], sca[samp].rearrange("p (j i) -> p j i", i=2), rHISTA, writes=[rHISTA])
                    kb.dma("sp", HISTM[:], scm[samp].rearrange("p (c i) -> p c i", i=3), rHISTM, writes=[rHISTM])
                    kb.dma("sp", ST[:], sst[samp], rSTh[0], writes=[rSTh[0], rSTh[1]])
                kb.op("act", lambda e: e.copy(out=SBF[0][:], in_=ST[:]), reads=[rSTh[0], rSTh[1]],
                      writes=[rSBFh[0][0], rSBFh[0][1]])

            if ti not in p0_done:
                round_robin(p0_gens(ti))

            gate(1)
            def b1_panel(q):
                pan, rpan = next_panel()
                for b in range(nb):
                    bk = acc_bank()
                    mm_run(PS[0:tb, bk, :], [(XN[:, kc, b * 128:b * 128 + tb], pan[:, kc, :]) for kc in range(8)],
                           reads=[rXN, rpan], writes=[rB[bk]])
                    kb.op("act", lambda e, bk=bk, b=b, q=q: e.activation(
                        out=SZ[0][0:tb, b, q * 512:(q + 1) * 512], in_=PS[0:tb, bk, :], func=AF.Silu),
                        reads=[rB[bk]], writes=[SZ[1]])
            gate(2)
            def b2_s1(q):
                pan, rpan = next_panel()
                xb = XBJ[q % 3]
                for jj in range(4):
                    c = q * 4 + jj
                    kb.op("pool", lambda e, jj=jj, c=c, xb=xb: e.tensor_copy(out=xb[jj][0][:, 0:3], in_=HISTM[:, c, :]),
                          reads=[rHISTM], writes=[xb[jj][1]])
                    bk = acc_bank()
                    mm_run(PS[:, bk, 0:T], [(pan[:, kc, jj * 128:(jj + 1) * 128], XN[:, kc, 0:T]) for kc in range(8)],
                           reads=[rXN, rpan], writes=[rB[bk]])
                    kb.op("act", lambda e, bk=bk, jj=jj, xb=xb: e.copy(out=xb[jj][0][:, 3:3 + T], in_=PS[:, bk, 0:T]),
                          reads=[rB[bk]], writes=[xb[jj][1]])
                    kb.op("pool", lambda e, jj=jj, c=c, xb=xb: e.tensor_copy(out=HISTM[:, c, :], in_=xb[jj][0][:, T:T + 3]),
                          reads=[xb[jj][1]], writes=[rHISTM])

            def b2_s2(q):
                xb = XBJ[q % 3]
                cv = CVJ[q % 2]
                for jj in range(4):
                    c = q * 4 + jj
                    kb.op("act", lambda e, jj=jj, c=c, xb=xb, cv=cv: e.activation(
                        out=cv[jj][0][:, 0:T], in_=xb[jj][0][:, 0:T], func=AF.Identity,
                        scale=PF[:, PF_CSW + c * 4:PF_CSW + c * 4 + 1]),
                        reads=[xb[jj][1], rPF], writes=[cv[jj][1]])
                for i in (1, 2, 3):
                    for jj in range(4):
                        c = q * 4 + jj
                        kb.op("dve", lambda e, jj=jj, c=c, i=i, xb=xb, cv=cv: e.scalar_tensor_tensor(
                            out=cv[jj][0][:, 0:T], in0=xb[jj][0][:, i:i + T],
                            scalar=PF[:, PF_CSW + c * 4 + i:PF_CSW + c * 4 + i + 1], in1=cv[jj][0][:, 0:T],
                            op0=ALU.mult, op1=ALU.add),
                            reads=[xb[jj][1], cv[jj][1], rPF], writes=[cv[jj][1]])

            def b2_s3(q):
                cv = CVJ[q % 2]
                for jj in range(4):
                    c = q * 4 + jj
                    if c < 16:
                        dst, dres = XSF[0][:, c, 0:T], XSF[1]
                    elif c < 24:
                        dst, dres = BF_[0][:, c - 16, 0:T], BF_[1]
                    else:
                        dst, dres = CF[0][:, c - 24, 0:T], CF[1]
                    kb.op("act", lambda e, jj=jj, c=c, dst=dst, cv=cv: e.activation(
                        out=dst, in_=cv[jj][0][:, 0:T], func=AF.Silu, bias=PF[:, PF_CSB + c:PF_CSB + c + 1]),
                        reads=[cv[jj][1], rPF], writes=[dres])


            gate(3)
            def g_dt(b):
                sm = SM[:, b, :]
                rsm = rSM[b]
                smh = SMH[:, b, :]
                rsmh = rSMH[b]
                tok = slice(b * 128, b * 128 + tb)
                bk = 4 + b
                mm_run(PS[0:tb, bk, 0:32], [(XN[:, kc, tok], DTW[:, kc, :]) for kc in range(8)],
                       reads=[rXN, rDTW], writes=[rB[bk]])
                yield
                kb.op("dve", lambda e: e.tensor_tensor(out=sm[0:tb, C_T0:C_T0 + 32], in0=PS[0:tb, bk, 0:32],
                                                       in1=PF[0:tb, PF_DTB:PF_DTB + 32], op=ALU.add),
                      reads=[rB[bk], rPF], writes=[rsm])
                yield
                kb.op("act", lambda e: e.activation(out=sm[0:tb, C_T0:C_T0 + 32], in_=sm[0:tb, C_T0:C_T0 + 32],
                                                    func=AF.Exp), reads=[rsm], writes=[rsm])
                yield
                kb.op("act", lambda e: e.activation(out=sm[0:tb, C_DT:C_DT + 32], in_=sm[0:tb, C_T0:C_T0 + 32],
                                                    func=AF.Ln, bias=EPSC[0:tb, 1:2]), reads=[rsm, rEPSC], writes=[rsm])
                yield
                kb.op("dve", lambda e: e.tensor_tensor(out=sm[0:tb, C_ADT:C_ADT + 32], in0=sm[0:tb, C_DT:C_DT + 32],
                                                       in1=ANEG[0:tb, :], op=ALU.mult),
                      reads=[rsm, rANEG], writes=[rsm])
                yield
                kb.op("dve", lambda e: e.tensor_copy(out=smh[0:tb, 0:32], in_=sm[0:tb, C_ADT:C_ADT + 32]),
                      reads=[rsm], writes=[rsmh])
                yield
                kb.op("dve", lambda e: e.tensor_tensor(out=smh[0:tb, 32:64], in0=sm[0:tb, C_ADT:C_ADT + 32],
                                                       in1=smh[0:tb, 0:32], op=ALU.subtract),
                      reads=[rsm, rsmh], writes=[rsmh])
                yield
                bk2 = 4 + b
                mm_run(PS[0:tb, bk2, 0:32], [(tri128[0:tb, 0:tb], smh[0:tb, 0:32]), (tri128[0:tb, 0:tb], smh[0:tb, 32:64])],
                       reads=[rsmh, rCB], writes=[rB[bk2]])
                mm_run(PS[:, bk2, 32:64], [(onesA[0:tb, :], smh[0:tb, 0:32]), (onesA[0:tb, :], smh[0:tb, 32:64])],
                       reads=[rsmh, rCB], writes=[rB[bk2]])
                if nch == 2:
                    mm_run(PS[:, bk2, 64:96], [(onesB[0:tb, :], smh[0:tb, 0:32]), (onesB[0:tb, :], smh[0:tb, 32:64])],
                           reads=[rsmh, rCB], writes=[rB[bk2]])
                yield
                kb.op("dve", lambda e: e.tensor_copy(out=sm[0:tb, C_ACUM:C_ACUM + 32], in_=PS[0:tb, bk2, 0:32]),
                      reads=[rB[bk2]], writes=[rsm])
                yield
                kb.op("act", lambda e: e.activation(out=sm[0:tb, C_EA:C_EA + 32], in_=PS[0:tb, bk2, 0:32],
                                                    func=AF.Exp), reads=[rB[bk2]], writes=[rsm])
                kb.op("act", lambda e: e.activation(out=sm[0:VARP, C_CD:C_CD + 32 * nch],
                                                    in_=PS[0:VARP, bk2, 32:32 + 32 * nch], func=AF.Exp),
                      reads=[rB[bk2]], writes=[rsm])
                yield
            dtg = [g_dt(b) for b in range(nb)]

            def step_dt():
                for g_ in list(dtg):
                    try:
                        next(g_)
                    except StopIteration:
                        dtg.remove(g_)

            for q in range(8 + 2):
                if q < 8:
                    b2_s1(q)
                if 1 <= q <= 8:
                    b2_s2(q - 1)
                if q < 8 and q % 2 == 1:
                    b1_panel(q // 2)
                if q >= 2:
                    b2_s3(q - 2)
                step_dt()
            if last:
                kb.dma("sp", ocm[sidx].rearrange("p (c i) -> p c i", i=3), HISTM[:], rHISTM, reads=[rHISTM])
            round_robin(dtg)

            def g_prep_banks():
                return (4, 5), (6, 7)

            def g_prep(b):
                sm = SM[:, b, :]
                rsm = rSM[b]
                smh = SMH[:, b, :]
                rsmh = rSMH[b]
                tok = slice(b * 128, b * 128 + tb)
                xst, rxst = XStP[b % 2]
                btt, rbtt = BtP[b % 2]
                xc, rxc = XCP[b % 2]
                p, p2 = g_prep_banks()
                tr_run([(PSb[0:tb, p[c // 8], (c % 8) * 128:(c % 8 + 1) * 128], XSF[0][:, c, tok]) for c in range(16)],
                       reads=[XSF[1], rCB], writes=[rB[p[0]], rB[p[1]]], idn=ident)
                yield
                kb.op("act", lambda e: e.copy(out=xst[0:tb, :], in_=PSb[0:tb, p[0]:p[1] + 1, :].rearrange("p a n -> p (a n)")),
                      reads=[rB[p[0]], rB[p[1]]], writes=[rxst])
                yield
                kb.op("dve", lambda e: e.tensor_tensor(
                    out=xc[0:tb, :].rearrange("p (h d) -> p h d", h=32),
                    in0=PSb[0:tb, p[0]:p[1] + 1, :].rearrange("p a (h d) -> p (a h) d", d=64),
                    in1=sm[0:tb, C_DT:C_DT + 32].unsqueeze(2).broadcast_to([tb, 32, 64]), op=ALU.mult),
                    reads=[rB[p[0]], rB[p[1]], rsm], writes=[rxc])
                yield
                tr_run([(PSb[0:tb, p2[0], g * 128:(g + 1) * 128], BF_[0][:, g, tok]) for g in range(8)],
                       reads=[BF_[1], rCB], writes=[rB[p2[0]]], idn=ident)
                yield
                kb.op("act", lambda e: e.copy(out=btt[0:tb, :], in_=PSb[0:tb, p2[0], :]), reads=[rB[p2[0]]], writes=[rbtt])
                yield
                for (rr_, lo) in ((RHI, 0), (RLO, 32)):
                    kb.op("dve", lambda e, rr_=rr_, lo=lo: e.tensor_tensor(
                        out=rr_[0][0:tb, :, :], in0=smh[0:tb, lo:lo + 32].unsqueeze(2).broadcast_to([tb, 32, 64]),
                        in1=tri2[0:tb, :].unsqueeze(1).broadcast_to([tb, 32, 64]), op=ALU.mult),
                        reads=[rsmh, rCB], writes=[rr_[1]])
                    yield

            def g_chain(b, hf):
                c_ = hf
                sm = SM[:, b, :]
                rsm = rSM[b]
                tok0 = b * 128
                hs = slice(hf * 16, hf * 16 + 16)
                xst, rxst = XStP[b % 2]
                btt, rbtt = BtP[b % 2]
                xcd, rxcd = XCDh[hf]
                xc, rxc = XCP[b % 2]
                mypair = (2 * c_, 2 * c_ + 1)

                def scratch():
                    return mypair
                de, rde = DEc[c_]
                mh, rmh = MHc[c_]
                t1, rt1 = T1c[c_]
                y1, ry1 = Y1c[c_]
                yn, ryn = YNc[c_]
                dec = DECT[:, b, hs]
                rdec = rDECT[b][hf]
                yo = (4, 5) if c_ == 0 else (6, 7)
                stc = STATC + 32 * b + 8 * c_
                rstc = rSTATc[b][c_]
                p = scratch()
                for nn in range(2):
                    hh = slice(hf * 16 + nn * 8, hf * 16 + nn * 8 + 8)
                    mm_run(PS[0:tb, p[nn], :],
                           [(bones[0:tb, 0:tb], RHI[0][0:tb, hh, :]), (bones[0:tb, 0:tb], RLO[0][0:tb, hh, :]),
                            (ident[0:tb, 0:tb], negm[0:tb, nn * 512:(nn + 1) * 512])],
                           reads=[RHI[1], RLO[1], rCB], writes=[rB[p[nn]]])
                yield
                kb.op("dve", lambda e: e.tensor_tensor(
                    out=de[0:tb, :, :], in0=PS[0:tb, p[0]:p[1] + 1, :].rearrange("p a (h l) -> p (a h) l", l=64),
                    in1=sm[0:tb, C_ACUM + hs.start:C_ACUM + hs.stop].unsqueeze(2).broadcast_to([tb, 16, 64]),
                    op=ALU.subtract), reads=[rB[p[0]], rB[p[1]], rsm], writes=[rde])
                pc = scratch()
                for g in range(4):
                    gg = hf * 4 + g
                    for ch in range(nch):
                        r0 = ch * 64
                        cs = slice(tok0 + r0, tok0 + r0 + 64)
                        mm_run(PS[r0:r0 + 64, pc[0], g * 64:(g + 1) * 64], [(BF_[0][:, gg, cs], CF[0][:, gg, cs])],
                               reads=[BF_[1], CF[1]], writes=[rB[pc[0]]])
                yield
                kb.op("act", lambda e: e.activation(out=de[0:tb, :, :], in_=de[0:tb, :, :], func=AF.Exp),
                      reads=[rde], writes=[rde])
                yield
                kb.op("pool", lambda e: e.tensor_copy(out=dec[0:tb, :].unsqueeze(2), in_=de[0:tb, :, 63:64]),
                      reads=[rde], writes=[rdec])
                kb.op("dve", lambda e: e.tensor_tensor(
                    out=mh[0:tb, :, :].rearrange("p (g j) l -> p g j l", g=4),
                    in0=de[0:tb, :, :].rearrange("p (g j) l -> p g j l", g=4),
                    in1=PS[0:tb, pc[0], 0:256].rearrange("p (g l) -> p g l", g=4).unsqueeze(2).broadcast_to([tb, 4, 4, 64]),
                    op=ALU.mult), reads=[rde, rB[pc[0]]], writes=[rmh])
                yield
                kb.op("dve", lambda e: e.tensor_tensor(
                    out=xcd[0:tb, :].rearrange("p (h d) -> p h d", h=16),
                    in0=xc[0:tb, hf * 1024:(hf + 1) * 1024].rearrange("p (h d) -> p h d", h=16),
                    in1=dec[0:tb, :].unsqueeze(2).broadcast_to([tb, 16, 64]),
                    op=ALU.mult), reads=[rxc, rdec], writes=[rxcd])
                yield
                pd = scratch()
                for ch in range(nch):
                    r0 = ch * 64
                    for hl in range(16):
                        h = hf * 16 + hl
                        mm_run(PS[r0:r0 + 64, pd[hl // 8], (hl % 8) * 64:(hl % 8 + 1) * 64],
                               [(mh[r0:r0 + 64, hl, :], xc[r0:r0 + 64, h * 64:(h + 1) * 64])],
                               reads=[rmh, rxc], writes=[rB[pd[0]], rB[pd[1]]])
                yield
                yds = de[0:tb, :, :].rearrange("p h l -> p (h l)")
                kb.op("act", lambda e: e.copy(out=yds, in_=PS[0:tb, pd[0]:pd[1] + 1, :].rearrange("p a n -> p (a n)")),
                      reads=[rB[pd[0]], rB[pd[1]]], writes=[rde])
                yield
                for ch in range(nch):
                    r0 = ch * 64
                    cs = slice(tok0 + r0, tok0 + r0 + 64)
                    sb_in, rsb_in = SBF[ch % 2], rSBFh[ch % 2][hf]
                    if nch == 2:
                        sb_out, rsb_out = SBF[(ch + 1) % 2], rSBFh[(ch + 1) % 2][hf]
                    else:
                        sb_out, rsb_out = SBF[0], rSBFh[0][hf]
                    for g in range(4):
                        gg = hf * 4 + g
                        mm_run(PS[r0:r0 + 64, yo[g // 2], (g % 2) * 256:(g % 2 + 1) * 256],
                               [(CF[0][:, gg, cs], sb_in[:, gg * 256:(gg + 1) * 256])],
                               reads=[CF[1], rsb_in], writes=[rB[yo[0]], rB[yo[1]]])
                    ps_ = scratch()
                    for g in range(4):
                        gg = hf * 4 + g
                        mm_run(PS[:, ps_[g // 2], (g % 2) * 256:(g % 2 + 1) * 256],
                               [(btt[r0:r0 + 64, gg * 128:(gg + 1) * 128], xcd[r0:r0 + 64, g * 256:(g + 1) * 256])],
                               reads=[rbtt, rxcd], writes=[rB[ps_[0]], rB[ps_[1]]])
                    yield
                    stv = ST[:, hf * 1024:(hf + 1) * 1024]
                    kb.op("dve", lambda e, ch=ch: e.tensor_tensor(
                        out=t1[:, :].rearrange("p (h d) -> p h d", h=16), in0=stv.rearrange("p (h d) -> p h d", h=16),
                        in1=sm[:, C_CD + ch * 32 + hs.start:C_CD + ch * 32 + hs.stop].unsqueeze(2).broadcast_to([128, 16, 64]),
                        op=ALU.mult), reads=[rSTh[hf], rsm], writes=[rt1])
                    yield
                    kb.op("dve", lambda e, ps_=ps_: e.tensor_tensor(
                        out=stv, in0=PS[:, ps_[0]:ps_[1] + 1, :].rearrange("p a n -> p (a n)"), in1=t1[:, :], op=ALU.add),
                        reads=[rB[ps_[0]], rB[ps_[1]], rt1], writes=[rSTh[hf]])
                    yield
                    kb.op("act", lambda e, sb_out=sb_out: e.copy(out=sb_out[:, hf * 1024:(hf + 1) * 1024], in_=stv),
                          reads=[rSTh[hf]], writes=[rsb_out])
                    yield
                kb.op("dve", lambda e: e.tensor_tensor(
                    out=y1[0:tb, :].rearrange("p (h d) -> p h d", h=16),
                    in0=PS[0:tb, yo[0]:yo[1] + 1, :].rearrange("p a (h d) -> p (a h) d", d=64),
                    in1=sm[0:tb, C_EA + hs.start:C_EA + hs.stop].unsqueeze(2).broadcast_to([tb, 16, 64]),
                    op=ALU.mult), reads=[rB[yo[0]], rB[yo[1]], rsm], writes=[ry1])
                yield
                kb.op("dve", lambda e: e.tensor_tensor(out=y1[0:tb, :], in0=y1[0:tb, :], in1=yds, op=ALU.add),
                      reads=[ry1, rde], writes=[ry1])
                yield
                kb.op("dve", lambda e: e.tensor_tensor(
                    out=t1[0:tb, :].rearrange("p (h d) -> p h d", h=16),
                    in0=xst[0:tb, hf * 1024:(hf + 1) * 1024].rearrange("p (h d) -> p h d", h=16),
                    in1=PF[0:tb, PF_DSK + hs.start:PF_DSK + hs.stop].unsqueeze(2).broadcast_to([tb, 16, 64]),
                    op=ALU.mult), reads=[rxst, rPF], writes=[rt1])
                yield
                kb.op("dve", lambda e: e.tensor_tensor(out=t1[0:tb, :], in0=t1[0:tb, :], in1=y1[0:tb, :], op=ALU.add),
                      reads=[rt1, ry1], writes=[rt1])
                yield
                kb.op("dve", lambda e: e.tensor_tensor(
                    out=y1[0:tb, :], in0=t1[0:tb, :], in1=SZ[0][0:tb, b, hf * 1024:(hf + 1) * 1024], op=ALU.mult),
                    reads=[rt1, SZ[1]], writes=[ry1])
                yield
                if hf == 0 and b == 0:
                    dump("yg0", y1[0:tb, :], ry1, [tb, 1024], F32)
                    dump("yds0", yds, rde, [tb, 1024], F32)
                for g in range(4):
                    kb.op("act", lambda e, g=g: e.activation(
                        out=t1[0:tb, g * 256:(g + 1) * 256], in_=y1[0:tb, g * 256:(g + 1) * 256],
                        func=AF.Square, accum_out=STAT[0:tb, stc + g:stc + g + 1]),
                        reads=[ry1], writes=[rt1, rstc])
                yield
                yield from g_rstd(stc, rstc, 0, 4, 256, tb, w=4)
                for g in range(4):
                    kb.op("act", lambda e, g=g: e.activation(
                        out=yn[0:tb, g * 256:(g + 1) * 256], in_=y1[0:tb, g * 256:(g + 1) * 256],
                        func=AF.Identity, scale=STAT[0:tb, stc + 4 + g:stc + 5 + g]),
                        reads=[ry1, rstc], writes=[ryn])
                yield
                pt = scratch()
                tr_run([(PSb[:, pt[0], c * 128:c * 128 + tb], yn[0:tb, c * 128:(c + 1) * 128]) for c in range(8)],
                       reads=[ryn, rCB], writes=[rB[pt[0]]], idn=ident[0:tb, 0:tb])
                yield
                kb.op("act", lambda e: e.copy(
                    out=YNF[0][:, hf * 8:(hf + 1) * 8, b * 128:b * 128 + tb],
                    in_=PSb[:, pt[0], :].rearrange("p (k t) -> p k t", k=8)[:, :, 0:tb]), reads=[rB[pt[0]]], writes=[YNF[1]])
                yield

            for _ in g_prep(0):
                pass
            for b in range(nb):
                gens = []
                if b + 1 < nb:
                    gens.append((g_prep(b + 1), 0))
                gens += [(g_chain(b, 0), 0), (g_chain(b, 1), CH_OFF)]
                if os.environ.get('SEQ'):
                    for g_, _o in gens:
                        for _ in g_:
                            pass
                else:
                    round_robin_off(gens)
            if last:
                kb.dma("sp", ost[sidx], ST[:], rSTh[0], reads=[rSTh[0], rSTh[1]])
            dump("xn", XN[:, :, 0:T], rXN, [128, 8, T], BF16)
            dump("ynf", YNF[0][:, :, 0:T], YNF[1], [128, 16, T], BF16)

            gate(4)
            def fm_group(pan, rpan, jj, act, ract):
                bk = acc_bank()
                mm_run(PS[:, bk, 0:T], [(pan[:, kc, jj * 128:(jj + 1) * 128], act[:, kc, 0:T]) for kc in range(8)],
                       reads=[ract, rpan], writes=[rB[bk]])
                return bk

            for hf in range(2):
                pan, rpan = next_panel()
                for jj in range(4):
                    bk = fm_group(pan, rpan, jj, XN, rXN)
                    kb.op("act", lambda e, bk=bk, jj=jj: e.copy(out=ACJ[jj][0][:, 0:T], in_=PS[:, bk, 0:T]),
                          reads=[rB[bk]], writes=[ACJ[jj][1]])
                pan, rpan = next_panel()
                for jj in range(4):
                    j = hf * 4 + jj
                    kb.op("pool", lambda e, jj=jj, j=j: e.tensor_copy(out=UJ[jj][0][:, 0:2], in_=HISTA[:, j, :]),
                          reads=[rHISTA], writes=[UJ[jj][1]])
                    bk = fm_group(pan, rpan, jj, XN, rXN)
                    kb.op("dve", lambda e, bk=bk, jj=jj: e.tensor_tensor(out=UJ[jj][0][:, 2:2 + T], in0=PS[:, bk, 0:T],
                                                                        in1=ACJ[jj][0][:, 0:T], op=ALU.mult),
                          reads=[rB[bk], ACJ[jj][1]], writes=[UJ[jj][1]])
                    kb.op("pool", lambda e, jj=jj, j=j: e.tensor_copy(out=HISTA[:, j, :], in_=UJ[jj][0][:, T:T + 2]),
                          reads=[UJ[jj][1]], writes=[rHISTA])
                for jj in range(4):
                    j = hf * 4 + jj
                    kb.op("act", lambda e, jj=jj, j=j: e.activation(
                        out=TTJ[jj][0][:, 0:T], in_=UJ[jj][0][:, 0:T], func=AF.Identity,
                        scale=PF[:, PF_CAW + j * 3:PF_CAW + j * 3 + 1]), reads=[UJ[jj][1], rPF], writes=[TTJ[jj][1]])
                pan, rpan = next_panel()
                bks = [fm_group(pan, rpan, jj, XN, rXN) for jj in range(4)]
                for i in (1, 2):
                    for jj in range(4):
                        j = hf * 4 + jj
                        kb.op("dve", lambda e, jj=jj, j=j, i=i: e.scalar_tensor_tensor(
                            out=TTJ[jj][0][:, 0:T], in0=UJ[jj][0][:, i:i + T],
                            scalar=PF[:, PF_CAW + j * 3 + i:PF_CAW + j * 3 + i + 1], in1=TTJ[jj][0][:, 0:T],
                            op0=ALU.mult, op1=ALU.add), reads=[UJ[jj][1], TTJ[jj][1], rPF], writes=[TTJ[jj][1]])
                for jj in range(4):
                    j = hf * 4 + jj
                    bk = bks[jj]
                    kb.op("dve", lambda e, bk=bk, jj=jj, j=j: e.tensor_tensor(out=V[0][:, j, 0:T], in0=PS[:, bk, 0:T],
                                                                             in1=TTJ[jj][0][:, 0:T], op=ALU.mult),
                          reads=[rB[bk], TTJ[jj][1]], writes=[V[1]])
            if last:
                kb.dma("sp", oca[sidx].rearrange("p (j i) -> p j i", i=2), HISTA[:], rHISTA, reads=[rHISTA])
            gate(5)
            dump("v", V[0][:, :, 0:T], V[1], [128, 8, T], BF16)
            for hf in range(2):
                pan, rpan = next_panel()
                for jj in range(4):
                    j = hf * 4 + jj
                    bk = fm_group(pan, rpan, jj, XN, rXN)
                    kb.op("act", lambda e, bk=bk, jj=jj, j=j: e.activation(
                        out=GJ[jj][0][:, 0:T], in_=PS[:, bk, 0:T], func=AF.Sigmoid, bias=PF[:, PF_BG + j:PF_BG + j + 1]),
                        reads=[rB[bk], rPF], writes=[GJ[jj][1]])
                pan, rpan = next_panel()
                for jj in range(4):
                    j = hf * 4 + jj
                    bk = fm_group(pan, rpan, jj, V[0], V[1])
                    kb.op("dve", lambda e, bk=bk, jj=jj, j=j: e.tensor_tensor(out=PA[0][:, j, 0:T], in0=PS[:, bk, 0:T],
                                                                             in1=GJ[jj][0][:, 0:T], op=ALU.mult),
                          reads=[rB[bk], GJ[jj][1]], writes=[PA[1]])
            dump("pa", PA[0][:, :, 0:T], PA[1], [128, 8, T], F32)
            for hf in range(2):
                pan, rpan = next_panel()
                for jj in range(4):
                    j = hf * 4 + jj
                    bk = fm_group(pan, rpan, jj, XN, rXN)
                    kb.op("act", lambda e, bk=bk, jj=jj, j=j: e.activation(
                        out=GJ[jj][0][:, 0:T], in_=PS[:, bk, 0:T], func=AF.Sigmoid,
                        bias=PF[:, PF_BG + 8 + j:PF_BG + 8 + j + 1]), reads=[rB[bk], rPF], writes=[GJ[jj][1]])
                pan0, rpan0 = next_panel()
                pan1, rpan1 = next_panel()
                for jj in range(4):
                    j = hf * 4 + jj
                    bk = acc_bank()
                    pairs = [(pan0[:, kc, jj * 128:(jj + 1) * 128], YNF[0][:, kc, 0:T]) for kc in range(8)] + \
                            [(pan1[:, kc, jj * 128:(jj + 1) * 128], YNF[0][:, 8 + kc, 0:T]) for kc in range(8)]
                    mm_run(PS[:, bk, 0:T], pairs, reads=[YNF[1], rpan0, rpan1], writes=[rB[bk]])
                    kb.op("dve", lambda e, bk=bk, jj=jj: e.tensor_tensor(out=GJ[jj][0][:, 0:T], in0=PS[:, bk, 0:T],
                                                                        in1=GJ[jj][0][:, 0:T], op=ALU.mult),
                          reads=[rB[bk], GJ[jj][1]], writes=[GJ[jj][1]])
                    kb.op("dve", lambda e, jj=jj, j=j: e.tensor_tensor(out=MG[0][:, j, 0:T], in0=GJ[jj][0][:, 0:T],
                                                                      in1=PA[0][:, j, 0:T], op=ALU.add),
                          reads=[GJ[jj][1], PA[1]], writes=[MG[1]])
            gate(6)
            dump("mg", MG[0][:, :, 0:T], MG[1], [128, 8, T], BF16)
            def g_post(b, src, wcol, addsrc):
                (sap, sres) = src
                (aap, ares) = addsrc
                o = 16 * b
                rst = rSTATb[b]
                jk, rjk = JUNK4[b]
                kb.op("act", lambda e: e.activation(out=jk[0:tb, :], in_=sap[0:tb, :], func=AF.Square,
                                                    accum_out=STAT[0:tb, o:o + 1]), reads=[sres], writes=[rjk, rst])
                yield
                yield from g_rstd(o, rst, 0, 1, 1024, tb)
                kb.op("dve", lambda e: e.scalar_tensor_tensor(
                    out=sap[0:tb, :], in0=sap[0:tb, :], scalar=STAT[0:tb, o + 1:o + 2], in1=PF[0:tb, wcol:wcol + 1024],
                    op0=ALU.mult, op1=ALU.mult), reads=[sres, rst, rPF], writes=[sres])
                yield
                kb.op("dve", lambda e: e.tensor_tensor(out=sap[0:tb, :], in0=sap[0:tb, :], in1=aap[0:tb, :], op=ALU.add),
                      reads=[sres, ares], writes=[sres])
                yield

            def g_wo(b):
                xt, rxt = XT2[b]
                kb.dma("sp", xt[0:tb, :], xd[b * 128:b * 128 + tb, :], rxt, writes=[rxt])
                yield from g_post(b, MO[b], PF_NMPOST, XT2[b])
                yield from g_norm_to_fm(MO[b][0], MO[b][1], b, tb, HS16[b], XN, rXN)
            wpan = [next_panel(), next_panel()]
            wog = []
            for b in range(nb):
                for hf in range(2):
                    pan, rpan = wpan[hf]
                    bk = acc_bank()
                    mm_run(PS[0:tb, bk, :], [(MG[0][:, kc, b * 128:b * 128 + tb], pan[:, kc, :]) for kc in range(8)],
                           reads=[MG[1], rpan], writes=[rB[bk]])
                    kb.op("act", lambda e, bk=bk, b=b, hf=hf: e.copy(out=MO[b][0][0:tb, hf * 512:(hf + 1) * 512],
                                                                    in_=PS[0:tb, bk, :]), reads=[rB[bk]], writes=[MO[b][1]])
                wog.append(g_wo(b))
                for _rep in range(2):
                    for g_ in list(wog):
                        try:
                            next(g_)
                        except StopIteration:
                            wog.remove(g_)
            round_robin(wog)
            gate(7)
            dump("h", MO[0][0][0:tb, :], MO[0][1], [tb, 1024], F32)
            for q in range(8):
                pan, rpan = next_panel()
                for jj in range(4):
                    c = q * 4 + jj
                    bk = acc_bank()
                    mm_run(PS[:, bk, 0:T], [(pan[:, kc, jj * 128:(jj + 1) * 128], XN[:, kc, 0:T]) for kc in range(8)],
                           reads=[rXN, rpan], writes=[rB[bk]])
                    rt, rrt = RT[c % 4]
                    kb.op("act", lambda e, bk=bk, rt=rt: e.activation(out=rt[:, 0:T], in_=PS[:, bk, 0:T], func=AF.Relu),
                          reads=[rB[bk]], writes=[rrt])
                    kb.op("dve", lambda e, rt=rt, c=c: e.tensor_tensor(out=FF[0][:, c, 0:T], in0=rt[:, 0:T], in1=rt[:, 0:T],
                                                                      op=ALU.mult), reads=[rrt], writes=[FF[1]])
            nxt = p0_gens(ti + 1) if (ti + 1 < len(tiles) and stop_after is None) else []
            for hf in range(2):
                for kg in range(4):
                    pan, rpan = next_panel()
                    for b in range(nb):
                        mm_run(PS[0:tb, b, :],
                               [(FF[0][:, kg * 8 + kc, b * 128:b * 128 + tb], pan[:, kc, :]) for kc in range(8)],
                               reads=[FF[1], rpan], writes=[rB[b]], first=(kg == 0), last=(kg == 3))
                    for g_ in list(nxt):
                        try:
                            next(g_)
                        except StopIteration:
                            nxt.remove(g_)
                for b in range(nb):
                    kb.op("act", lambda e, b=b, hf=hf: e.copy(out=FO[b][0][0:tb, hf * 512:(hf + 1) * 512], in_=PS[0:tb, b, :]),
                          reads=[rB[b]], writes=[FO[b][1]])
            round_robin(nxt)
            dump("ff", FF[0][:, :, 0:T], FF[1], [128, 32, T], BF16)
            dump("fo", FO[0][0][0:tb, :], FO[0][1], [tb, 1024], F32)
            def g_fin(b):
                yield from g_post(b, FO[b], PF_NFPOST, MO[b])
                kb.dma("sp", yd[b * 128:b * 128 + tb, :], FO[b][0][0:tb, :], FO[b][1], reads=[FO[b][1]])
            round_robin([g_fin(b) for b in range(nb)])
        for ti, tl in enumerate(tiles):
            try:
                run_tile(ti, *tl)
            except _Stop:
                pass
        kb.finish()
        block = es.enter_context(nc.Block())
        kb.replay(block)
    return nc


def _host_tables(p):
    pf = np.zeros((128, PF_COLS), np.float32)
    pf[:, PF_CAW:PF_CAW + 24] = p["conv_a_w"][0].reshape(3, 8, 128).transpose(2, 1, 0).reshape(128, 24)
    pf[:, PF_CSW:PF_CSW + 128] = p["conv_ssm_w"][0].reshape(4, 32, 128).transpose(2, 1, 0).reshape(128, 128)
    pf[:, PF_CSB:PF_CSB + 32] = p["conv_ssm_b"][0].reshape(32, 128).T
    pf[:, PF_BG:PF_BG + 16] = p["b_gate"][0].reshape(16, 128).T
    pf[:, PF_NMP:PF_NMP + 8] = p["norm_mix_pre"][0].reshape(8, 128).T
    pf[:, PF_NFP:PF_NFP + 8] = p["norm_ffn_pre"][0].reshape(8, 128).T
    pf[:, PF_SSN:PF_SSN + 16] = p["ssm_norm"][0].reshape(16, 128).T
    pf[:, PF_DTB:PF_DTB + 32] = p["dt_bias"][0][None, :]
    pf[:, PF_ALOG:PF_ALOG + 32] = p["a_log"][0][None, :]
    pf[:, PF_DSK:PF_DSK + 32] = p["d_skip"][0][None, :]
    pf[:, PF_NMPOST:PF_NMPOST + 1024] = p["norm_mix_post"][0][None, :]
    pf[:, PF_NFPOST:PF_NFPOST + 1024] = p["norm_ffn_post"][0][None, :]
    cb = np.zeros((128, CB_COLS), np.float32)
    k = np.arange(128)
    cb[:, CB_ID:CB_ID + 128] = np.eye(128)
    l = np.arange(64)
    cb[:, CB_TRI2:CB_TRI2 + 64] = ((k[:, None] % 64) <= l[None, :])
    same = (k[:, None] // 64) == (k[None, :] // 64)
    cb[:, CB_BONES:CB_BONES + 128] = same
    cb[:, CB_ONESA:CB_ONESA + 128] = (k[:, None] < 64) * np.ones((1, 128))
    cb[:, CB_ONESB:CB_ONESB + 128] = (k[:, None] >= 64) * np.ones((1, 128))
    cb[:, CB_TRI128:CB_TRI128 + 128] = same & ((k[:, None] % 64) <= (k[None, :] % 64))
    neg = np.where((k[:, None] % 64) > l[None, :], -30000.0, 0.0)
    cb[:, CB_NEG:CB_NEG + 1024] = np.tile(neg, (1, 16))
    return pf, cb.astype(ml_dtypes.bfloat16)


_NC_CACHE = {}


def kernel(x_prompt, x_sample, state_conv_a, state_conv_ssm, state_ssm, norm_mix_pre, w_in, b_gate, conv_a_w, w_a_out,
           conv_ssm_w, conv_ssm_b, dt_bias, a_log, d_skip, ssm_norm, w_ssm_out, w_o, norm_mix_post, norm_ffn_pre,
           w_ff1, w_ff2, norm_ffn_post, _tiles_limit=None, _cores=None, _dbg=False, _stop=None):
    p = dict(norm_mix_pre=norm_mix_pre, b_gate=b_gate, conv_a_w=conv_a_w, conv_ssm_w=conv_ssm_w, conv_ssm_b=conv_ssm_b,
             dt_bias=dt_bias, a_log=a_log, d_skip=d_skip, ssm_norm=ssm_norm, norm_mix_post=norm_mix_post,
             norm_ffn_pre=norm_ffn_pre, norm_ffn_post=norm_ffn_post)
    p = {k: np.asarray(v, np.float32) for k, v in p.items()}
    pf, cb = _host_tables(p)
    x_prompt = np.asarray(x_prompt, np.float32)
    x_sample = np.asarray(x_sample, np.float32)
    sca_all = np.asarray(state_conv_a, np.float32)[0]
    scm_all = np.asarray(state_conv_ssm, np.float32)[0]
    sst_all = np.asarray(state_ssm, np.float32)[0]
    key = (None if _tiles_limit is None else tuple(_tiles_limit), _dbg, _stop)
    if key not in _NC_CACHE:
        _NC_CACHE[key] = build_program(_tiles_limit, _dbg, _stop)
    nc = _NC_CACHE[key]
    cores = list(range(NCORES)) if _cores is None else _cores
    shared = dict(pf=pf, cb=cb, w_in=np.ascontiguousarray(np.asarray(w_in, np.float32)[0]),
                  w_a_out=np.ascontiguousarray(np.asarray(w_a_out, np.float32)[0]),
                  w_ssm_out=np.ascontiguousarray(np.asarray(w_ssm_out, np.float32)[0]),
                  w_o=np.ascontiguousarray(np.asarray(w_o, np.float32)[0]),
                  w_ff1=np.ascontiguousarray(np.asarray(w_ff1, np.float32)[0]),
                  w_ff2=np.ascontiguousarray(np.asarray(w_ff2, np.float32)[0]))
    in_maps = []
    for c in cores:
        s2 = slice(2 * c, 2 * c + 2)
        m = dict(shared)
        m["xp"] = np.ascontiguousarray(x_prompt[s2])
        m["xs"] = np.ascontiguousarray(x_sample[s2])
        m["sca"] = np.ascontiguousarray(sca_all[s2].reshape(2, 2, 8, 128).transpose(0, 3, 2, 1).reshape(2, 128, 16))
        m["scm"] = np.ascontiguousarray(scm_all[s2].reshape(2, 3, 32, 128).transpose(0, 3, 2, 1).reshape(2, 128, 96))
        m["sst"] = np.ascontiguousarray(sst_all[s2].reshape(2, 2048, 128).transpose(0, 2, 1))
        in_maps.append(m)
    res = run_bass_kernel_spmd(nc, in_maps, core_ids=list(range(len(cores))))
    R = res.results
    if _dbg:
        kernel.dbg = {k: v for k, v in R[0].items() if k.startswith("dbg_")}
    yp = np.concatenate([r["yp"] for r in R], axis=0)
    ys_ = np.concatenate([r["ys"] for r in R], axis=0)
    oca = np.stack([r["oca"] for r in R])
    ocm = np.stack([r["ocm"] for r in R])
    ost = np.stack([r["ost"] for r in R])
    n = len(cores)

    def un_ca(a):
        return a.reshape(n * 2, 128, 8, 2).transpose(0, 3, 2, 1).reshape(n * 2, 2, 1024)

    def un_cm(a):
        return a.reshape(n * 2, 128, 32, 3).transpose(0, 3, 2, 1).reshape(n * 2, 3, 4096)

    def un_st(a):
        return a.reshape(n * 2, 128, 2048).transpose(0, 2, 1).reshape(n * 2, 32, 64, 128)

    outs = (yp, ys_,
            un_ca(oca[:, 0:2])[None], un_cm(ocm[:, 0:2])[None], un_st(ost[:, 0:2])[None],
            un_ca(oca[:, 2:4])[None], un_cm(ocm[:, 2:4])[None], un_st(ost[:, 2:4])[None])
    return tuple(np.ascontiguousarray(o, dtype=np.float32) for o in outs)
```

```python
import numpy as np
import ml_dtypes
from contextlib import ExitStack
import concourse.bass as bass
import concourse.mybir as mybir
from concourse.bass_utils import run_bass_kernel_spmd

F32 = mybir.dt.float32
BF16 = mybir.dt.bfloat16
AF = mybir.ActivationFunctionType
ALU = mybir.AluOpType
AX = mybir.AxisListType

NCORES = 8
D = 1024
SEQ = 2048
DSEQ = 64
PROJ = 11296
OFF_AB, OFF_AC, OFF_AH, OFF_Z, OFF_XBC, OFF_DT, OFF_GATE = 0, 1024, 2048, 3072, 5120, 9216, 9248
EPS = 1e-6
NS = 4

PF_CAW = 0
PF_CSW = 24
PF_CSB = 152
PF_BG = 184
PF_NMP = 200
PF_NFP = 208
PF_SSN = 216
PF_DTB = 232
PF_ALOG = 264
PF_DSK = 296
PF_NMPOST = 328
PF_NFPOST = 1352
PF_COLS = 2376
CB_ID = 0
CB_TRI2 = 128
CB_BONES = 192
CB_ONESA = 320
CB_ONESB = 448
CB_TRI128 = 576
CB_NEG = 704
CB_COLS = 1728


class Res:
    __slots__ = ("name", "lw", "rd", "ov", "dsem", "dcnt", "rng", "excl", "disj")

    def __init__(self, name):
        self.name = name
        self.lw = None
        self.rd = {}
        self.ov = []
        self.dsem = None
        self.dcnt = 0
        self.rng = None
        self.disj = False
        self.excl = False


class KB:
    ENG = ("pe", "act", "dve", "pool")

    def __init__(self, nc, es):
        self.nc = nc
        self.es = es
        self.streams = {k: [] for k in ("pe", "act", "dve", "pool", "sp")}
        self.sem = {e: es.enter_context(nc.semaphore("s_" + e)) for e in self.ENG}
        self.tick = {e: 0 for e in self.ENG}
        self.waited = {k: {} for k in self.streams}
        self.dsems = []
        self.all_res = []

    def res(self, name):
        r = Res(name)
        self.all_res.append(r)
        return r

    def _dsem(self, r):
        if r.dsem is None:
            r.dsem = self.es.enter_context(self.nc.semaphore("d%d" % len(self.dsems)))
            self.dsems.append(r)
        return r.dsem

    def _deps(self, eng, reads, writes):
        deps = []
        for r in reads:
            if r.lw is not None:
                deps.append(r.lw)
            if r.excl:
                for q, t in r.rd.items():
                    if q != eng:
                        deps.append((q, t))
        same_ok = (eng == "pe")
        for w in writes:
            for x in [w] + w.ov:
                if x.lw is not None and (x.lw[0] != eng or not (same_ok or (x is w and w.disj))):
                    deps.append(x.lw)
                for q, t in x.rd.items():
                    if q != eng or not same_ok:
                        deps.append((q, t))
        wd = self.waited[eng]
        for q, t in deps:
            if wd.get(q, 0) < t:
                wd[q] = t
                self.streams[eng].append(("w", q, t))

    def wait(self, eng, q, t):
        wd = self.waited[eng]
        if wd.get(q, 0) < t:
            wd[q] = t
            self.streams[eng].append(("w", q, t))

    def op(self, eng, fn, reads=(), writes=(), inc=True):
        self._deps(eng, reads, writes)
        t = self.tick[eng] + 1
        if inc:
            self.tick[eng] = t
        self.streams[eng].append(("op", fn, inc))
        for r in reads:
            r.rd[eng] = t
        for w in writes:
            w.lw = (eng, t)
            w.rd = {}
        return t

    def dma(self, eng, out, in_, dres, reads=(), writes=()):
        self._deps(eng, reads, writes)
        self._dsem(dres)
        dres.dcnt += 16
        key = (id(dres), dres.dcnt)
        q = ("d", id(dres))
        self._dq = getattr(self, "_dq", {})
        self._dq[q] = dres
        self.streams[eng].append(("dma", out, in_, dres))
        for r in reads:
            r.rd[q] = dres.dcnt
        for w in writes:
            w.lw = (q, dres.dcnt)
            w.rd = {}

    def semof(self, q):
        if isinstance(q, tuple):
            return self._dq[q].dsem
        return self.sem[q]

    def finish(self):
        for r in self.dsems:
            self.wait("sp", ("d", id(r)), r.dcnt)

    def replay(self, block):
        handles = {"pe": block.tensor, "act": block.scalar, "dve": block.vector, "pool": block.gpsimd,
                   "sp": block.sync}
        for eng, deco in handles.items():
            stream = self.streams[eng]
            own = self.sem.get(eng)

            def body(e, stream=stream, own=own):
                for ent in stream:
                    k = ent[0]
                    if k == "w":
                        e.wait_ge(self.semof(ent[1]), ent[2])
                    elif k == "op":
                        ins = ent[1](e)
                        if ent[2]:
                            ins.then_inc(own, 1)
                    else:
                        e.dma_start(out=ent[1], in_=ent[2]).then_inc(ent[3].dsem, 16)
            deco(body)


class Arena:
    def __init__(self, kb, tensor, nbytes):
        self.kb = kb
        self.t = tensor
        self.nbytes = nbytes
        self.views = []

    def view(self, name, off, shape, dt):
        esz = 2 if dt == BF16 else 4
        n = int(np.prod(shape))
        nb = n * esz
        assert off % 4 == 0 and off + nb <= self.nbytes, (name, off, nb, self.nbytes)
        w0, w1 = off // 4, (off + nb + 3) // 4
        ap = self.t[:, w0:w1]
        if dt == BF16:
            ap = ap.bitcast(BF16)[:, 0:n]
        if len(shape) == 2:
            ap = ap.rearrange("p (a b) -> p a b", a=shape[0])
        elif len(shape) == 3:
            ap = ap.rearrange("p (a b c) -> p a b c", a=shape[0], b=shape[1])
        r = self.kb.res(name)
        r.rng = (off, off + nb)
        for (o, rr) in self.views:
            if o.rng[0] < r.rng[1] and r.rng[0] < o.rng[1]:
                r.ov.append(o)
                o.ov.append(r)
        self.views.append((r, ap))
        return ap, r


import os
VARP = int(os.environ.get('VARP', '128'))
CH_OFF = int(os.environ.get('CH_OFF', '0'))
NACC = int(os.environ.get('NACC', '4'))


class _Stop(Exception):
    pass


def build_program(tiles_limit=None, dbg=False, stop_after=None):
    nc = bass.Bass("TRN2", target_bir_lowering=False)
    dram_in = lambda name, shape, dt=F32: nc.dram_tensor(name, shape, dt, kind="ExternalInput").ap()
    dram_out = lambda name, shape, dt=F32: nc.dram_tensor(name, shape, dt, kind="ExternalOutput").ap()
    xp = dram_in("xp", [2, SEQ, D])
    xs = dram_in("xs", [2, DSEQ, D])
    sca = dram_in("sca", [2, 128, 16])
    scm = dram_in("scm", [2, 128, 96])
    sst = dram_in("sst", [2, 128, 2048])
    pf_d = dram_in("pf", [128, PF_COLS])
    cb_d = dram_in("cb", [128, CB_COLS], BF16)
    w_in = dram_in("w_in", [D, PROJ])
    w_a = dram_in("w_a_out", [D, D])
    w_s = dram_in("w_ssm_out", [2 * D, D])
    w_o = dram_in("w_o", [D, D])
    w_f1 = dram_in("w_ff1", [D, 4 * D])
    w_f2 = dram_in("w_ff2", [4 * D, D])
    yp = dram_out("yp", [2, SEQ, D])
    ys = dram_out("ys", [2, DSEQ, D])
    oca = dram_out("oca", [4, 128, 16])
    ocm = dram_out("ocm", [4, 128, 96])
    ost = dram_out("ost", [4, 128, 2048])
    wi_b = nc.dram_tensor("wi_b", [D, PROJ], BF16).ap()
    wa_b = nc.dram_tensor("wa_b", [D, D], BF16).ap()
    ws_b = nc.dram_tensor("ws_b", [2 * D, D], BF16).ap()
    wo_b = nc.dram_tensor("wo_b", [D, D], BF16).ap()
    wf1_b = nc.dram_tensor("wf1_b", [D, 4 * D], BF16).ap()
    wf2_b = nc.dram_tensor("wf2_b", [4 * D, D], BF16).ap()

    with ExitStack() as es:
        kb = KB(nc, es)
        sbt = lambda name, shape, dt: es.enter_context(nc.sbuf_tensor(name, shape, dt))
        PF = sbt("PF", [128, PF_COLS], F32); rPF = kb.res("PF")
        CB = sbt("CB", [128, CB_COLS], BF16); rCB = kb.res("CB")
        PAN = [sbt("PAN%d" % i, [128, 8, 512], BF16) for i in range(NS)]
        rPAN = [kb.res("PAN%d" % i) for i in range(NS)]
        XN = sbt("XN", [128, 8, 512], BF16); rXN = kb.res("XN")
        DTW = sbt("DTW", [128, 8, 32], BF16); rDTW = kb.res("DTW")
        HISTA = sbt("HISTA", [128, 2, 8, 2], F32); rHISTA = kb.res("HISTA")
        HISTM = sbt("HISTM", [128, 2, 32, 3], F32); rHISTM = kb.res("HISTM")
        ST = sbt("ST", [128, 2048], F32)
        rSTh = [kb.res("ST_h%d" % i) for i in range(2)]
        SBF = [sbt("SBF%d" % i, [128, 2048], BF16) for i in range(2)]
        rSBFh = [[kb.res("SBF%d_h%d" % (i, hf)) for hf in range(2)] for i in range(2)]
        DECT = sbt("DECT", [128, 4, 32], F32)
        rDECT = [[kb.res("DEC%d_%d" % (b, hf)) for hf in range(2)] for b in range(4)]
        STATC = 64
        rSTATc = [[kb.res("STATC%d_%d" % (b, c_)) for c_ in range(2)] for b in range(4)]
        ANEG = sbt("ANEG", [128, 32], F32); rANEG = kb.res("ANEG")
        SM = sbt("SM", [128, 4, 320], F32)
        rSM = [kb.res("SM%d" % b) for b in range(4)]
        STAT = sbt("STAT", [128, 256], F32)
        rSTATb = [kb.res("STAT%d" % b) for b in range(4)]
        EPSC = sbt("EPSC", [128, 2], F32); rEPSC = kb.res("EPSC")
        C_DT, C_ADT, C_ACUM, C_EA, C_CD, C_DEC, C_T0, C_RS = 0, 32, 64, 96, 128, 192, 224, 256
        SMH = sbt("SMH", [128, 4, 64], BF16)
        rSMH = [kb.res("SMH%d" % b) for b in range(4)]
        KBY = 1024
        AR_BYTES = 128 * KBY
        ARt = sbt("ARENA", [128, AR_BYTES // 4], F32)
        ar = Arena(kb, ARt, AR_BYTES)
        XT = [ar.view("XT%d" % i, 81 * KBY + i * 4 * KBY, [1024], F32) for i in range(4)]
        XS16 = [ar.view("XS16_%d" % i, 97 * KBY + i * 2 * KBY, [1024], BF16) for i in range(4)]
        JUNK4 = [ar.view("JUNK%d" % i, 105 * KBY + i * 2 * KBY, [1024], BF16) for i in range(4)]
        ST1 = ar.view("ST1", 8 * KBY, [2048], F32)
        SZ = ar.view("SZ", 0, [4, 2048], BF16)
        XSF = ar.view("XSF", 16 * KBY, [16, 512], BF16)
        BF_ = ar.view("BF", 32 * KBY, [8, 512], BF16)
        CF = ar.view("CF", 40 * KBY, [8, 512], BF16)
        XBJ = [[ar.view("XB%d_%d" % (p_, jj), 48 * KBY + (p_ * 4 + jj) * 2064, [516], F32) for jj in range(4)]
               for p_ in range(3)]
        CVJ = [[ar.view("CV%d_%d" % (p_, jj), 73 * KBY + (p_ * 4 + jj) * 2048, [512], F32) for jj in range(4)]
               for p_ in range(2)]
        XStP = [ar.view("XSt%d" % i, 48 * KBY + i * 4 * KBY, [2048], BF16) for i in range(2)]
        BtP = [ar.view("Bt%d" % i, 56 * KBY + i * 2 * KBY, [1024], BF16) for i in range(2)]
        XCDh = [ar.view("XCD_%d" % hf, 60 * KBY + hf * 2 * KBY, [1024], BF16) for hf in range(2)]
        XCP = [ar.view("XC%d" % i, 64 * KBY + i * 4 * KBY, [2048], BF16) for i in range(2)]
        RHI = ar.view("RHI", 72 * KBY, [32, 64], BF16)
        RLO = ar.view("RLO", 76 * KBY, [32, 64], BF16)
        DEc = [ar.view("DE%d" % i, 80 * KBY + i * 4 * KBY, [16, 64], F32) for i in range(2)]
        MHc = [ar.view("MH%d" % i, 88 * KBY + i * 2 * KBY, [16, 64], BF16) for i in range(2)]
        T1c = [ar.view("T1_%d" % i, 92 * KBY + i * 4 * KBY, [1024], F32) for i in range(2)]
        Y1c = [ar.view("Y1_%d" % i, 100 * KBY + i * 4 * KBY, [1024], F32) for i in range(2)]
        YNc = [ar.view("YN_%d" % i, 108 * KBY + i * 2 * KBY, [1024], BF16) for i in range(2)]
        YNF = ar.view("YNF", 112 * KBY, [16, 512], BF16)
        ACJ = [ar.view("AC_%d" % jj, jj * 2048, [512], F32) for jj in range(4)]
        UJ = [ar.view("U_%d" % jj, 8 * KBY + jj * 2064, [516], F32) for jj in range(4)]
        TTJ = [ar.view("TT_%d" % jj, 17 * KBY + jj * 2048, [512], F32) for jj in range(4)]
        V = ar.view("V", 25 * KBY, [8, 512], BF16)
        GJ = [ar.view("G_%d" % jj, 33 * KBY + jj * 2048, [512], F32) for jj in range(4)]
        PA = ar.view("PA", 41 * KBY, [8, 512], F32)
        MG = ar.view("MG", 57 * KBY, [8, 512], BF16)
        MO = [ar.view("MO%d" % b, 65 * KBY + b * 4 * KBY, [1024], F32) for b in range(4)]
        XT2 = [ar.view("XT2_%d" % i, 81 * KBY + i * 4 * KBY, [1024], F32) for i in range(4)]
        HS16 = [ar.view("HS16_%d" % i, 97 * KBY + i * 2 * KBY, [1024], BF16) for i in range(4)]
        FF = ar.view("FF", 0, [32, 512], BF16)
        RT = [ar.view("RT%d" % i, 32 * KBY + i * 2 * KBY, [512], F32) for i in range(4)]
        FO = [ar.view("FO%d" % b, 40 * KBY + b * 4 * KBY, [1024], F32) for b in range(4)]
        SGW = 2824
        NSG = 4
        SGI = [ar.view("SGI%d" % i, i * 12 * KBY, [SGW], F32) for i in range(NSG)]
        SGO = [ar.view("SGO%d" % i, 48 * KBY + i * 6 * KBY, [SGW], BF16) for i in range(NSG)]
        PS = es.enter_context(nc.psum_tensor("PS", [128, 8, 512], F32))
        rB = [kb.res("bank%d" % i) for i in range(8)]
        for r_ in rB:
            r_.excl = True
        PSb = PS[:].bitcast(BF16)

        for r_ in [XSF[1], BF_[1], CF[1], V[1], PA[1], MG[1], FF[1], YNF[1], SZ[1], rXN] + [m_[1] for m_ in MO] + \
                [f_[1] for f_ in FO]:
            r_.disj = True
        ident = CB[:, CB_ID:CB_ID + 128]
        tri2 = CB[:, CB_TRI2:CB_TRI2 + 64]
        bones = CB[:, CB_BONES:CB_BONES + 128]
        onesA = CB[:, CB_ONESA:CB_ONESA + 128]
        onesB = CB[:, CB_ONESB:CB_ONESB + 128]
        tri128 = CB[:, CB_TRI128:CB_TRI128 + 128]
        negm = CB[:, CB_NEG:CB_NEG + 1024]

        dbg_names = []

        def dump(name, ap, res, shape, dt):
            if not dbg:
                return
            d = nc.dram_tensor("dbg_" + name, shape, dt, kind="ExternalOutput").ap()
            kb.dma("sp", d, ap, res, reads=[res])
            dbg_names.append("dbg_" + name)

        kb.dma("sp", PF[:], pf_d, rPF, writes=[rPF])
        kb.dma("sp", CB[:], cb_d, rCB, writes=[rCB])
        kb.op("pool", lambda e: e.memset(EPSC[:, 0:1], EPS), writes=[rEPSC])
        kb.op("pool", lambda e: e.memset(EPSC[:, 1:2], 1.0), writes=[rEPSC])
        kb.op("act", lambda e: e.activation(out=ANEG[:], in_=PF[:, PF_ALOG:PF_ALOG + 32], func=AF.Exp),
              reads=[rPF], writes=[rANEG])
        kb.op("dve", lambda e: e.tensor_scalar(out=ANEG[:], in0=ANEG[:], scalar1=-1.0, scalar2=None, op0=ALU.mult),
              reads=[rANEG], writes=[rANEG])

        conv_jobs = []
        def add_mat(src, dst, K, N, piece, scol):
            for rc in range(K // 128):
                for c0 in range(0, N, piece):
                    w = min(piece, N - c0)
                    conv_jobs.append((src[rc * 128:(rc + 1) * 128, c0:c0 + w], dst[rc * 128:(rc + 1) * 128, c0:c0 + w],
                                      None if scol is None else scol + rc, w))
        add_mat(w_in, wi_b, D, PROJ, SGW, PF_NMP)
        add_mat(w_a, wa_b, D, D, 1024, None)
        add_mat(w_s, ws_b, 2 * D, D, 1024, PF_SSN)
        add_mat(w_o, wo_b, D, D, 1024, None)
        add_mat(w_f1, wf1_b, D, 4 * D, 2048, PF_NFP)
        add_mat(w_f2, wf2_b, 4 * D, D, 1024, None)
        def conv_store(i):
            (src, dst, scol, w) = conv_jobs[i]
            so, ro = SGO[i % NSG]
            kb.dma("sp", dst, so[:, 0:w], ro, reads=[ro])
        for i, (src, dst, scol, w) in enumerate(conv_jobs):
            si, ri = SGI[i % NSG]
            so, ro = SGO[i % NSG]
            if i >= NSG - 1:
                conv_store(i - (NSG - 1))
            kb.dma("sp", si[:, 0:w], src, ri, writes=[ri])
            eng = "dve" if (i % 5) in (0, 2, 4) else "act"
            if eng == "dve":
                if scol is None:
                    kb.op(eng, lambda e, so=so, si=si, w=w: e.tensor_copy(out=so[:, 0:w], in_=si[:, 0:w]),
                          reads=[ri], writes=[ro])
                else:
                    kb.op(eng, lambda e, so=so, si=si, w=w, scol=scol: e.tensor_scalar(
                        out=so[:, 0:w], in0=si[:, 0:w], scalar1=PF[:, scol:scol + 1], scalar2=None, op0=ALU.mult),
                        reads=[ri, rPF], writes=[ro])
            else:
                if scol is None:
                    kb.op(eng, lambda e, so=so, si=si, w=w: e.copy(out=so[:, 0:w], in_=si[:, 0:w]),
                          reads=[ri], writes=[ro])
                else:
                    kb.op(eng, lambda e, so=so, si=si, w=w, scol=scol: e.activation(
                        out=so[:, 0:w], in_=si[:, 0:w], func=AF.Identity, scale=PF[:, scol:scol + 1]),
                        reads=[ri, rPF], writes=[ro])
        for i in range(max(0, len(conv_jobs) - (NSG - 1)), len(conv_jobs)):
            conv_store(i)
        for so, ro in SGO:
            kb.wait("sp", ("d", id(ro)), ro.dcnt)
        kb.dma("sp", DTW[:], wi_b[:, OFF_DT:OFF_DT + 32].rearrange("(kc p) n -> p kc n", p=128), rDTW, writes=[rDTW])

        tiles = []
        for s in range(2):
            for ti in range(4):
                tiles.append((s, xp[s, ti * 512:(ti + 1) * 512, :], yp[s, ti * 512:(ti + 1) * 512, :], 512,
                              ti == 0, ti == 3, None))
        tiles.append((2, None, None, 128, True, True, 0))
        if tiles_limit is not None:
            tiles = [tiles[i] for i in tiles_limit]

        def panel_list(T):
            pl = []
            for q in range(8):
                pl.append((wi_b, 0, OFF_XBC + q * 512))
                if q % 2 == 1:
                    pl.append((wi_b, 0, OFF_Z + (q // 2) * 512))
            for hf in range(2):
                pl.append((wi_b, 0, OFF_AC + hf * 512))
                pl.append((wi_b, 0, OFF_AH + hf * 512))
                pl.append((wi_b, 0, OFF_AB + hf * 512))
            for hf in range(2):
                pl.append((wi_b, 0, OFF_GATE + hf * 512))
                pl.append((wa_b, 0, hf * 512))
            for hf in range(2):
                pl.append((wi_b, 0, OFF_GATE + 1024 + hf * 512))
                pl.append((ws_b, 0, hf * 512))
                pl.append((ws_b, 1, hf * 512))
            for hf in range(2):
                pl.append((wo_b, 0, hf * 512))
            for q in range(8):
                pl.append((wf1_b, 0, q * 512))
            for hf in range(2):
                for kg in range(4):
                    pl.append((wf2_b, kg, hf * 512))
            return pl
        all_panels = []
        for tl in tiles:
            all_panels += panel_list(tl[3])
        pstate = {"issued": 0, "used": 0}

        def issue_panels(upto):
            while pstate["issued"] < min(upto, len(all_panels)):
                i = pstate["issued"]
                mat, kg, c0 = all_panels[i]
                slot = i % NS
                src = mat[kg * 1024:(kg + 1) * 1024, c0:c0 + 512].rearrange("(kc p) n -> p kc n", p=128)
                kb.dma("sp", PAN[slot][:], src, rPAN[slot], writes=[rPAN[slot]])
                pstate["issued"] += 1

        def next_panel():
            i = pstate["used"]
            issue_panels(i + NS - 1)
            pstate["used"] += 1
            return PAN[i % NS], rPAN[i % NS]

        bank_rr = {"i": 0}

        def acc_bank():
            i = bank_rr["i"]
            bank_rr["i"] = (i + 1) % NACC
            return i

        def mm_run(out, pairs, reads, writes, first=True, last=True):
            n = len(pairs)
            for i, (l, r) in enumerate(pairs):
                kb.op("pe", lambda e, l=l, r=r, st=(first and i == 0), sp_=(last and i == n - 1):
                      e.matmul(out, lhsT=l, rhs=r, start=st, stop=sp_),
                      reads=reads if i == 0 else (), writes=writes if i == 0 else (), inc=(i == n - 1))

        def tr_run(outs_ins, reads, writes, idn):
            n = len(outs_ins)
            for i, (o, a) in enumerate(outs_ins):
                kb.op("pe", lambda e, o=o, a=a: e.transpose(o, a, idn),
                      reads=reads if i == 0 else (), writes=writes if i == 0 else (), inc=(i == n - 1))

        def round_robin(gens):
            gens = list(gens)
            while gens:
                for g_ in list(gens):
                    try:
                        next(g_)
                    except StopIteration:
                        gens.remove(g_)

        def g_rstd(o, rst, ss_col, rs_col, n, tb, w=1):
            kb.op("act", lambda e: e.activation(out=STAT[0:tb, o + rs_col:o + rs_col + w], in_=STAT[0:tb, o + ss_col:o + ss_col + w],
                                                func=AF.Ln, scale=1.0 / n, bias=EPSC[0:tb, 0:1]),
                  reads=[rst, rEPSC], writes=[rst])
            yield
            kb.op("act", lambda e: e.activation(out=STAT[0:tb, o + rs_col:o + rs_col + w], in_=STAT[0:tb, o + rs_col:o + rs_col + w],
                                                func=AF.Exp, scale=-0.5),
                  reads=[rst], writes=[rst])
            yield

        def round_robin_off(gens):
            live = list(gens)
            step = 0
            while live:
                for item in list(live):
                    if item[1] <= step:
                        try:
                            next(item[0])
                        except StopIteration:
                            live.remove(item)
                step += 1

        def g_norm_to_fm(src_ap, src_res, b, tb, s16, dst, dst_res):
            (s16ap, s16res) = s16
            o = 16 * b
            rst = rSTATb[b]
            jk, rjk = JUNK4[b]
            kb.op("act", lambda e: e.activation(out=jk[0:tb, :], in_=src_ap[0:tb, :], func=AF.Square,
                                                accum_out=STAT[0:tb, o:o + 1]),
                  reads=[src_res], writes=[rjk, rst])
            yield
            yield from g_rstd(o, rst, 0, 1, 1024, tb)
            kb.op("dve", lambda e: e.tensor_scalar(out=s16ap[0:tb, :], in0=src_ap[0:tb, :], scalar1=STAT[0:tb, o + 1:o + 2],
                                                   scalar2=None, op0=ALU.mult),
                  reads=[src_res, rst], writes=[s16res])
            yield
            tpb = 4 + b
            tr_run([(PSb[:, tpb, kc * 128:kc * 128 + tb], s16ap[0:tb, kc * 128:(kc + 1) * 128]) for kc in range(8)],
                   reads=[s16res, rCB], writes=[rB[tpb]], idn=ident[0:tb, 0:tb])
            yield
            kb.op("act", lambda e: e.copy(out=dst[:, :, b * 128:b * 128 + tb],
                                          in_=PSb[:, tpb, :].rearrange("p (k t) -> p k t", k=8)[:, :, 0:tb]),
                  reads=[rB[tpb]], writes=[dst_res])
            yield

        def gate(n):
            if stop_after is not None and n > stop_after:
                raise _Stop()

        p0_done = set()

        def io_rows(ti, b, out=False):
            (_sidx, xd_, yd_, T_, _f, _l, samp_) = tiles[ti]
            if samp_ is None:
                tb_ = min(T_, 128)
                src = yd_ if out else xd_
                return [(src[b * 128:b * 128 + tb_, :], 0, tb_)]
            src = ys if out else xs
            return [(src[0], 0, 64), (src[1], 64, 64)]

        def p0_gens(ti):
            (_sidx, xd_, _yd, T_, _f, _l, _s) = tiles[ti]
            tb_ = min(T_, 128)

            def g_p0(b):
                xt, rxt = XT[b]
                for (src_, p0_, n_) in io_rows(ti, b):
                    kb.dma("sp", xt[p0_:p0_ + n_, :], src_, rxt, writes=[rxt])
                yield from g_norm_to_fm(xt, rxt, b, tb_, XS16[b], XN, rXN)
            p0_done.add(ti)
            return [g_p0(b) for b in range(T_ // tb_)]

        def run_tile(ti, sidx, xd, yd, T, first, last, samp):
            tb = min(T, 128)
            nb = T // tb
            nch = tb // 64
            seg = samp is not None
            nseg = 2 if seg else 1
            sl_ = T // nseg
            if first:
                if not seg:
                    kb.op("pool", lambda e: e.memset(HISTA[:, 0], 0.0), writes=[rHISTA])
                    kb.op("pool", lambda e: e.memset(HISTM[:, 0], 0.0), writes=[rHISTM])
                    kb.op("pool", lambda e: e.memset(ST[:], 0.0), writes=[rSTh[0], rSTh[1]])
                else:
                    for s_ in range(2):
                        kb.dma("sp", HISTA[:, s_], sca[s_].rearrange("p (j i) -> p j i", i=2), rHISTA, writes=[rHISTA])
                        kb.dma("sp", HISTM[:, s_], scm[s_].rearrange("p (c i) -> p c i", i=3), rHISTM, writes=[rHISTM])
                    kb.dma("sp", ST[:], sst[0], rSTh[0], writes=[rSTh[0], rSTh[1]])
                    kb.dma("sp", ST1[0][:, :], sst[1], ST1[1], writes=[ST1[1]])
                    kb.op("act", lambda e: e.copy(out=SBF[1][:], in_=ST1[0][:, :]), reads=[ST1[1]],
                          writes=[rSBFh[1][0], rSBFh[1][1]])
                kb.op("act", lambda e: e.copy(out=SBF[0][:], in_=ST[:]), reads=[rSTh[0], rSTh[1]],
                      writes=[rSBFh[0][0], rSBFh[0][1]])

            W3 = nseg * (3 + sl_) - 3
            W2 = nseg * (2 + sl_) - 2

            def segv(ap2d, H, lo, hi):
                P_ = H + sl_
                if nseg == 1:
                    return ap2d[:, lo:hi].unsqueeze(1)
                return ap2d[:, 0:nseg * P_].rearrange("p (s w) -> p s w", s=nseg)[:, :, lo:hi]

            if ti not in p0_done:
                round_robin(p0_gens(ti))

            gate(1)
            def b1_panel(q):
                pan, rpan = next_panel()
                for b in range(nb):
                    bk = acc_bank()
                    mm_run(PS[0:tb, bk, :], [(XN[:, kc, b * 128:b * 128 + tb], pan[:, kc, :]) for kc in range(8)],
                           reads=[rXN, rpan], writes=[rB[bk]])
                    kb.op("act", lambda e, bk=bk, b=b, q=q: e.activation(
                        out=SZ[0][0:tb, b, q * 512:(q + 1) * 512], in_=PS[0:tb, bk, :], func=AF.Silu),
                        reads=[rB[bk]], writes=[SZ[1]])
            gate(2)
            def b2_s1(q):
                pan, rpan = next_panel()
                xb = XBJ[q % 3]
                for jj in range(4):
                    c = q * 4 + jj
                    kb.op("pool", lambda e, jj=jj, c=c, xb=xb: e.tensor_copy(out=segv(xb[jj][0], 3, 0, 3), in_=HISTM[:, 0:nseg, c, :]),
                          reads=[rHISTM], writes=[xb[jj][1]])
                    bk = acc_bank()
                    mm_run(PS[:, bk, 0:T], [(pan[:, kc, jj * 128:(jj + 1) * 128], XN[:, kc, 0:T]) for kc in range(8)],
                           reads=[rXN, rpan], writes=[rB[bk]])
                    kb.op("act", lambda e, bk=bk, jj=jj, xb=xb: e.copy(
                        out=segv(xb[jj][0], 3, 3, 3 + sl_), in_=PS[:, bk, 0:T].rearrange("p (s w) -> p s w", s=nseg)),
                        reads=[rB[bk]], writes=[xb[jj][1]])
                    kb.op("pool", lambda e, jj=jj, c=c, xb=xb: e.tensor_copy(out=HISTM[:, 0:nseg, c, :], in_=segv(xb[jj][0], 3, sl_, sl_ + 3)),
                          reads=[xb[jj][1]], writes=[rHISTM])

            def b2_s2(q):
                xb = XBJ[q % 3]
                cv = CVJ[q % 2]
                for jj in range(4):
                    c = q * 4 + jj
                    kb.op("act", lambda e, jj=jj, c=c, xb=xb, cv=cv: e.activation(
                        out=cv[jj][0][:, 0:W3], in_=xb[jj][0][:, 0:W3], func=AF.Identity,
                        scale=PF[:, PF_CSW + c * 4:PF_CSW + c * 4 + 1]),
                        reads=[xb[jj][1], rPF], writes=[cv[jj][1]])
                for i in (1, 2, 3):
                    for jj in range(4):
                        c = q * 4 + jj
                        kb.op("dve", lambda e, jj=jj, c=c, i=i, xb=xb, cv=cv: e.scalar_tensor_tensor(
                            out=cv[jj][0][:, 0:W3], in0=xb[jj][0][:, i:i + W3],
                            scalar=PF[:, PF_CSW + c * 4 + i:PF_CSW + c * 4 + i + 1], in1=cv[jj][0][:, 0:W3],
                            op0=ALU.mult, op1=ALU.add),
                            reads=[xb[jj][1], cv[jj][1], rPF], writes=[cv[jj][1]])

            def b2_s3(q):
                cv = CVJ[q % 2]
                for jj in range(4):
                    c = q * 4 + jj
                    if c < 16:
                        dst, dres = XSF[0][:, c, 0:T], XSF[1]
                    elif c < 24:
                        dst, dres = BF_[0][:, c - 16, 0:T], BF_[1]
                    else:
                        dst, dres = CF[0][:, c - 24, 0:T], CF[1]
                    kb.op("act", lambda e, jj=jj, c=c, dst=dst, cv=cv: e.activation(
                        out=dst.rearrange("p (s w) -> p s w", s=nseg), in_=segv(cv[jj][0], 3, 0, sl_), func=AF.Silu,
                        bias=PF[:, PF_CSB + c:PF_CSB + c + 1]),
                        reads=[cv[jj][1], rPF], writes=[dres])


            gate(3)
            def g_dt(b):
                sm = SM[:, b, :]
                rsm = rSM[b]
                smh = SMH[:, b, :]
                rsmh = rSMH[b]
                tok = slice(b * 128, b * 128 + tb)
                bk = 4 + b
                mm_run(PS[0:tb, bk, 0:32], [(XN[:, kc, tok], DTW[:, kc, :]) for kc in range(8)],
                       reads=[rXN, rDTW], writes=[rB[bk]])
                yield
                kb.op("dve", lambda e: e.tensor_tensor(out=sm[0:tb, C_T0:C_T0 + 32], in0=PS[0:tb, bk, 0:32],
                                                       in1=PF[0:tb, PF_DTB:PF_DTB + 32], op=ALU.add),
                      reads=[rB[bk], rPF], writes=[rsm])
                yield
                kb.op("act", lambda e: e.activation(out=sm[0:tb, C_T0:C_T0 + 32], in_=sm[0:tb, C_T0:C_T0 + 32],
                                                    func=AF.Exp), reads=[rsm], writes=[rsm])
                yield
                kb.op("act", lambda e: e.activation(out=sm[0:tb, C_DT:C_DT + 32], in_=sm[0:tb, C_T0:C_T0 + 32],
                                                    func=AF.Ln, bias=EPSC[0:tb, 1:2]), reads=[rsm, rEPSC], writes=[rsm])
                yield
                kb.op("dve", lambda e: e.tensor_tensor(out=sm[0:tb, C_ADT:C_ADT + 32], in0=sm[0:tb, C_DT:C_DT + 32],
                                                       in1=ANEG[0:tb, :], op=ALU.mult),
                      reads=[rsm, rANEG], writes=[rsm])
                yield
                kb.op("dve", lambda e: e.tensor_copy(out=smh[0:tb, 0:32], in_=sm[0:tb, C_ADT:C_ADT + 32]),
                      reads=[rsm], writes=[rsmh])
                yield
                kb.op("dve", lambda e: e.tensor_tensor(out=smh[0:tb, 32:64], in0=sm[0:tb, C_ADT:C_ADT + 32],
                                                       in1=smh[0:tb, 0:32], op=ALU.subtract),
                      reads=[rsm, rsmh], writes=[rsmh])
                yield
                bk2 = 4 + b
                mm_run(PS[0:tb, bk2, 0:32], [(tri128[0:tb, 0:tb], smh[0:tb, 0:32]), (tri128[0:tb, 0:tb], smh[0:tb, 32:64])],
                       reads=[rsmh, rCB], writes=[rB[bk2]])
                mm_run(PS[:, bk2, 32:64], [(onesA[0:tb, :], smh[0:tb, 0:32]), (onesA[0:tb, :], smh[0:tb, 32:64])],
                       reads=[rsmh, rCB], writes=[rB[bk2]])
                if nch == 2:
                    mm_run(PS[:, bk2, 64:96], [(onesB[0:tb, :], smh[0:tb, 0:32]), (onesB[0:tb, :], smh[0:tb, 32:64])],
                           reads=[rsmh, rCB], writes=[rB[bk2]])
                yield
                kb.op("dve", lambda e: e.tensor_copy(out=sm[0:tb, C_ACUM:C_ACUM + 32], in_=PS[0:tb, bk2, 0:32]),
                      reads=[rB[bk2]], writes=[rsm])
                yield
                kb.op("act", lambda e: e.activation(out=sm[0:tb, C_EA:C_EA + 32], in_=PS[0:tb, bk2, 0:32],
                                                    func=AF.Exp), reads=[rB[bk2]], writes=[rsm])
                kb.op("act", lambda e: e.activation(out=sm[0:VARP, C_CD:C_CD + 32 * nch],
                                                    in_=PS[0:VARP, bk2, 32:32 + 32 * nch], func=AF.Exp),
                      reads=[rB[bk2]], writes=[rsm])
                yield
            dtg = [g_dt(b) for b in range(nb)]

            def step_dt():
                for g_ in list(dtg):
                    try:
                        next(g_)
                    except StopIteration:
                        dtg.remove(g_)

            for q in range(8 + 2):
                if q < 8:
                    b2_s1(q)
                if 1 <= q <= 8:
                    b2_s2(q - 1)
                if q < 8 and q % 2 == 1:
                    b1_panel(q // 2)
                if q >= 2:
                    b2_s3(q - 2)
                step_dt()
            if last:
                for s_ in range(nseg):
                    kb.dma("sp", ocm[sidx + s_].rearrange("p (c i) -> p c i", i=3), HISTM[:, s_], rHISTM, reads=[rHISTM])
            round_robin(dtg)

            def g_prep_banks():
                return (4, 5), (6, 7)

            def g_prep(b):
                sm = SM[:, b, :]
                rsm = rSM[b]
                smh = SMH[:, b, :]
                rsmh = rSMH[b]
                tok = slice(b * 128, b * 128 + tb)
                xst, rxst = XStP[b % 2]
                btt, rbtt = BtP[b % 2]
                xc, rxc = XCP[b % 2]
                p, p2 = g_prep_banks()
                tr_run([(PSb[0:tb, p[c // 8], (c % 8) * 128:(c % 8 + 1) * 128], XSF[0][:, c, tok]) for c in range(16)],
                       reads=[XSF[1], rCB], writes=[rB[p[0]], rB[p[1]]], idn=ident)
                yield
                kb.op("act", lambda e: e.copy(out=xst[0:tb, :], in_=PSb[0:tb, p[0]:p[1] + 1, :].rearrange("p a n -> p (a n)")),
                      reads=[rB[p[0]], rB[p[1]]], writes=[rxst])
                yield
                kb.op("dve", lambda e: e.tensor_tensor(
                    out=xc[0:tb, :].rearrange("p (h d) -> p h d", h=32),
                    in0=PSb[0:tb, p[0]:p[1] + 1, :].rearrange("p a (h d) -> p (a h) d", d=64),
                    in1=sm[0:tb, C_DT:C_DT + 32].unsqueeze(2).broadcast_to([tb, 32, 64]), op=ALU.mult),
                    reads=[rB[p[0]], rB[p[1]], rsm], writes=[rxc])
                yield
                tr_run([(PSb[0:tb, p2[0], g * 128:(g + 1) * 128], BF_[0][:, g, tok]) for g in range(8)],
                       reads=[BF_[1], rCB], writes=[rB[p2[0]]], idn=ident)
                yield
                kb.op("act", lambda e: e.copy(out=btt[0:tb, :], in_=PSb[0:tb, p2[0], :]), reads=[rB[p2[0]]], writes=[rbtt])
                yield
                for (rr_, lo) in ((RHI, 0), (RLO, 32)):
                    kb.op("dve", lambda e, rr_=rr_, lo=lo: e.tensor_tensor(
                        out=rr_[0][0:tb, :, :], in0=smh[0:tb, lo:lo + 32].unsqueeze(2).broadcast_to([tb, 32, 64]),
                        in1=tri2[0:tb, :].unsqueeze(1).broadcast_to([tb, 32, 64]), op=ALU.mult),
                        reads=[rsmh, rCB], writes=[rr_[1]])
                    yield

            def g_chain(b, hf):
                c_ = hf
                sm = SM[:, b, :]
                rsm = rSM[b]
                tok0 = b * 128
                hs = slice(hf * 16, hf * 16 + 16)
                xst, rxst = XStP[b % 2]
                btt, rbtt = BtP[b % 2]
                xcd, rxcd = XCDh[hf]
                xc, rxc = XCP[b % 2]
                mypair = (2 * c_, 2 * c_ + 1)

                def scratch():
                    return mypair
                de, rde = DEc[c_]
                mh, rmh = MHc[c_]
                t1, rt1 = T1c[c_]
                y1, ry1 = Y1c[c_]
                yn, ryn = YNc[c_]
                dec = DECT[:, b, hs]
                rdec = rDECT[b][hf]
                yo = (4, 5) if c_ == 0 else (6, 7)
                stc = STATC + 32 * b + 8 * c_
                rstc = rSTATc[b][c_]
                p = scratch()
                for nn in range(2):
                    hh = slice(hf * 16 + nn * 8, hf * 16 + nn * 8 + 8)
                    mm_run(PS[0:tb, p[nn], :],
                           [(bones[0:tb, 0:tb], RHI[0][0:tb, hh, :]), (bones[0:tb, 0:tb], RLO[0][0:tb, hh, :]),
                            (ident[0:tb, 0:tb], negm[0:tb, nn * 512:(nn + 1) * 512])],
                           reads=[RHI[1], RLO[1], rCB], writes=[rB[p[nn]]])
                yield
                kb.op("dve", lambda e: e.tensor_tensor(
                    out=de[0:tb, :, :], in0=PS[0:tb, p[0]:p[1] + 1, :].rearrange("p a (h l) -> p (a h) l", l=64),
                    in1=sm[0:tb, C_ACUM + hs.start:C_ACUM + hs.stop].unsqueeze(2).broadcast_to([tb, 16, 64]),
                    op=ALU.subtract), reads=[rB[p[0]], rB[p[1]], rsm], writes=[rde])
                pc = scratch()
                for g in range(4):
                    gg = hf * 4 + g
                    for ch in range(nch):
                        r0 = ch * 64
                        cs = slice(tok0 + r0, tok0 + r0 + 64)
                        mm_run(PS[r0:r0 + 64, pc[0], g * 64:(g + 1) * 64], [(BF_[0][:, gg, cs], CF[0][:, gg, cs])],
                               reads=[BF_[1], CF[1]], writes=[rB[pc[0]]])
                yield
                kb.op("act", lambda e: e.activation(out=de[0:tb, :, :], in_=de[0:tb, :, :], func=AF.Exp),
                      reads=[rde], writes=[rde])
                yield
                kb.op("pool", lambda e: e.tensor_copy(out=dec[0:tb, :].unsqueeze(2), in_=de[0:tb, :, 63:64]),
                      reads=[rde], writes=[rdec])
                kb.op("dve", lambda e: e.tensor_tensor(
                    out=mh[0:tb, :, :].rearrange("p (g j) l -> p g j l", g=4),
                    in0=de[0:tb, :, :].rearrange("p (g j) l -> p g j l", g=4),
                    in1=PS[0:tb, pc[0], 0:256].rearrange("p (g l) -> p g l", g=4).unsqueeze(2).broadcast_to([tb, 4, 4, 64]),
                    op=ALU.mult), reads=[rde, rB[pc[0]]], writes=[rmh])
                yield
                kb.op("dve", lambda e: e.tensor_tensor(
                    out=xcd[0:tb, :].rearrange("p (h d) -> p h d", h=16),
                    in0=xc[0:tb, hf * 1024:(hf + 1) * 1024].rearrange("p (h d) -> p h d", h=16),
                    in1=dec[0:tb, :].unsqueeze(2).broadcast_to([tb, 16, 64]),
                    op=ALU.mult), reads=[rxc, rdec], writes=[rxcd])
                yield
                pd = scratch()
                for ch in range(nch):
                    r0 = ch * 64
                    for hl in range(16):
                        h = hf * 16 + hl
                        mm_run(PS[r0:r0 + 64, pd[hl // 8], (hl % 8) * 64:(hl % 8 + 1) * 64],
                               [(mh[r0:r0 + 64, hl, :], xc[r0:r0 + 64, h * 64:(h + 1) * 64])],
                               reads=[rmh, rxc], writes=[rB[pd[0]], rB[pd[1]]])
                yield
                yds = de[0:tb, :, :].rearrange("p h l -> p (h l)")
                kb.op("act", lambda e: e.copy(out=yds, in_=PS[0:tb, pd[0]:pd[1] + 1, :].rearrange("p a n -> p (a n)")),
                      reads=[rB[pd[0]], rB[pd[1]]], writes=[rde])
                yield
                for ch in range(nch):
                    r0 = ch * 64
                    cs = slice(tok0 + r0, tok0 + r0 + 64)
                    sb_in, rsb_in = SBF[ch % 2], rSBFh[ch % 2][hf]
                    if nch == 2:
                        sb_out, rsb_out = SBF[(ch + 1) % 2], rSBFh[(ch + 1) % 2][hf]
                    else:
                        sb_out, rsb_out = SBF[0], rSBFh[0][hf]
                    for g in range(4):
                        gg = hf * 4 + g
                        mm_run(PS[r0:r0 + 64, yo[g // 2], (g % 2) * 256:(g % 2 + 1) * 256],
                               [(CF[0][:, gg, cs], sb_in[:, gg * 256:(gg + 1) * 256])],
                               reads=[CF[1], rsb_in], writes=[rB[yo[0]], rB[yo[1]]])
                    ps_ = scratch()
                    for g in range(4):
                        gg = hf * 4 + g
                        mm_run(PS[:, ps_[g // 2], (g % 2) * 256:(g % 2 + 1) * 256],
                               [(btt[r0:r0 + 64, gg * 128:(gg + 1) * 128], xcd[r0:r0 + 64, g * 256:(g + 1) * 256])],
                               reads=[rbtt, rxcd], writes=[rB[ps_[0]], rB[ps_[1]]])
                    yield
                    if seg and ch == 1:
                        stv = ST1[0][:, hf * 1024:(hf + 1) * 1024]
                        rstv = ST1[1]
                    else:
                        stv = ST[:, hf * 1024:(hf + 1) * 1024]
                        rstv = rSTh[hf]
                    kb.op("dve", lambda e, ch=ch, stv=stv: e.tensor_tensor(
                        out=t1[:, :].rearrange("p (h d) -> p h d", h=16), in0=stv.rearrange("p (h d) -> p h d", h=16),
                        in1=sm[:, C_CD + ch * 32 + hs.start:C_CD + ch * 32 + hs.stop].unsqueeze(2).broadcast_to([128, 16, 64]),
                        op=ALU.mult), reads=[rstv, rsm], writes=[rt1])
                    yield
                    kb.op("dve", lambda e, ps_=ps_, stv=stv: e.tensor_tensor(
                        out=stv, in0=PS[:, ps_[0]:ps_[1] + 1, :].rearrange("p a n -> p (a n)"), in1=t1[:, :], op=ALU.add),
                        reads=[rB[ps_[0]], rB[ps_[1]], rt1], writes=[rstv])
                    yield
                    if not seg:
                        kb.op("act", lambda e, sb_out=sb_out, stv=stv: e.copy(out=sb_out[:, hf * 1024:(hf + 1) * 1024], in_=stv),
                              reads=[rstv], writes=[rsb_out])
                        yield
                kb.op("dve", lambda e: e.tensor_tensor(
                    out=y1[0:tb, :].rearrange("p (h d) -> p h d", h=16),
                    in0=PS[0:tb, yo[0]:yo[1] + 1, :].rearrange("p a (h d) -> p (a h) d", d=64),
                    in1=sm[0:tb, C_EA + hs.start:C_EA + hs.stop].unsqueeze(2).broadcast_to([tb, 16, 64]),
                    op=ALU.mult), reads=[rB[yo[0]], rB[yo[1]], rsm], writes=[ry1])
                yield
                kb.op("dve", lambda e: e.tensor_tensor(out=y1[0:tb, :], in0=y1[0:tb, :], in1=yds, op=ALU.add),
                      reads=[ry1, rde], writes=[ry1])
                yield
                kb.op("dve", lambda e: e.tensor_tensor(
                    out=t1[0:tb, :].rearrange("p (h d) -> p h d", h=16),
                    in0=xst[0:tb, hf * 1024:(hf + 1) * 1024].rearrange("p (h d) -> p h d", h=16),
                    in1=PF[0:tb, PF_DSK + hs.start:PF_DSK + hs.stop].unsqueeze(2).broadcast_to([tb, 16, 64]),
                    op=ALU.mult), reads=[rxst, rPF], writes=[rt1])
                yield
                kb.op("dve", lambda e: e.tensor_tensor(out=t1[0:tb, :], in0=t1[0:tb, :], in1=y1[0:tb, :], op=ALU.add),
                      reads=[rt1, ry1], writes=[rt1])
                yield
                kb.op("dve", lambda e: e.tensor_tensor(
                    out=y1[0:tb, :], in0=t1[0:tb, :], in1=SZ[0][0:tb, b, hf * 1024:(hf + 1) * 1024], op=ALU.mult),
                    reads=[rt1, SZ[1]], writes=[ry1])
                yield
                if hf == 0 and b == 0:
                    dump("yg0", y1[0:tb, :], ry1, [tb, 1024], F32)
                    dump("yds0", yds, rde, [tb, 1024], F32)
                def g_tail():
                    for g in range(4):
                        kb.op("act", lambda e, g=g: e.activation(
                            out=t1[0:tb, g * 256:(g + 1) * 256], in_=y1[0:tb, g * 256:(g + 1) * 256],
                            func=AF.Square, accum_out=STAT[0:tb, stc + g:stc + g + 1]),
                            reads=[ry1], writes=[rt1, rstc])
                    yield
                    yield from g_rstd(stc, rstc, 0, 4, 256, tb, w=4)
                    for g in range(4):
                        kb.op("act", lambda e, g=g: e.activation(
                            out=yn[0:tb, g * 256:(g + 1) * 256], in_=y1[0:tb, g * 256:(g + 1) * 256],
                            func=AF.Identity, scale=STAT[0:tb, stc + 4 + g:stc + 5 + g]),
                            reads=[ry1, rstc], writes=[ryn])
                    yield
                    ptb = yo[0]
                    tr_run([(PSb[:, ptb, c * 128:c * 128 + tb], yn[0:tb, c * 128:(c + 1) * 128]) for c in range(8)],
                           reads=[ryn, rCB], writes=[rB[ptb]], idn=ident[0:tb, 0:tb])
                    yield
                    kb.op("act", lambda e: e.copy(
                        out=YNF[0][:, hf * 8:(hf + 1) * 8, b * 128:b * 128 + tb],
                        in_=PSb[:, ptb, :].rearrange("p (k t) -> p k t", k=8)[:, :, 0:tb]), reads=[rB[ptb]], writes=[YNF[1]])
                    yield
                tails.append(g_tail())

            tails = []
            for _ in g_prep(0):
                pass
            for b in range(nb):
                gens = []
                if b + 1 < nb:
                    gens.append((g_prep(b + 1), 0))
                gens += [(t_, 0) for t_ in tails]
                del tails[:]
                gens += [(g_chain(b, 0), 0), (g_chain(b, 1), CH_OFF)]
                if os.environ.get('SEQ'):
                    for g_, _o in gens:
                        for _ in g_:
                            pass
                else:
                    round_robin_off(gens)
            round_robin(tails)
            del tails[:]
            if last:
                kb.dma("sp", ost[sidx], ST[:], rSTh[0], reads=[rSTh[0], rSTh[1]])
                if seg:
                    kb.dma("sp", ost[sidx + 1], ST1[0][:, :], ST1[1], reads=[ST1[1]])
            dump("xn", XN[:, :, 0:T], rXN, [128, 8, T], BF16)
            dump("ynf", YNF[0][:, :, 0:T], YNF[1], [128, 16, T], BF16)

            gate(4)
            def fm_group(pan, rpan, jj, act, ract):
                bk = acc_bank()
                mm_run(PS[:, bk, 0:T], [(pan[:, kc, jj * 128:(jj + 1) * 128], act[:, kc, 0:T]) for kc in range(8)],
                       reads=[ract, rpan], writes=[rB[bk]])
                return bk

            for hf in range(2):
                pan, rpan = next_panel()
                for jj in range(4):
                    bk = fm_group(pan, rpan, jj, XN, rXN)
                    kb.op("act", lambda e, bk=bk, jj=jj: e.copy(out=ACJ[jj][0][:, 0:T], in_=PS[:, bk, 0:T]),
                          reads=[rB[bk]], writes=[ACJ[jj][1]])
                pan, rpan = next_panel()
                for jj in range(4):
                    j = hf * 4 + jj
                    kb.op("pool", lambda e, jj=jj, j=j: e.tensor_copy(out=segv(UJ[jj][0], 2, 0, 2), in_=HISTA[:, 0:nseg, j, :]),
                          reads=[rHISTA], writes=[UJ[jj][1]])
                    bk = fm_group(pan, rpan, jj, XN, rXN)
                    kb.op("dve", lambda e, bk=bk, jj=jj: e.tensor_tensor(
                        out=segv(UJ[jj][0], 2, 2, 2 + sl_), in0=PS[:, bk, 0:T].rearrange("p (s w) -> p s w", s=nseg),
                        in1=ACJ[jj][0][:, 0:T].rearrange("p (s w) -> p s w", s=nseg), op=ALU.mult),
                          reads=[rB[bk], ACJ[jj][1]], writes=[UJ[jj][1]])
                    kb.op("pool", lambda e, jj=jj, j=j: e.tensor_copy(out=HISTA[:, 0:nseg, j, :], in_=segv(UJ[jj][0], 2, sl_, sl_ + 2)),
                          reads=[UJ[jj][1]], writes=[rHISTA])
                for jj in range(4):
                    j = hf * 4 + jj
                    kb.op("act", lambda e, jj=jj, j=j: e.activation(
                        out=TTJ[jj][0][:, 0:W2], in_=UJ[jj][0][:, 0:W2], func=AF.Identity,
                        scale=PF[:, PF_CAW + j * 3:PF_CAW + j * 3 + 1]), reads=[UJ[jj][1], rPF], writes=[TTJ[jj][1]])
                pan, rpan = next_panel()
                bks = [fm_group(pan, rpan, jj, XN, rXN) for jj in range(4)]
                for i in (1, 2):
                    for jj in range(4):
                        j = hf * 4 + jj
                        kb.op("dve", lambda e, jj=jj, j=j, i=i: e.scalar_tensor_tensor(
                            out=TTJ[jj][0][:, 0:W2], in0=UJ[jj][0][:, i:i + W2],
                            scalar=PF[:, PF_CAW + j * 3 + i:PF_CAW + j * 3 + i + 1], in1=TTJ[jj][0][:, 0:W2],
                            op0=ALU.mult, op1=ALU.add), reads=[UJ[jj][1], TTJ[jj][1], rPF], writes=[TTJ[jj][1]])
                for jj in range(4):
                    j = hf * 4 + jj
                    bk = bks[jj]
                    kb.op("dve", lambda e, bk=bk, jj=jj, j=j: e.tensor_tensor(
                        out=V[0][:, j, 0:T].rearrange("p (s w) -> p s w", s=nseg),
                        in0=PS[:, bk, 0:T].rearrange("p (s w) -> p s w", s=nseg),
                        in1=segv(TTJ[jj][0], 2, 0, sl_), op=ALU.mult),
                          reads=[rB[bk], TTJ[jj][1]], writes=[V[1]])
            if last:
                for s_ in range(nseg):
                    kb.dma("sp", oca[sidx + s_].rearrange("p (j i) -> p j i", i=2), HISTA[:, s_], rHISTA, reads=[rHISTA])
            gate(5)
            dump("v", V[0][:, :, 0:T], V[1], [128, 8, T], BF16)
            for hf in range(2):
                pan, rpan = next_panel()
                for jj in range(4):
                    j = hf * 4 + jj
                    bk = fm_group(pan, rpan, jj, XN, rXN)
                    kb.op("act", lambda e, bk=bk, jj=jj, j=j: e.activation(
                        out=GJ[jj][0][:, 0:T], in_=PS[:, bk, 0:T], func=AF.Sigmoid, bias=PF[:, PF_BG + j:PF_BG + j + 1]),
                        reads=[rB[bk], rPF], writes=[GJ[jj][1]])
                pan, rpan = next_panel()
                for jj in range(4):
                    j = hf * 4 + jj
                    bk = fm_group(pan, rpan, jj, V[0], V[1])
                    kb.op("dve", lambda e, bk=bk, jj=jj, j=j: e.tensor_tensor(out=PA[0][:, j, 0:T], in0=PS[:, bk, 0:T],
                                                                             in1=GJ[jj][0][:, 0:T], op=ALU.mult),
                          reads=[rB[bk], GJ[jj][1]], writes=[PA[1]])
            dump("pa", PA[0][:, :, 0:T], PA[1], [128, 8, T], F32)
            for hf in range(2):
                pan, rpan = next_panel()
                for jj in range(4):
                    j = hf * 4 + jj
                    bk = fm_group(pan, rpan, jj, XN, rXN)
                    kb.op("act", lambda e, bk=bk, jj=jj, j=j: e.activation(
                        out=GJ[jj][0][:, 0:T], in_=PS[:, bk, 0:T], func=AF.Sigmoid,
                        bias=PF[:, PF_BG + 8 + j:PF_BG + 8 + j + 1]), reads=[rB[bk], rPF], writes=[GJ[jj][1]])
                pan0, rpan0 = next_panel()
                pan1, rpan1 = next_panel()
                for jj in range(4):
                    j = hf * 4 + jj
                    bk = acc_bank()
                    pairs = [(pan0[:, kc, jj * 128:(jj + 1) * 128], YNF[0][:, kc, 0:T]) for kc in range(8)] + \
                            [(pan1[:, kc, jj * 128:(jj + 1) * 128], YNF[0][:, 8 + kc, 0:T]) for kc in range(8)]
                    mm_run(PS[:, bk, 0:T], pairs, reads=[YNF[1], rpan0, rpan1], writes=[rB[bk]])
                    kb.op("dve", lambda e, bk=bk, jj=jj: e.tensor_tensor(out=GJ[jj][0][:, 0:T], in0=PS[:, bk, 0:T],
                                                                        in1=GJ[jj][0][:, 0:T], op=ALU.mult),
                          reads=[rB[bk], GJ[jj][1]], writes=[GJ[jj][1]])
                    kb.op("dve", lambda e, jj=jj, j=j: e.tensor_tensor(out=MG[0][:, j, 0:T], in0=GJ[jj][0][:, 0:T],
                                                                      in1=PA[0][:, j, 0:T], op=ALU.add),
                          reads=[GJ[jj][1], PA[1]], writes=[MG[1]])
            gate(6)
            dump("mg", MG[0][:, :, 0:T], MG[1], [128, 8, T], BF16)
            def g_post(b, src, wcol, addsrc):
                (sap, sres) = src
                (aap, ares) = addsrc
                o = 16 * b
                rst = rSTATb[b]
                jk, rjk = JUNK4[b]
                kb.op("act", lambda e: e.activation(out=jk[0:tb, :], in_=sap[0:tb, :], func=AF.Square,
                                                    accum_out=STAT[0:tb, o:o + 1]), reads=[sres], writes=[rjk, rst])
                yield
                yield from g_rstd(o, rst, 0, 1, 1024, tb)
                kb.op("dve", lambda e: e.scalar_tensor_tensor(
                    out=sap[0:tb, :], in0=sap[0:tb, :], scalar=STAT[0:tb, o + 1:o + 2], in1=PF[0:tb, wcol:wcol + 1024],
                    op0=ALU.mult, op1=ALU.mult), reads=[sres, rst, rPF], writes=[sres])
                yield
                kb.op("dve", lambda e: e.tensor_tensor(out=sap[0:tb, :], in0=sap[0:tb, :], in1=aap[0:tb, :], op=ALU.add),
                      reads=[sres, ares], writes=[sres])
                yield

            def g_wo(b):
                xt, rxt = XT2[b]
                for (src_, p0_, n_) in io_rows(ti, b):
                    kb.dma("sp", xt[p0_:p0_ + n_, :], src_, rxt, writes=[rxt])
                yield from g_post(b, MO[b], PF_NMPOST, XT2[b])
                yield from g_norm_to_fm(MO[b][0], MO[b][1], b, tb, HS16[b], XN, rXN)
            wpan = [next_panel(), next_panel()]
            wog = []
            for b in range(nb):
                for hf in range(2):
                    pan, rpan = wpan[hf]
                    bk = acc_bank()
                    mm_run(PS[0:tb, bk, :], [(MG[0][:, kc, b * 128:b * 128 + tb], pan[:, kc, :]) for kc in range(8)],
                           reads=[MG[1], rpan], writes=[rB[bk]])
                    kb.op("act", lambda e, bk=bk, b=b, hf=hf: e.copy(out=MO[b][0][0:tb, hf * 512:(hf + 1) * 512],
                                                                    in_=PS[0:tb, bk, :]), reads=[rB[bk]], writes=[MO[b][1]])
                wog.append(g_wo(b))
                for _rep in range(2):
                    for g_ in list(wog):
                        try:
                            next(g_)
                        except StopIteration:
                            wog.remove(g_)
            round_robin(wog)
            gate(7)
            dump("h", MO[0][0][0:tb, :], MO[0][1], [tb, 1024], F32)
            for q in range(8):
                pan, rpan = next_panel()
                for jj in range(4):
                    c = q * 4 + jj
                    bk = acc_bank()
                    mm_run(PS[:, bk, 0:T], [(pan[:, kc, jj * 128:(jj + 1) * 128], XN[:, kc, 0:T]) for kc in range(8)],
                           reads=[rXN, rpan], writes=[rB[bk]])
                    rt, rrt = RT[c % 4]
                    kb.op("act", lambda e, bk=bk, rt=rt: e.activation(out=rt[:, 0:T], in_=PS[:, bk, 0:T], func=AF.Relu),
                          reads=[rB[bk]], writes=[rrt])
                    kb.op("dve", lambda e, rt=rt, c=c: e.tensor_tensor(out=FF[0][:, c, 0:T], in0=rt[:, 0:T], in1=rt[:, 0:T],
                                                                      op=ALU.mult), reads=[rrt], writes=[FF[1]])
            nxt = p0_gens(ti + 1) if (ti + 1 < len(tiles) and stop_after is None) else []
            for hf in range(2):
                for kg in range(4):
                    pan, rpan = next_panel()
                    for b in range(nb):
                        mm_run(PS[0:tb, b, :],
                               [(FF[0][:, kg * 8 + kc, b * 128:b * 128 + tb], pan[:, kc, :]) for kc in range(8)],
                               reads=[FF[1], rpan], writes=[rB[b]], first=(kg == 0), last=(kg == 3))
                    for g_ in list(nxt):
                        try:
                            next(g_)
                        except StopIteration:
                            nxt.remove(g_)
                for b in range(nb):
                    kb.op("act", lambda e, b=b, hf=hf: e.copy(out=FO[b][0][0:tb, hf * 512:(hf + 1) * 512], in_=PS[0:tb, b, :]),
                          reads=[rB[b]], writes=[FO[b][1]])
            round_robin(nxt)
            dump("ff", FF[0][:, :, 0:T], FF[1], [128, 32, T], BF16)
            dump("fo", FO[0][0][0:tb, :], FO[0][1], [tb, 1024], F32)
            def g_fin(b):
                yield from g_post(b, FO[b], PF_NFPOST, MO[b])
                for (dst_, p0_, n_) in io_rows(ti, b, out=True):
                    kb.dma("sp", dst_, FO[b][0][p0_:p0_ + n_, :], FO[b][1], reads=[FO[b][1]])
            round_robin([g_fin(b) for b in range(nb)])
        for ti, tl in enumerate(tiles):
            try:
                run_tile(ti, *tl)
            except _Stop:
                pass
        kb.finish()
        block = es.enter_context(nc.Block())
        kb.replay(block)
    return nc


def _host_tables(p):
    pf = np.zeros((128, PF_COLS), np.float32)
    pf[:, PF_CAW:PF_CAW + 24] = p["conv_a_w"][0].reshape(3, 8, 128).transpose(2, 1, 0).reshape(128, 24)
    pf[:, PF_CSW:PF_CSW + 128] = p["conv_ssm_w"][0].reshape(4, 32, 128).transpose(2, 1, 0).reshape(128, 128)
    pf[:, PF_CSB:PF_CSB + 32] = p["conv_ssm_b"][0].reshape(32, 128).T
    pf[:, PF_BG:PF_BG + 16] = p["b_gate"][0].reshape(16, 128).T
    pf[:, PF_NMP:PF_NMP + 8] = p["norm_mix_pre"][0].reshape(8, 128).T
    pf[:, PF_NFP:PF_NFP + 8] = p["norm_ffn_pre"][0].reshape(8, 128).T
    pf[:, PF_SSN:PF_SSN + 16] = p["ssm_norm"][0].reshape(16, 128).T
    pf[:, PF_DTB:PF_DTB + 32] = p["dt_bias"][0][None, :]
    pf[:, PF_ALOG:PF_ALOG + 32] = p["a_log"][0][None, :]
    pf[:, PF_DSK:PF_DSK + 32] = p["d_skip"][0][None, :]
    pf[:, PF_NMPOST:PF_NMPOST + 1024] = p["norm_mix_post"][0][None, :]
    pf[:, PF_NFPOST:PF_NFPOST + 1024] = p["norm_ffn_post"][0][None, :]
    cb = np.zeros((128, CB_COLS), np.float32)
    k = np.arange(128)
    cb[:, CB_ID:CB_ID + 128] = np.eye(128)
    l = np.arange(64)
    cb[:, CB_TRI2:CB_TRI2 + 64] = ((k[:, None] % 64) <= l[None, :])
    same = (k[:, None] // 64) == (k[None, :] // 64)
    cb[:, CB_BONES:CB_BONES + 128] = same
    cb[:, CB_ONESA:CB_ONESA + 128] = (k[:, None] < 64) * np.ones((1, 128))
    cb[:, CB_ONESB:CB_ONESB + 128] = (k[:, None] >= 64) * np.ones((1, 128))
    cb[:, CB_TRI128:CB_TRI128 + 128] = same & ((k[:, None] % 64) <= (k[None, :] % 64))
    neg = np.where((k[:, None] % 64) > l[None, :], -30000.0, 0.0)
    cb[:, CB_NEG:CB_NEG + 1024] = np.tile(neg, (1, 16))
    return pf, cb.astype(ml_dtypes.bfloat16)


_NC_CACHE = {}


def kernel(x_prompt, x_sample, state_conv_a, state_conv_ssm, state_ssm, norm_mix_pre, w_in, b_gate, conv_a_w, w_a_out,
           conv_ssm_w, conv_ssm_b, dt_bias, a_log, d_skip, ssm_norm, w_ssm_out, w_o, norm_mix_post, norm_ffn_pre,
           w_ff1, w_ff2, norm_ffn_post, _tiles_limit=None, _cores=None, _dbg=False, _stop=None):
    p = dict(norm_mix_pre=norm_mix_pre, b_gate=b_gate, conv_a_w=conv_a_w, conv_ssm_w=conv_ssm_w, conv_ssm_b=conv_ssm_b,
             dt_bias=dt_bias, a_log=a_log, d_skip=d_skip, ssm_norm=ssm_norm, norm_mix_post=norm_mix_post,
             norm_ffn_pre=norm_ffn_pre, norm_ffn_post=norm_ffn_post)
    p = {k: np.asarray(v, np.float32) for k, v in p.items()}
    pf, cb = _host_tables(p)
    x_prompt = np.asarray(x_prompt, np.float32)
    x_sample = np.asarray(x_sample, np.float32)
    sca_all = np.asarray(state_conv_a, np.float32)[0]
    scm_all = np.asarray(state_conv_ssm, np.float32)[0]
    sst_all = np.asarray(state_ssm, np.float32)[0]
    key = (None if _tiles_limit is None else tuple(_tiles_limit), _dbg, _stop)
    if key not in _NC_CACHE:
        _NC_CACHE[key] = build_program(_tiles_limit, _dbg, _stop)
    nc = _NC_CACHE[key]
    cores = list(range(NCORES)) if _cores is None else _cores
    shared = dict(pf=pf, cb=cb, w_in=np.ascontiguousarray(np.asarray(w_in, np.float32)[0]),
                  w_a_out=np.ascontiguousarray(np.asarray(w_a_out, np.float32)[0]),
                  w_ssm_out=np.ascontiguousarray(np.asarray(w_ssm_out, np.float32)[0]),
                  w_o=np.ascontiguousarray(np.asarray(w_o, np.float32)[0]),
                  w_ff1=np.ascontiguousarray(np.asarray(w_ff1, np.float32)[0]),
                  w_ff2=np.ascontiguousarray(np.asarray(w_ff2, np.float32)[0]))
    in_maps = []
    for c in cores:
        s2 = slice(2 * c, 2 * c + 2)
        m = dict(shared)
        m["xp"] = np.ascontiguousarray(x_prompt[s2])
        m["xs"] = np.ascontiguousarray(x_sample[s2])
        m["sca"] = np.ascontiguousarray(sca_all[s2].reshape(2, 2, 8, 128).transpose(0, 3, 2, 1).reshape(2, 128, 16))
        m["scm"] = np.ascontiguousarray(scm_all[s2].reshape(2, 3, 32, 128).transpose(0, 3, 2, 1).reshape(2, 128, 96))
        m["sst"] = np.ascontiguousarray(sst_all[s2].reshape(2, 2048, 128).transpose(0, 2, 1))
        in_maps.append(m)
    res = run_bass_kernel_spmd(nc, in_maps, core_ids=list(range(len(cores))))
    R = res.results
    if _dbg:
        kernel.dbg = {k: v for k, v in R[0].items() if k.startswith("dbg_")}
    yp = np.concatenate([r["yp"] for r in R], axis=0)
    ys_ = np.concatenate([r["ys"] for r in R], axis=0)
    oca = np.stack([r["oca"] for r in R])
    ocm = np.stack([r["ocm"] for r in R])
    ost = np.stack([r["ost"] for r in R])
    n = len(cores)

    def un_ca(a):
        return a.reshape(n * 2, 128, 8, 2).transpose(0, 3, 2, 1).reshape(n * 2, 2, 1024)

    def un_cm(a):
        return a.reshape(n * 2, 128, 32, 3).transpose(0, 3, 2, 1).reshape(n * 2, 3, 4096)

    def un_st(a):
        return a.reshape(n * 2, 128, 2048).transpose(0, 2, 1).reshape(n * 2, 32, 64, 128)

    outs = (yp, ys_,
            un_ca(oca[:, 0:2])[None], un_cm(ocm[:, 0:2])[None], un_st(ost[:, 0:2])[None],
            un_ca(oca[:, 2:4])[None], un_cm(ocm[:, 2:4])[None], un_st(ost[:, 2:4])[None])
    return tuple(np.ascontiguousarray(o, dtype=np.float32) for o in outs)
```

```python
import numpy as np
import ml_dtypes
from contextlib import ExitStack
import concourse.bass as bass
import concourse.mybir as mybir
from concourse.bass_utils import run_bass_kernel_spmd

F32 = mybir.dt.float32
BF16 = mybir.dt.bfloat16
AF = mybir.ActivationFunctionType
ALU = mybir.AluOpType
AX = mybir.AxisListType

NCORES = 8
D = 1024
SEQ = 2048
DSEQ = 64
PROJ = 11296
OFF_AB, OFF_AC, OFF_AH, OFF_Z, OFF_XBC, OFF_DT, OFF_GATE = 0, 1024, 2048, 3072, 5120, 9216, 9248
EPS = 1e-6
NS = 4

PF_CAW = 0
PF_CSW = 24
PF_CSB = 152
PF_BG = 184
PF_NMP = 200
PF_NFP = 208
PF_SSN = 216
PF_DTB = 232
PF_ALOG = 264
PF_DSK = 296
PF_NMPOST = 328
PF_NFPOST = 1352
PF_COLS = 2376
CB_ID = 0
CB_TRI2 = 128
CB_BONES = 192
CB_ONESA = 320
CB_ONESB = 448
CB_TRI128 = 576
CB_NEG = 704
CB_COLS = 1728


class Res:
    __slots__ = ("name", "lw", "rd", "ov", "dsem", "dcnt", "rng", "excl", "disj")

    def __init__(self, name):
        self.name = name
        self.lw = None
        self.rd = {}
        self.ov = []
        self.dsem = None
        self.dcnt = 0
        self.rng = None
        self.disj = False
        self.excl = False


class KB:
    ENG = ("pe", "act", "dve", "pool")

    def __init__(self, nc, es):
        self.nc = nc
        self.es = es
        self.streams = {k: [] for k in ("pe", "act", "dve", "pool", "sp")}
        self.sem = {e: es.enter_context(nc.semaphore("s_" + e)) for e in self.ENG}
        self.tick = {e: 0 for e in self.ENG}
        self.waited = {k: {} for k in self.streams}
        self.dsems = []
        self.all_res = []

    def res(self, name):
        r = Res(name)
        self.all_res.append(r)
        return r

    def _dsem(self, r):
        if r.dsem is None:
            r.dsem = self.es.enter_context(self.nc.semaphore("d%d" % len(self.dsems)))
            self.dsems.append(r)
        return r.dsem

    def _deps(self, eng, reads, writes):
        deps = []
        for r in reads:
            if r.lw is not None:
                deps.append(r.lw)
            if r.excl:
                for q, t in r.rd.items():
                    if q != eng:
                        deps.append((q, t))
        same_ok = (eng == "pe")
        for w in writes:
            for x in [w] + w.ov:
                if x.lw is not None and (x.lw[0] != eng or not (same_ok or (x is w and w.disj))):
                    deps.append(x.lw)
                for q, t in x.rd.items():
                    if q != eng or not same_ok:
                        deps.append((q, t))
        wd = self.waited[eng]
        for q, t in deps:
            if wd.get(q, 0) < t:
                wd[q] = t
                self.streams[eng].append(("w", q, t))

    def wait(self, eng, q, t):
        wd = self.waited[eng]
        if wd.get(q, 0) < t:
            wd[q] = t
            self.streams[eng].append(("w", q, t))

    def op(self, eng, fn, reads=(), writes=(), inc=True):
        self._deps(eng, reads, writes)
        t = self.tick[eng] + 1
        if inc:
            self.tick[eng] = t
        self.streams[eng].append(("op", fn, inc))
        for r in reads:
            r.rd[eng] = t
        for w in writes:
            w.lw = (eng, t)
            w.rd = {}
        return t

    def dma(self, eng, out, in_, dres, reads=(), writes=()):
        self._deps(eng, reads, writes)
        self._dsem(dres)
        dres.dcnt += 16
        key = (id(dres), dres.dcnt)
        q = ("d", id(dres))
        self._dq = getattr(self, "_dq", {})
        self._dq[q] = dres
        self.streams[eng].append(("dma", out, in_, dres))
        for r in reads:
            r.rd[q] = dres.dcnt
        for w in writes:
            w.lw = (q, dres.dcnt)
            w.rd = {}

    def semof(self, q):
        if isinstance(q, tuple):
            return self._dq[q].dsem
        return self.sem[q]

    def finish(self):
        for r in self.dsems:
            self.wait("sp", ("d", id(r)), r.dcnt)

    def replay(self, block):
        handles = {"pe": block.tensor, "act": block.scalar, "dve": block.vector, "pool": block.gpsimd,
                   "sp": block.sync}
        for eng, deco in handles.items():
            stream = self.streams[eng]
            own = self.sem.get(eng)

            def body(e, stream=stream, own=own):
                for ent in stream:
                    k = ent[0]
                    if k == "w":
                        e.wait_ge(self.semof(ent[1]), ent[2])
                    elif k == "op":
                        ins = ent[1](e)
                        if ent[2]:
                            ins.then_inc(own, 1)
                    else:
                        e.dma_start(out=ent[1], in_=ent[2]).then_inc(ent[3].dsem, 16)
            deco(body)


class Arena:
    def __init__(self, kb, tensor, nbytes):
        self.kb = kb
        self.t = tensor
        self.nbytes = nbytes
        self.views = []

    def view(self, name, off, shape, dt):
        esz = 2 if dt == BF16 else 4
        n = int(np.prod(shape))
        nb = n * esz
        assert off % 4 == 0 and off + nb <= self.nbytes, (name, off, nb, self.nbytes)
        w0, w1 = off // 4, (off + nb + 3) // 4
        ap = self.t[:, w0:w1]
        if dt == BF16:
            ap = ap.bitcast(BF16)[:, 0:n]
        if len(shape) == 2:
            ap = ap.rearrange("p (a b) -> p a b", a=shape[0])
        elif len(shape) == 3:
            ap = ap.rearrange("p (a b c) -> p a b c", a=shape[0], b=shape[1])
        r = self.kb.res(name)
        r.rng = (off, off + nb)
        for (o, rr) in self.views:
            if o.rng[0] < r.rng[1] and r.rng[0] < o.rng[1]:
                r.ov.append(o)
                o.ov.append(r)
        self.views.append((r, ap))
        return ap, r


import os
VARP = 128
CH_OFF = 0
NACC = 4


class _Stop(Exception):
    pass


def build_program(tiles_limit=None, dbg=False, stop_after=None):
    nc = bass.Bass("TRN2", target_bir_lowering=False)
    dram_in = lambda name, shape, dt=F32: nc.dram_tensor(name, shape, dt, kind="ExternalInput").ap()
    dram_out = lambda name, shape, dt=F32: nc.dram_tensor(name, shape, dt, kind="ExternalOutput").ap()
    xp = dram_in("xp", [2, SEQ, D])
    xs = dram_in("xs", [2, DSEQ, D])
    sca = dram_in("sca", [2, 128, 16])
    scm = dram_in("scm", [2, 128, 96])
    sst = dram_in("sst", [2, 128, 2048])
    pf_d = dram_in("pf", [128, PF_COLS])
    cb_d = dram_in("cb", [128, CB_COLS], BF16)
    w_in = dram_in("w_in", [D, PROJ])
    w_a = dram_in("w_a_out", [D, D])
    w_s = dram_in("w_ssm_out", [2 * D, D])
    w_o = dram_in("w_o", [D, D])
    w_f1 = dram_in("w_ff1", [D, 4 * D])
    w_f2 = dram_in("w_ff2", [4 * D, D])
    yp = dram_out("yp", [2, SEQ, D])
    ys = dram_out("ys", [2, DSEQ, D])
    oca = dram_out("oca", [4, 128, 16])
    ocm = dram_out("ocm", [4, 128, 96])
    ost = dram_out("ost", [4, 128, 2048])
    wi_b = nc.dram_tensor("wi_b", [D, PROJ], BF16).ap()
    wa_b = nc.dram_tensor("wa_b", [D, D], BF16).ap()
    ws_b = nc.dram_tensor("ws_b", [2 * D, D], BF16).ap()
    wo_b = nc.dram_tensor("wo_b", [D, D], BF16).ap()
    wf1_b = nc.dram_tensor("wf1_b", [D, 4 * D], BF16).ap()
    wf2_b = nc.dram_tensor("wf2_b", [4 * D, D], BF16).ap()

    with ExitStack() as es:
        kb = KB(nc, es)
        sbt = lambda name, shape, dt: es.enter_context(nc.sbuf_tensor(name, shape, dt))
        PF = sbt("PF", [128, PF_COLS], F32); rPF = kb.res("PF")
        CB = sbt("CB", [128, CB_COLS], BF16); rCB = kb.res("CB")
        PAN = [sbt("PAN%d" % i, [128, 8, 512], BF16) for i in range(NS)]
        rPAN = [kb.res("PAN%d" % i) for i in range(NS)]
        XN = sbt("XN", [128, 8, 512], BF16); rXN = kb.res("XN")
        DTW = sbt("DTW", [128, 8, 32], BF16); rDTW = kb.res("DTW")
        HISTA = sbt("HISTA", [128, 2, 8, 2], F32); rHISTA = kb.res("HISTA")
        HISTM = sbt("HISTM", [128, 2, 32, 3], F32); rHISTM = kb.res("HISTM")
        ST = sbt("ST", [128, 2048], F32)
        rSTh = [kb.res("ST_h%d" % i) for i in range(2)]
        SBF = [sbt("SBF%d" % i, [128, 2048], BF16) for i in range(2)]
        rSBFh = [[kb.res("SBF%d_h%d" % (i, hf)) for hf in range(2)] for i in range(2)]
        DECT = sbt("DECT", [128, 4, 32], F32)
        rDECT = [[kb.res("DEC%d_%d" % (b, hf)) for hf in range(2)] for b in range(4)]
        STATC = 64
        rSTATc = [[kb.res("STATC%d_%d" % (b, c_)) for c_ in range(2)] for b in range(4)]
        ANEG = sbt("ANEG", [128, 32], F32); rANEG = kb.res("ANEG")
        SM = sbt("SM", [128, 4, 320], F32)
        rSM = [kb.res("SM%d" % b) for b in range(4)]
        STAT = sbt("STAT", [128, 256], F32)
        rSTATb = [kb.res("STAT%d" % b) for b in range(4)]
        EPSC = sbt("EPSC", [128, 2], F32); rEPSC = kb.res("EPSC")
        C_DT, C_ADT, C_ACUM, C_EA, C_CD, C_DEC, C_T0, C_RS = 0, 32, 64, 96, 128, 192, 224, 256
        SMH = sbt("SMH", [128, 4, 64], BF16)
        rSMH = [kb.res("SMH%d" % b) for b in range(4)]
        KBY = 1024
        AR_BYTES = 128 * KBY
        ARt = sbt("ARENA", [128, AR_BYTES // 4], F32)
        ar = Arena(kb, ARt, AR_BYTES)
        XT = [ar.view("XT%d" % i, 81 * KBY + i * 4 * KBY, [1024], F32) for i in range(4)]
        XS16 = [ar.view("XS16_%d" % i, 97 * KBY + i * 2 * KBY, [1024], BF16) for i in range(4)]
        JUNK4 = [ar.view("JUNK%d" % i, 105 * KBY + i * 2 * KBY, [1024], BF16) for i in range(4)]
        ST1 = ar.view("ST1", 8 * KBY, [2048], F32)
        SZ = ar.view("SZ", 0, [4, 2048], BF16)
        XSF = ar.view("XSF", 16 * KBY, [16, 512], BF16)
        BF_ = ar.view("BF", 32 * KBY, [8, 512], BF16)
        CF = ar.view("CF", 40 * KBY, [8, 512], BF16)
        XBJ = [[ar.view("XB%d_%d" % (p_, jj), 48 * KBY + (p_ * 4 + jj) * 2064, [516], F32) for jj in range(4)]
               for p_ in range(3)]
        CVJ = [[ar.view("CV%d_%d" % (p_, jj), 73 * KBY + (p_ * 4 + jj) * 2048, [512], F32) for jj in range(4)]
               for p_ in range(2)]
        XStP = [ar.view("XSt%d" % i, 48 * KBY + i * 4 * KBY, [2048], BF16) for i in range(2)]
        BtP = [ar.view("Bt%d" % i, 56 * KBY + i * 2 * KBY, [1024], BF16) for i in range(2)]
        XCDh = [ar.view("XCD_%d" % hf, 60 * KBY + hf * 2 * KBY, [1024], BF16) for hf in range(2)]
        XCP = [ar.view("XC%d" % i, 64 * KBY + i * 4 * KBY, [2048], BF16) for i in range(2)]
        RHI = ar.view("RHI", 72 * KBY, [32, 64], BF16)
        RLO = ar.view("RLO", 76 * KBY, [32, 64], BF16)
        DEc = [ar.view("DE%d" % i, 80 * KBY + i * 4 * KBY, [16, 64], F32) for i in range(2)]
        MHc = [ar.view("MH%d" % i, 88 * KBY + i * 2 * KBY, [16, 64], BF16) for i in range(2)]
        T1c = [ar.view("T1_%d" % i, 92 * KBY + i * 4 * KBY, [1024], F32) for i in range(2)]
        Y1c = [ar.view("Y1_%d" % i, 100 * KBY + i * 4 * KBY, [1024], F32) for i in range(2)]
        YNc = [ar.view("YN_%d" % i, 108 * KBY + i * 2 * KBY, [1024], BF16) for i in range(2)]
        YNF = ar.view("YNF", 112 * KBY, [16, 512], BF16)
        ACJ = [ar.view("AC_%d" % jj, jj * 2048, [512], F32) for jj in range(4)]
        UJ = [ar.view("U_%d" % jj, 8 * KBY + jj * 2064, [516], F32) for jj in range(4)]
        TTJ = [ar.view("TT_%d" % jj, 17 * KBY + jj * 2048, [512], F32) for jj in range(4)]
        V = ar.view("V", 25 * KBY, [8, 512], BF16)
        GJ = [ar.view("G_%d" % jj, 33 * KBY + jj * 2048, [512], F32) for jj in range(4)]
        PA = ar.view("PA", 41 * KBY, [8, 512], F32)
        MG = ar.view("MG", 57 * KBY, [8, 512], BF16)
        MO = [ar.view("MO%d" % b, 65 * KBY + b * 4 * KBY, [1024], F32) for b in range(4)]
        XT2 = [ar.view("XT2_%d" % i, 81 * KBY + i * 4 * KBY, [1024], F32) for i in range(4)]
        HS16 = [ar.view("HS16_%d" % i, 97 * KBY + i * 2 * KBY, [1024], BF16) for i in range(4)]
        FF = ar.view("FF", 0, [32, 512], BF16)
        RT = [ar.view("RT%d" % i, 32 * KBY + i * 2 * KBY, [512], F32) for i in range(4)]
        FO = [ar.view("FO%d" % b, 40 * KBY + b * 4 * KBY, [1024], F32) for b in range(4)]
        SGW = 2824
        NSG = 4
        SGI = [ar.view("SGI%d" % i, i * 12 * KBY, [SGW], F32) for i in range(NSG)]
        SGO = [ar.view("SGO%d" % i, 48 * KBY + i * 6 * KBY, [SGW], BF16) for i in range(NSG)]
        PS = es.enter_context(nc.psum_tensor("PS", [128, 8, 512], F32))
        rB = [kb.res("bank%d" % i) for i in range(8)]
        for r_ in rB:
            r_.excl = True
        PSb = PS[:].bitcast(BF16)

        for r_ in [XSF[1], BF_[1], CF[1], V[1], PA[1], MG[1], FF[1], YNF[1], SZ[1], rXN] + [m_[1] for m_ in MO] + \
                [f_[1] for f_ in FO]:
            r_.disj = True
        ident = CB[:, CB_ID:CB_ID + 128]
        tri2 = CB[:, CB_TRI2:CB_TRI2 + 64]
        bones = CB[:, CB_BONES:CB_BONES + 128]
        onesA = CB[:, CB_ONESA:CB_ONESA + 128]
        onesB = CB[:, CB_ONESB:CB_ONESB + 128]
        tri128 = CB[:, CB_TRI128:CB_TRI128 + 128]
        negm = CB[:, CB_NEG:CB_NEG + 1024]

        dbg_names = []

        def dump(name, ap, res, shape, dt):
            if not dbg:
                return
            d = nc.dram_tensor("dbg_" + name, shape, dt, kind="ExternalOutput").ap()
            kb.dma("sp", d, ap, res, reads=[res])
            dbg_names.append("dbg_" + name)

        kb.dma("sp", PF[:], pf_d, rPF, writes=[rPF])
        kb.dma("sp", CB[:], cb_d, rCB, writes=[rCB])
        kb.op("pool", lambda e: e.memset(EPSC[:, 0:1], EPS), writes=[rEPSC])
        kb.op("pool", lambda e: e.memset(EPSC[:, 1:2], 1.0), writes=[rEPSC])
        kb.op("act", lambda e: e.activation(out=ANEG[:], in_=PF[:, PF_ALOG:PF_ALOG + 32], func=AF.Exp),
              reads=[rPF], writes=[rANEG])
        kb.op("dve", lambda e: e.tensor_scalar(out=ANEG[:], in0=ANEG[:], scalar1=-1.0, scalar2=None, op0=ALU.mult),
              reads=[rANEG], writes=[rANEG])

        conv_jobs = []
        def add_mat(src, dst, K, N, piece, scol):
            for rc in range(K // 128):
                for c0 in range(0, N, piece):
                    w = min(piece, N - c0)
                    conv_jobs.append((src[rc * 128:(rc + 1) * 128, c0:c0 + w], dst[rc * 128:(rc + 1) * 128, c0:c0 + w],
                                      None if scol is None else scol + rc, w))
        add_mat(w_in, wi_b, D, PROJ, SGW, PF_NMP)
        add_mat(w_a, wa_b, D, D, 1024, None)
        add_mat(w_s, ws_b, 2 * D, D, 1024, PF_SSN)
        add_mat(w_o, wo_b, D, D, 1024, None)
        add_mat(w_f1, wf1_b, D, 4 * D, 2048, PF_NFP)
        add_mat(w_f2, wf2_b, 4 * D, D, 1024, None)
        def conv_store(i):
            (src, dst, scol, w) = conv_jobs[i]
            so, ro = SGO[i % NSG]
            kb.dma("sp", dst, so[:, 0:w], ro, reads=[ro])
        for i, (src, dst, scol, w) in enumerate(conv_jobs):
            si, ri = SGI[i % NSG]
            so, ro = SGO[i % NSG]
            if i >= NSG - 1:
                conv_store(i - (NSG - 1))
            kb.dma("sp", si[:, 0:w], src, ri, writes=[ri])
            eng = "dve" if (i % 5) in (0, 2, 4) else "act"
            if eng == "dve":
                if scol is None:
                    kb.op(eng, lambda e, so=so, si=si, w=w: e.tensor_copy(out=so[:, 0:w], in_=si[:, 0:w]),
                          reads=[ri], writes=[ro])
                else:
                    kb.op(eng, lambda e, so=so, si=si, w=w, scol=scol: e.tensor_scalar(
                        out=so[:, 0:w], in0=si[:, 0:w], scalar1=PF[:, scol:scol + 1], scalar2=None, op0=ALU.mult),
                        reads=[ri, rPF], writes=[ro])
            else:
                if scol is None:
                    kb.op(eng, lambda e, so=so, si=si, w=w: e.copy(out=so[:, 0:w], in_=si[:, 0:w]),
                          reads=[ri], writes=[ro])
                else:
                    kb.op(eng, lambda e, so=so, si=si, w=w, scol=scol: e.activation(
                        out=so[:, 0:w], in_=si[:, 0:w], func=AF.Identity, scale=PF[:, scol:scol + 1]),
                        reads=[ri, rPF], writes=[ro])
        for i in range(max(0, len(conv_jobs) - (NSG - 1)), len(conv_jobs)):
            conv_store(i)
        for so, ro in SGO:
            kb.wait("sp", ("d", id(ro)), ro.dcnt)
        kb.dma("sp", DTW[:], wi_b[:, OFF_DT:OFF_DT + 32].rearrange("(kc p) n -> p kc n", p=128), rDTW, writes=[rDTW])

        tiles = []
        for s in range(2):
            for ti in range(4):
                tiles.append((s, xp[s, ti * 512:(ti + 1) * 512, :], yp[s, ti * 512:(ti + 1) * 512, :], 512,
                              ti == 0, ti == 3, None))
        tiles.append((2, None, None, 128, True, True, 0))
        if tiles_limit is not None:
            tiles = [tiles[i] for i in tiles_limit]

        def panel_list(T):
            pl = []
            for q in range(8):
                pl.append((wi_b, 0, OFF_XBC + q * 512))
                if q % 2 == 1:
                    pl.append((wi_b, 0, OFF_Z + (q // 2) * 512))
            for hf in range(2):
                pl.append((wi_b, 0, OFF_AC + hf * 512))
                pl.append((wi_b, 0, OFF_AH + hf * 512))
                pl.append((wi_b, 0, OFF_AB + hf * 512))
            for hf in range(2):
                pl.append((wi_b, 0, OFF_GATE + hf * 512))
                pl.append((wa_b, 0, hf * 512))
            for hf in range(2):
                pl.append((wi_b, 0, OFF_GATE + 1024 + hf * 512))
                pl.append((ws_b, 0, hf * 512))
                pl.append((ws_b, 1, hf * 512))
            for hf in range(2):
                pl.append((wo_b, 0, hf * 512))
            for q in range(8):
                pl.append((wf1_b, 0, q * 512))
            for hf in range(2):
                for kg in range(4):
                    pl.append((wf2_b, kg, hf * 512))
            return pl
        all_panels = []
        for tl in tiles:
            all_panels += panel_list(tl[3])
        pstate = {"issued": 0, "used": 0}

        def issue_panels(upto):
            while pstate["issued"] < min(upto, len(all_panels)):
                i = pstate["issued"]
                mat, kg, c0 = all_panels[i]
                slot = i % NS
                src = mat[kg * 1024:(kg + 1) * 1024, c0:c0 + 512].rearrange("(kc p) n -> p kc n", p=128)
                kb.dma("sp", PAN[slot][:], src, rPAN[slot], writes=[rPAN[slot]])
                pstate["issued"] += 1

        def next_panel():
            i = pstate["used"]
            issue_panels(i + NS - 1)
            pstate["used"] += 1
            return PAN[i % NS], rPAN[i % NS]

        bank_rr = {"i": 0}

        def acc_bank():
            i = bank_rr["i"]
            bank_rr["i"] = (i + 1) % NACC
            return i

        def mm_run(out, pairs, reads, writes, first=True, last=True):
            n = len(pairs)
            for i, (l, r) in enumerate(pairs):
                kb.op("pe", lambda e, l=l, r=r, st=(first and i == 0), sp_=(last and i == n - 1):
                      e.matmul(out, lhsT=l, rhs=r, start=st, stop=sp_),
                      reads=reads if i == 0 else (), writes=writes if i == 0 else (), inc=(i == n - 1))

        def tr_run(outs_ins, reads, writes, idn):
            n = len(outs_ins)
            for i, (o, a) in enumerate(outs_ins):
                kb.op("pe", lambda e, o=o, a=a: e.transpose(o, a, idn),
                      reads=reads if i == 0 else (), writes=writes if i == 0 else (), inc=(i == n - 1))

        def round_robin(gens):
            gens = list(gens)
            while gens:
                for g_ in list(gens):
                    try:
                        next(g_)
                    except StopIteration:
                        gens.remove(g_)

        def g_rstd(o, rst, ss_col, rs_col, n, tb, w=1):
            kb.op("act", lambda e: e.activation(out=STAT[0:tb, o + rs_col:o + rs_col + w], in_=STAT[0:tb, o + ss_col:o + ss_col + w],
                                                func=AF.Ln, scale=1.0 / n, bias=EPSC[0:tb, 0:1]),
                  reads=[rst, rEPSC], writes=[rst])
            yield
            kb.op("act", lambda e: e.activation(out=STAT[0:tb, o + rs_col:o + rs_col + w], in_=STAT[0:tb, o + rs_col:o + rs_col + w],
                                                func=AF.Exp, scale=-0.5),
                  reads=[rst], writes=[rst])
            yield

        def round_robin_off(gens):
            live = list(gens)
            step = 0
            while live:
                for item in list(live):
                    if item[1] <= step:
                        try:
                            next(item[0])
                        except StopIteration:
                            live.remove(item)
                step += 1

        def g_norm_to_fm(src_ap, src_res, b, tb, s16, dst, dst_res):
            (s16ap, s16res) = s16
            o = 16 * b
            rst = rSTATb[b]
            jk, rjk = JUNK4[b]
            kb.op("act", lambda e: e.activation(out=jk[0:tb, :], in_=src_ap[0:tb, :], func=AF.Square,
                                                accum_out=STAT[0:tb, o:o + 1]),
                  reads=[src_res], writes=[rjk, rst])
            yield
            yield from g_rstd(o, rst, 0, 1, 1024, tb)
            kb.op("dve", lambda e: e.tensor_scalar(out=s16ap[0:tb, :], in0=src_ap[0:tb, :], scalar1=STAT[0:tb, o + 1:o + 2],
                                                   scalar2=None, op0=ALU.mult),
                  reads=[src_res, rst], writes=[s16res])
            yield
            tpb = 4 + b
            tr_run([(PSb[:, tpb, kc * 128:kc * 128 + tb], s16ap[0:tb, kc * 128:(kc + 1) * 128]) for kc in range(8)],
                   reads=[s16res, rCB], writes=[rB[tpb]], idn=ident[0:tb, 0:tb])
            yield
            kb.op("act", lambda e: e.copy(out=dst[:, :, b * 128:b * 128 + tb],
                                          in_=PSb[:, tpb, :].rearrange("p (k t) -> p k t", k=8)[:, :, 0:tb]),
                  reads=[rB[tpb]], writes=[dst_res])
            yield

        def gate(n):
            if stop_after is not None and n > stop_after:
                raise _Stop()

        p0_done = set()

        def io_rows(ti, b, out=False):
            (_sidx, xd_, yd_, T_, _f, _l, samp_) = tiles[ti]
            if samp_ is None:
                tb_ = min(T_, 128)
                src = yd_ if out else xd_
                return [(src[b * 128:b * 128 + tb_, :], 0, tb_)]
            src = ys if out else xs
            return [(src[0], 0, 64), (src[1], 64, 64)]

        def p0_gens(ti):
            (_sidx, xd_, _yd, T_, _f, _l, _s) = tiles[ti]
            tb_ = min(T_, 128)

            def g_p0(b):
                xt, rxt = XT[b]
                for (src_, p0_, n_) in io_rows(ti, b):
                    kb.dma("sp", xt[p0_:p0_ + n_, :], src_, rxt, writes=[rxt])
                yield from g_norm_to_fm(xt, rxt, b, tb_, XS16[b], XN, rXN)
            p0_done.add(ti)
            return [g_p0(b) for b in range(T_ // tb_)]

        def run_tile(ti, sidx, xd, yd, T, first, last, samp):
            tb = min(T, 128)
            nb = T // tb
            nch = tb // 64
            seg = samp is not None
            nseg = 2 if seg else 1
            sl_ = T // nseg
            if first:
                if not seg:
                    kb.op("pool", lambda e: e.memset(HISTA[:, 0], 0.0), writes=[rHISTA])
                    kb.op("pool", lambda e: e.memset(HISTM[:, 0], 0.0), writes=[rHISTM])
                    kb.op("pool", lambda e: e.memset(ST[:], 0.0), writes=[rSTh[0], rSTh[1]])
                else:
                    for s_ in range(2):
                        kb.dma("sp", HISTA[:, s_], sca[s_].rearrange("p (j i) -> p j i", i=2), rHISTA, writes=[rHISTA])
                        kb.dma("sp", HISTM[:, s_], scm[s_].rearrange("p (c i) -> p c i", i=3), rHISTM, writes=[rHISTM])
                    kb.dma("sp", ST[:], sst[0], rSTh[0], writes=[rSTh[0], rSTh[1]])
                    kb.dma("sp", ST1[0][:, :], sst[1], ST1[1], writes=[ST1[1]])
                    kb.op("act", lambda e: e.copy(out=SBF[1][:], in_=ST1[0][:, :]), reads=[ST1[1]],
                          writes=[rSBFh[1][0], rSBFh[1][1]])
                kb.op("act", lambda e: e.copy(out=SBF[0][:], in_=ST[:]), reads=[rSTh[0], rSTh[1]],
                      writes=[rSBFh[0][0], rSBFh[0][1]])

            W3 = nseg * (3 + sl_) - 3
            W2 = nseg * (2 + sl_) - 2

            def segv(ap2d, H, lo, hi):
                P_ = H + sl_
                if nseg == 1:
                    return ap2d[:, lo:hi].unsqueeze(1)
                return ap2d[:, 0:nseg * P_].rearrange("p (s w) -> p s w", s=nseg)[:, :, lo:hi]

            if ti not in p0_done:
                round_robin(p0_gens(ti))

            gate(1)
            def b1_panel(q):
                pan, rpan = next_panel()
                for b in range(nb):
                    bk = acc_bank()
                    mm_run(PS[0:tb, bk, :], [(XN[:, kc, b * 128:b * 128 + tb], pan[:, kc, :]) for kc in range(8)],
                           reads=[rXN, rpan], writes=[rB[bk]])
                    kb.op("act", lambda e, bk=bk, b=b, q=q: e.activation(
                        out=SZ[0][0:tb, b, q * 512:(q + 1) * 512], in_=PS[0:tb, bk, :], func=AF.Silu),
                        reads=[rB[bk]], writes=[SZ[1]])
            gate(2)
            def b2_s1(q):
                pan, rpan = next_panel()
                xb = XBJ[q % 3]
                for jj in range(4):
                    c = q * 4 + jj
                    kb.op("pool", lambda e, jj=jj, c=c, xb=xb: e.tensor_copy(out=segv(xb[jj][0], 3, 0, 3), in_=HISTM[:, 0:nseg, c, :]),
                          reads=[rHISTM], writes=[xb[jj][1]])
                    bk = acc_bank()
                    mm_run(PS[:, bk, 0:T], [(pan[:, kc, jj * 128:(jj + 1) * 128], XN[:, kc, 0:T]) for kc in range(8)],
                           reads=[rXN, rpan], writes=[rB[bk]])
                    kb.op("act", lambda e, bk=bk, jj=jj, xb=xb: e.copy(
                        out=segv(xb[jj][0], 3, 3, 3 + sl_), in_=PS[:, bk, 0:T].rearrange("p (s w) -> p s w", s=nseg)),
                        reads=[rB[bk]], writes=[xb[jj][1]])
                    kb.op("pool", lambda e, jj=jj, c=c, xb=xb: e.tensor_copy(out=HISTM[:, 0:nseg, c, :], in_=segv(xb[jj][0], 3, sl_, sl_ + 3)),
                          reads=[xb[jj][1]], writes=[rHISTM])

            def b2_s2(q):
                xb = XBJ[q % 3]
                cv = CVJ[q % 2]
                for jj in range(4):
                    c = q * 4 + jj
                    kb.op("act", lambda e, jj=jj, c=c, xb=xb, cv=cv: e.activation(
                        out=cv[jj][0][:, 0:W3], in_=xb[jj][0][:, 0:W3], func=AF.Identity,
                        scale=PF[:, PF_CSW + c * 4:PF_CSW + c * 4 + 1]),
                        reads=[xb[jj][1], rPF], writes=[cv[jj][1]])
                for i in (1, 2, 3):
                    for jj in range(4):
                        c = q * 4 + jj
                        kb.op("dve", lambda e, jj=jj, c=c, i=i, xb=xb, cv=cv: e.scalar_tensor_tensor(
                            out=cv[jj][0][:, 0:W3], in0=xb[jj][0][:, i:i + W3],
                            scalar=PF[:, PF_CSW + c * 4 + i:PF_CSW + c * 4 + i + 1], in1=cv[jj][0][:, 0:W3],
                            op0=ALU.mult, op1=ALU.add),
                            reads=[xb[jj][1], cv[jj][1], rPF], writes=[cv[jj][1]])

            def b2_s3(q):
                cv = CVJ[q % 2]
                for jj in range(4):
                    c = q * 4 + jj
                    if c < 16:
                        dst, dres = XSF[0][:, c, 0:T], XSF[1]
                    elif c < 24:
                        dst, dres = BF_[0][:, c - 16, 0:T], BF_[1]
                    else:
                        dst, dres = CF[0][:, c - 24, 0:T], CF[1]
                    kb.op("act", lambda e, jj=jj, c=c, dst=dst, cv=cv: e.activation(
                        out=dst.rearrange("p (s w) -> p s w", s=nseg), in_=segv(cv[jj][0], 3, 0, sl_), func=AF.Silu,
                        bias=PF[:, PF_CSB + c:PF_CSB + c + 1]),
                        reads=[cv[jj][1], rPF], writes=[dres])


            gate(3)
            def g_dt(b):
                sm = SM[:, b, :]
                rsm = rSM[b]
                smh = SMH[:, b, :]
                rsmh = rSMH[b]
                tok = slice(b * 128, b * 128 + tb)
                bk = 4 + b
                mm_run(PS[0:tb, bk, 0:32], [(XN[:, kc, tok], DTW[:, kc, :]) for kc in range(8)],
                       reads=[rXN, rDTW], writes=[rB[bk]])
                yield
                kb.op("dve", lambda e: e.tensor_tensor(out=sm[0:tb, C_T0:C_T0 + 32], in0=PS[0:tb, bk, 0:32],
                                                       in1=PF[0:tb, PF_DTB:PF_DTB + 32], op=ALU.add),
                      reads=[rB[bk], rPF], writes=[rsm])
                yield
                kb.op("act", lambda e: e.activation(out=sm[0:tb, C_T0:C_T0 + 32], in_=sm[0:tb, C_T0:C_T0 + 32],
                                                    func=AF.Exp), reads=[rsm], writes=[rsm])
                yield
                kb.op("act", lambda e: e.activation(out=sm[0:tb, C_DT:C_DT + 32], in_=sm[0:tb, C_T0:C_T0 + 32],
                                                    func=AF.Ln, bias=EPSC[0:tb, 1:2]), reads=[rsm, rEPSC], writes=[rsm])
                yield
                kb.op("dve", lambda e: e.tensor_tensor(out=sm[0:tb, C_ADT:C_ADT + 32], in0=sm[0:tb, C_DT:C_DT + 32],
                                                       in1=ANEG[0:tb, :], op=ALU.mult),
                      reads=[rsm, rANEG], writes=[rsm])
                yield
                kb.op("dve", lambda e: e.tensor_copy(out=smh[0:tb, 0:32], in_=sm[0:tb, C_ADT:C_ADT + 32]),
                      reads=[rsm], writes=[rsmh])
                yield
                kb.op("dve", lambda e: e.tensor_tensor(out=smh[0:tb, 32:64], in0=sm[0:tb, C_ADT:C_ADT + 32],
                                                       in1=smh[0:tb, 0:32], op=ALU.subtract),
                      reads=[rsm, rsmh], writes=[rsmh])
                yield
                bk2 = 4 + b
                mm_run(PS[0:tb, bk2, 0:32], [(tri128[0:tb, 0:tb], smh[0:tb, 0:32]), (tri128[0:tb, 0:tb], smh[0:tb, 32:64])],
                       reads=[rsmh, rCB], writes=[rB[bk2]])
                mm_run(PS[:, bk2, 32:64], [(onesA[0:tb, :], smh[0:tb, 0:32]), (onesA[0:tb, :], smh[0:tb, 32:64])],
                       reads=[rsmh, rCB], writes=[rB[bk2]])
                if nch == 2:
                    mm_run(PS[:, bk2, 64:96], [(onesB[0:tb, :], smh[0:tb, 0:32]), (onesB[0:tb, :], smh[0:tb, 32:64])],
                           reads=[rsmh, rCB], writes=[rB[bk2]])
                yield
                kb.op("dve", lambda e: e.tensor_copy(out=sm[0:tb, C_ACUM:C_ACUM + 32], in_=PS[0:tb, bk2, 0:32]),
                      reads=[rB[bk2]], writes=[rsm])
                yield
                kb.op("act", lambda e: e.activation(out=sm[0:tb, C_EA:C_EA + 32], in_=PS[0:tb, bk2, 0:32],
                                                    func=AF.Exp), reads=[rB[bk2]], writes=[rsm])
                kb.op("act", lambda e: e.activation(out=sm[0:VARP, C_CD:C_CD + 32 * nch],
                                                    in_=PS[0:VARP, bk2, 32:32 + 32 * nch], func=AF.Exp),
                      reads=[rB[bk2]], writes=[rsm])
                yield
            dtg = [g_dt(b) for b in range(nb)]

            def step_dt():
                for g_ in list(dtg):
                    try:
                        next(g_)
                    except StopIteration:
                        dtg.remove(g_)

            for q in range(8 + 2):
                if q < 8:
                    b2_s1(q)
                if 1 <= q <= 8:
                    b2_s2(q - 1)
                if q < 8 and q % 2 == 1:
                    b1_panel(q // 2)
                if q >= 2:
                    b2_s3(q - 2)
                step_dt()
            if last:
                for s_ in range(nseg):
                    kb.dma("sp", ocm[sidx + s_].rearrange("p (c i) -> p c i", i=3), HISTM[:, s_], rHISTM, reads=[rHISTM])
            round_robin(dtg)

            def g_prep_banks():
                return (4, 5), (6, 7)

            def g_prep(b):
                sm = SM[:, b, :]
                rsm = rSM[b]
                smh = SMH[:, b, :]
                rsmh = rSMH[b]
                tok = slice(b * 128, b * 128 + tb)
                xst, rxst = XStP[b % 2]
                btt, rbtt = BtP[b % 2]
                xc, rxc = XCP[b % 2]
                p, p2 = g_prep_banks()
                tr_run([(PSb[0:tb, p[c // 8], (c % 8) * 128:(c % 8 + 1) * 128], XSF[0][:, c, tok]) for c in range(16)],
                       reads=[XSF[1], rCB], writes=[rB[p[0]], rB[p[1]]], idn=ident)
                yield
                kb.op("act", lambda e: e.copy(out=xst[0:tb, :], in_=PSb[0:tb, p[0]:p[1] + 1, :].rearrange("p a n -> p (a n)")),
                      reads=[rB[p[0]], rB[p[1]]], writes=[rxst])
                yield
                kb.op("dve", lambda e: e.tensor_tensor(
                    out=xc[0:tb, :].rearrange("p (h d) -> p h d", h=32),
                    in0=PSb[0:tb, p[0]:p[1] + 1, :].rearrange("p a (h d) -> p (a h) d", d=64),
                    in1=sm[0:tb, C_DT:C_DT + 32].unsqueeze(2).broadcast_to([tb, 32, 64]), op=ALU.mult),
                    reads=[rB[p[0]], rB[p[1]], rsm], writes=[rxc])
                yield
                tr_run([(PSb[0:tb, p2[0], g * 128:(g + 1) * 128], BF_[0][:, g, tok]) for g in range(8)],
                       reads=[BF_[1], rCB], writes=[rB[p2[0]]], idn=ident)
                yield
                kb.op("act", lambda e: e.copy(out=btt[0:tb, :], in_=PSb[0:tb, p2[0], :]), reads=[rB[p2[0]]], writes=[rbtt])
                yield
                for (rr_, lo) in ((RHI, 0), (RLO, 32)):
                    kb.op("dve", lambda e, rr_=rr_, lo=lo: e.tensor_tensor(
                        out=rr_[0][0:tb, :, :], in0=smh[0:tb, lo:lo + 32].unsqueeze(2).broadcast_to([tb, 32, 64]),
                        in1=tri2[0:tb, :].unsqueeze(1).broadcast_to([tb, 32, 64]), op=ALU.mult),
                        reads=[rsmh, rCB], writes=[rr_[1]])
                    yield

            def g_chain(b, hf):
                c_ = hf
                sm = SM[:, b, :]
                rsm = rSM[b]
                tok0 = b * 128
                hs = slice(hf * 16, hf * 16 + 16)
                xst, rxst = XStP[b % 2]
                btt, rbtt = BtP[b % 2]
                xcd, rxcd = XCDh[hf]
                xc, rxc = XCP[b % 2]
                mypair = (2 * c_, 2 * c_ + 1)

                def scratch():
                    return mypair
                de, rde = DEc[c_]
                mh, rmh = MHc[c_]
                t1, rt1 = T1c[c_]
                y1, ry1 = Y1c[c_]
                yn, ryn = YNc[c_]
                dec = DECT[:, b, hs]
                rdec = rDECT[b][hf]
                yo = (4, 5) if c_ == 0 else (6, 7)
                stc = STATC + 32 * b + 8 * c_
                rstc = rSTATc[b][c_]
                p = scratch()
                for nn in range(2):
                    hh = slice(hf * 16 + nn * 8, hf * 16 + nn * 8 + 8)
                    mm_run(PS[0:tb, p[nn], :],
                           [(bones[0:tb, 0:tb], RHI[0][0:tb, hh, :]), (bones[0:tb, 0:tb], RLO[0][0:tb, hh, :]),
                            (ident[0:tb, 0:tb], negm[0:tb, nn * 512:(nn + 1) * 512])],
                           reads=[RHI[1], RLO[1], rCB], writes=[rB[p[nn]]])
                yield
                kb.op("dve", lambda e: e.tensor_tensor(
                    out=de[0:tb, :, :], in0=PS[0:tb, p[0]:p[1] + 1, :].rearrange("p a (h l) -> p (a h) l", l=64),
                    in1=sm[0:tb, C_ACUM + hs.start:C_ACUM + hs.stop].unsqueeze(2).broadcast_to([tb, 16, 64]),
                    op=ALU.subtract), reads=[rB[p[0]], rB[p[1]], rsm], writes=[rde])
                pc = scratch()
                for g in range(4):
                    gg = hf * 4 + g
                    for ch in range(nch):
                        r0 = ch * 64
                        cs = slice(tok0 + r0, tok0 + r0 + 64)
                        mm_run(PS[r0:r0 + 64, pc[0], g * 64:(g + 1) * 64], [(BF_[0][:, gg, cs], CF[0][:, gg, cs])],
                               reads=[BF_[1], CF[1]], writes=[rB[pc[0]]])
                yield
                kb.op("act", lambda e: e.activation(out=de[0:tb, :, :], in_=de[0:tb, :, :], func=AF.Exp),
                      reads=[rde], writes=[rde])
                yield
                kb.op("pool", lambda e: e.tensor_copy(out=dec[0:tb, :].unsqueeze(2), in_=de[0:tb, :, 63:64]),
                      reads=[rde], writes=[rdec])
                kb.op("dve", lambda e: e.tensor_tensor(
                    out=mh[0:tb, :, :].rearrange("p (g j) l -> p g j l", g=4),
                    in0=de[0:tb, :, :].rearrange("p (g j) l -> p g j l", g=4),
                    in1=PS[0:tb, pc[0], 0:256].rearrange("p (g l) -> p g l", g=4).unsqueeze(2).broadcast_to([tb, 4, 4, 64]),
                    op=ALU.mult), reads=[rde, rB[pc[0]]], writes=[rmh])
                yield
                kb.op("dve", lambda e: e.tensor_tensor(
                    out=xcd[0:tb, :].rearrange("p (h d) -> p h d", h=16),
                    in0=xc[0:tb, hf * 1024:(hf + 1) * 1024].rearrange("p (h d) -> p h d", h=16),
                    in1=dec[0:tb, :].unsqueeze(2).broadcast_to([tb, 16, 64]),
                    op=ALU.mult), reads=[rxc, rdec], writes=[rxcd])
                yield
                pd = scratch()
                for ch in range(nch):
                    r0 = ch * 64
                    for hl in range(16):
                        h = hf * 16 + hl
                        mm_run(PS[r0:r0 + 64, pd[hl // 8], (hl % 8) * 64:(hl % 8 + 1) * 64],
                               [(mh[r0:r0 + 64, hl, :], xc[r0:r0 + 64, h * 64:(h + 1) * 64])],
                               reads=[rmh, rxc], writes=[rB[pd[0]], rB[pd[1]]])
                yield
                yds = de[0:tb, :, :].rearrange("p h l -> p (h l)")
                kb.op("act", lambda e: e.copy(out=yds, in_=PS[0:tb, pd[0]:pd[1] + 1, :].rearrange("p a n -> p (a n)")),
                      reads=[rB[pd[0]], rB[pd[1]]], writes=[rde])
                yield
                for ch in range(nch):
                    r0 = ch * 64
                    cs = slice(tok0 + r0, tok0 + r0 + 64)
                    sb_in, rsb_in = SBF[ch % 2], rSBFh[ch % 2][hf]
                    if nch == 2:
                        sb_out, rsb_out = SBF[(ch + 1) % 2], rSBFh[(ch + 1) % 2][hf]
                    else:
                        sb_out, rsb_out = SBF[0], rSBFh[0][hf]
                    for g in range(4):
                        gg = hf * 4 + g
                        mm_run(PS[r0:r0 + 64, yo[g // 2], (g % 2) * 256:(g % 2 + 1) * 256],
                               [(CF[0][:, gg, cs], sb_in[:, gg * 256:(gg + 1) * 256])],
                               reads=[CF[1], rsb_in], writes=[rB[yo[0]], rB[yo[1]]])
                    ps_ = scratch()
                    for g in range(4):
                        gg = hf * 4 + g
                        mm_run(PS[:, ps_[g // 2], (g % 2) * 256:(g % 2 + 1) * 256],
                               [(btt[r0:r0 + 64, gg * 128:(gg + 1) * 128], xcd[r0:r0 + 64, g * 256:(g + 1) * 256])],
                               reads=[rbtt, rxcd], writes=[rB[ps_[0]], rB[ps_[1]]])
                    yield
                    if seg and ch == 1:
                        stv = ST1[0][:, hf * 1024:(hf + 1) * 1024]
                        rstv = ST1[1]
                    else:
                        stv = ST[:, hf * 1024:(hf + 1) * 1024]
                        rstv = rSTh[hf]
                    kb.op("dve", lambda e, ch=ch, stv=stv: e.tensor_tensor(
                        out=t1[:, :].rearrange("p (h d) -> p h d", h=16), in0=stv.rearrange("p (h d) -> p h d", h=16),
                        in1=sm[:, C_CD + ch * 32 + hs.start:C_CD + ch * 32 + hs.stop].unsqueeze(2).broadcast_to([128, 16, 64]),
                        op=ALU.mult), reads=[rstv, rsm], writes=[rt1])
                    yield
                    kb.op("dve", lambda e, ps_=ps_, stv=stv: e.tensor_tensor(
                        out=stv, in0=PS[:, ps_[0]:ps_[1] + 1, :].rearrange("p a n -> p (a n)"), in1=t1[:, :], op=ALU.add),
                        reads=[rB[ps_[0]], rB[ps_[1]], rt1], writes=[rstv])
                    yield
                    if not seg:
                        kb.op("act", lambda e, sb_out=sb_out, stv=stv: e.copy(out=sb_out[:, hf * 1024:(hf + 1) * 1024], in_=stv),
                              reads=[rstv], writes=[rsb_out])
                        yield
                kb.op("dve", lambda e: e.tensor_tensor(
                    out=y1[0:tb, :].rearrange("p (h d) -> p h d", h=16),
                    in0=PS[0:tb, yo[0]:yo[1] + 1, :].rearrange("p a (h d) -> p (a h) d", d=64),
                    in1=sm[0:tb, C_EA + hs.start:C_EA + hs.stop].unsqueeze(2).broadcast_to([tb, 16, 64]),
                    op=ALU.mult), reads=[rB[yo[0]], rB[yo[1]], rsm], writes=[ry1])
                yield
                kb.op("dve", lambda e: e.tensor_tensor(out=y1[0:tb, :], in0=y1[0:tb, :], in1=yds, op=ALU.add),
                      reads=[ry1, rde], writes=[ry1])
                yield
                kb.op("dve", lambda e: e.tensor_tensor(
                    out=t1[0:tb, :].rearrange("p (h d) -> p h d", h=16),
                    in0=xst[0:tb, hf * 1024:(hf + 1) * 1024].rearrange("p (h d) -> p h d", h=16),
                    in1=PF[0:tb, PF_DSK + hs.start:PF_DSK + hs.stop].unsqueeze(2).broadcast_to([tb, 16, 64]),
                    op=ALU.mult), reads=[rxst, rPF], writes=[rt1])
                yield
                kb.op("dve", lambda e: e.tensor_tensor(out=t1[0:tb, :], in0=t1[0:tb, :], in1=y1[0:tb, :], op=ALU.add),
                      reads=[rt1, ry1], writes=[rt1])
                yield
                kb.op("dve", lambda e: e.tensor_tensor(
                    out=y1[0:tb, :], in0=t1[0:tb, :], in1=SZ[0][0:tb, b, hf * 1024:(hf + 1) * 1024], op=ALU.mult),
                    reads=[rt1, SZ[1]], writes=[ry1])
                yield
                if hf == 0 and b == 0:
                    dump("yg0", y1[0:tb, :], ry1, [tb, 1024], F32)
                    dump("yds0", yds, rde, [tb, 1024], F32)
                def g_tail():
                    for g in range(4):
                        kb.op("act", lambda e, g=g: e.activation(
                            out=t1[0:tb, g * 256:(g + 1) * 256], in_=y1[0:tb, g * 256:(g + 1) * 256],
                            func=AF.Square, accum_out=STAT[0:tb, stc + g:stc + g + 1]),
                            reads=[ry1], writes=[rt1, rstc])
                    yield
                    yield from g_rstd(stc, rstc, 0, 4, 256, tb, w=4)
                    for g in range(4):
                        kb.op("act", lambda e, g=g: e.activation(
                            out=yn[0:tb, g * 256:(g + 1) * 256], in_=y1[0:tb, g * 256:(g + 1) * 256],
                            func=AF.Identity, scale=STAT[0:tb, stc + 4 + g:stc + 5 + g]),
                            reads=[ry1, rstc], writes=[ryn])
                    yield
                    ptb = yo[0]
                    tr_run([(PSb[:, ptb, c * 128:c * 128 + tb], yn[0:tb, c * 128:(c + 1) * 128]) for c in range(8)],
                           reads=[ryn, rCB], writes=[rB[ptb]], idn=ident[0:tb, 0:tb])
                    yield
                    kb.op("act", lambda e: e.copy(
                        out=YNF[0][:, hf * 8:(hf + 1) * 8, b * 128:b * 128 + tb],
                        in_=PSb[:, ptb, :].rearrange("p (k t) -> p k t", k=8)[:, :, 0:tb]), reads=[rB[ptb]], writes=[YNF[1]])
                    yield
                tails.append(g_tail())

            tails = []
            for _ in g_prep(0):
                pass
            for b in range(nb):
                gens = []
                if b + 1 < nb:
                    gens.append((g_prep(b + 1), 0))
                gens += [(t_, 0) for t_ in tails]
                del tails[:]
                gens += [(g_chain(b, 0), 0), (g_chain(b, 1), CH_OFF)]
                if False:
                    for g_, _o in gens:
                        for _ in g_:
                            pass
                else:
                    round_robin_off(gens)
            if last:
                kb.dma("sp", ost[sidx], ST[:], rSTh[0], reads=[rSTh[0], rSTh[1]])
                if seg:
                    kb.dma("sp", ost[sidx + 1], ST1[0][:, :], ST1[1], reads=[ST1[1]])
            dump("xn", XN[:, :, 0:T], rXN, [128, 8, T], BF16)
            dump("ynf", YNF[0][:, :, 0:T], YNF[1], [128, 16, T], BF16)

            gate(4)
            def fm_group(pan, rpan, jj, act, ract):
                bk = acc_bank()
                mm_run(PS[:, bk, 0:T], [(pan[:, kc, jj * 128:(jj + 1) * 128], act[:, kc, 0:T]) for kc in range(8)],
                       reads=[ract, rpan], writes=[rB[bk]])
                return bk

            def step_tails():
                for g_ in list(tails):
                    try:
                        next(g_)
                    except StopIteration:
                        tails.remove(g_)

            for hf in range(2):
                pan, rpan = next_panel()
                for jj in range(4):
                    bk = fm_group(pan, rpan, jj, XN, rXN)
                    kb.op("act", lambda e, bk=bk, jj=jj: e.copy(out=ACJ[jj][0][:, 0:T], in_=PS[:, bk, 0:T]),
                          reads=[rB[bk]], writes=[ACJ[jj][1]])
                    step_tails()
                pan, rpan = next_panel()
                for jj in range(4):
                    j = hf * 4 + jj
                    kb.op("pool", lambda e, jj=jj, j=j: e.tensor_copy(out=segv(UJ[jj][0], 2, 0, 2), in_=HISTA[:, 0:nseg, j, :]),
                          reads=[rHISTA], writes=[UJ[jj][1]])
                    bk = fm_group(pan, rpan, jj, XN, rXN)
                    kb.op("dve", lambda e, bk=bk, jj=jj: e.tensor_tensor(
                        out=segv(UJ[jj][0], 2, 2, 2 + sl_), in0=PS[:, bk, 0:T].rearrange("p (s w) -> p s w", s=nseg),
                        in1=ACJ[jj][0][:, 0:T].rearrange("p (s w) -> p s w", s=nseg), op=ALU.mult),
                          reads=[rB[bk], ACJ[jj][1]], writes=[UJ[jj][1]])
                    kb.op("pool", lambda e, jj=jj, j=j: e.tensor_copy(out=HISTA[:, 0:nseg, j, :], in_=segv(UJ[jj][0], 2, sl_, sl_ + 2)),
                          reads=[UJ[jj][1]], writes=[rHISTA])
                for jj in range(4):
                    j = hf * 4 + jj
                    kb.op("act", lambda e, jj=jj, j=j: e.activation(
                        out=TTJ[jj][0][:, 0:W2], in_=UJ[jj][0][:, 0:W2], func=AF.Identity,
                        scale=PF[:, PF_CAW + j * 3:PF_CAW + j * 3 + 1]), reads=[UJ[jj][1], rPF], writes=[TTJ[jj][1]])
                pan, rpan = next_panel()
                bks = [fm_group(pan, rpan, jj, XN, rXN) for jj in range(4)]
                for i in (1, 2):
                    for jj in range(4):
                        j = hf * 4 + jj
                        kb.op("dve", lambda e, jj=jj, j=j, i=i: e.scalar_tensor_tensor(
                            out=TTJ[jj][0][:, 0:W2], in0=UJ[jj][0][:, i:i + W2],
                            scalar=PF[:, PF_CAW + j * 3 + i:PF_CAW + j * 3 + i + 1], in1=TTJ[jj][0][:, 0:W2],
                            op0=ALU.mult, op1=ALU.add), reads=[UJ[jj][1], TTJ[jj][1], rPF], writes=[TTJ[jj][1]])
                for jj in range(4):
                    j = hf * 4 + jj
                    bk = bks[jj]
                    kb.op("dve", lambda e, bk=bk, jj=jj, j=j: e.tensor_tensor(
                        out=V[0][:, j, 0:T].rearrange("p (s w) -> p s w", s=nseg),
                        in0=PS[:, bk, 0:T].rearrange("p (s w) -> p s w", s=nseg),
                        in1=segv(TTJ[jj][0], 2, 0, sl_), op=ALU.mult),
                          reads=[rB[bk], TTJ[jj][1]], writes=[V[1]])
            round_robin(tails)
            del tails[:]
            if last:
                for s_ in range(nseg):
                    kb.dma("sp", oca[sidx + s_].rearrange("p (j i) -> p j i", i=2), HISTA[:, s_], rHISTA, reads=[rHISTA])
            gate(5)
            dump("v", V[0][:, :, 0:T], V[1], [128, 8, T], BF16)
            for hf in range(2):
                pan, rpan = next_panel()
                for jj in range(4):
                    j = hf * 4 + jj
                    bk = fm_group(pan, rpan, jj, XN, rXN)
                    kb.op("act", lambda e, bk=bk, jj=jj, j=j: e.activation(
                        out=GJ[jj][0][:, 0:T], in_=PS[:, bk, 0:T], func=AF.Sigmoid, bias=PF[:, PF_BG + j:PF_BG + j + 1]),
                        reads=[rB[bk], rPF], writes=[GJ[jj][1]])
                pan, rpan = next_panel()
                for jj in range(4):
                    j = hf * 4 + jj
                    bk = fm_group(pan, rpan, jj, V[0], V[1])
                    kb.op("dve", lambda e, bk=bk, jj=jj, j=j: e.tensor_tensor(out=PA[0][:, j, 0:T], in0=PS[:, bk, 0:T],
                                                                             in1=GJ[jj][0][:, 0:T], op=ALU.mult),
                          reads=[rB[bk], GJ[jj][1]], writes=[PA[1]])
            dump("pa", PA[0][:, :, 0:T], PA[1], [128, 8, T], F32)
            for hf in range(2):
                pan, rpan = next_panel()
                for jj in range(4):
                    j = hf * 4 + jj
                    bk = fm_group(pan, rpan, jj, XN, rXN)
                    kb.op("act", lambda e, bk=bk, jj=jj, j=j: e.activation(
                        out=GJ[jj][0][:, 0:T], in_=PS[:, bk, 0:T], func=AF.Sigmoid,
                        bias=PF[:, PF_BG + 8 + j:PF_BG + 8 + j + 1]), reads=[rB[bk], rPF], writes=[GJ[jj][1]])
                pan0, rpan0 = next_panel()
                pan1, rpan1 = next_panel()
                for jj in range(4):
                    j = hf * 4 + jj
                    bk = acc_bank()
                    pairs = [(pan0[:, kc, jj * 128:(jj + 1) * 128], YNF[0][:, kc, 0:T]) for kc in range(8)] + \
                            [(pan1[:, kc, jj * 128:(jj + 1) * 128], YNF[0][:, 8 + kc, 0:T]) for kc in range(8)]
                    mm_run(PS[:, bk, 0:T], pairs, reads=[YNF[1], rpan0, rpan1], writes=[rB[bk]])
                    kb.op("dve", lambda e, bk=bk, jj=jj: e.tensor_tensor(out=GJ[jj][0][:, 0:T], in0=PS[:, bk, 0:T],
                                                                        in1=GJ[jj][0][:, 0:T], op=ALU.mult),
                          reads=[rB[bk], GJ[jj][1]], writes=[GJ[jj][1]])
                    kb.op("dve", lambda e, jj=jj, j=j: e.tensor_tensor(out=MG[0][:, j, 0:T], in0=GJ[jj][0][:, 0:T],
                                                                      in1=PA[0][:, j, 0:T], op=ALU.add),
                          reads=[GJ[jj][1], PA[1]], writes=[MG[1]])
            gate(6)
            dump("mg", MG[0][:, :, 0:T], MG[1], [128, 8, T], BF16)
            def g_post(b, src, wcol, addsrc):
                (sap, sres) = src
                (aap, ares) = addsrc
                o = 16 * b
                rst = rSTATb[b]
                jk, rjk = JUNK4[b]
                kb.op("act", lambda e: e.activation(out=jk[0:tb, :], in_=sap[0:tb, :], func=AF.Square,
                                                    accum_out=STAT[0:tb, o:o + 1]), reads=[sres], writes=[rjk, rst])
                yield
                yield from g_rstd(o, rst, 0, 1, 1024, tb)
                kb.op("dve", lambda e: e.scalar_tensor_tensor(
                    out=sap[0:tb, :], in0=sap[0:tb, :], scalar=STAT[0:tb, o + 1:o + 2], in1=PF[0:tb, wcol:wcol + 1024],
                    op0=ALU.mult, op1=ALU.mult), reads=[sres, rst, rPF], writes=[sres])
                yield
                kb.op("dve", lambda e: e.tensor_tensor(out=sap[0:tb, :], in0=sap[0:tb, :], in1=aap[0:tb, :], op=ALU.add),
                      reads=[sres, ares], writes=[sres])
                yield

            def g_wo(b):
                xt, rxt = XT2[b]
                for (src_, p0_, n_) in io_rows(ti, b):
                    kb.dma("sp", xt[p0_:p0_ + n_, :], src_, rxt, writes=[rxt])
                yield from g_post(b, MO[b], PF_NMPOST, XT2[b])
                yield from g_norm_to_fm(MO[b][0], MO[b][1], b, tb, HS16[b], XN, rXN)
            wpan = [next_panel(), next_panel()]
            wog = []
            for b in range(nb):
                for hf in range(2):
                    pan, rpan = wpan[hf]
                    bk = acc_bank()
                    mm_run(PS[0:tb, bk, :], [(MG[0][:, kc, b * 128:b * 128 + tb], pan[:, kc, :]) for kc in range(8)],
                           reads=[MG[1], rpan], writes=[rB[bk]])
                    kb.op("act", lambda e, bk=bk, b=b, hf=hf: e.copy(out=MO[b][0][0:tb, hf * 512:(hf + 1) * 512],
                                                                    in_=PS[0:tb, bk, :]), reads=[rB[bk]], writes=[MO[b][1]])
                wog.append(g_wo(b))
                for _rep in range(2):
                    for g_ in list(wog):
                        try:
                            next(g_)
                        except StopIteration:
                            wog.remove(g_)
            round_robin(wog)
            gate(7)
            dump("h", MO[0][0][0:tb, :], MO[0][1], [tb, 1024], F32)
            for q in range(8):
                pan, rpan = next_panel()
                for jj in range(4):
                    c = q * 4 + jj
                    bk = acc_bank()
                    mm_run(PS[:, bk, 0:T], [(pan[:, kc, jj * 128:(jj + 1) * 128], XN[:, kc, 0:T]) for kc in range(8)],
                           reads=[rXN, rpan], writes=[rB[bk]])
                    rt, rrt = RT[c % 4]
                    kb.op("act", lambda e, bk=bk, rt=rt: e.activation(out=rt[:, 0:T], in_=PS[:, bk, 0:T], func=AF.Relu),
                          reads=[rB[bk]], writes=[rrt])
                    kb.op("dve", lambda e, rt=rt, c=c: e.tensor_tensor(out=FF[0][:, c, 0:T], in0=rt[:, 0:T], in1=rt[:, 0:T],
                                                                      op=ALU.mult), reads=[rrt], writes=[FF[1]])
            nxt = p0_gens(ti + 1) if (ti + 1 < len(tiles) and stop_after is None) else []
            for hf in range(2):
                for kg in range(4):
                    pan, rpan = next_panel()
                    for b in range(nb):
                        mm_run(PS[0:tb, b, :],
                               [(FF[0][:, kg * 8 + kc, b * 128:b * 128 + tb], pan[:, kc, :]) for kc in range(8)],
                               reads=[FF[1], rpan], writes=[rB[b]], first=(kg == 0), last=(kg == 3))
                    for g_ in list(nxt):
                        try:
                            next(g_)
                        except StopIteration:
                            nxt.remove(g_)
                for b in range(nb):
                    kb.op("act", lambda e, b=b, hf=hf: e.copy(out=FO[b][0][0:tb, hf * 512:(hf + 1) * 512], in_=PS[0:tb, b, :]),
                          reads=[rB[b]], writes=[FO[b][1]])
            round_robin(nxt)
            dump("ff", FF[0][:, :, 0:T], FF[1], [128, 32, T], BF16)
            dump("fo", FO[0][0][0:tb, :], FO[0][1], [tb, 1024], F32)
            def g_fin(b):
                yield from g_post(b, FO[b], PF_NFPOST, MO[b])
                for (dst_, p0_, n_) in io_rows(ti, b, out=True):
                    kb.dma("sp", dst_, FO[b][0][p0_:p0_ + n_, :], FO[b][1], reads=[FO[b][1]])
            round_robin([g_fin(b) for b in range(nb)])
        for ti, tl in enumerate(tiles):
            try:
                run_tile(ti, *tl)
            except _Stop:
                pass
        kb.finish()
        block = es.enter_context(nc.Block())
        kb.replay(block)
    return nc


def _host_tables(p):
    pf = np.zeros((128, PF_COLS), np.float32)
    pf[:, PF_CAW:PF_CAW + 24] = p["conv_a_w"][0].reshape(3, 8, 128).transpose(2, 1, 0).reshape(128, 24)
    pf[:, PF_CSW:PF_CSW + 128] = p["conv_ssm_w"][0].reshape(4, 32, 128).transpose(2, 1, 0).reshape(128, 128)
    pf[:, PF_CSB:PF_CSB + 32] = p["conv_ssm_b"][0].reshape(32, 128).T
    pf[:, PF_BG:PF_BG + 16] = p["b_gate"][0].reshape(16, 128).T
    pf[:, PF_NMP:PF_NMP + 8] = p["norm_mix_pre"][0].reshape(8, 128).T
    pf[:, PF_NFP:PF_NFP + 8] = p["norm_ffn_pre"][0].reshape(8, 128).T
    pf[:, PF_SSN:PF_SSN + 16] = p["ssm_norm"][0].reshape(16, 128).T
    pf[:, PF_DTB:PF_DTB + 32] = p["dt_bias"][0][None, :]
    pf[:, PF_ALOG:PF_ALOG + 32] = p["a_log"][0][None, :]
    pf[:, PF_DSK:PF_DSK + 32] = p["d_skip"][0][None, :]
    pf[:, PF_NMPOST:PF_NMPOST + 1024] = p["norm_mix_post"][0][None, :]
    pf[:, PF_NFPOST:PF_NFPOST + 1024] = p["norm_ffn_post"][0][None, :]
    cb = np.zeros((128, CB_COLS), np.float32)
    k = np.arange(128)
    cb[:, CB_ID:CB_ID + 128] = np.eye(128)
    l = np.arange(64)
    cb[:, CB_TRI2:CB_TRI2 + 64] = ((k[:, None] % 64) <= l[None, :])
    same = (k[:, None] // 64) == (k[None, :] // 64)
    cb[:, CB_BONES:CB_BONES + 128] = same
    cb[:, CB_ONESA:CB_ONESA + 128] = (k[:, None] < 64) * np.ones((1, 128))
    cb[:, CB_ONESB:CB_ONESB + 128] = (k[:, None] >= 64) * np.ones((1, 128))
    cb[:, CB_TRI128:CB_TRI128 + 128] = same & ((k[:, None] % 64) <= (k[None, :] % 64))
    neg = np.where((k[:, None] % 64) > l[None, :], -30000.0, 0.0)
    cb[:, CB_NEG:CB_NEG + 1024] = np.tile(neg, (1, 16))
    return pf, cb.astype(ml_dtypes.bfloat16)


_NC_CACHE = {}


def kernel(x_prompt, x_sample, state_conv_a, state_conv_ssm, state_ssm, norm_mix_pre, w_in, b_gate, conv_a_w, w_a_out,
           conv_ssm_w, conv_ssm_b, dt_bias, a_log, d_skip, ssm_norm, w_ssm_out, w_o, norm_mix_post, norm_ffn_pre,
           w_ff1, w_ff2, norm_ffn_post, _tiles_limit=None, _cores=None, _dbg=False, _stop=None):
    p = dict(norm_mix_pre=norm_mix_pre, b_gate=b_gate, conv_a_w=conv_a_w, conv_ssm_w=conv_ssm_w, conv_ssm_b=conv_ssm_b,
             dt_bias=dt_bias, a_log=a_log, d_skip=d_skip, ssm_norm=ssm_norm, norm_mix_post=norm_mix_post,
             norm_ffn_pre=norm_ffn_pre, norm_ffn_post=norm_ffn_post)
    p = {k: np.asarray(v, np.float32) for k, v in p.items()}
    pf, cb = _host_tables(p)
    x_prompt = np.asarray(x_prompt, np.float32)
    x_sample = np.asarray(x_sample, np.float32)
    sca_all = np.asarray(state_conv_a, np.float32)[0]
    scm_all = np.asarray(state_conv_ssm, np.float32)[0]
    sst_all = np.asarray(state_ssm, np.float32)[0]
    key = (None if _tiles_limit is None else tuple(_tiles_limit), _dbg, _stop)
    if key not in _NC_CACHE:
        _NC_CACHE[key] = build_program(_tiles_limit, _dbg, _stop)
    nc = _NC_CACHE[key]
    cores = list(range(NCORES)) if _cores is None else _cores
    shared = dict(pf=pf, cb=cb, w_in=np.ascontiguousarray(np.asarray(w_in, np.float32)[0]),
                  w_a_out=np.ascontiguousarray(np.asarray(w_a_out, np.float32)[0]),
                  w_ssm_out=np.ascontiguousarray(np.asarray(w_ssm_out, np.float32)[0]),
                  w_o=np.ascontiguousarray(np.asarray(w_o, np.float32)[0]),
                  w_ff1=np.ascontiguousarray(np.asarray(w_ff1, np.float32)[0]),
                  w_ff2=np.ascontiguousarray(np.asarray(w_ff2, np.float32)[0]))
    in_maps = []
    for c in cores:
        s2 = slice(2 * c, 2 * c + 2)
        m = dict(shared)
        m["xp"] = np.ascontiguousarray(x_prompt[s2])
        m["xs"] = np.ascontiguousarray(x_sample[s2])
        m["sca"] = np.ascontiguousarray(sca_all[s2].reshape(2, 2, 8, 128).transpose(0, 3, 2, 1).reshape(2, 128, 16))
        m["scm"] = np.ascontiguousarray(scm_all[s2].reshape(2, 3, 32, 128).transpose(0, 3, 2, 1).reshape(2, 128, 96))
        m["sst"] = np.ascontiguousarray(sst_all[s2].reshape(2, 2048, 128).transpose(0, 2, 1))
        in_maps.append(m)
    res = run_bass_kernel_spmd(nc, in_maps, core_ids=list(range(len(cores))))
    R = res.results
    if _dbg:
        kernel.dbg = {k: v for k, v in R[0].items() if k.startswith("dbg_")}
    yp = np.concatenate([r["yp"] for r in R], axis=0)
    ys_ = np.concatenate([r["ys"] for r in R], axis=0)
    oca = np.stack([r["oca"] for r in R])
    ocm = np.stack([r["ocm"] for r in R])
    ost = np.stack([r["ost"] for r in R])
    n = len(cores)

    def un_ca(a):
        return a.reshape(n * 2, 128, 8, 2).transpose(0, 3, 2, 1).reshape(n * 2, 2, 1024)

    def un_cm(a):
        return a.reshape(n * 2, 128, 32, 3).transpose(0, 3, 2, 1).reshape(n * 2, 3, 4096)

    def un_st(a):
        return a.reshape(n * 2, 128, 2048).transpose(0, 2, 1).reshape(n * 2, 32, 64, 128)

    outs = (yp, ys_,
            un_ca(oca[:, 0:2])[None], un_cm(ocm[:, 0:2])[None], un_st(ost[:, 0:2])[None],
            un_ca(oca[:, 2:4])[None], un_cm(ocm[:, 2:4])[None], un_st(ost[:, 2:4])[None])
    return tuple(np.ascontiguousarray(o, dtype=np.float32) for o in outs)
```
